# Optimizing a Trainium2 kernel written in Bass

```python
import math
import jax, jax.numpy as jnp
from jax import lax
import numpy as np

D_MODEL = 1024
BATCH = 8
SEQ = 2048
DEPTH = 2
DEC_BATCH = 128
DEC_SEQ = 4
PAST_LEN = 16384
PAGE_SIZE = 128

N_MEM = 256
N_EVEN = (DEPTH + 1) // 2
N_ODD = DEPTH // 2
A_WIDTH = D_MODEL // 2
A_DK = 128
A_HEADS = A_WIDTH // A_DK
A_DV = A_WIDTH // A_HEADS
B_WIDTH = D_MODEL - A_WIDTH
B_GROUP = 16
B_GROUPS = B_WIDTH // B_GROUP
B_STATE = 64
S5_MAX_RE = -1e-4
C_HEADS = 4
C_KEY = D_MODEL // 2
C_VAL = D_MODEL
C_DK = C_KEY // C_HEADS
C_DV = C_VAL // C_HEADS
C_GATE_RANK = 16
C_GATE_TAU = 16.0
CHUNK = 64
X_HEADS = 4
X_HD = D_MODEL // X_HEADS
FFN_DIM = 2816
CONV_W = 3
EPS = 1e-6

AB_COLS = 4 * A_WIDTH + B_WIDTH
C_COLS = 2 * C_KEY + 2 * C_VAL + C_GATE_RANK

kernel_name = "hgrn2_s5_gla_memxattn_convffn_step"


def rms_norm(x, g):
    xf = x.astype(jnp.float32)
    y = xf * lax.rsqrt(jnp.mean(xf * xf, axis=-1, keepdims=True) + EPS)
    return (y * g.astype(jnp.float32)).astype(x.dtype)


def gated_linear_attention(q, k, v, log_g, s0):
    bsz, L, H, _ = q.shape
    dv = v.shape[-1]
    c = math.gcd(CHUNK, L)
    n = L // c

    def to_chunks(t):
        return jnp.moveaxis(t.reshape(bsz, n, c, H, t.shape[-1]), 1, 0)

    qs, ks, vs, gs = to_chunks(q), to_chunks(k), to_chunks(v), to_chunks(log_g.astype(jnp.float32))
    mask = jnp.tril(jnp.ones((c, c), bool))[None, :, :, None, None]

    def step(S, inp):
        qc, kc, vc, gc = inp
        b = jnp.cumsum(gc, axis=1)
        b_last = b[:, -1:]
        diff = b[:, :, None] - b[:, None]
        decay = jnp.where(mask, jnp.exp(jnp.where(mask, diff, 0.0)), 0.0)
        scores = jnp.sum(qc[:, :, None] * kc[:, None] * decay, axis=-1)
        o = (jnp.einsum('bijh,bjhv->bihv', scores, vc)
             + jnp.einsum('bihk,bhkv->bihv', qc * jnp.exp(b), S))
        S_new = (jnp.exp(b_last[:, 0])[..., None] * S
                 + jnp.einsum('bjhk,bjhv->bhkv', kc * jnp.exp(b_last - b), vc))
        return S_new, o

    S, o = lax.scan(step, s0.astype(jnp.float32), (qs, ks, vs, gs))
    o = jnp.moveaxis(o, 0, 1).reshape(bsz, L, H, dv)
    return o, S


def hgrn2_mixer(q, f, i, g, lb, gnorm, s0):
    bsz, L, _ = q.shape
    forget = lb + (1.0 - lb) * jax.nn.sigmoid(f.astype(jnp.float32))
    key = 1.0 - forget
    log_g = jnp.log(forget)
    qh = jax.nn.silu(q).reshape(bsz, L, A_HEADS, A_DK)
    o, S = gated_linear_attention(qh, key.reshape(bsz, L, A_HEADS, A_DK),
                                  i.reshape(bsz, L, A_HEADS, A_DV),
                                  log_g.reshape(bsz, L, A_HEADS, A_DK), s0)
    o = rms_norm(o, gnorm) * jax.nn.silu(g.reshape(bsz, L, A_HEADS, A_DV).astype(jnp.float32))
    return o.reshape(bsz, L, A_WIDTH), S


def s5_mixer(u, lam_re, lam_im, log_step, b_re, b_im, c_re, c_im, d, w_glu, b_glu, x0_re, x0_im):
    bsz, L, _ = u.shape
    ug = u.reshape(bsz, L, B_GROUPS, B_GROUP).astype(jnp.float32)
    lr = jnp.minimum(lam_re.astype(jnp.float32), S5_MAX_RE)
    li = lam_im.astype(jnp.float32)
    dt = jnp.exp(log_step.astype(jnp.float32))[:, None]
    mag = jnp.exp(lr * dt)
    a_re = mag * jnp.cos(li * dt)
    a_im = mag * jnp.sin(li * dt)
    den = lr * lr + li * li
    z_re = ((a_re - 1.0) * lr + a_im * li) / den
    z_im = (a_im * lr - (a_re - 1.0) * li) / den
    bb_re = z_re[..., None] * b_re - z_im[..., None] * b_im
    bb_im = z_re[..., None] * b_im + z_im[..., None] * b_re
    bu_re = jnp.einsum('blgp,gnp->blgn', ug, bb_re)
    bu_im = jnp.einsum('blgp,gnp->blgn', ug, bb_im)
    a_re_t = jnp.broadcast_to(a_re, (1, L, B_GROUPS, B_STATE))
    a_im_t = jnp.broadcast_to(a_im, (1, L, B_GROUPS, B_STATE))

    def combine(e1, e2):
        a1r, a1i, h1r, h1i = e1
        a2r, a2i, h2r, h2i = e2
        return (a2r * a1r - a2i * a1i, a2r * a1i + a2i * a1r,
                a2r * h1r - a2i * h1i + h2r, a2r * h1i + a2i * h1r + h2i)

    pr, pi, hr, hi = lax.associative_scan(combine, (a_re_t, a_im_t, bu_re, bu_im), axis=1)
    x0r = x0_re.astype(jnp.float32)[:, None]
    x0i = x0_im.astype(jnp.float32)[:, None]
    xr = hr + pr * x0r - pi * x0i
    xi = hi + pr * x0i + pi * x0r
    y = (jnp.einsum('blgn,gpn->blgp', xr, c_re) - jnp.einsum('blgn,gpn->blgp', xi, c_im)
         + d * ug)
    y = jax.nn.gelu(y.reshape(bsz, L, B_WIDTH))
    y = y * jax.nn.sigmoid(y @ w_glu + b_glu)
    return y, xr[:, -1], xi[:, -1]


def gla_mixer(proj, w_gate_up, b_gate, gnorm, s0):
    bsz, L, _ = proj.shape
    q, k, v, r, gd = jnp.split(proj, [C_KEY, 2 * C_KEY, 2 * C_KEY + C_VAL, 2 * C_KEY + 2 * C_VAL], axis=-1)
    log_a = jax.nn.log_sigmoid((gd @ w_gate_up + b_gate).astype(jnp.float32)) / C_GATE_TAU
    o, S = gated_linear_attention((q * (C_DK ** -0.5)).reshape(bsz, L, C_HEADS, C_DK),
                                  k.reshape(bsz, L, C_HEADS, C_DK),
                                  v.reshape(bsz, L, C_HEADS, C_DV),
                                  log_a.reshape(bsz, L, C_HEADS, C_DK), s0)
    o = rms_norm(o, gnorm) * jax.nn.silu(r.reshape(bsz, L, C_HEADS, C_DV).astype(jnp.float32))
    return o.reshape(bsz, L, C_VAL), S


def memory_kv(mem, g, w_kv):
    bsz, m, _ = mem.shape
    k, v = jnp.split(rms_norm(mem, g) @ w_kv, 2, axis=-1)
    return k.reshape(bsz, m, X_HEADS, X_HD), v.reshape(bsz, m, X_HEADS, X_HD)


def cross_attend(h, mk, mv, w_q, w_o):
    bsz, L, _ = h.shape
    q = (h @ w_q).reshape(bsz, L, X_HEADS, X_HD)
    s = jnp.einsum('blhd,bmhd->bhlm', q, mk).astype(jnp.float32) * (X_HD ** -0.5)
    p = jax.nn.softmax(s, axis=-1).astype(mv.dtype)
    o = jnp.einsum('bhlm,bmhd->blhd', p, mv).reshape(bsz, L, D_MODEL)
    return o.astype(h.dtype) @ w_o


def conv_ffn(h, w_up, conv_w, conv_b, w_down, prev):
    L = h.shape[1]
    up = h @ w_up
    ext = jnp.concatenate([prev.astype(up.dtype), up], axis=1)
    c = conv_b + conv_w[0] * ext[:, 0:L]
    for j in range(1, CONV_W):
        c = c + conv_w[j] * ext[:, j:j + L]
    a, b = jnp.split(c, 2, axis=-1)
    y = (jax.nn.silu(a) * b).astype(h.dtype)
    return y @ w_down, ext[:, L:]


def trunk(x, mem_k, mem_v, s_hgrn, s5_re, s5_im, s_gla, s_conv, p):
    lb_all = jnp.cumsum(jax.nn.softmax(p['hgrn_lb'].astype(jnp.float32), axis=0), axis=0)
    new_hgrn, new_s5r, new_s5i, new_gla, new_conv = [], [], [], [], []
    for l in range(DEPTH):
        h = rms_norm(x, p['norm_mix'][l])
        if l % 2 == 0:
            e = l // 2
            proj = h @ p['w_in_ab'][e]
            q, f, i, g, u = jnp.split(proj, [A_WIDTH, 2 * A_WIDTH, 3 * A_WIDTH, 4 * A_WIDTH], axis=-1)
            o_a, s_a = hgrn2_mixer(q, f, i, g, lb_all[l], p['hgrn_gnorm'][e], s_hgrn[e])
            o_b, sr, si = s5_mixer(u, p['s5_lam_re'][e], p['s5_lam_im'][e], p['s5_log_step'][e],
                                   p['s5_b_re'][e], p['s5_b_im'][e], p['s5_c_re'][e], p['s5_c_im'][e],
                                   p['s5_d'][e], p['s5_w_glu'][e], p['s5_b_glu'][e], s5_re[e], s5_im[e])
            mixed = jnp.concatenate([o_a.astype(x.dtype), o_b.astype(x.dtype)], axis=-1) @ p['w_out_ab'][e]
            new_hgrn.append(s_a)
            new_s5r.append(sr)
            new_s5i.append(si)
        else:
            o_idx = l // 2
            o_c, s_c = gla_mixer(h @ p['w_in_c'][o_idx], p['gla_w_gate_up'][o_idx], p['gla_b_gate'][o_idx],
                                 p['gla_gnorm'][o_idx], s_gla[o_idx])
            mixed = o_c.astype(x.dtype) @ p['w_out_c'][o_idx]
            new_gla.append(s_c)
        x = x + mixed
        h = rms_norm(x, p['norm_cross'][l])
        x = x + cross_attend(h, mem_k[l], mem_v[l], p['xa_w_q'][l], p['xa_w_o'][l])
        h = rms_norm(x, p['norm_ffn'][l])
        ff, conv_state = conv_ffn(h, p['ffn_w_up'][l], p['ffn_conv_w'][l], p['ffn_conv_b'][l],
                                  p['ffn_w_down'][l], s_conv[l])
        x = x + ff
        new_conv.append(conv_state)
    y = rms_norm(x, p['norm_final'])
    return (y, jnp.stack(new_hgrn), jnp.stack(new_s5r), jnp.stack(new_s5i),
            jnp.stack(new_gla), jnp.stack(new_conv))


def setup_inputs(seed: int = 0) -> dict:
    key = jax.random.key(seed)
    ks = iter(jax.random.split(key, 64))

    def nrm(shape, scale):
        return scale * jax.random.normal(next(ks), shape, jnp.float32)

    D = D_MODEL
    F2 = 2 * FFN_DIM
    lam_im_base = jnp.pi * jnp.arange(B_STATE, dtype=jnp.float32)
    return {
        'x_prompt': nrm((BATCH, SEQ, D), 1.0),
        'x_sample': nrm((DEC_BATCH, DEC_SEQ, D), 1.0),
        'mem_prompt': nrm((BATCH, N_MEM, D), 1.0),
        'cache_mem_k': nrm((DEPTH, DEC_BATCH, N_MEM, X_HEADS, X_HD), 1.0),
        'cache_mem_v': nrm((DEPTH, DEC_BATCH, N_MEM, X_HEADS, X_HD), 1.0),
        'state_hgrn': nrm((N_EVEN, DEC_BATCH, A_HEADS, A_DK, A_DV), 0.5),
        'state_s5_re': nrm((N_EVEN, DEC_BATCH, B_GROUPS, B_STATE), 0.1),
        'state_s5_im': nrm((N_EVEN, DEC_BATCH, B_GROUPS, B_STATE), 0.1),
        'state_gla': nrm((N_ODD, DEC_BATCH, C_HEADS, C_DK, C_DV), 1.0),
        'state_ffn_conv': nrm((DEPTH, DEC_BATCH, CONV_W - 1, F2), 1.0),
        'norm_mix': 1.0 + nrm((DEPTH, D), 0.01),
        'norm_cross': 1.0 + nrm((DEPTH, D), 0.01),
        'norm_mem': 1.0 + nrm((DEPTH, D), 0.01),
        'norm_ffn': 1.0 + nrm((DEPTH, D), 0.01),
        'norm_final': 1.0 + nrm((D,), 0.01),
        'w_in_ab': nrm((N_EVEN, D, AB_COLS), D ** -0.5),
        'hgrn_lb': nrm((DEPTH + 1, A_WIDTH), 0.1),
        'hgrn_gnorm': 1.0 + nrm((N_EVEN, A_DV), 0.01),
        's5_lam_re': -0.5 + nrm((N_EVEN, B_GROUPS, B_STATE), 0.01),
        's5_lam_im': lam_im_base + nrm((N_EVEN, B_GROUPS, B_STATE), 0.01),
        's5_log_step': jax.random.uniform(next(ks), (N_EVEN, B_GROUPS), jnp.float32,
                                          math.log(1e-3), math.log(1e-1)),
        's5_b_re': nrm((N_EVEN, B_GROUPS, B_STATE, B_GROUP), (2 * B_GROUP) ** -0.5),
        's5_b_im': nrm((N_EVEN, B_GROUPS, B_STATE, B_GROUP), (2 * B_GROUP) ** -0.5),
        's5_c_re': nrm((N_EVEN, B_GROUPS, B_GROUP, B_STATE), (2 * B_STATE) ** -0.5),
        's5_c_im': nrm((N_EVEN, B_GROUPS, B_GROUP, B_STATE), (2 * B_STATE) ** -0.5),
        's5_d': nrm((N_EVEN, B_GROUPS, B_GROUP), 1.0),
        's5_w_glu': nrm((N_EVEN, B_WIDTH, B_WIDTH), B_WIDTH ** -0.5),
        's5_b_glu': nrm((N_EVEN, B_WIDTH), 0.01),
        'w_out_ab': nrm((N_EVEN, A_WIDTH + B_WIDTH, D), (A_WIDTH + B_WIDTH) ** -0.5),
        'w_in_c': nrm((N_ODD, D, C_COLS), D ** -0.5),
        'gla_w_gate_up': nrm((N_ODD, C_GATE_RANK, C_KEY), C_GATE_RANK ** -0.5),
        'gla_b_gate': nrm((N_ODD, C_KEY), 0.1),
        'gla_gnorm': 1.0 + nrm((N_ODD, C_DV), 0.01),
        'w_out_c': nrm((N_ODD, C_VAL, D), C_VAL ** -0.5),
        'xa_w_q': nrm((DEPTH, D, D), D ** -0.5),
        'xa_w_kv': nrm((DEPTH, D, 2 * D), D ** -0.5),
        'xa_w_o': nrm((DEPTH, D, D), D ** -0.5),
        'ffn_w_up': nrm((DEPTH, D, F2), D ** -0.5),
        'ffn_conv_w': nrm((DEPTH, CONV_W, F2), CONV_W ** -0.5),
        'ffn_conv_b': nrm((DEPTH, F2), 0.01),
        'ffn_w_down': nrm((DEPTH, FFN_DIM, D), FFN_DIM ** -0.5),
    }


def reference(x_prompt, x_sample, mem_prompt, cache_mem_k, cache_mem_v, state_hgrn, state_s5_re,
              state_s5_im, state_gla, state_ffn_conv, norm_mix, norm_cross, norm_mem, norm_ffn,
              norm_final, w_in_ab, hgrn_lb, hgrn_gnorm, s5_lam_re, s5_lam_im, s5_log_step, s5_b_re,
              s5_b_im, s5_c_re, s5_c_im, s5_d, s5_w_glu, s5_b_glu, w_out_ab, w_in_c, gla_w_gate_up,
              gla_b_gate, gla_gnorm, w_out_c, xa_w_q, xa_w_kv, xa_w_o, ffn_w_up, ffn_conv_w,
              ffn_conv_b, ffn_w_down):
    p = dict(norm_mix=norm_mix, norm_cross=norm_cross, norm_ffn=norm_ffn, norm_final=norm_final,
             w_in_ab=w_in_ab, hgrn_lb=hgrn_lb, hgrn_gnorm=hgrn_gnorm, s5_lam_re=s5_lam_re,
             s5_lam_im=s5_lam_im, s5_log_step=s5_log_step, s5_b_re=s5_b_re, s5_b_im=s5_b_im,
             s5_c_re=s5_c_re, s5_c_im=s5_c_im, s5_d=s5_d, s5_w_glu=s5_w_glu, s5_b_glu=s5_b_glu,
             w_out_ab=w_out_ab, w_in_c=w_in_c, gla_w_gate_up=gla_w_gate_up, gla_b_gate=gla_b_gate,
             gla_gnorm=gla_gnorm, w_out_c=w_out_c, xa_w_q=xa_w_q, xa_w_o=xa_w_o,
             ffn_w_up=ffn_w_up, ffn_conv_w=ffn_conv_w, ffn_conv_b=ffn_conv_b, ffn_w_down=ffn_w_down)

    mks, mvs = [], []
    for l in range(DEPTH):
        mk, mv = memory_kv(mem_prompt, norm_mem[l], xa_w_kv[l])
        mks.append(mk)
        mvs.append(mv)
    mem_k_p = jnp.stack(mks)
    mem_v_p = jnp.stack(mvs)
    f32 = jnp.float32
    y_prompt, hgrn_p, s5_re_p, s5_im_p, gla_p, conv_p = trunk(
        x_prompt, mem_k_p, mem_v_p,
        jnp.zeros((N_EVEN, BATCH, A_HEADS, A_DK, A_DV), f32),
        jnp.zeros((N_EVEN, BATCH, B_GROUPS, B_STATE), f32),
        jnp.zeros((N_EVEN, BATCH, B_GROUPS, B_STATE), f32),
        jnp.zeros((N_ODD, BATCH, C_HEADS, C_DK, C_DV), f32),
        jnp.zeros((DEPTH, BATCH, CONV_W - 1, 2 * FFN_DIM), f32), p)

    y_sample, hgrn_s, s5_re_s, s5_im_s, gla_s, conv_s = trunk(
        x_sample, cache_mem_k, cache_mem_v, state_hgrn, state_s5_re, state_s5_im, state_gla,
        state_ffn_conv, p)

    return (y_prompt, y_sample, hgrn_p, s5_re_p, s5_im_p, gla_p, mem_k_p, mem_v_p, conv_p,
            hgrn_s, s5_re_s, s5_im_s, gla_s, conv_s)
```

```python
import contextlib
import math
import numpy as np
import concourse.bass as bass
import concourse.mybir as mybir
from concourse.bass_utils import run_bass_kernel_spmd

F32 = mybir.dt.float32
BF16 = mybir.dt.bfloat16
AF = mybir.ActivationFunctionType
ALU = mybir.AluOpType

D = 1024
KD = 8
NSEQ_S = 16
LS = 4
NS = NSEQ_S * LS
FFN = 2816
NFC = 22
NMEM = 256
EPS = 1e-6
GC = 64


class Dep:
    __slots__ = ("writer", "readers", "excl")

    def __init__(self, excl=False):
        self.writer = None
        self.readers = {}
        self.excl = excl


class Eng:
    def __init__(self, name, h, sem, is_pe=False):
        self.name, self.h, self.sem, self.is_pe = name, h, sem, is_pe
        self.cnt = 0
        self.waited = {}
        self.n_ins = 0

    def wait(self, ev):
        sem, val = ev
        if sem is self.sem and self.is_pe:
            return
        if self.waited.get(sem, 0) >= val:
            return
        self.waited[sem] = val
        self.h.wait_ge(sem, val)


class Ctx:
    def __init__(self, nc, stack, n_dma_sems=(20, 28, 8)):
        self.nc = nc
        mk = lambda n: stack.enter_context(nc.semaphore(n))
        self.E = {
            "pe": Eng("pe", nc.tensor, mk("s_pe"), is_pe=True),
            "act": Eng("act", nc.scalar, mk("s_act")),
            "dve": Eng("dve", nc.vector, mk("s_dve")),
            "pool": Eng("pool", nc.gpsimd, mk("s_pool")),
            "sp": Eng("sp", nc.sync, mk("s_sp")),
        }
        self.dma_sems = {}
        for q, n in zip(("sp", "pool", "act"), n_dma_sems):
            self.dma_sems[q] = [[mk(f"d_{q}{i}"), 0] for i in range(n)]
        self.dma_rr = {"sp": 0, "pool": 0, "act": 0}

    def _pre(self, E, reads, writes):
        for d in reads:
            if d.writer is not None:
                E.wait(d.writer)
            if d.excl:
                for ev in d.readers.values():
                    if ev[0] is not E.sem:
                        E.wait(ev)
        for d in writes:
            if d.writer is not None:
                E.wait(d.writer)
            for ev in d.readers.values():
                E.wait(ev)

    @staticmethod
    def _post(ev, reads, writes):
        for d in reads:
            d.readers[ev[0]] = ev
        for d in writes:
            d.writer = ev
            d.readers = {}

    def op(self, eng, fn, reads=(), writes=(), inc=True):
        E = self.E[eng]
        self._pre(E, reads, writes)
        ins = fn(E.h)
        E.n_ins += 1
        if inc:
            E.cnt += 1
            ins.then_inc(E.sem, 1)
            ev = (E.sem, E.cnt)
        else:
            ev = (E.sem, E.cnt + 1)
        self._post(ev, reads, writes)
        return ins

    def dma(self, q, out, in_, reads=(), writes=(), **kw):
        E = self.E[q]
        self._pre(E, reads, writes)
        pool = self.dma_sems[q]
        i = self.dma_rr[q]
        self.dma_rr[q] = (i + 1) % len(pool)
        slot = pool[i]
        if slot[1] > 0:
            E.wait((slot[0], slot[1]))
        slot[1] += 16
        ins = E.h.dma_start(out=out, in_=in_, **kw)
        ins.then_inc(slot[0], 16)
        E.n_ins += 1
        self._post((slot[0], slot[1]), reads, writes)
        return ins

    def barrier(self, engines=("pe", "act", "dve", "pool", "sp")):
        for en in engines:
            E = self.E[en]
            for F in self.E.values():
                if F is not E and F.cnt > 0:
                    E.wait((F.sem, F.cnt))
            for pool in self.dma_sems.values():
                for slot in pool:
                    if slot[1] > 0:
                        E.wait((slot[0], slot[1]))


class View:
    def __init__(self, ap):
        self.ap = ap
        self.d = Dep()

    def __getitem__(self, k):
        return self.ap[k]


class Buf:
    def __init__(self, t):
        self.t = t
        self.d = Dep()

    def __getitem__(self, k):
        return self.t[k]


class Cfg:
    def __init__(self, LP=2048, phases=("A0", "X0", "F0", "A1", "X1", "F1")):
        self.LP = LP
        self.T = LP + NS
        self.phases = phases
        self.NTA = 256
        self.xmode = "full"
        self.NTB = 128


def tiles_of(LP, n):
    ts = [(i, n, False) for i in range(0, LP, n)]
    ts.append((LP, NS, True))
    return ts


def build(cfg):
    nc = bass.Bass("TRN2", target_bir_lowering=False)
    LP, T = cfg.LP, cfg.T
    P = 128

    def din(name, shape, dt=F32):
        return nc.dram_tensor(name, list(shape), dt, kind="ExternalInput").ap()

    def dout(name, shape):
        return nc.dram_tensor(name, list(shape), F32, kind="ExternalOutput").ap()

    I = {}
    I["xT"] = din("xT", [KD, P, T])
    I["ident"] = din("ident", [P, P])
    I["norms"] = din("norms", [P, 7, KD])
    I["ffn_up"] = din("ffn_up", [2, KD, P, 2 * FFN])
    I["ffn_dn"] = din("ffn_dn", [2, NFC, P, D])
    I["convw"] = din("convw", [2, P, 2 * NFC, 3])
    I["convb"] = din("convb", [2, P, 2 * NFC])
    I["convst"] = din("convst", [2, P, 2 * NFC, NSEQ_S, 2])
    I["cmask"] = din("cmask", [P, 464])
    I["cmask2"] = din("cmask2", [P, 384])
    I["w_in_c"] = din("w_in_c", [KD, P, 3088])
    I["w_out_c"] = din("w_out_c", [KD, P, D])
    I["gla_wgu"] = din("gla_wgu", [16, 512])
    I["gla_small"] = din("gla_small", [P, 6])
    I["st_gla"] = din("st_gla", [NSEQ_S, 4, P, 256])
    I["w_in_ab"] = din("w_in_ab", [KD, P, 2560])
    I["w_out_ab"] = din("w_out_ab", [KD, P, D])
    I["hg_small"] = din("hg_small", [P, 13])
    I["st_hgrn"] = din("st_hgrn", [NSEQ_S, 4, P, 128])
    I["s5_sm"] = din("s5_sm", [P, 3, 16])
    I["s5_row"] = din("s5_row", [3, 2048])
    I["s5_bexp"] = din("s5_bexp", [2, P, 4, 512])
    I["s5_cexp"] = din("s5_cexp", [2, P, 16, 128])
    I["s5_small"] = din("s5_small", [P, 8])
    I["s5_wglu"] = din("s5_wglu", [4, P, 512])
    I["s5_x0"] = din("s5_x0", [2, P, 16, NSEQ_S])
    I["memT"] = din("memT", [KD, P, NMEM])
    I["normmem"] = din("normmem", [P, 2, KD])
    I["wkv"] = din("wkv", [2, KD, P, 2 * D])
    I["wq"] = din("wq", [2, KD, P, D])
    I["wo"] = din("wo", [2, KD, P, D])
    I["kTc"] = din("kTc", [2, NSEQ_S, KD, P, NMEM])
    I["vc"] = din("vc", [2, NSEQ_S, 2, P, D])
    O = {}
    O["mem_kT"] = dout("mem_kT", [2, KD, P, NMEM])
    O["mem_v"] = dout("mem_v", [2, 2, P, D])
    O["hgrn_p"] = dout("hgrn_p", [4, P, 128])
    O["hgrn_s"] = dout("hgrn_s", [NSEQ_S, 4, P, 128])
    O["s5_p"] = dout("s5_p", [2, P, 16])
    O["s5_s"] = dout("s5_s", [2, P, 16, NSEQ_S])
    O["gla_p"] = dout("gla_p", [4, P, 256])
    O["gla_s"] = dout("gla_s", [NSEQ_S, 4, P, 256])
    O["yT"] = dout("yT", [KD, P, T])
    O["conv_p"] = dout("conv_p", [2, P, 2 * NFC, 2])
    O["conv_s"] = dout("conv_s", [2, P, 2 * NFC, NSEQ_S, 2])

    with contextlib.ExitStack() as st:
        cx = Ctx(nc, st)

        uid = [0]

        def sb(name, shape, dt, stack=st):
            uid[0] += 1
            return Buf(stack.enter_context(nc.sbuf_tensor(f"s{uid[0]}_{name}", list(shape), dt)))

        banks = [Buf(st.enter_context(nc.psum_tensor(f"bank{i}", [P, 512], F32))) for i in range(8)]
        for b_ in banks:
            b_.d.excl = True
        bank_rr = [0]

        reserved = set()

        def bank():
            while True:
                i = bank_rr[0]
                bank_rr[0] = (bank_rr[0] + 1) % 8
                if i not in reserved:
                    return banks[i]

        def reserve():
            b = bank()
            reserved.add(banks.index(b))
            return b

        def release(b):
            reserved.discard(banks.index(b))

        x = sb("x", [P, KD, T], F32)
        xd = {}
        for s0 in range(0, T, 256):
            xd[s0] = Dep()

        def xdeps(s0, n):
            return [xd[k] for k in range((s0 // 256) * 256, s0 + n, 256)]

        ident = sb("ident", [P, P], F32)
        ident16 = sb("ident16", [P, P], BF16)
        ones16 = sb("ones16", [P, P], BF16)
        norms = sb("norms", [P, 7, KD], F32)
        outdep = Dep()

        cx.dma("sp", ident[:], I["ident"], writes=[ident.d])
        cx.dma("sp", norms[:], I["norms"], writes=[norms.d])
        cx.op("dve", lambda e: e.tensor_copy(ident16[:], ident[:]), reads=[ident.d], writes=[ident16.d])
        cx.op("dve", lambda e: e.memset(ones16[:], 1.0), writes=[ones16.d])
        for s0 in range(0, T, 256):
            n = min(256, T - s0)
            cx.dma("sp", x[:, :, s0:s0 + n], I["xT"][:, :, s0:s0 + n].rearrange("k p t -> p k t"),
                   writes=[xd[s0]])

        def rmsnorm(src_fn, src_deps, n, gidx, out_fn, out_deps, scr, nk=KD, dim=D, out_eng="dve"):
            sq, rstd = scr["sq16"], scr["rstd"]
            for k in range(nk):
                cx.op("act", lambda e, k=k: e.activation(sq[:, k, :n], src_fn(k), AF.Square),
                      reads=src_deps, writes=[sq.d])
            pb = bank()
            for k in range(nk):
                cx.op("pe", lambda e, k=k: e.matmul(pb[:, :n], ones16[:], sq[:, k, :n], start=(k == 0), stop=(k == nk - 1)),
                      reads=[sq.d, ones16.d], writes=[pb.d], inc=(k == nk - 1))
            cx.op("act", lambda e: e.activation(rstd[:, :n], pb[:, :n], AF.Ln, bias=scr["eps"][:, 0:1], scale=1.0 / dim),
                  reads=[pb.d, scr["eps"].d], writes=[rstd.d])
            cx.op("act", lambda e: e.activation(rstd[:, :n], rstd[:, :n], AF.Exp, scale=-0.5), reads=[rstd.d], writes=[rstd.d])
            for k in range(nk):
                cx.op(out_eng, lambda e, k=k: e.scalar_tensor_tensor(out_fn(k), src_fn(k), norms[:, gidx, k:k + 1], rstd[:, :n],
                                                                      ALU.mult, ALU.mult),
                      reads=list(src_deps) + [rstd.d, norms.d], writes=out_deps)

        epsb = sb("epsb", [P, 1], F32)
        cx.op("dve", lambda e: e.memset(epsb[:], EPS), writes=[epsb.d])

        def ffn_phase(l):
            with contextlib.ExitStack() as ph:
                hall = sb("hall", [P, KD, T], BF16, ph)
                hd = {s0: Dep() for s0 in range(0, T, 512)}
                cw = sb("cw", [P, 2 * NFC, 3], F32, ph)
                cb = sb("cb", [P, 2 * NFC], F32, ph)
                cst = sb("cst", [P, 2 * NFC, NSEQ_S, 2], F32, ph)
                tail = sb("tail", [P, 2 * NFC, 2], F32, ph)
                cso = sb("cso", [P, 2 * NFC, NSEQ_S, 2], F32, ph)
                cx.dma("sp", cw[:], I["convw"][l], writes=[cw.d])
                cx.dma("sp", cb[:], I["convb"][l], writes=[cb.d])
                cx.dma("sp", cst[:], I["convst"][l], writes=[cst.d])
                cx.op("dve", lambda e: e.memset(tail[:], 0.0), writes=[tail.d])
                tl = tiles_of(LP, 512)
                GS = 4
                groups = [(c0, min(GS, NFC - c0)) for c0 in range(0, NFC, GS)]
                NG = len(groups)
                RING = 2
                wup = [sb(f"wup{i}", [P, KD, 2, 128 * GS], BF16, ph) for i in range(RING)]
                wdn = [sb(f"wdn{i}", [P, GS, D], BF16, ph) for i in range(RING)]
                upv = I["ffn_up"][l].rearrange("k p n -> p k n")

                def load_group(g):
                    r = g % RING
                    c0, gs = groups[g]
                    for half in range(2):
                        cstart = half * FFN + 128 * c0
                        cx.dma("pool", wup[r][:, :, half, :128 * gs], upv[:, :, cstart:cstart + 128 * gs], writes=[wup[r].d])
                    cx.dma("pool", wdn[r][:, :gs, :], I["ffn_dn"][l][c0:c0 + gs].rearrange("j p n -> p j n"), writes=[wdn[r].d])

                load_group(0)
                scr = {"sq16": sb("f_sq16", [P, KD, 512], BF16, ph), "rstd": sb("f_rstd", [P, 512], F32, ph), "eps": epsb}
                for (s0, n, smp) in tl:
                    rmsnorm(lambda k: x[:, k, s0:s0 + n], xdeps(s0, n), n, 4 + l,
                            lambda k: hall[:, k, s0:s0 + n], [hd[s0]], scr)
                ext = [sb(f"ext{i}", [P, 512 + 2 * NSEQ_S], F32, ph) for i in range(3)]
                cc = [sb(f"cc{i}", [P, 512], F32, ph) for i in range(3)]
                sa = [sb(f"sa{i}", [P, 512], F32, ph) for i in range(2)]
                yb = [sb(f"yb{i}", [P, GS, 512], BF16, ph) for i in range(2)]
                itc = [0]
                steps = [(g, ti) for g in range(NG) for ti in range(len(tl))]

                def up_part(si):
                    g, ti = steps[si]
                    s0, n, smp = tl[ti]
                    r = g % RING
                    c0, gs = groups[g]
                    if True:
                        nseq, L = (NSEQ_S, LS) if smp else (1, n)
                        yt = yb[si % 2]
                        for j in range(gs):
                            it = itc[0]
                            ch = [c0 + j, NFC + c0 + j]
                            cres = []
                            for half in range(2):
                                c = ch[half]
                                pb = bank()
                                for k in range(KD):
                                    cx.op("pe", lambda e, k=k, half=half, pb=pb: e.matmul(
                                        pb[:, :n], wup[r][:, k, half, 128 * j:128 * j + 128], hall[:, k, s0:s0 + n],
                                        start=(k == 0), stop=(k == KD - 1)),
                                        reads=[wup[r].d, hd[s0]], writes=[pb.d], inc=(k == KD - 1))
                                et = ext[(2 * it + half) % 3]
                                ct = cc[(2 * it + half) % 3]
                                ev = et[:, :nseq * (L + 2)].rearrange("p (s l) -> p s l", s=nseq)
                                pv = pb[:, :n].rearrange("p (s l) -> p s l", s=nseq)
                                cv = ct[:, :n].rearrange("p (s l) -> p s l", s=nseq)
                                if smp:
                                    cx.op("pool", lambda e, ev=ev, c=c: e.tensor_copy(ev[:, :, 0:2], cst[:, c, :, :]),
                                          reads=[cst.d], writes=[et.d])
                                else:
                                    cx.op("pool", lambda e, ev=ev, c=c: e.tensor_copy(ev[:, :, 0:2], tail[:, c:c + 1, :]),
                                          reads=[tail.d], writes=[et.d])
                                cx.op("act", lambda e, ev=ev, pv=pv: e.copy(ev[:, :, 2:L + 2], pv), reads=[pb.d], writes=[et.d])
                                cx.op("act", lambda e, cv=cv, pv=pv, c=c: e.activation(cv, pv, AF.Identity, bias=cb[:, c:c + 1],
                                                                                      scale=cw[:, c, 2:3]),
                                      reads=[pb.d, cw.d, cb.d], writes=[ct.d])
                                cx.op("dve", lambda e, cv=cv, ev=ev, c=c: e.scalar_tensor_tensor(cv, ev[:, :, 1:L + 1], cw[:, c, 1:2], cv,
                                                                                              ALU.mult, ALU.add),
                                      reads=[et.d, cw.d, ct.d], writes=[ct.d])
                                cx.op("dve", lambda e, cv=cv, ev=ev, c=c: e.scalar_tensor_tensor(cv, ev[:, :, 0:L], cw[:, c, 0:1], cv,
                                                                                              ALU.mult, ALU.add),
                                      reads=[et.d, cw.d, ct.d], writes=[ct.d])
                                if smp:
                                    cx.op("pool", lambda e, ev=ev, c=c: e.tensor_copy(cso[:, c, :, :], ev[:, :, L:L + 2]),
                                          reads=[et.d], writes=[cso.d])
                                else:
                                    cx.op("pool", lambda e, ev=ev, c=c: e.tensor_copy(tail[:, c:c + 1, :], ev[:, :, L:L + 2]),
                                          reads=[et.d], writes=[tail.d])
                                cres.append(ct)
                            st_ = sa[it % 2]
                            cx.op("act", lambda e, st_=st_, ct=cres[0]: e.activation(st_[:, :n], ct[:, :n], AF.Silu),
                                  reads=[cres[0].d], writes=[st_.d])
                            cx.op("dve", lambda e, st_=st_, ct=cres[1], yt=yt: e.tensor_tensor(yt[:, j, :n], st_[:, :n], ct[:, :n], ALU.mult),
                                  reads=[st_.d, cres[1].d], writes=[yt.d])
                            itc[0] += 1

                def down_part(si):
                    g, ti = steps[si]
                    s0, n, smp = tl[ti]
                    r = g % RING
                    c0, gs = groups[g]
                    yt = yb[si % 2]
                    if True:
                        for oc in range(KD):
                            pb = bank()
                            for j in range(gs):
                                cx.op("pe", lambda e, j=j, pb=pb, oc=oc, yt=yt: e.matmul(
                                    pb[:, :n], wdn[r][:, j, 128 * oc:128 * oc + 128], yt[:, j, :n], start=(j == 0), stop=(j == gs - 1)),
                                    reads=[wdn[r].d, yt.d], writes=[pb.d], inc=(j == gs - 1))
                            cx.op("dve", lambda e, pb=pb, oc=oc: e.tensor_tensor(x[:, oc, s0:s0 + n], x[:, oc, s0:s0 + n], pb[:, :n], ALU.add),
                                  reads=[pb.d] + xdeps(s0, n), writes=xdeps(s0, n))

                for si in range(len(steps) + 1):
                    if si < len(steps):
                        up_part(si)
                    if si >= 1:
                        down_part(si - 1)
                    if si < len(steps) and steps[si][1] == 0:
                        g = steps[si][0]
                        if g + RING - 1 < NG:
                            load_group(g + RING - 1)
                cx.dma("sp", O["conv_p"][l], tail[:], reads=[tail.d], writes=[outdep])
                cx.dma("sp", O["conv_s"][l], cso[:], reads=[cso.d], writes=[outdep])
                cx.barrier()


        KT16 = [None, None]
        V16 = [None, None]

        def mem_phase(l, after_w=None):
            with contextlib.ExitStack() as ph:
                memT = sb("memT", [P, KD, NMEM], F32, ph)
                nm = sb("nm", [P, 2, KD], F32, ph)
                sq = sb("m_sq", [P, KD, NMEM], BF16, ph)
                rstd = sb("m_rstd", [P, NMEM], F32, ph)
                cx.dma("sp", memT[:], I["memT"].rearrange("k p m -> p k m"), writes=[memT.d])
                cx.dma("sp", nm[:], I["normmem"], writes=[nm.d])
                cx.op("act", lambda e: e.activation(sq[:], memT[:], AF.Square), reads=[memT.d], writes=[sq.d])
                pb = bank()
                for k in range(KD):
                    cx.op("pe", lambda e, k=k: e.matmul(pb[:, :NMEM], ones16[:], sq[:, k, :], start=(k == 0), stop=(k == KD - 1)),
                          reads=[sq.d, ones16.d], writes=[pb.d], inc=(k == KD - 1))
                cx.op("act", lambda e: e.activation(rstd[:], pb[:, :NMEM], AF.Ln, bias=epsb[:, 0:1], scale=1.0 / D),
                      reads=[pb.d, epsb.d], writes=[rstd.d])
                cx.op("act", lambda e: e.activation(rstd[:], rstd[:], AF.Exp, scale=-0.5), reads=[rstd.d], writes=[rstd.d])
                if True:
                    wkv = sb(f"wkv{l}", [P, KD, 2 * D], BF16, ph)
                    memn = sb(f"memn{l}", [P, KD, NMEM], BF16, ph)
                    ko = sb(f"ko{l}", [P, KD, NMEM], F32, ph)
                    vo = sb(f"vo{l}", [P, 2, D], F32, ph)
                    wv = I["wkv"][l].rearrange("k p n -> p k n")
                    for hh in range(2):
                        cx.dma("pool", wkv[:, :, hh * D:(hh + 1) * D], wv[:, :, hh * D:(hh + 1) * D], writes=[wkv.d])
                    if after_w is not None:
                        after_w()
                    for k in range(KD):
                        cx.op("dve", lambda e, k=k: e.scalar_tensor_tensor(memn[:, k, :], memT[:, k, :], nm[:, l, k:k + 1], rstd[:],
                                                                          ALU.mult, ALU.mult),
                              reads=[memT.d, nm.d, rstd.d], writes=[memn.d])
                    for c in range(KD):
                        pb = bank()
                        for k in range(KD):
                            cx.op("pe", lambda e, k=k, c=c, pb=pb: e.matmul(pb[:, :NMEM], wkv[:, k, 128 * c:128 * c + 128], memn[:, k, :],
                                                                           start=(k == 0), stop=(k == KD - 1)),
                                  reads=[wkv.d, memn.d], writes=[pb.d], inc=(k == KD - 1))
                        cx.op("act", lambda e, c=c, pb=pb: e.copy(KT16[l][:, c, :], pb[:, :NMEM]), reads=[pb.d], writes=[KT16[l].d])
                        cx.op("dve", lambda e, c=c, pb=pb: e.tensor_copy(ko[:, c, :], pb[:, :NMEM]), reads=[pb.d], writes=[ko.d])
                    cx.dma("sp", O["mem_kT"][l].rearrange("k p m -> p k m"), ko[:], reads=[ko.d], writes=[outdep])
                    for mc in range(2):
                        for cb_ in range(2):
                            pb = bank()
                            for k in range(KD):
                                cx.op("pe", lambda e, k=k, mc=mc, cb_=cb_, pb=pb: e.matmul(
                                    pb[:, :512], memn[:, k, 128 * mc:128 * mc + 128], wkv[:, k, D + 512 * cb_:D + 512 * cb_ + 512],
                                    start=(k == 0), stop=(k == KD - 1)),
                                    reads=[wkv.d, memn.d], writes=[pb.d], inc=(k == KD - 1))
                            cx.op("act", lambda e, mc=mc, cb_=cb_, pb=pb: e.copy(V16[l][:, mc, 512 * cb_:512 * cb_ + 512], pb[:, :512]),
                                  reads=[pb.d], writes=[V16[l].d])
                            cx.op("dve", lambda e, mc=mc, cb_=cb_, pb=pb: e.tensor_copy(vo[:, mc, 512 * cb_:512 * cb_ + 512], pb[:, :512]),
                                  reads=[pb.d], writes=[vo.d])
                    cx.dma("sp", O["mem_v"][l].rearrange("m p n -> p m n"), vo[:], reads=[vo.d], writes=[outdep])
                cx.barrier()

        def xattn_phase(l):
            with contextlib.ExitStack() as ph:
                KT16[l] = sb(f"KT16_{l}", [P, KD, NMEM], BF16, ph)
                V16[l] = sb(f"V16_{l}", [P, 2, D], BF16, ph)
                wq = sb("wq", [P, KD, D], BF16, ph)
                wo = sb("wo", [P, KD, D], BF16, ph)
                mem_phase(l, after_w=lambda: (cx.dma("pool", wq[:], I["wq"][l].rearrange("k p n -> p k n"), writes=[wq.d]),
                                              cx.dma("pool", wo[:], I["wo"][l].rearrange("k p n -> p k n"), writes=[wo.d])))
                if cfg.xmode == "mem":
                    return
                scr = {"sq16": sb("x_sq16", [P, KD, 512], BF16, ph), "rstd": sb("x_rstd", [P, 512], F32, ph), "eps": epsb}
                h = sb("x_h", [P, KD, 512], BF16, ph)
                qa = sb("x_qa", [P, KD, T], BF16, ph)
                NR = 3
                pTb = [sb(f"x_pT{i}", [P, 2, 512], BF16, ph) for i in range(NR)]
                rsb = [sb(f"x_rs{i}", [P, 512], F32, ph) for i in range(2)]
                ktr = [sb(f"x_kt{i}", [P, KD, NMEM], BF16, ph) for i in range(2)]
                vtr = [sb(f"x_vt{i}", [P, 2, D], BF16, ph) for i in range(2)]
                pTs = sb("x_pTs", [P, 2, 256], BF16, ph)
                pTs_d = [Dep() for _ in range(NSEQ_S)]
                tl = tiles_of(LP, 512)
                qd = [[Dep() for _ in range(4)] for _ in tl]
                order = [len(tl) - 1] + list(range(len(tl) - 1))
                for ti in order:
                    s0, n, smp = tl[ti]
                    rmsnorm(lambda k: x[:, k, s0:s0 + n], xdeps(s0, n), n, 2 + l, lambda k: h[:, k, :n], [h.d], scr)
                    for c in range(KD):
                        pb = bank()
                        for k in range(KD):
                            cx.op("pe", lambda e, k=k, c=c, pb=pb: e.matmul(pb[:, :n], wq[:, k, 128 * c:128 * c + 128], h[:, k, :n],
                                                                           start=(k == 0), stop=(k == KD - 1)),
                                  reads=[wq.d, h.d], writes=[pb.d], inc=(k == KD - 1))
                        if c % 2 == 0:
                            cx.op("act", lambda e, c=c, pb=pb: e.copy(qa[:, c, s0:s0 + n], pb[:, :n]), reads=[pb.d], writes=[qd[ti][c // 2]])
                        else:
                            cx.op("dve", lambda e, c=c, pb=pb: e.tensor_copy(qa[:, c, s0:s0 + n], pb[:, :n]), reads=[pb.d], writes=[qd[ti][c // 2]])
                S_ps = reserve()
                O_ps = reserve()
                Sv = S_ps[:, :].rearrange("p (m c) -> p m c", m=2)
                sN = len(tl) - 1
                sS0 = tl[sN][0]

                def sample_seq(b):
                    kt, vt = ktr[b % 2], vtr[b % 2]
                    cx.dma("pool", kt[:], I["kTc"][l, b].rearrange("k p m -> p k m"), writes=[kt.d])
                    cx.dma("pool", vt[:], I["vc"][l, b].rearrange("m p n -> p m n"), writes=[vt.d])
                    for hd in range(4):
                        for mc in range(2):
                            for dc in range(2):
                                col = mc * 256 + (b * 4 + hd) * 4
                                cx.op("pe", lambda e, mc=mc, dc=dc, hd=hd, col=col: e.matmul(
                                    S_ps[:, col:col + 4], kt[:, 2 * hd + dc, 128 * mc:128 * mc + 128], qa[:, 2 * hd + dc, sS0 + 4 * b:sS0 + 4 * b + 4],
                                    start=(dc == 0), stop=(dc == 1)),
                                    reads=[kt.d, qd[sN][hd]], writes=[S_ps.d], inc=(dc == 1 and mc == 1 and hd == 3))
                    cx.op("act", lambda e: e.activation(pTs[:, :, 16 * b:16 * b + 16], Sv[:, :, 16 * b:16 * b + 16], AF.Exp, scale=1.0 / 16.0),
                          reads=[S_ps.d], writes=[pTs_d[b]])
                    for hd in range(4):
                        for dc in range(2):
                            for mc in range(2):
                                col = (2 * hd + dc) * 64 + 4 * b
                                pc = (b * 4 + hd) * 4
                                cx.op("pe", lambda e, mc=mc, dc=dc, hd=hd, col=col, pc=pc: e.matmul(
                                    O_ps[:, col:col + 4], vt[:, mc, (2 * hd + dc) * 128:(2 * hd + dc) * 128 + 128], pTs[:, mc, pc:pc + 4],
                                    start=(mc == 0), stop=(mc == 1)),
                                    reads=[vt.d, pTs_d[b]], writes=[O_ps.d], inc=(mc == 1 and dc == 1 and hd == 3))

                items = [(ti, hd) for ti in range(len(tl) - 1) for hd in range(4)]
                sc_banks = {}

                def SC(i):
                    ti, hd = items[i]
                    s0, n, smp = tl[ti]
                    pT = pTb[i % NR]
                    pbs = [bank(), bank()]
                    for mc in range(2):
                        for dc in range(2):
                            cx.op("pe", lambda e, mc=mc, dc=dc: e.matmul(
                                pbs[mc][:, :n], KT16[l][:, 2 * hd + dc, 128 * mc:128 * mc + 128], qa[:, 2 * hd + dc, s0:s0 + n],
                                start=(dc == 0), stop=(dc == 1)),
                                reads=[KT16[l].d, qd[ti][hd]], writes=[pbs[mc].d], inc=(dc == 1))
                    for mc in range(2):
                        cx.op("act", lambda e, mc=mc: e.activation(pT[:, mc, :n], pbs[mc][:, :n], AF.Exp, scale=1.0 / 16.0),
                              reads=[pbs[mc].d], writes=[pT.d])

                def REST(i):
                    ti, hd = items[i]
                    s0, n, smp = tl[ti]
                    pT, rs = pTb[i % NR], rsb[i % 2]
                    pbsum = bank()
                    for mc in range(2):
                        cx.op("pe", lambda e, mc=mc: e.matmul(pbsum[:, :n], ones16[:], pT[:, mc, :n], start=(mc == 0), stop=(mc == 1)),
                              reads=[pT.d, ones16.d], writes=[pbsum.d], inc=(mc == 1))
                    cx.op("act", lambda e: e.activation(rs[:, :n], pbsum[:, :n], AF.Ln), reads=[pbsum.d], writes=[rs.d])
                    cx.op("act", lambda e: e.activation(rs[:, :n], rs[:, :n], AF.Exp, scale=-1.0), reads=[rs.d], writes=[rs.d])
                    for dc in range(2):
                        pbo = bank()
                        for mc in range(2):
                            cx.op("pe", lambda e, mc=mc, dc=dc, pbo=pbo: e.matmul(
                                pbo[:, :n], V16[l][:, mc, (2 * hd + dc) * 128:(2 * hd + dc) * 128 + 128], pT[:, mc, :n],
                                start=(mc == 0), stop=(mc == 1)),
                                reads=[V16[l].d, pT.d], writes=[pbo.d], inc=(mc == 1))
                        cx.op("dve", lambda e, dc=dc, pbo=pbo: e.tensor_tensor(qa[:, 2 * hd + dc, s0:s0 + n], pbo[:, :n], rs[:, :n], ALU.mult),
                              reads=[pbo.d, rs.d], writes=[qd[ti][hd]])

                NI = len(items)
                nb_done = 0
                if NI > 0:
                    SC(0)
                for i in range(NI):
                    if i + 1 < NI:
                        SC(i + 1)
                    REST(i)
                    want = ((i + 1) * NSEQ_S + NI - 1) // NI
                    while nb_done < min(want, NSEQ_S):
                        sample_seq(nb_done)
                        nb_done += 1
                while nb_done < NSEQ_S:
                    sample_seq(nb_done)
                    nb_done += 1
                pbsum = bank()
                for mc in range(2):
                    cx.op("pe", lambda e, mc=mc, pbsum=pbsum: e.matmul(pbsum[:, :256], ones16[:], pTs[:, mc, :], start=(mc == 0), stop=(mc == 1)),
                          reads=pTs_d + [ones16.d], writes=[pbsum.d], inc=(mc == 1))
                rs = rsb[0]
                cx.op("act", lambda e: e.activation(rs[:, :256], pbsum[:, :256], AF.Ln), reads=[pbsum.d], writes=[rs.d])
                cx.op("act", lambda e: e.activation(rs[:, :256], rs[:, :256], AF.Exp, scale=-1.0), reads=[rs.d], writes=[rs.d])
                rsv = rs[:, :256].rearrange("p (b h t) -> p b h t", b=NSEQ_S, h=4)
                Ov = O_ps[:, :].rearrange("p (c b t) -> p c b t", c=KD, b=NSEQ_S)
                aov = qa[:, :, sS0:sS0 + NS].rearrange("p c (b t) -> p c b t", b=NSEQ_S)
                for hd in range(4):
                    for dc in range(2):
                        c = 2 * hd + dc
                        cx.op("dve", lambda e, c=c, hd=hd: e.tensor_tensor(aov[:, c], Ov[:, c], rsv[:, :, hd, :], ALU.mult),
                              reads=[O_ps.d, rs.d], writes=[qd[sN][hd]])
                release(S_ps)
                release(O_ps)
                for ti in range(len(tl)):
                    s0, n, smp = tl[ti]
                    for oc in range(KD):
                        pb = bank()
                        for k in range(KD):
                            cx.op("pe", lambda e, k=k, oc=oc, pb=pb: e.matmul(pb[:, :n], wo[:, k, 128 * oc:128 * oc + 128], qa[:, k, s0:s0 + n],
                                                                            start=(k == 0), stop=(k == KD - 1)),
                                  reads=[wo.d] + qd[ti], writes=[pb.d], inc=(k == KD - 1))
                        cx.op("dve", lambda e, pb=pb, oc=oc: e.tensor_tensor(x[:, oc, s0:s0 + n], x[:, oc, s0:s0 + n], pb[:, :n], ALU.add),
                              reads=[pb.d] + xdeps(s0, n), writes=xdeps(s0, n))
                cx.barrier()

        cm = sb("cmask", [P, 464], F32)
        cx.dma("sp", cm[:], I["cmask"], writes=[cm.d])
        maskC16 = sb("maskC16", [64, 64], BF16)
        maskS16 = sb("maskS16", [64, 64], BF16)
        msel16 = sb("msel16", [64, 16], BF16)
        cx.op("dve", lambda e: e.tensor_copy(maskC16[:], cm[:64, 0:64]), reads=[cm.d], writes=[maskC16.d])
        cx.op("dve", lambda e: e.tensor_copy(maskS16[:], cm[:64, 64:128]), reads=[cm.d], writes=[maskS16.d])
        cx.op("dve", lambda e: e.tensor_copy(msel16[:], cm[:64, 128:144]), reads=[cm.d], writes=[msel16.d])

        def gla_alloc(ph, dvc, NT, CK=GC):
            dv = 128 * dvc
            g = {}
            g["ring"] = [[sb(f"g_bt{i}", [P, NT], F32, ph), sb(f"g_eb{i}", [P, NT], F32, ph), sb(f"g_enb{i}", [P, NT], F32, ph)] for i in range(2)]
            g["qt16"] = sb("g_qt16", [P, 4, NT], BF16, ph)
            g["kt16"] = sb("g_kt16", [P, 4, NT], BF16, ph)
            g["ebl"] = sb("g_ebl", [P, 4, 16], F32, ph)
            g["kexp"] = [sb(f"g_kexp{i}", [64, NSEQ_S, 128], BF16, ph) for i in range(1)] * 2
            g["S32a"] = sb("g_S32a", [P, 4, dv], F32, ph)
            g["S16a"] = sb("g_S16a", [P, 4, dv], BF16, ph)
            RA = 1 if dvc == 2 else 2
            g["am4"] = [sb(f"g_am4_{i}", [CK, 4, CK], BF16, ph) for i in range(RA)] * (2 // RA)
            g["ktm4"] = [sb(f"g_ktm4_{i}", [CK, 4, 128], BF16, ph) for i in range(RA)] * (2 // RA)
            R4 = 2 if dvc == 2 else 4
            R2 = 1 if dvc == 2 else 2
            g["s32r"] = [sb(f"g_s32r{i}", [P, dv], F32, ph) for i in range(R4)] * (4 // R4)
            g["s16r"] = [sb(f"g_s16r{i}", [P, dv], BF16, ph) for i in range(R4)] * (4 // R4)
            g["so"] = [sb(f"g_so{i}", [P, dv], F32, ph) for i in range(R4)] * (4 // R4)
            g["o32"] = [sb(f"g_o32{i}", [P, dvc, NT], F32, ph) for i in range(R2)] * (2 // R2)
            g["osq"] = [sb(f"g_osq{i}", [P, dvc, NT], BF16, ph) for i in range(R2)] * (2 // R2)
            g["rstd"] = [sb(f"g_rstd{i}", [P, NT], F32, ph) for i in range(R2)] * (2 // R2)
            cx.op("pool", lambda e: e.memset(g["S32a"][:], 0.0), writes=[g["S32a"].d])
            cx.op("pool", lambda e: e.memset(g["S16a"][:], 0.0), writes=[g["S16a"].d])
            return g

        def gla_prep(g, n, smp, scale, q32, k32, lg32, CK=GC, mres_p=None):
            C = LS if smp else CK
            nch = n // C
            mres = cm[:, 400:464] if smp else (cm[:, 144:144 + n] if mres_p is None else mres_p[:, :n])
            qt16, kt16, ebl = g["qt16"], g["kt16"], g["ebl"]
            for hd in range(4):
                bt_, eb_, enb_ = g["ring"][hd % 2]
                cx.op("dve", lambda e, hd=hd, bt_=bt_: e.tensor_tensor_scan(bt_[:, :n], mres, lg32[:, hd, :n], 0.0, ALU.mult, ALU.add),
                      reads=[lg32.d, cm.d], writes=[bt_.d])
                cx.op("act", lambda e, bt_=bt_, eb_=eb_: e.activation(eb_[:, :n], bt_[:, :n], AF.Exp, scale=scale), reads=[bt_.d], writes=[eb_.d])
                cx.op("act", lambda e, bt_=bt_, enb_=enb_: e.activation(enb_[:, :n], bt_[:, :n], AF.Exp, scale=-scale), reads=[bt_.d], writes=[enb_.d])
                cx.op("dve", lambda e, hd=hd, eb_=eb_: e.tensor_tensor(qt16[:, hd, :n], q32[:, hd, :n], eb_[:, :n], ALU.mult),
                      reads=[q32.d, eb_.d], writes=[qt16.d])
                cx.op("pool", lambda e, hd=hd, enb_=enb_: e.tensor_tensor(kt16[:, hd, :n], k32[:, hd, :n], enb_[:, :n], ALU.mult),
                      reads=[k32.d, enb_.d], writes=[kt16.d])
                ebv = eb_[:, :n].rearrange("p (c j) -> p c j", j=C)
                cx.op("act", lambda e, hd=hd, ebv=ebv: e.copy(ebl[:, hd, :nch], ebv[:, :, C - 1]), reads=[eb_.d], writes=[ebl.d])

        def gla_chunks(g, n, smp, dvc, v_tm, st_in, st_out, CK=GC, maskCK=None):
            dv = 128 * dvc
            if maskCK is None:
                maskCK = maskC16
            C = LS if smp else CK
            nch = n // C
            qt16, kt16, ebl = g["qt16"], g["kt16"], g["ebl"]
            pbo = [reserve() for _ in range(4)]
            if not smp:
                S32a, S16a = g["S32a"], g["S16a"]
                hpb = 512 // dv
                for ci in range(nch):
                    cs = slice(ci * CK, ci * CK + CK)
                    am, ktm = g["am4"][ci % 2], g["ktm4"][ci % 2]
                    pbA, pbT = bank(), bank()
                    pbTv = pbT[:, :].bitcast(BF16)
                    for hd in range(4):
                        cx.op("pe", lambda e, hd=hd: e.matmul(pbA[:CK, hd * CK:(hd + 1) * CK], kt16[:, hd, cs], qt16[:, hd, cs], start=True, stop=True),
                              reads=[kt16.d, qt16.d], writes=[pbA.d], inc=(hd == 3))
                    for hd in range(4):
                        cx.op("pe", lambda e, hd=hd: e.transpose(pbTv[:CK, hd * 128:(hd + 1) * 128], kt16[:, hd, cs], ident16[:]),
                              reads=[kt16.d, ident16.d], writes=[pbT.d], inc=(hd == 3))
                    cx.op("dve", lambda e: e.tensor_tensor(am[:], pbA[:CK, :4 * CK].rearrange("p (h c) -> p h c", h=4),
                                                          maskCK[:, :].unsqueeze(1).broadcast_to([CK, 4, CK]), ALU.mult),
                          reads=[pbA.d, maskCK.d], writes=[am.d])
                    cx.op("act", lambda e: e.copy(ktm[:], pbTv[:CK, :512].rearrange("p (h c) -> p h c", h=4)), reads=[pbT.d], writes=[ktm.d])
                    for hd in range(4):
                        for dc in range(dvc):
                            oc = slice(dc * 256 + ci * CK, dc * 256 + ci * CK + CK)
                            vs = slice(hd * dv + 128 * dc, hd * dv + 128 * dc + 128)
                            cx.op("pe", lambda e, hd=hd, oc=oc, vs=vs: e.matmul(pbo[hd][:, oc], v_tm[:CK, ci, vs], am[:, hd, :], start=True, stop=False),
                                  reads=[v_tm.d, am.d], writes=[pbo[hd].d], inc=False)
                            cx.op("pe", lambda e, hd=hd, oc=oc, dc=dc: e.matmul(pbo[hd][:, oc], S16a[:, hd, 128 * dc:128 * dc + 128], qt16[:, hd, cs],
                                                                                 start=False, stop=True),
                                  reads=[S16a.d, qt16.d], writes=[pbo[hd].d], inc=(dc == dvc - 1))
                    pbS = [bank() for _ in range(4 // hpb)]
                    for hd in range(4):
                        bi, col = hd // hpb, (hd % hpb) * dv
                        cx.op("pe", lambda e, hd=hd, bi=bi, col=col: e.matmul(pbS[bi][:, col:col + dv], ktm[:, hd, :], v_tm[:CK, ci, hd * dv:(hd + 1) * dv],
                                                                             start=True, stop=True),
                              reads=[ktm.d, v_tm.d], writes=[pbS[bi].d], inc=(hd % hpb == hpb - 1))
                    for bi in range(4 // hpb):
                        h0 = bi * hpb
                        S32v = S32a[:, h0:h0 + hpb, :]
                        psv = pbS[bi][:, :].rearrange("p (h d) -> p h d", h=hpb)
                        cx.op("dve", lambda e, S32v=S32v, psv=psv: e.tensor_tensor(S32v, psv, S32v, ALU.add), reads=[pbS[bi].d, S32a.d], writes=[S32a.d])
                        cx.op("dve", lambda e, S32v=S32v, h0=h0: e.tensor_tensor(S32v, S32v, ebl[:, h0:h0 + hpb, ci:ci + 1].broadcast_to([P, hpb, dv]), ALU.mult),
                              reads=[S32a.d, ebl.d], writes=[S32a.d])
                        cx.op("act", lambda e, S32v=S32v, h0=h0: e.copy(S16a[:, h0:h0 + hpb, :], S32v), reads=[S32a.d], writes=[S16a.d])
                return pbo
            mask16 = maskS16 if smp else maskC16
            nchunks = 1 if smp else nch
            CT = n if smp else GC
            for ci in range(nchunks):
                cs = slice(ci * CT, ci * CT + CT)
                for hd in range(4):
                    am, ktm = g["am4"][0][:, hd, :], g["ktm4"][0][:, hd, :]
                    am_d, ktm_d = g["am4"][0].d, g["ktm4"][0].d
                    pbA = bank()
                    cx.op("pe", lambda e, hd=hd, pbA=pbA: e.matmul(pbA[:CT, :CT], kt16[:, hd, cs], qt16[:, hd, cs], start=True, stop=True),
                          reads=[kt16.d, qt16.d], writes=[pbA.d])
                    pbT = bank()
                    pbTv = pbT[:, :].bitcast(BF16)
                    cx.op("pe", lambda e, hd=hd, pbTv=pbTv: e.transpose(pbTv[:CT, :128], kt16[:, hd, cs], ident16[:]),
                          reads=[kt16.d, ident16.d], writes=[pbT.d])
                    cx.op("dve", lambda e, am=am, pbA=pbA: e.tensor_tensor(am[:CT, :CT], pbA[:CT, :CT], mask16[:CT, :CT], ALU.mult),
                          reads=[pbA.d, mask16.d], writes=[am_d])
                    cx.op("act", lambda e, ktm=ktm, pbTv=pbTv: e.copy(ktm[:CT, :], pbTv[:CT, :128]), reads=[pbT.d], writes=[ktm_d])
                    for dc in range(dvc):
                        oc = slice(dc * 256 + ci * CT, dc * 256 + ci * CT + CT)
                        vs = slice(hd * dv + 128 * dc, hd * dv + 128 * dc + 128)
                        if not smp:
                            cx.op("pe", lambda e, hd=hd, oc=oc, vs=vs, am=am: e.matmul(pbo[hd][:, oc], v_tm[:CT, ci, vs], am[:CT, :CT], start=True, stop=False),
                                  reads=[v_tm.d, am_d], writes=[pbo[hd].d], inc=False)
                            cx.op("pe", lambda e, hd=hd, oc=oc, dc=dc: e.matmul(pbo[hd][:, oc], g["S16"][hd][:, 128 * dc:128 * dc + 128], qt16[:, hd, cs],
                                                                                 start=False, stop=True),
                                  reads=[g["S16"][hd].d, qt16.d], writes=[pbo[hd].d], inc=(dc == dvc - 1))
                    if not smp:
                        pbS = bank()
                        cx.op("pe", lambda e, hd=hd, pbS=pbS: e.matmul(pbS[:, :dv], ident[:], g["S32"][hd][:], start=True, stop=False),
                              reads=[ident.d, g["S32"][hd].d], writes=[pbS.d], inc=False)
                        cx.op("pe", lambda e, hd=hd, pbS=pbS, ktm=ktm: e.matmul(pbS[:, :dv], ktm[:CT, :], v_tm[:CT, ci, hd * dv:(hd + 1) * dv], start=False, stop=True),
                              reads=[ktm_d, v_tm.d], writes=[pbS.d])
                        cx.op("act", lambda e, hd=hd, pbS=pbS: e.activation(g["S32"][hd][:], pbS[:, :dv], AF.Copy, scale=ebl[:, hd, ci:ci + 1]),
                              reads=[pbS.d, ebl.d], writes=[g["S32"][hd].d])
                        cx.op("dve", lambda e, hd=hd, pbS=pbS: e.tensor_scalar(g["S16"][hd][:], pbS[:, :dv], ebl[:, hd, ci:ci + 1], None, ALU.mult),
                              reads=[pbS.d, ebl.d], writes=[g["S16"][hd].d])
                    else:
                        kexp = g["kexp"][hd % 2]
                        cx.op("dve", lambda e, kexp=kexp, ktm=ktm: e.tensor_tensor(
                            kexp[:, :, :], ktm[:64, :].unsqueeze(1).broadcast_to([64, NSEQ_S, 128]),
                            msel16[:, :].unsqueeze(2).broadcast_to([64, NSEQ_S, 128]), ALU.mult),
                            reads=[ktm_d, msel16.d], writes=[kexp.d])
                        for b in range(NSEQ_S):
                            r = (hd * NSEQ_S + b) % 4
                            s32, s16, so = g["s32r"][r], g["s16r"][r], g["so"][r]
                            cx.dma("pool", s32[:], st_in[b, hd], writes=[s32.d])
                            cx.dma("pool", s16[:], st_in[b, hd], writes=[s16.d])
                            for dc in range(dvc):
                                oc = slice(dc * 256 + 4 * b, dc * 256 + 4 * b + 4)
                                vs = slice(hd * dv + 128 * dc, hd * dv + 128 * dc + 128)
                                cx.op("pe", lambda e, hd=hd, oc=oc, dc=dc, s16=s16, b=b: e.matmul(
                                    pbo[hd][:, oc], s16[:, 128 * dc:128 * dc + 128], qt16[:, hd, 4 * b:4 * b + 4], start=True, stop=False),
                                    reads=[s16.d, qt16.d], writes=[pbo[hd].d], inc=False)
                                cx.op("pe", lambda e, hd=hd, oc=oc, vs=vs, am=am, b=b: e.matmul(
                                    pbo[hd][:, oc], v_tm[:CT, ci, vs], am[:CT, 4 * b:4 * b + 4], start=False, stop=True),
                                    reads=[v_tm.d, am_d], writes=[pbo[hd].d], inc=(dc == dvc - 1 and b == NSEQ_S - 1))
                            pbS = bank()
                            cx.op("pe", lambda e, hd=hd, pbS=pbS, kexp=kexp, b=b: e.matmul(pbS[:, :dv], kexp[:, b, :], v_tm[:CT, ci, hd * dv:(hd + 1) * dv],
                                                                                        start=True, stop=True),
                                  reads=[kexp.d, v_tm.d], writes=[pbS.d])
                            cx.op("act", lambda e, hd=hd, s32=s32, b=b: e.activation(s32[:], s32[:], AF.Copy, scale=ebl[:, hd, b:b + 1]),
                                  reads=[s32.d, ebl.d], writes=[s32.d])
                            cx.op("dve", lambda e, hd=hd, pbS=pbS, so=so, b=b, s32=s32: e.scalar_tensor_tensor(so[:], pbS[:, :dv], ebl[:, hd, b:b + 1], s32[:],
                                                                                                   ALU.mult, ALU.add),
                                  reads=[pbS.d, ebl.d, s32.d], writes=[so.d])
                            cx.dma("sp", st_out[b, hd], so[:], reads=[so.d], writes=[outdep])
            return pbo

        def gla_post(g, n, dvc, pbo, gate32, gn, on16, on_c0):
            dv = 128 * dvc
            for hd in range(4):
                o32, osq, rstd = g["o32"][hd % 2], g["osq"][hd % 2], g["rstd"][hd % 2]
                for dc in range(dvc):
                    cx.op("act", lambda e, hd=hd, dc=dc, o32=o32: e.copy(o32[:, dc, :n], pbo[hd][:, dc * 256:dc * 256 + n]), reads=[pbo[hd].d], writes=[o32.d])
                    cx.op("act", lambda e, hd=hd, dc=dc, osq=osq: e.activation(osq[:, dc, :n], pbo[hd][:, dc * 256:dc * 256 + n], AF.Square),
                          reads=[pbo[hd].d], writes=[osq.d])
                pb = bank()
                for dc in range(dvc):
                    cx.op("pe", lambda e, dc=dc, pb=pb, osq=osq: e.matmul(pb[:, :n], ones16[:], osq[:, dc, :n], start=(dc == 0), stop=(dc == dvc - 1)),
                          reads=[osq.d, ones16.d], writes=[pb.d], inc=(dc == dvc - 1))
                cx.op("act", lambda e, pb=pb, rstd=rstd: e.activation(rstd[:, :n], pb[:, :n], AF.Ln, bias=epsb[:, 0:1], scale=1.0 / dv),
                      reads=[pb.d, epsb.d], writes=[rstd.d])
                cx.op("act", lambda e, rstd=rstd: e.activation(rstd[:, :n], rstd[:, :n], AF.Exp, scale=-0.5), reads=[rstd.d], writes=[rstd.d])
                for dc in range(dvc):
                    cx.op("pool", lambda e, dc=dc, o32=o32, rstd=rstd: e.tensor_tensor(o32[:, dc, :n], o32[:, dc, :n], rstd[:, :n], ALU.mult),
                          reads=[o32.d, rstd.d], writes=[o32.d])
                    cx.op("dve", lambda e, hd=hd, dc=dc, o32=o32: e.scalar_tensor_tensor(
                        on16[:, on_c0 + hd * dvc + dc, :n], o32[:, dc, :n], gn[:, dc:dc + 1], gate32[:, hd * dvc + dc, :n], ALU.mult, ALU.mult),
                        reads=[o32.d, gn.d, gate32.d], writes=[on16.d])
            for b_ in pbo:
                release(b_)

        def a1_phase():
            NT = 256
            with contextlib.ExitStack() as ph:
                w = sb("w_in_c", [P, KD, 3088], BF16, ph)
                wv = I["w_in_c"].rearrange("k p n -> p k n")
                for c0 in range(0, 3088, 772):
                    cx.dma("pool", w[:, :, c0:c0 + 772], wv[:, :, c0:c0 + 772], writes=[w.d])
                wout = sb("w_out_c", [P, KD, D], BF16, ph)
                cx.dma("pool", wout[:], I["w_out_c"].rearrange("k p n -> p k n"), writes=[wout.d])
                wgu = sb("wgu", [16, 512], BF16, ph)
                cx.dma("pool", wgu[:], I["gla_wgu"], writes=[wgu.d])
                gsm = sb("gsm", [P, 6], F32, ph)
                cx.dma("sp", gsm[:], I["gla_small"], writes=[gsm.d])
                gnb = sb("gnb", [P, 2], F32, ph)
                cx.op("dve", lambda e: e.tensor_copy(gnb[:], gsm[:, 4:6]), reads=[gsm.d], writes=[gnb.d])
                nbg = sb("nbg", [P, 4], F32, ph)
                cx.op("dve", lambda e: e.tensor_scalar(nbg[:], gsm[:, 0:4], -1.0, None, ALU.mult), reads=[gsm.d], writes=[nbg.d])
                scr = {"sq16": sb("a_sq16", [P, KD, NT], BF16, ph), "rstd": sb("a_rstd", [P, NT], F32, ph), "eps": epsb}
                h = sb("a_h", [P, KD, NT], BF16, ph)
                q32 = sb("a_q32", [P, 4, NT], F32, ph)
                k32 = sb("a_k32", [P, 4, NT], F32, ph)
                lg32 = sb("a_lg32", [P, 4, NT], F32, ph)
                sg32 = lg32
                gate32 = sb("a_gate16", [P, 8, NT], BF16, ph)
                gd16 = sb("a_gd16", [16, NT], BF16, ph)
                CK1 = 128
                v_tm = sb("a_vtm", [CK1, NT // CK1, 1024], BF16, ph)
                cm2 = sb("a_cm2", [P, 384], F32, ph)
                cx.dma("sp", cm2[:], I["cmask2"], writes=[cm2.d])
                mask128 = sb("a_mask128", [P, 128], BF16, ph)
                cx.op("dve", lambda e: e.tensor_copy(mask128[:], cm2[:, 0:128]), reads=[cm2.d], writes=[mask128.d])
                mres128 = View(cm2[:, 128:384])
                mres128.d = cm2.d
                on16 = sb("a_on16", [P, 8, NT], BF16, ph)
                g = gla_alloc(ph, 2, NT, CK=CK1)
                ev = [0]

                def evac(fn_act, fn_dve, reads, writes):
                    ev[0] += 1
                    if ev[0] % 2:
                        cx.op("act", fn_act, reads=reads, writes=writes)
                    else:
                        cx.op("dve", fn_dve, reads=reads, writes=writes)

                hb2 = [h, sb("a_h2", [P, KD, NT], BF16, ph)]
                tl = tiles_of(LP, NT)

                def proj(h_, c0, n, M=128):
                    pb = bank()
                    for k in range(KD):
                        cx.op("pe", lambda e, k=k, pb=pb: e.matmul(pb[:M, :n], w[:, k, c0:c0 + M], h_[:, k, :n], start=(k == 0), stop=(k == KD - 1)),
                              reads=[w.d, h_.d], writes=[pb.d], inc=(k == KD - 1))
                    return pb

                def stA(ti):
                    s0, n, smp = tl[ti]
                    h_ = hb2[ti % 2]
                    rmsnorm(lambda k: x[:, k, s0:s0 + n], xdeps(s0, n), n, 1, lambda k: h_[:, k, :n], [h_.d], scr)

                def stB(ti):
                    s0, n, smp = tl[ti]
                    h_ = hb2[ti % 2]
                    for hd in range(4):
                        pb = proj(h_, 128 * hd, n)
                        cx.op("act", lambda e, hd=hd, pb=pb: e.mul(q32[:, hd, :n], pb[:, :n], 128.0 ** -0.5), reads=[pb.d], writes=[q32.d])
                        pb = proj(h_, 512 + 128 * hd, n)
                        cx.op("dve", lambda e, hd=hd, pb=pb: e.tensor_copy(k32[:, hd, :n], pb[:, :n]), reads=[pb.d], writes=[k32.d])
                    pb = proj(h_, 3072, n, M=16)
                    cx.op("dve", lambda e, pb=pb: e.tensor_copy(gd16[:, :n], pb[:16, :n]), reads=[pb.d], writes=[gd16.d])
                    for hd in range(4):
                        pb = bank()
                        cx.op("pe", lambda e, hd=hd, pb=pb: e.matmul(pb[:, :n], wgu[:, 128 * hd:128 * hd + 128], gd16[:, :n], start=True, stop=True),
                              reads=[wgu.d, gd16.d], writes=[pb.d])
                        cx.op("act", lambda e, hd=hd, pb=pb: e.activation(sg32[:, hd, :n], pb[:, :n], AF.Exp, bias=nbg[:, hd:hd + 1], scale=-1.0),
                              reads=[pb.d, nbg.d], writes=[sg32.d])
                        cx.op("act", lambda e, hd=hd: e.activation(lg32[:, hd, :n], sg32[:, hd, :n], AF.Ln, bias=1.0), reads=[sg32.d], writes=[lg32.d])

                def stC(ti):
                    s0, n, smp = tl[ti]
                    gla_prep(g, n, smp, -1.0 / 16.0, q32, k32, lg32, CK=CK1, mres_p=mres128)

                def stDr(ti):
                    s0, n, smp = tl[ti]
                    h_ = hb2[ti % 2]
                    for c in range(8):
                        pb = proj(h_, 2048 + 128 * c, n)
                        cx.op("act", lambda e, c=c, pb=pb: e.activation(gate32[:, c, :n], pb[:, :n], AF.Silu), reads=[pb.d], writes=[gate32.d])

                def stDv(ti):
                    s0, n, smp = tl[ti]
                    h_ = hb2[ti % 2]
                    CT = n if smp else CK1
                    for ci in range(n // CT):
                        for cb_ in range(2):
                            pb = bank()
                            for k in range(KD):
                                cx.op("pe", lambda e, k=k, pb=pb, ci=ci, cb_=cb_: e.matmul(
                                    pb[:CT, :512], h_[:, k, ci * CT:ci * CT + CT], w[:, k, 1024 + 512 * cb_:1024 + 512 * cb_ + 512],
                                    start=(k == 0), stop=(k == KD - 1)),
                                    reads=[w.d, h_.d], writes=[pb.d], inc=(k == KD - 1))
                            cx.op("dve", lambda e, pb=pb, ci=ci, cb_=cb_: e.tensor_copy(v_tm[:CT, ci, 512 * cb_:512 * cb_ + 512], pb[:CT, :512]),
                                  reads=[pb.d], writes=[v_tm.d])

                pbos = {}

                def stE(ti):
                    s0, n, smp = tl[ti]
                    pbos[ti] = gla_chunks(g, n, smp, 2, v_tm, I["st_gla"], O["gla_s"], CK=CK1, maskCK=mask128)
                    if ti == len(tl) - 2:
                        for hd in range(4):
                            cx.dma("sp", O["gla_p"][hd], g["S32a"][:, hd, :], reads=[g["S32a"].d], writes=[outdep])

                def stF(ti):
                    s0, n, smp = tl[ti]
                    gla_post(g, n, 2, pbos.pop(ti), gate32, gnb, on16, 0)

                def stG(ti):
                    s0, n, smp = tl[ti]
                    for oc in range(KD):
                        pb = bank()
                        for k in range(KD):
                            cx.op("pe", lambda e, k=k, oc=oc, pb=pb: e.matmul(pb[:, :n], wout[:, k, 128 * oc:128 * oc + 128], on16[:, k, :n],
                                                                            start=(k == 0), stop=(k == KD - 1)),
                                  reads=[wout.d, on16.d], writes=[pb.d], inc=(k == KD - 1))
                        cx.op("dve", lambda e, pb=pb, oc=oc: e.tensor_tensor(x[:, oc, s0:s0 + n], x[:, oc, s0:s0 + n], pb[:, :n], ALU.add),
                              reads=[pb.d] + xdeps(s0, n), writes=xdeps(s0, n))

                NTL = len(tl)
                stA(0); stB(0); stC(0); stDv(0); stDr(0)
                if NTL > 1:
                    stA(1)
                stE(0)
                for ti in range(1, NTL):
                    stB(ti); stDv(ti); stF(ti - 1); stC(ti); stG(ti - 1); stDr(ti)
                    if ti + 1 < NTL:
                        stA(ti + 1)
                    if tl[ti][2]:
                        cx.barrier()
                        g["s32r"] = [View(q32[:, j, :]) for j in range(4)]
                        g["so"] = [View(k32[:, j, :]) for j in range(4)]
                        g["s16r"] = [View(hb2[0][:, j, :]) for j in range(4)]
                    stE(ti)
                stF(NTL - 1); stG(NTL - 1)
                cx.barrier()


        TWO_PI = 2.0 * math.pi

        def a0_phase(NTA, NTB):
            with contextlib.ExitStack() as ph0:
                hall = sb("hall0", [P, KD, T], BF16, ph0)
                hdp = {s0: Dep() for s0 in range(0, T, 128)}

                def hdeps(s0, n):
                    return [hdp[k] for k in range(s0, s0 + n, 128)]
                with contextlib.ExitStack() as ph:
                    scr = {"sq16": sb("a0_sq16", [P, KD, 256], BF16, ph), "rstd": sb("a0_rstd", [P, 256], F32, ph), "eps": epsb}
                    for (s0, n, smp) in tiles_of(LP, 256):
                        rmsnorm(lambda k: x[:, k, s0:s0 + n], xdeps(s0, n), n, 0, lambda k: hall[:, k, s0:s0 + n], hdeps(s0, n), scr)
                    cx.barrier()
                wv = I["w_in_ab"].rearrange("k p n -> p k n")
                wov = I["w_out_ab"].rearrange("k p n -> p k n")
                if "A0a" in cfg.phases or "A0" in cfg.phases:
                  with contextlib.ExitStack() as ph:
                    NT = NTA
                    w = sb("w_in_a", [P, KD, 2048], BF16, ph)
                    for c0 in range(0, 2048, 512):
                        cx.dma("pool", w[:, :, c0:c0 + 512], wv[:, :, c0:c0 + 512], writes=[w.d])
                    wout = sb("w_out_a", [P, 4, D], BF16, ph)
                    cx.dma("pool", wout[:], wov[:, 0:4, :], writes=[wout.d])
                    hsm = sb("hsm", [P, 13], F32, ph)
                    cx.dma("sp", hsm[:], I["hg_small"], writes=[hsm.d])
                    lbe = sb("lbe", [P, 4, 3], F32, ph)
                    lbs = sb("lbs", [P, 4], F32, ph)
                    lb = sb("lb", [P, 4], F32, ph)
                    oml = sb("oml", [P, 4], F32, ph)
                    gnb = sb("gnb0", [P, 1], F32, ph)
                    cx.op("act", lambda e: e.activation(lbe[:], hsm[:, 0:12].rearrange("p (c j) -> p c j", j=3), AF.Exp), reads=[hsm.d], writes=[lbe.d])
                    cx.op("dve", lambda e: e.tensor_tensor(lbs[:], lbe[:, :, 0], lbe[:, :, 1], ALU.add), reads=[lbe.d], writes=[lbs.d])
                    cx.op("dve", lambda e: e.tensor_tensor(lbs[:], lbs[:], lbe[:, :, 2], ALU.add), reads=[lbe.d, lbs.d], writes=[lbs.d])
                    cx.op("dve", lambda e: e.reciprocal(lbs[:], lbs[:]), reads=[lbs.d], writes=[lbs.d])
                    cx.op("dve", lambda e: e.tensor_tensor(lb[:], lbe[:, :, 0], lbs[:], ALU.mult), reads=[lbe.d, lbs.d], writes=[lb.d])
                    cx.op("dve", lambda e: e.tensor_scalar(oml[:], lb[:], -1.0, 1.0, ALU.mult, ALU.add), reads=[lb.d], writes=[oml.d])
                    cx.op("dve", lambda e: e.tensor_copy(gnb[:], hsm[:, 12:13]), reads=[hsm.d], writes=[gnb.d])
                    q32 = sb("h_q32", [P, 4, NT], F32, ph)
                    k32 = sb("h_k32", [P, 4, NT], F32, ph)
                    fg32 = sb("h_fg32", [P, 4, NT], F32, ph)
                    lg32 = sb("h_lg32", [P, 4, NT], F32, ph)
                    gate32 = sb("h_gate32", [P, 4, NT], F32, ph)
                    v_tm = sb("h_vtm", [64, max(NT // GC, 1), 512], BF16, ph)
                    on16 = sb("h_on16", [P, 4, NT], BF16, ph)
                    g = gla_alloc(ph, 1, NT)

                    def proj(c0, s0, n):
                        pb = bank()
                        for k in range(KD):
                            cx.op("pe", lambda e, k=k, pb=pb: e.matmul(pb[:, :n], w[:, k, c0:c0 + 128], hall[:, k, s0:s0 + n], start=(k == 0), stop=(k == KD - 1)),
                                  reads=[w.d] + hdeps(s0, n), writes=[pb.d], inc=(k == KD - 1))
                        return pb

                    tl = tiles_of(LP, NT)

                    def stB(ti):
                        s0, n, smp = tl[ti]
                        for hd in range(4):
                            pb = proj(128 * hd, s0, n)
                            cx.op("act", lambda e, hd=hd, pb=pb: e.activation(q32[:, hd, :n], pb[:, :n], AF.Silu), reads=[pb.d], writes=[q32.d])
                        for hd in range(4):
                            pb = proj(512 + 128 * hd, s0, n)
                            cx.op("act", lambda e, hd=hd, pb=pb: e.activation(fg32[:, hd, :n], pb[:, :n], AF.Sigmoid), reads=[pb.d], writes=[fg32.d])
                            cx.op("dve", lambda e, hd=hd: e.tensor_scalar(fg32[:, hd, :n], fg32[:, hd, :n], oml[:, hd:hd + 1], lb[:, hd:hd + 1], ALU.mult, ALU.add),
                                  reads=[fg32.d, oml.d, lb.d], writes=[fg32.d])
                        for hd in range(4):
                            cx.op("act", lambda e, hd=hd: e.activation(lg32[:, hd, :n], fg32[:, hd, :n], AF.Ln), reads=[fg32.d], writes=[lg32.d])
                            cx.op("pool", lambda e, hd=hd: e.tensor_scalar(k32[:, hd, :n], fg32[:, hd, :n], -1.0, 1.0, ALU.mult, ALU.add),
                                  reads=[fg32.d], writes=[k32.d])

                    def stC(ti):
                        s0, n, smp = tl[ti]
                        gla_prep(g, n, smp, 1.0, q32, k32, lg32)

                    def stDr(ti):
                        s0, n, smp = tl[ti]
                        for hd in range(4):
                            pb = proj(1536 + 128 * hd, s0, n)
                            cx.op("act", lambda e, hd=hd, pb=pb: e.activation(gate32[:, hd, :n], pb[:, :n], AF.Silu), reads=[pb.d], writes=[gate32.d])

                    def stDv(ti):
                        s0, n, smp = tl[ti]
                        CT = n if smp else GC
                        for ci in range(n // CT):
                            pb = bank()
                            for k in range(KD):
                                cx.op("pe", lambda e, k=k, pb=pb, ci=ci: e.matmul(
                                    pb[:CT, :512], hall[:, k, s0 + ci * CT:s0 + ci * CT + CT], w[:, k, 1024:1536], start=(k == 0), stop=(k == KD - 1)),
                                    reads=[w.d] + hdeps(s0, n), writes=[pb.d], inc=(k == KD - 1))
                            cx.op("dve", lambda e, pb=pb, ci=ci: e.tensor_copy(v_tm[:CT, ci, :], pb[:CT, :512]), reads=[pb.d], writes=[v_tm.d])

                    pbos = {}

                    def stE(ti):
                        s0, n, smp = tl[ti]
                        pbos[ti] = gla_chunks(g, n, smp, 1, v_tm, I["st_hgrn"], O["hgrn_s"])
                        if ti == len(tl) - 2:
                            for hd in range(4):
                                cx.dma("sp", O["hgrn_p"][hd], g["S32a"][:, hd, :], reads=[g["S32a"].d], writes=[outdep])

                    def stF(ti):
                        s0, n, smp = tl[ti]
                        gla_post(g, n, 1, pbos.pop(ti), gate32, gnb, on16, 0)

                    def stG(ti):
                        s0, n, smp = tl[ti]
                        for oc in range(KD):
                            pb = bank()
                            for k in range(4):
                                cx.op("pe", lambda e, k=k, oc=oc, pb=pb: e.matmul(pb[:, :n], wout[:, k, 128 * oc:128 * oc + 128], on16[:, k, :n],
                                                                                start=(k == 0), stop=(k == 3)),
                                      reads=[wout.d, on16.d], writes=[pb.d], inc=(k == 3))
                            cx.op("dve", lambda e, pb=pb, oc=oc: e.tensor_tensor(x[:, oc, s0:s0 + n], x[:, oc, s0:s0 + n], pb[:, :n], ALU.add),
                                  reads=[pb.d] + xdeps(s0, n), writes=xdeps(s0, n))

                    stB(0); stC(0); stDv(0); stDr(0); stE(0)
                    for ti in range(1, len(tl)):
                        stB(ti); stDv(ti); stF(ti - 1); stC(ti); stG(ti - 1); stDr(ti); stE(ti)
                    stF(len(tl) - 1); stG(len(tl) - 1)
                    cx.barrier()
                if "A0b" in cfg.phases or "A0" in cfg.phases:
                  with contextlib.ExitStack() as ph:
                    NT = NTB
                    w = sb("w_in_u", [P, KD, 512], BF16, ph)
                    cx.dma("pool", w[:], wv[:, :, 2048:2560], writes=[w.d])
                    wout = sb("w_out_b", [P, 4, D], BF16, ph)
                    cx.dma("pool", wout[:], wov[:, 4:8, :], writes=[wout.d])
                    wglu = sb("wglu", [P, 4, 512], BF16, ph)
                    cx.dma("pool", wglu[:], I["s5_wglu"].rearrange("k p n -> p k n"), writes=[wglu.d])
                    ssm = sb("ssm", [P, 8], F32, ph)
                    cx.dma("sp", ssm[:], I["s5_small"], writes=[ssm.d])
                    x0 = sb("s5x0", [P, 2, 16, NSEQ_S], F32, ph)
                    cx.dma("sp", x0[:], I["s5_x0"].rearrange("r p s b -> p r s b"), writes=[x0.d])
                    Ere = sb("Ere", [P, 16, NT], F32, ph)
                    Eim = sb("Eim", [P, 16, NT], F32, ph)
                    rho = sb("rho", [P, 16], F32, ph)
                    rhoms = sb("rhoms", [P, 16, NS], F32, ph)
                    BB = [sb(f"BB{r}", [P, 4, 512], BF16, ph) for r in range(2)]
                    CC = [sb(f"CC{r}", [P, 16, 128], BF16, ph) for r in range(2)]
                    xst = sb("xst", [P, 2, 16], F32, ph)
                    xso = sb("xso", [P, 2, 16, NSEQ_S], F32, ph)
                    cx.op("pool", lambda e: e.memset(xst[:], 0.0), writes=[xst.d])
                    ones32 = sb("ones32", [P, P], F32, ph)
                    cx.op("pool", lambda e: e.memset(ones32[:], 1.0), writes=[ones32.d])
                    with contextlib.ExitStack() as ps_:
                        R_S5_MAX_RE = -1e-4

                        def sincos(th, F, tag):
                            outs = []
                            for off, nm_ in ((0.0, "sin"), (0.5 * math.pi, "cos")):
                                a = sb(f"sc_a_{tag}{nm_}", [P, F], F32, ps_)
                                ki = sb(f"sc_k_{tag}{nm_}", [P, F], mybir.dt.int32, ps_)
                                kf = sb(f"sc_kf_{tag}{nm_}", [P, F], F32, ps_)
                                cx.op("dve", lambda e, a=a, off=off: e.tensor_scalar(a[:], th[:], 1.0, off, ALU.mult, ALU.add), reads=[th.d], writes=[a.d])
                                cx.op("dve", lambda e, a=a, kf=kf: e.tensor_scalar(kf[:], a[:], 1.0 / TWO_PI, None, ALU.mult), reads=[a.d], writes=[kf.d])
                                cx.op("dve", lambda e, ki=ki, kf=kf: e.tensor_copy(ki[:], kf[:]), reads=[kf.d], writes=[ki.d])
                                cx.op("dve", lambda e, ki=ki, kf=kf: e.tensor_copy(kf[:], ki[:]), reads=[ki.d], writes=[kf.d])
                                cx.op("dve", lambda e, a=a, kf=kf: e.scalar_tensor_tensor(a[:], kf[:], -TWO_PI, a[:], ALU.mult, ALU.add), reads=[a.d, kf.d], writes=[a.d])
                                cx.op("dve", lambda e, a=a, kf=kf: e.tensor_single_scalar(kf[:], a[:], math.pi, ALU.is_gt), reads=[a.d], writes=[kf.d])
                                cx.op("dve", lambda e, a=a, kf=kf: e.scalar_tensor_tensor(a[:], kf[:], -TWO_PI, a[:], ALU.mult, ALU.add), reads=[a.d, kf.d], writes=[a.d])
                                cx.op("dve", lambda e, a=a, kf=kf: e.tensor_single_scalar(kf[:], a[:], -math.pi, ALU.is_lt), reads=[a.d], writes=[kf.d])
                                cx.op("dve", lambda e, a=a, kf=kf: e.scalar_tensor_tensor(a[:], kf[:], TWO_PI, a[:], ALU.mult, ALU.add), reads=[a.d, kf.d], writes=[a.d])
                                cx.op("dve", lambda e, a=a: e.tensor_scalar(a[:], a[:], 3.14159, -3.14159, ALU.min, ALU.max), reads=[a.d], writes=[a.d])
                                cx.op("act", lambda e, a=a: e.activation(a[:], a[:], AF.Sin), reads=[a.d], writes=[a.d])
                                outs.append(a)
                            return outs

                        sm = sb("s5sm", [P, 3, 16], F32, ps_)
                        cx.dma("sp", sm[:], I["s5_sm"], writes=[sm.d])
                        F_ = 16
                        lr = sb("d_lr", [P, F_], F32, ps_)
                        dt = sb("d_dt", [P, F_], F32, ps_)
                        mag = sb("d_mag", [P, F_], F32, ps_)
                        th = sb("d_th", [P, F_], F32, ps_)
                        li_ = sm[:, 1, :]
                        cx.op("dve", lambda e: e.tensor_scalar_min(lr[:], sm[:, 0, :], R_S5_MAX_RE), reads=[sm.d], writes=[lr.d])
                        cx.op("act", lambda e: e.activation(dt[:], sm[:, 2, :], AF.Exp), reads=[sm.d], writes=[dt.d])
                        cx.op("dve", lambda e: e.tensor_tensor(mag[:], lr[:], dt[:], ALU.mult), reads=[lr.d, dt.d], writes=[mag.d])
                        cx.op("act", lambda e: e.activation(mag[:], mag[:], AF.Exp), reads=[mag.d], writes=[mag.d])
                        cx.op("dve", lambda e: e.tensor_tensor(th[:], li_, dt[:], ALU.mult), reads=[dt.d, sm.d], writes=[th.d])
                        sn, cs_ = sincos(th, F_, "sm")
                        cx.op("dve", lambda e: e.tensor_copy(rho[:], mag[:]), reads=[mag.d], writes=[rho.d])
                        cx.op("dve", lambda e: e.tensor_tensor(rhoms[:], rho[:, :].unsqueeze(2).broadcast_to([P, 16, NS]),
                                                              cm[:, 400:464].unsqueeze(1).broadcast_to([P, 16, NS]), ALU.mult),
                              reads=[rho.d, cm.d], writes=[rhoms.d])
                        Ed = Dep()
                        cx.op("dve", lambda e: e.tensor_copy(Ere[:, :, 0], cs_[:]), reads=[cs_.d], writes=[Ed])
                        cx.op("dve", lambda e: e.tensor_scalar(Eim[:, :, 0], sn[:], -1.0, None, ALU.mult), reads=[sn.d], writes=[Ed])
                        t1 = sb("e_t1", [P, 16, NT // 2], F32, ps_)
                        t2 = sb("e_t2", [P, 16, NT // 2], F32, ps_)
                        m_ = 1
                        while m_ < NT:
                            br = Ere[:, :, m_ - 1:m_].broadcast_to([P, 16, m_])
                            bi = Eim[:, :, m_ - 1:m_].broadcast_to([P, 16, m_])
                            ar, ai = Ere[:, :, 0:m_], Eim[:, :, 0:m_]
                            cx.op("dve", lambda e, ar=ar, br=br, m_=m_: e.tensor_tensor(t1[:, :, :m_], ar, br, ALU.mult), reads=[Ed], writes=[t1.d])
                            cx.op("pool", lambda e, ai=ai, bi=bi, m_=m_: e.tensor_tensor(t2[:, :, :m_], ai, bi, ALU.mult), reads=[Ed], writes=[t2.d])
                            cx.op("dve", lambda e, m_=m_: e.tensor_tensor(Ere[:, :, m_:2 * m_], t1[:, :, :m_], t2[:, :, :m_], ALU.subtract), reads=[t1.d, t2.d], writes=[Ed])
                            cx.op("dve", lambda e, ar=ar, bi=bi, m_=m_: e.tensor_tensor(t1[:, :, :m_], ar, bi, ALU.mult), reads=[Ed], writes=[t1.d])
                            cx.op("pool", lambda e, ai=ai, br=br, m_=m_: e.tensor_tensor(t2[:, :, :m_], ai, br, ALU.mult), reads=[Ed], writes=[t2.d])
                            cx.op("dve", lambda e, m_=m_: e.tensor_tensor(Eim[:, :, m_:2 * m_], t1[:, :, :m_], t2[:, :, :m_], ALU.add), reads=[t1.d, t2.d], writes=[Ed])
                            m_ *= 2
                        Ere.d = Ed
                        Eim.d = Ed
                        are = sb("r_are", [P, F_], F32, ps_)
                        aim = sb("r_aim", [P, F_], F32, ps_)
                        den = sb("r_den", [P, F_], F32, ps_)
                        tz = sb("r_tz", [P, F_], F32, ps_)
                        zz = [sb("r_zre", [P, F_], F32, ps_), sb("r_zim", [P, F_], F32, ps_)]
                        zre, zim = zz
                        cx.op("dve", lambda e: e.tensor_tensor(are[:], mag[:], cs_[:], ALU.mult), reads=[mag.d, cs_.d], writes=[are.d])
                        cx.op("dve", lambda e: e.tensor_scalar_add(are[:], are[:], -1.0), reads=[are.d], writes=[are.d])
                        cx.op("dve", lambda e: e.tensor_tensor(aim[:], mag[:], sn[:], ALU.mult), reads=[mag.d, sn.d], writes=[aim.d])
                        cx.op("dve", lambda e: e.tensor_tensor(den[:], lr[:], lr[:], ALU.mult), reads=[lr.d], writes=[den.d])
                        cx.op("dve", lambda e: e.tensor_tensor(tz[:], li_, li_, ALU.mult), reads=[sm.d], writes=[tz.d])
                        cx.op("dve", lambda e: e.tensor_tensor(den[:], den[:], tz[:], ALU.add), reads=[den.d, tz.d], writes=[den.d])
                        cx.op("dve", lambda e: e.reciprocal(den[:], den[:]), reads=[den.d], writes=[den.d])
                        cx.op("dve", lambda e: e.tensor_tensor(zre[:], are[:], lr[:], ALU.mult), reads=[are.d, lr.d], writes=[zre.d])
                        cx.op("dve", lambda e: e.tensor_tensor(tz[:], aim[:], li_, ALU.mult), reads=[aim.d, sm.d, tz.d], writes=[tz.d])
                        cx.op("dve", lambda e: e.tensor_tensor(zre[:], zre[:], tz[:], ALU.add), reads=[zre.d, tz.d], writes=[zre.d])
                        cx.op("dve", lambda e: e.tensor_tensor(zre[:], zre[:], den[:], ALU.mult), reads=[zre.d, den.d], writes=[zre.d])
                        cx.op("dve", lambda e: e.tensor_tensor(zim[:], aim[:], lr[:], ALU.mult), reads=[aim.d, lr.d], writes=[zim.d])
                        cx.op("dve", lambda e: e.tensor_tensor(tz[:], are[:], li_, ALU.mult), reads=[are.d, sm.d, tz.d], writes=[tz.d])
                        cx.op("dve", lambda e: e.tensor_tensor(zim[:], zim[:], tz[:], ALU.subtract), reads=[zim.d, tz.d], writes=[zim.d])
                        cx.op("dve", lambda e: e.tensor_tensor(zim[:], zim[:], den[:], ALU.mult), reads=[zim.d, den.d], writes=[zim.d])
                        bexp = [sb(f"bexp{r}", [P, 4, 512], F32, ps_) for r in range(2)]
                        for r in range(2):
                            cx.dma("sp", bexp[r][:], I["s5_bexp"][r], writes=[bexp[r].d])
                        dg = [sb(f"dg{i}", [P, P], F32, ps_) for i in range(4)]
                        ta = sb("r_ta", [P, 512], F32, ps_)
                        tb = sb("r_tb", [P, 512], F32, ps_)
                        di = 0
                        for c in range(4):
                            pz = [bank(), bank()]
                            for r in range(2):
                                for m in range(4):
                                    dgt = dg[di % 4]
                                    di += 1
                                    cx.op("dve", lambda e, dgt=dgt, r=r, c=c, m=m: e.tensor_scalar(dgt[:], ident[:], zz[r][:, 4 * c + m:4 * c + m + 1], None, ALU.mult),
                                          reads=[ident.d, zz[r].d], writes=[dgt.d])
                                    cx.op("pe", lambda e, dgt=dgt, r=r, m=m, pz=pz: e.matmul(pz[r][:, 128 * m:128 * m + 128], ones32[:], dgt[:], start=True, stop=True),
                                          reads=[ones32.d, dgt.d], writes=[pz[r].d])
                            cx.op("dve", lambda e, c=c, pz=pz: e.tensor_tensor(ta[:], pz[0][:, :], bexp[0][:, c, :], ALU.mult), reads=[pz[0].d, bexp[0].d], writes=[ta.d])
                            cx.op("dve", lambda e, c=c, pz=pz: e.tensor_tensor(tb[:], pz[1][:, :], bexp[1][:, c, :], ALU.mult), reads=[pz[1].d, bexp[1].d], writes=[tb.d])
                            cx.op("dve", lambda e, c=c: e.tensor_tensor(BB[0][:, c, :], ta[:], tb[:], ALU.subtract), reads=[ta.d, tb.d], writes=[BB[0].d])
                            cx.op("dve", lambda e, c=c, pz=pz: e.tensor_tensor(ta[:], pz[0][:, :], bexp[1][:, c, :], ALU.mult), reads=[pz[0].d, bexp[1].d, ta.d], writes=[ta.d])
                            cx.op("dve", lambda e, c=c, pz=pz: e.tensor_tensor(tb[:], pz[1][:, :], bexp[0][:, c, :], ALU.mult), reads=[pz[1].d, bexp[0].d, tb.d], writes=[tb.d])
                            cx.op("dve", lambda e, c=c: e.tensor_tensor(BB[1][:, c, :], ta[:], tb[:], ALU.add), reads=[ta.d, tb.d], writes=[BB[1].d])
                        cexp = sb("cexp", [P, 16, 128], F32, ps_)
                        cx.dma("sp", cexp[:], I["s5_cexp"][0], writes=[cexp.d])
                        cx.op("act", lambda e: e.copy(CC[0][:], cexp[:]), reads=[cexp.d], writes=[CC[0].d])
                        cexp2 = cexp
                        cx.dma("sp", cexp2[:], I["s5_cexp"][1], writes=[cexp2.d])
                        cx.op("act", lambda e: e.mul(CC[1][:], cexp2[:], -1.0), reads=[cexp2.d], writes=[CC[1].d])
                        cx.barrier()
                    RG = 3
                    u32 = [sb(f"s_u32{i}", [P, 4, NT], F32, ph) for i in range(2)]
                    u16 = [sb(f"s_u16{i}", [P, 4, NT], BF16, ph) for i in range(2)]
                    tt = [[sb(f"s_tt{i}_{j}", [P, 2, NT], F32, ph) for j in range(4)] for i in range(1)] * 2
                    wre = [sb(f"s_wre{i}", [P, 2, NT], F32, ph) for i in range(1)] * 2
                    wim = [sb(f"s_wim{i}", [P, 2, NT], F32, ph) for i in range(1)] * 2
                    zre_ = [sb(f"s_zre{i}", [P, 2, NT], F32, ph) for i in range(RG)]
                    zim_ = [sb(f"s_zim{i}", [P, 2, NT], F32, ph) for i in range(RG)]
                    pp = [[sb(f"s_pp{i}_{j}", [P, 2, NT], F32, ph) for j in range(4)] for i in range(2)]
                    xre32 = [sb(f"s_xre{i}", [P, 2, NT], F32, ph) for i in range(2)]
                    xim32 = [sb(f"s_xim{i}", [P, 2, NT], F32, ph) for i in range(2)]
                    xre16 = [sb(f"s_xre16{i}", [P, 4, NT], BF16, ph) for i in range(2)]
                    xim16 = [sb(f"s_xim16{i}", [P, 4, NT], BF16, ph) for i in range(2)]
                    y32 = sb("s_y32", [P, 4, NT], F32, ph)
                    gt = sb("s_gt", [P, 4, NT], F32, ph)
                    yg32 = sb("s_yg32", [P, 4, NT], F32, ph)
                    yg16 = sb("s_yg16", [P, 4, NT], BF16, ph)
                    sgl = sb("s_sgl", [P, NT], F32, ph)
                    on16 = sb("s_on16", [P, 4, NT], BF16, ph)
                    tmpx = sb("s_tmpx", [P, NSEQ_S], F32, ph)
                    tl = tiles_of(LP, NT)

                    def uproj(ti):
                        s0, n, smp = tl[ti]
                        for c in range(4):
                            pb = bank()
                            for k in range(KD):
                                cx.op("pe", lambda e, k=k, pb=pb, c=c: e.matmul(pb[:, :n], w[:, k, 128 * c:128 * c + 128], hall[:, k, s0:s0 + n],
                                                                               start=(k == 0), stop=(k == KD - 1)),
                                      reads=[w.d] + hdeps(s0, n), writes=[pb.d], inc=(k == KD - 1))
                            cx.op("act", lambda e, pb=pb, c=c: e.copy(u32[ti % 2][:, c, :n], pb[:, :n]), reads=[pb.d], writes=[u32[ti % 2].d])
                            cx.op("act", lambda e, pb=pb, c=c: e.copy(u16[ti % 2][:, c, :n], pb[:, :n]), reads=[pb.d], writes=[u16[ti % 2].d])

                    def geom(ti, s_):
                        s0, n, smp = tl[ti]
                        nb, L = (NSEQ_S, LS) if smp else (1, n)
                        if smp:
                            er = Ere[:, s_:s_ + 2, 0:L].unsqueeze(2).broadcast_to([P, 2, nb, L])
                            ei = Eim[:, s_:s_ + 2, 0:L].unsqueeze(2).broadcast_to([P, 2, nb, L])
                            v3 = lambda ap: ap.rearrange("p c (b j) -> p c b j", b=nb)
                        else:
                            er = Ere[:, s_:s_ + 2, 0:n]
                            ei = Eim[:, s_:s_ + 2, 0:n]
                            v3 = lambda ap: ap
                        return s0, n, smp, nb, L, er, ei, v3

                    def stageA(it, ti, s_):
                        s0, n, smp, nb, L, er, ei, v3 = geom(ti, s_)
                        pbk = bank()
                        ut = u16[ti % 2]
                        for ri in range(2):
                            for q in range(2):
                                s2 = s_ + q
                                c, m = s2 // 4, s2 % 4
                                col = (2 * ri + q) * NT
                                cx.op("pe", lambda e, ri=ri, c=c, m=m, col=col: e.matmul(pbk[:, col:col + n], BB[ri][:, c, 128 * m:128 * m + 128], ut[:, c, :n],
                                                                                        start=True, stop=True),
                                      reads=[BB[ri].d, ut.d], writes=[pbk.d], inc=(ri == 1 and q == 1))
                        pk = pbk[:, :].rearrange("p (r q t) -> p r q t", r=2, q=2)
                        pbr, pbi = pk[:, 0, :, :n], pk[:, 1, :, :n]
                        t_ = tt[it % 2]
                        wr_, wi_ = wre[it % 2], wim[it % 2]
                        cx.op("dve", lambda e: e.tensor_tensor(v3(t_[0][:, :, :n]), er, v3(pbr), ALU.mult), reads=[Ere.d, pbk.d], writes=[t_[0].d])
                        cx.op("dve", lambda e: e.tensor_tensor(v3(t_[1][:, :, :n]), ei, v3(pbi), ALU.mult), reads=[Eim.d, pbk.d], writes=[t_[1].d])
                        cx.op("dve", lambda e: e.tensor_tensor(v3(t_[2][:, :, :n]), er, v3(pbi), ALU.mult), reads=[Ere.d, pbk.d], writes=[t_[2].d])
                        cx.op("dve", lambda e: e.tensor_tensor(v3(t_[3][:, :, :n]), ei, v3(pbr), ALU.mult), reads=[Eim.d, pbk.d], writes=[t_[3].d])

                    def stageA2(it, ti, s_):
                        s0, n, smp, nb, L, er, ei, v3 = geom(ti, s_)
                        t_ = tt[it % 2]
                        wr_, wi_ = wre[it % 2], wim[it % 2]
                        cx.op("dve", lambda e: e.tensor_tensor(wr_[:, :, :n], t_[0][:, :, :n], t_[1][:, :, :n], ALU.subtract), reads=[t_[0].d, t_[1].d], writes=[wr_.d])
                        cx.op("dve", lambda e: e.tensor_tensor(wi_[:, :, :n], t_[2][:, :, :n], t_[3][:, :, :n], ALU.add), reads=[t_[2].d, t_[3].d], writes=[wi_.d])

                    def stageA3(it, ti, s_):
                        s0, n, smp, nb, L, er, ei, v3 = geom(ti, s_)
                        wr_, wi_ = wre[it % 2], wim[it % 2]
                        zr_, zi_ = zre_[it % RG], zim_[it % RG]
                        for q in range(2):
                            s2 = s_ + q
                            if smp:
                                for ri, wt in ((0, wr_), (1, wi_)):
                                    cx.op("dve", lambda e, ri=ri, s2=s2: e.tensor_scalar(tmpx[:], x0[:, ri, s2, :], rho[:, s2:s2 + 1], None, ALU.mult),
                                          reads=[x0.d, rho.d], writes=[tmpx.d])
                                    w3 = wt[:, q, :n].rearrange("p (b j) -> p b j", b=nb)
                                    cx.op("dve", lambda e, w3=w3: e.tensor_tensor(w3[:, :, 0], w3[:, :, 0], tmpx[:], ALU.add), reads=[wt.d, tmpx.d], writes=[wt.d])
                                d0 = rhoms[:, s2, :]
                                cx.op("dve", lambda e, q=q, d0=d0: e.tensor_tensor_scan(zr_[:, q, :n], d0, wr_[:, q, :n], 0.0, ALU.mult, ALU.add),
                                      reads=[rhoms.d, wr_.d], writes=[zr_.d])
                                cx.op("dve", lambda e, q=q, d0=d0: e.tensor_tensor_scan(zi_[:, q, :n], d0, wi_[:, q, :n], 0.0, ALU.mult, ALU.add),
                                      reads=[rhoms.d, wi_.d], writes=[zi_.d])
                            else:
                                d0 = rho[:, s2:s2 + 1].broadcast_to([P, n])
                                cx.op("dve", lambda e, q=q, d0=d0, s2=s2: e.tensor_tensor_scan(zr_[:, q, :n], d0, wr_[:, q, :n], xst[:, 0, s2:s2 + 1], ALU.mult, ALU.add),
                                      reads=[rho.d, wr_.d, xst.d], writes=[zr_.d])
                                cx.op("dve", lambda e, q=q, d0=d0, s2=s2: e.tensor_tensor_scan(zi_[:, q, :n], d0, wi_[:, q, :n], xst[:, 1, s2:s2 + 1], ALU.mult, ALU.add),
                                      reads=[rho.d, wi_.d, xst.d], writes=[zi_.d])

                    def stageB(it, ti, s_):
                        s0, n, smp, nb, L, er, ei, v3 = geom(ti, s_)
                        zr_, zi_ = zre_[it % RG], zim_[it % RG]
                        p_ = pp[it % 2]
                        cx.op("pool", lambda e: e.tensor_tensor(v3(p_[0][:, :, :n]), er, v3(zr_[:, :, :n]), ALU.mult), reads=[Ere.d, zr_.d], writes=[p_[0].d])
                        cx.op("pool", lambda e: e.tensor_tensor(v3(p_[1][:, :, :n]), ei, v3(zi_[:, :, :n]), ALU.mult), reads=[Eim.d, zi_.d], writes=[p_[1].d])
                        cx.op("pool", lambda e: e.tensor_tensor(v3(p_[2][:, :, :n]), er, v3(zi_[:, :, :n]), ALU.mult), reads=[Ere.d, zi_.d], writes=[p_[2].d])
                        cx.op("pool", lambda e: e.tensor_tensor(v3(p_[3][:, :, :n]), ei, v3(zr_[:, :, :n]), ALU.mult), reads=[Eim.d, zr_.d], writes=[p_[3].d])

                    def stageC(it, ti, s_):
                        s0, n, smp, nb, L, er, ei, v3 = geom(ti, s_)
                        c, m = s_ // 4, s_ % 4
                        p_ = pp[it % 2]
                        xr_, xi_ = xre32[it % 2], xim32[it % 2]
                        x16r, x16i = xre16[(ti * 4 + c) % 2], xim16[(ti * 4 + c) % 2]
                        cx.op("pool", lambda e: e.tensor_tensor(xr_[:, :, :n], p_[0][:, :, :n], p_[1][:, :, :n], ALU.add), reads=[p_[0].d, p_[1].d], writes=[xr_.d])
                        cx.op("pool", lambda e: e.tensor_tensor(xi_[:, :, :n], p_[2][:, :, :n], p_[3][:, :, :n], ALU.subtract), reads=[p_[2].d, p_[3].d], writes=[xi_.d])
                        cx.op("act", lambda e: e.copy(x16r[:, m:m + 2, :n], xr_[:, :, :n]), reads=[xr_.d], writes=[x16r.d])
                        cx.op("act", lambda e: e.copy(x16i[:, m:m + 2, :n], xi_[:, :, :n]), reads=[xi_.d], writes=[x16i.d])
                        if smp:
                            for q in range(2):
                                xv = xr_[:, q, :n].rearrange("p (b j) -> p b j", b=nb)
                                cx.op("act", lambda e, q=q, xv=xv: e.copy(xso[:, 0, s_ + q, :], xv[:, :, L - 1]), reads=[xr_.d], writes=[xso.d])
                                xv2 = xi_[:, q, :n].rearrange("p (b j) -> p b j", b=nb)
                                cx.op("act", lambda e, q=q, xv2=xv2: e.copy(xso[:, 1, s_ + q, :], xv2[:, :, L - 1]), reads=[xi_.d], writes=[xso.d])
                        else:
                            cx.op("act", lambda e: e.copy(xst[:, 0, s_:s_ + 2], xr_[:, :, n - 1]), reads=[xr_.d], writes=[xst.d])
                            cx.op("act", lambda e: e.copy(xst[:, 1, s_:s_ + 2], xi_[:, :, n - 1]), reads=[xi_.d], writes=[xst.d])
                        if m == 2:
                            pby = bank()
                            for mm_ in range(4):
                                cx.op("pe", lambda e, mm_=mm_: e.matmul(pby[:, :n], CC[0][:, 4 * c + mm_, :], x16r[:, mm_, :n], start=(mm_ == 0), stop=False),
                                      reads=[CC[0].d, x16r.d], writes=[pby.d], inc=False)
                                cx.op("pe", lambda e, mm_=mm_: e.matmul(pby[:, :n], CC[1][:, 4 * c + mm_, :], x16i[:, mm_, :n], start=False, stop=(mm_ == 3)),
                                      reads=[CC[1].d, x16i.d], writes=[pby.d], inc=(mm_ == 3))
                            ut = u32[ti % 2]
                            cx.op("dve", lambda e: e.scalar_tensor_tensor(y32[:, c, :n], ut[:, c, :n], ssm[:, c:c + 1], pby[:, :n], ALU.mult, ALU.add),
                                  reads=[pby.d, ut.d, ssm.d], writes=[y32.d])
                        if s_ == 14:
                            tile_post(ti)

                    def tile_post(ti):
                        s0, n, smp = tl[ti]
                        if ti == len(tl) - 2:
                            cx.dma("sp", O["s5_p"].rearrange("r p s -> p r s"), xst[:], reads=[xst.d], writes=[outdep])
                        cx.op("pool", lambda e: e.tensor_tensor(gt[:, :, :n], y32[:, :, :n], y32[:, :, :n], ALU.mult), reads=[y32.d], writes=[gt.d])
                        cx.op("pool", lambda e: e.tensor_scalar(gt[:, :, :n], gt[:, :, :n], 0.044715, 1.0, ALU.mult, ALU.add), reads=[gt.d], writes=[gt.d])
                        cx.op("pool", lambda e: e.tensor_tensor(gt[:, :, :n], gt[:, :, :n], y32[:, :, :n], ALU.mult), reads=[gt.d, y32.d], writes=[gt.d])
                        cx.op("act", lambda e: e.activation(gt[:, :, :n], gt[:, :, :n], AF.Sigmoid, scale=2.0 * math.sqrt(2.0 / math.pi)), reads=[gt.d], writes=[gt.d])
                        cx.op("dve", lambda e: e.tensor_tensor(yg32[:, :, :n], y32[:, :, :n], gt[:, :, :n], ALU.mult), reads=[gt.d, y32.d], writes=[yg32.d])
                        cx.op("act", lambda e: e.copy(yg16[:, :, :n], yg32[:, :, :n]), reads=[yg32.d], writes=[yg16.d])
                        for c in range(4):
                            pb = bank()
                            for k in range(4):
                                cx.op("pe", lambda e, k=k, pb=pb, c=c: e.matmul(pb[:, :n], wglu[:, k, 128 * c:128 * c + 128], yg16[:, k, :n], start=(k == 0), stop=(k == 3)),
                                      reads=[wglu.d, yg16.d], writes=[pb.d], inc=(k == 3))
                            cx.op("act", lambda e, pb=pb, c=c: e.activation(sgl[:, :n], pb[:, :n], AF.Sigmoid, bias=ssm[:, 4 + c:5 + c]), reads=[pb.d, ssm.d], writes=[sgl.d])
                            cx.op("dve", lambda e, c=c: e.tensor_tensor(on16[:, c, :n], yg32[:, c, :n], sgl[:, :n], ALU.mult), reads=[yg32.d, sgl.d], writes=[on16.d])
                        for oc in range(KD):
                            pb = bank()
                            for k in range(4):
                                cx.op("pe", lambda e, k=k, oc=oc, pb=pb: e.matmul(pb[:, :n], wout[:, k, 128 * oc:128 * oc + 128], on16[:, k, :n], start=(k == 0), stop=(k == 3)),
                                      reads=[wout.d, on16.d], writes=[pb.d], inc=(k == 3))
                            cx.op("dve", lambda e, pb=pb, oc=oc: e.tensor_tensor(x[:, oc, s0:s0 + n], x[:, oc, s0:s0 + n], pb[:, :n], ALU.add),
                                  reads=[pb.d] + xdeps(s0, n), writes=xdeps(s0, n))

                    items = [(ti, s_) for ti in range(len(tl)) for s_ in range(0, 16, 2)]
                    NI = len(items)
                    uproj(0)
                    for i in range(NI + 3):
                        if i < NI:
                            ti, s_ = items[i]
                            if s_ == 8 and ti + 1 < len(tl):
                                uproj(ti + 1)
                            stageA(i, ti, s_)
                        if 0 <= i - 1 < NI:
                            stageA3(i - 1, *items[i - 1])
                        if i < NI:
                            stageA2(i, *items[i])
                        if 0 <= i - 2 < NI:
                            stageB(i - 2, *items[i - 2])
                        if 0 <= i - 3 < NI:
                            stageC(i - 3, *items[i - 3])
                    cx.dma("sp", O["s5_s"].rearrange("r p s b -> p r s b"), xso[:], reads=[xso.d], writes=[outdep])
                    cx.barrier()

        for l in range(2):
            if l == 0 and any(p in cfg.phases for p in ("A0", "A0a", "A0b")):
                a0_phase(cfg.NTA, cfg.NTB)
            if l == 1 and "A1" in cfg.phases:
                a1_phase()
            if f"X{l}" in cfg.phases:
                xattn_phase(l)
            if f"F{l}" in cfg.phases:
                ffn_phase(l)

        with contextlib.ExitStack() as ph:
            scr = {"sq16": sb("n_sq16", [P, KD, 512], BF16, ph), "rstd": sb("n_rstd", [P, 512], F32, ph), "eps": epsb}
            yo = [sb(f"yo{i}", [P, KD, 512], F32, ph) for i in range(2)]
            for i, (s0, n, smp) in enumerate(tiles_of(LP, 512)):
                yt = yo[i % 2]
                rmsnorm(lambda k: x[:, k, s0:s0 + n], xdeps(s0, n), n, 6, lambda k: yt[:, k, :n], [yt.d], scr)
                cx.dma("sp", O["yT"][:, :, s0:s0 + n].rearrange("k p t -> p k t"), yt[:, :, :n], reads=[yt.d], writes=[outdep])
            cx.barrier()
        cx.barrier(engines=("sp",))
        stats = {k: (e.n_ins, e.cnt) for k, e in cx.E.items()}
    return nc, stats


def fm(v, nchunk):
    v = np.asarray(v)
    lead = v.shape[:-1]
    r = v.reshape(lead + (nchunk, 128))
    return np.ascontiguousarray(np.moveaxis(r, -1, 0))


def prep_core(inp, c, cfg):
    LP, T = cfg.LP, cfg.T
    m = {}
    xp = inp["x_prompt"][c, :LP]
    xs = inp["x_sample"][NSEQ_S * c:NSEQ_S * (c + 1)].reshape(NS, D)
    xT = np.concatenate([xp, xs], axis=0).T
    m["xT"] = np.ascontiguousarray(xT.reshape(KD, 128, T))
    m["ident"] = np.eye(128, dtype=np.float32)
    nl = [inp["norm_mix"][0], inp["norm_mix"][1], inp["norm_cross"][0], inp["norm_cross"][1],
          inp["norm_ffn"][0], inp["norm_ffn"][1], inp["norm_final"]]
    m["norms"] = np.ascontiguousarray(np.stack([v.reshape(KD, 128).T for v in nl], axis=1))
    m["ffn_up"] = np.ascontiguousarray(inp["ffn_w_up"].reshape(2, KD, 128, 2 * FFN))
    m["ffn_dn"] = np.ascontiguousarray(inp["ffn_w_down"].reshape(2, NFC, 128, D))
    cw = inp["ffn_conv_w"].reshape(2, 3, 2 * NFC, 128)
    m["convw"] = np.ascontiguousarray(cw.transpose(0, 3, 2, 1))
    m["convb"] = np.ascontiguousarray(inp["ffn_conv_b"].reshape(2, 2 * NFC, 128).transpose(0, 2, 1))
    cs = inp["state_ffn_conv"][:, NSEQ_S * c:NSEQ_S * (c + 1)]
    cs = cs.reshape(2, NSEQ_S, 2, 2 * NFC, 128)
    m["convst"] = np.ascontiguousarray(cs.transpose(0, 4, 3, 1, 2))
    cmk = np.zeros((128, 464), np.float32)
    jj, ii = np.meshgrid(np.arange(64), np.arange(64), indexing="ij")
    cmk[:64, 0:64] = (jj <= ii)
    cmk[:64, 64:128] = (jj <= ii) & (jj // LS == ii // LS)
    cmk[:64, 128:144] = (np.arange(64)[:, None] // LS == np.arange(NSEQ_S)[None, :])
    cmk[:, 144:400] = (np.arange(256) % GC != 0)[None, :]
    cmk[:, 400:464] = (np.arange(64) % LS != 0)[None, :]
    m["cmask"] = cmk
    cm2 = np.zeros((128, 384), np.float32)
    j2, i2 = np.meshgrid(np.arange(128), np.arange(128), indexing="ij")
    cm2[:, 0:128] = (j2 <= i2)
    cm2[:, 128:384] = (np.arange(256) % 128 != 0)[None, :]
    m["cmask2"] = cm2
    m["w_in_c"] = inp["w_in_c"][0].reshape(KD, 128, 3088)
    m["w_out_c"] = inp["w_out_c"][0].reshape(KD, 128, D)
    m["gla_wgu"] = inp["gla_w_gate_up"][0]
    m["gla_small"] = np.concatenate([inp["gla_b_gate"][0].reshape(4, 128).T, inp["gla_gnorm"][0].reshape(2, 128).T], axis=1)
    m["st_gla"] = inp["state_gla"][0, NSEQ_S * c:NSEQ_S * (c + 1)]
    m["w_in_ab"] = inp["w_in_ab"][0].reshape(KD, 128, 2560)
    m["w_out_ab"] = inp["w_out_ab"][0].reshape(KD, 128, D)
    lbr = inp["hgrn_lb"].reshape(3, 4, 128).transpose(2, 1, 0).reshape(128, 12)
    m["hg_small"] = np.concatenate([lbr, inp["hgrn_gnorm"][0].reshape(128, 1)], axis=1)
    m["st_hgrn"] = inp["state_hgrn"][0, NSEQ_S * c:NSEQ_S * (c + 1)]

    def sm16(a):
        return a.reshape(16, 128).T
    ls_full = np.repeat(inp["s5_log_step"][0][:, None], 64, axis=1)
    m["s5_sm"] = np.stack([sm16(inp["s5_lam_re"][0]), sm16(inp["s5_lam_im"][0]), sm16(ls_full)], axis=1)
    m["s5_row"] = np.stack([inp["s5_lam_re"][0].reshape(2048), inp["s5_lam_im"][0].reshape(2048), ls_full.reshape(2048)], axis=0)
    bexp = np.zeros((2, 128, 4, 512), np.float32)
    cexp = np.zeros((2, 128, 16, 128), np.float32)
    for r, (bsrc, csrc) in enumerate(((inp["s5_b_re"][0], inp["s5_c_re"][0]), (inp["s5_b_im"][0], inp["s5_c_im"][0]))):
        for g_ in range(32):
            cc_, gl = g_ // 8, g_ % 8
            bexp[r, gl * 16:(gl + 1) * 16, cc_, gl * 64:(gl + 1) * 64] = bsrc[g_].T
            s_, two = g_ // 2, g_ % 2
            cexp[r, two * 64:(two + 1) * 64, s_, gl * 16:(gl + 1) * 16] = csrc[g_].T
    m["s5_bexp"] = bexp
    m["s5_cexp"] = cexp
    m["s5_small"] = np.concatenate([inp["s5_d"][0].reshape(4, 128).T, inp["s5_b_glu"][0].reshape(4, 128).T], axis=1)
    m["s5_wglu"] = inp["s5_w_glu"][0].reshape(4, 128, 512)
    x0 = np.stack([inp["state_s5_re"][0, NSEQ_S * c:NSEQ_S * (c + 1)], inp["state_s5_im"][0, NSEQ_S * c:NSEQ_S * (c + 1)]], 0)
    m["s5_x0"] = x0.reshape(2, NSEQ_S, 16, 128).transpose(0, 3, 2, 1)
    m["memT"] = inp["mem_prompt"][c].T.reshape(KD, 128, NMEM)
    m["normmem"] = np.stack([inp["norm_mem"][l].reshape(KD, 128).T for l in range(2)], axis=1)
    m["wkv"] = inp["xa_w_kv"].reshape(2, KD, 128, 2 * D)
    m["wq"] = inp["xa_w_q"].reshape(2, KD, 128, D)
    m["wo"] = inp["xa_w_o"].reshape(2, KD, 128, D)
    ck = inp["cache_mem_k"][:, NSEQ_S * c:NSEQ_S * (c + 1)].reshape(2, NSEQ_S, NMEM, D)
    m["kTc"] = ck.transpose(0, 1, 3, 2).reshape(2, NSEQ_S, KD, 128, NMEM)
    m["vc"] = inp["cache_mem_v"][:, NSEQ_S * c:NSEQ_S * (c + 1)].reshape(2, NSEQ_S, 2, 128, D)
    return {k: np.ascontiguousarray(v, dtype=np.float32) for k, v in m.items()}


_CACHE = {}


def run(inputs, cfg, n_cores=8):
    key = (cfg.LP, tuple(cfg.phases), cfg.xmode)
    if key not in _CACHE:
        _CACHE[key] = build(cfg)
    nc, stats = _CACHE[key]
    in_maps = [prep_core(inputs, c, cfg) for c in range(n_cores)]
    res = run_bass_kernel_spmd(nc, in_maps, core_ids=list(range(n_cores)))
    return res.results, stats


def assemble(results, cfg, n_cores=8):
    LP, T = cfg.LP, cfg.T
    out = {}
    yp, ys, cp, cs = [], [], [], []
    mk, mv = [], []
    glp, gls = [], []
    hgp, hgs, s5p, s5s = [], [], [], []
    for c in range(n_cores):
        r = results[c]
        hgp.append(r["hgrn_p"]); hgs.append(r["hgrn_s"])
        s5p.append(r["s5_p"].transpose(0, 2, 1).reshape(2, 32, 64))
        s5s.append(r["s5_s"].transpose(0, 3, 2, 1).reshape(2, NSEQ_S, 32, 64))
        glp.append(r["gla_p"]); gls.append(r["gla_s"])
        mk.append(r["mem_kT"].reshape(2, D, NMEM).transpose(0, 2, 1).reshape(2, NMEM, 4, 256))
        mv.append(r["mem_v"].reshape(2, NMEM, 4, 256))
        yT = r["yT"].reshape(D, T)
        yp.append(yT[:, :LP].T)
        ys.append(yT[:, LP:].T.reshape(NSEQ_S, LS, D))
        cp.append(r["conv_p"].transpose(0, 3, 2, 1).reshape(2, 2, 2 * FFN))
        cs.append(r["conv_s"].transpose(0, 3, 4, 2, 1).reshape(2, NSEQ_S, 2, 2 * FFN))
    out["hgrn_p"] = np.stack(hgp, 0)[None]
    out["hgrn_s"] = np.concatenate(hgs, 0)[None]
    s5p = np.stack(s5p, 1); s5s = np.concatenate(s5s, 1)
    out["s5_re_p"] = s5p[0][None]; out["s5_im_p"] = s5p[1][None]
    out["s5_re_s"] = s5s[0][None]; out["s5_im_s"] = s5s[1][None]
    out["gla_p"] = np.stack(glp, 0)[None]
    out["gla_s"] = np.concatenate(gls, 0)[None]
    out["mem_k_p"] = np.stack(mk, 1)
    out["mem_v_p"] = np.stack(mv, 1)
    out["y_prompt"] = np.stack(yp, 0)
    out["y_sample"] = np.concatenate(ys, 0)
    out["conv_p"] = np.stack(cp, 1)
    out["conv_s"] = np.concatenate(cs, 1)
    return out


def kernel(**inputs):
    cfg = Cfg()
    inputs = {k: np.asarray(v) for k, v in inputs.items()}
    results, _ = run(inputs, cfg)
    o = assemble(results, cfg)
    names = ["y_prompt", "y_sample", "hgrn_p", "s5_re_p", "s5_im_p", "gla_p", "mem_k_p", "mem_v_p", "conv_p",
             "hgrn_s", "s5_re_s", "s5_im_s", "gla_s", "conv_s"]
    return tuple(np.ascontiguousarray(o[k], dtype=np.float32) for k in names)
```

```python
import contextlib
import math
import numpy as np
import concourse.bass as bass
import concourse.mybir as mybir
from concourse.bass_utils import run_bass_kernel_spmd

F32 = mybir.dt.float32
BF16 = mybir.dt.bfloat16
AF = mybir.ActivationFunctionType
ALU = mybir.AluOpType

D = 1024
KD = 8
NSEQ_S = 16
LS = 4
NS = NSEQ_S * LS
FFN = 2816
NFC = 22
NMEM = 256
EPS = 1e-6
GC = 64


class Dep:
    __slots__ = ("writer", "readers", "excl")

    def __init__(self, excl=False):
        self.writer = None
        self.readers = {}
        self.excl = excl


class Eng:
    def __init__(self, name, h, sem, is_pe=False):
        self.name, self.h, self.sem, self.is_pe = name, h, sem, is_pe
        self.cnt = 0
        self.waited = {}
        self.n_ins = 0

    def wait(self, ev):
        sem, val = ev
        if sem is self.sem and self.is_pe:
            return
        if self.waited.get(sem, 0) >= val:
            return
        self.waited[sem] = val
        self.h.wait_ge(sem, val)


class Ctx:
    def __init__(self, nc, stack, n_dma_sems=(20, 28, 8)):
        self.nc = nc
        mk = lambda n: stack.enter_context(nc.semaphore(n))
        self.E = {
            "pe": Eng("pe", nc.tensor, mk("s_pe"), is_pe=True),
            "act": Eng("act", nc.scalar, mk("s_act")),
            "dve": Eng("dve", nc.vector, mk("s_dve")),
            "pool": Eng("pool", nc.gpsimd, mk("s_pool")),
            "sp": Eng("sp", nc.sync, mk("s_sp")),
        }
        self.dma_sems = {}
        for q, n in zip(("sp", "pool", "act"), n_dma_sems):
            self.dma_sems[q] = [[mk(f"d_{q}{i}"), 0] for i in range(n)]
        self.dma_rr = {"sp": 0, "pool": 0, "act": 0}

    def _pre(self, E, reads, writes):
        for d in reads:
            if d.writer is not None:
                E.wait(d.writer)
            if d.excl:
                for ev in d.readers.values():
                    if ev[0] is not E.sem:
                        E.wait(ev)
        for d in writes:
            if d.writer is not None:
                E.wait(d.writer)
            for ev in d.readers.values():
                E.wait(ev)

    @staticmethod
    def _post(ev, reads, writes):
        for d in reads:
            d.readers[ev[0]] = ev
        for d in writes:
            d.writer = ev
            d.readers = {}

    def op(self, eng, fn, reads=(), writes=(), inc=True):
        E = self.E[eng]
        self._pre(E, reads, writes)
        ins = fn(E.h)
        E.n_ins += 1
        if inc:
            E.cnt += 1
            ins.then_inc(E.sem, 1)
            ev = (E.sem, E.cnt)
        else:
            ev = (E.sem, E.cnt + 1)
        self._post(ev, reads, writes)
        return ins

    def dma(self, q, out, in_, reads=(), writes=(), **kw):
        E = self.E[q]
        self._pre(E, reads, writes)
        pool = self.dma_sems[q]
        i = self.dma_rr[q]
        self.dma_rr[q] = (i + 1) % len(pool)
        slot = pool[i]
        if slot[1] > 0:
            E.wait((slot[0], slot[1]))
        slot[1] += 16
        ins = E.h.dma_start(out=out, in_=in_, **kw)
        ins.then_inc(slot[0], 16)
        E.n_ins += 1
        self._post((slot[0], slot[1]), reads, writes)
        return ins

    def barrier(self, engines=("pe", "act", "dve", "pool", "sp")):
        for en in engines:
            E = self.E[en]
            for F in self.E.values():
                if F is not E and F.cnt > 0:
                    E.wait((F.sem, F.cnt))
            for pool in self.dma_sems.values():
                for slot in pool:
                    if slot[1] > 0:
                        E.wait((slot[0], slot[1]))


class View:
    def __init__(self, ap):
        self.ap = ap
        self.d = Dep()

    def __getitem__(self, k):
        return self.ap[k]


class Buf:
    def __init__(self, t):
        self.t = t
        self.d = Dep()

    def __getitem__(self, k):
        return self.t[k]


class Cfg:
    def __init__(self, LP=2048, phases=("A0", "X0", "F0", "A1", "X1", "F1")):
        self.LP = LP
        self.T = LP + NS
        self.phases = phases
        self.NTA = 256
        self.xmode = "full"
        self.NTB = 128


def tiles_of(LP, n):
    ts = [(i, n, False) for i in range(0, LP, n)]
    ts.append((LP, NS, True))
    return ts


def build(cfg):
    nc = bass.Bass("TRN2", target_bir_lowering=False)
    LP, T = cfg.LP, cfg.T
    P = 128

    def din(name, shape, dt=F32):
        return nc.dram_tensor(name, list(shape), dt, kind="ExternalInput").ap()

    def dout(name, shape):
        return nc.dram_tensor(name, list(shape), F32, kind="ExternalOutput").ap()

    I = {}
    I["xT"] = din("xT", [KD, P, T])
    I["ident"] = din("ident", [P, P])
    I["norms"] = din("norms", [P, 7, KD])
    I["ffn_up"] = din("ffn_up", [2, KD, P, 2 * FFN])
    I["ffn_dn"] = din("ffn_dn", [2, NFC, P, D])
    I["convw"] = din("convw", [2, P, 2 * NFC, 3])
    I["convb"] = din("convb", [2, P, 2 * NFC])
    I["convst"] = din("convst", [2, P, 2 * NFC, NSEQ_S, 2])
    I["cmask"] = din("cmask", [P, 464])
    I["cmask2"] = din("cmask2", [P, 384])
    I["w_in_c"] = din("w_in_c", [KD, P, 3088])
    I["w_out_c"] = din("w_out_c", [KD, P, D])
    I["gla_wgu"] = din("gla_wgu", [16, 512])
    I["gla_small"] = din("gla_small", [P, 6])
    I["st_gla"] = din("st_gla", [NSEQ_S, 4, P, 256])
    I["w_in_ab"] = din("w_in_ab", [KD, P, 2560])
    I["w_out_ab"] = din("w_out_ab", [KD, P, D])
    I["hg_small"] = din("hg_small", [P, 13])
    I["st_hgrn"] = din("st_hgrn", [NSEQ_S, 4, P, 128])
    I["s5_sm"] = din("s5_sm", [P, 3, 16])
    I["s5_row"] = din("s5_row", [3, 2048])
    I["s5_bexp"] = din("s5_bexp", [2, P, 4, 512])
    I["s5_cexp"] = din("s5_cexp", [2, P, 16, 128])
    I["s5_small"] = din("s5_small", [P, 8])
    I["s5_wglu"] = din("s5_wglu", [4, P, 512])
    I["s5_x0"] = din("s5_x0", [2, P, 16, NSEQ_S])
    I["memT"] = din("memT", [KD, P, NMEM])
    I["normmem"] = din("normmem", [P, 2, KD])
    I["wkv"] = din("wkv", [2, KD, P, 2 * D])
    I["wq"] = din("wq", [2, KD, P, D])
    I["wo"] = din("wo", [2, KD, P, D])
    I["kTc"] = din("kTc", [2, NSEQ_S, KD, P, NMEM])
    I["vc"] = din("vc", [2, NSEQ_S, 2, P, D])
    O = {}
    O["mem_kT"] = dout("mem_kT", [2, KD, P, NMEM])
    O["mem_v"] = dout("mem_v", [2, 2, P, D])
    O["hgrn_p"] = dout("hgrn_p", [4, P, 128])
    O["hgrn_s"] = dout("hgrn_s", [NSEQ_S, 4, P, 128])
    O["s5_p"] = dout("s5_p", [2, P, 16])
    O["s5_s"] = dout("s5_s", [2, P, 16, NSEQ_S])
    O["gla_p"] = dout("gla_p", [4, P, 256])
    O["gla_s"] = dout("gla_s", [NSEQ_S, 4, P, 256])
    O["yT"] = dout("yT", [KD, P, T])
    O["conv_p"] = dout("conv_p", [2, P, 2 * NFC, 2])
    O["conv_s"] = dout("conv_s", [2, P, 2 * NFC, NSEQ_S, 2])

    with contextlib.ExitStack() as st:
        cx = Ctx(nc, st)

        uid = [0]

        def sb(name, shape, dt, stack=st):
            uid[0] += 1
            return Buf(stack.enter_context(nc.sbuf_tensor(f"s{uid[0]}_{name}", list(shape), dt)))

        banks = [Buf(st.enter_context(nc.psum_tensor(f"bank{i}", [P, 512], F32))) for i in range(8)]
        for b_ in banks:
            b_.d.excl = True
        bank_rr = [0]

        reserved = set()

        def bank():
            while True:
                i = bank_rr[0]
                bank_rr[0] = (bank_rr[0] + 1) % 8
                if i not in reserved:
                    return banks[i]

        def reserve():
            b = bank()
            reserved.add(banks.index(b))
            return b

        def release(b):
            reserved.discard(banks.index(b))

        x = sb("x", [P, KD, T], F32)
        xd = {}
        for s0 in range(0, T, 256):
            xd[s0] = Dep()

        def xdeps(s0, n):
            return [xd[k] for k in range((s0 // 256) * 256, s0 + n, 256)]

        ident = sb("ident", [P, P], F32)
        ident16 = sb("ident16", [P, P], BF16)
        ones16 = sb("ones16", [P, P], BF16)
        norms = sb("norms", [P, 7, KD], F32)
        outdep = Dep()

        cx.dma("sp", ident[:], I["ident"], writes=[ident.d])
        cx.dma("sp", norms[:], I["norms"], writes=[norms.d])
        cx.op("dve", lambda e: e.tensor_copy(ident16[:], ident[:]), reads=[ident.d], writes=[ident16.d])
        cx.op("dve", lambda e: e.memset(ones16[:], 1.0), writes=[ones16.d])
        for s0 in range(0, T, 256):
            n = min(256, T - s0)
            cx.dma("sp", x[:, :, s0:s0 + n], I["xT"][:, :, s0:s0 + n].rearrange("k p t -> p k t"),
                   writes=[xd[s0]])

        def rmsnorm(src_fn, src_deps, n, gidx, out_fn, out_deps, scr, nk=KD, dim=D, out_eng="dve"):
            sq, rstd = scr["sq16"], scr["rstd"]
            for k in range(nk):
                cx.op("act", lambda e, k=k: e.activation(sq[:, k, :n], src_fn(k), AF.Square),
                      reads=src_deps, writes=[sq.d])
            pb = bank()
            for k in range(nk):
                cx.op("pe", lambda e, k=k: e.matmul(pb[:, :n], ones16[:], sq[:, k, :n], start=(k == 0), stop=(k == nk - 1)),
                      reads=[sq.d, ones16.d], writes=[pb.d], inc=(k == nk - 1))
            cx.op("act", lambda e: e.activation(rstd[:, :n], pb[:, :n], AF.Ln, bias=scr["eps"][:, 0:1], scale=1.0 / dim),
                  reads=[pb.d, scr["eps"].d], writes=[rstd.d])
            cx.op("act", lambda e: e.activation(rstd[:, :n], rstd[:, :n], AF.Exp, scale=-0.5), reads=[rstd.d], writes=[rstd.d])
            for k in range(nk):
                cx.op(out_eng, lambda e, k=k: e.scalar_tensor_tensor(out_fn(k), src_fn(k), norms[:, gidx, k:k + 1], rstd[:, :n],
                                                                      ALU.mult, ALU.mult),
                      reads=list(src_deps) + [rstd.d, norms.d], writes=out_deps)

        epsb = sb("epsb", [P, 1], F32)
        cx.op("dve", lambda e: e.memset(epsb[:], EPS), writes=[epsb.d])

        def ffn_phase(l):
            with contextlib.ExitStack() as ph:
                hall = sb("hall", [P, KD, T], BF16, ph)
                hd = {s0: Dep() for s0 in range(0, T, 512)}
                cw = sb("cw", [P, 2 * NFC, 3], F32, ph)
                cb = sb("cb", [P, 2 * NFC], F32, ph)
                cst = sb("cst", [P, 2 * NFC, NSEQ_S, 2], F32, ph)
                tail = sb("tail", [P, 2 * NFC, 2], F32, ph)
                cso = sb("cso", [P, 2 * NFC, NSEQ_S, 2], F32, ph)
                cx.dma("sp", cw[:], I["convw"][l], writes=[cw.d])
                cx.dma("sp", cb[:], I["convb"][l], writes=[cb.d])
                cx.dma("sp", cst[:], I["convst"][l], writes=[cst.d])
                cx.op("dve", lambda e: e.memset(tail[:], 0.0), writes=[tail.d])
                tl = tiles_of(LP, 512)
                GS = 4
                groups = [(c0, min(GS, NFC - c0)) for c0 in range(0, NFC, GS)]
                NG = len(groups)
                RING = 2
                wup = [sb(f"wup{i}", [P, KD, 2, 128 * GS], BF16, ph) for i in range(RING)]
                wdn = [sb(f"wdn{i}", [P, GS, D], BF16, ph) for i in range(RING)]
                upv = I["ffn_up"][l].rearrange("k p n -> p k n")

                def load_group(g):
                    r = g % RING
                    c0, gs = groups[g]
                    for half in range(2):
                        cstart = half * FFN + 128 * c0
                        cx.dma("pool", wup[r][:, :, half, :128 * gs], upv[:, :, cstart:cstart + 128 * gs], writes=[wup[r].d])
                    cx.dma("pool", wdn[r][:, :gs, :], I["ffn_dn"][l][c0:c0 + gs].rearrange("j p n -> p j n"), writes=[wdn[r].d])

                load_group(0)
                with contextlib.ExitStack() as phn:
                    scr = {"sq16": sb("f_sq16", [P, KD, 512], BF16, phn), "rstd": sb("f_rstd", [P, 512], F32, phn), "eps": epsb}
                    for (s0, n, smp) in tl:
                        rmsnorm(lambda k: x[:, k, s0:s0 + n], xdeps(s0, n), n, 4 + l,
                                lambda k: hall[:, k, s0:s0 + n], [hd[s0]], scr)
                    cx.barrier()
                ext = [sb(f"ext{i}", [P, 512 + 2 * NSEQ_S], F32, ph) for i in range(4)]
                cc = [sb(f"cc{i}", [P, 512], F32, ph) for i in range(4)]
                sa = [sb(f"sa{i}", [P, 512], F32, ph) for i in range(2)]
                yb = [sb(f"yb{i}", [P, GS, 512], BF16, ph) for i in range(2)]
                itc = [0]
                steps = [(g, ti) for g in range(NG) for ti in range(len(tl))]

                def up_part(si):
                    g, ti = steps[si]
                    s0, n, smp = tl[ti]
                    r = g % RING
                    c0, gs = groups[g]
                    if True:
                        nseq, L = (NSEQ_S, LS) if smp else (1, n)
                        yt = yb[si % 2]
                        for j in range(gs):
                            it = itc[0]
                            ch = [c0 + j, NFC + c0 + j]
                            cres = []
                            for half in range(2):
                                c = ch[half]
                                pb = bank()
                                for k in range(KD):
                                    cx.op("pe", lambda e, k=k, half=half, pb=pb: e.matmul(
                                        pb[:, :n], wup[r][:, k, half, 128 * j:128 * j + 128], hall[:, k, s0:s0 + n],
                                        start=(k == 0), stop=(k == KD - 1)),
                                        reads=[wup[r].d, hd[s0]], writes=[pb.d], inc=(k == KD - 1))
                                et = ext[(2 * it + half) % 4]
                                ct = cc[(2 * it + half) % 4]
                                ev = et[:, :nseq * (L + 2)].rearrange("p (s l) -> p s l", s=nseq)
                                pv = pb[:, :n].rearrange("p (s l) -> p s l", s=nseq)
                                cv = ct[:, :n].rearrange("p (s l) -> p s l", s=nseq)
                                if smp:
                                    cx.op("pool", lambda e, ev=ev, c=c: e.tensor_copy(ev[:, :, 0:2], cst[:, c, :, :]),
                                          reads=[cst.d], writes=[et.d])
                                else:
                                    cx.op("pool", lambda e, ev=ev, c=c: e.tensor_copy(ev[:, :, 0:2], tail[:, c:c + 1, :]),
                                          reads=[tail.d], writes=[et.d])
                                cx.op("act", lambda e, ev=ev, pv=pv: e.copy(ev[:, :, 2:L + 2], pv), reads=[pb.d], writes=[et.d])
                                cx.op("act", lambda e, cv=cv, pv=pv, c=c: e.activation(cv, pv, AF.Identity, bias=cb[:, c:c + 1],
                                                                                      scale=cw[:, c, 2:3]),
                                      reads=[pb.d, cw.d, cb.d], writes=[ct.d])
                                cx.op("dve", lambda e, cv=cv, ev=ev, c=c: e.scalar_tensor_tensor(cv, ev[:, :, 1:L + 1], cw[:, c, 1:2], cv,
                                                                                              ALU.mult, ALU.add),
                                      reads=[et.d, cw.d, ct.d], writes=[ct.d])
                                cx.op("dve", lambda e, cv=cv, ev=ev, c=c: e.scalar_tensor_tensor(cv, ev[:, :, 0:L], cw[:, c, 0:1], cv,
                                                                                              ALU.mult, ALU.add),
                                      reads=[et.d, cw.d, ct.d], writes=[ct.d])
                                if smp:
                                    cx.op("pool", lambda e, ev=ev, c=c: e.tensor_copy(cso[:, c, :, :], ev[:, :, L:L + 2]),
                                          reads=[et.d], writes=[cso.d])
                                else:
                                    cx.op("pool", lambda e, ev=ev, c=c: e.tensor_copy(tail[:, c:c + 1, :], ev[:, :, L:L + 2]),
                                          reads=[et.d], writes=[tail.d])
                                cres.append(ct)
                            st_ = sa[it % 2]
                            cx.op("act", lambda e, st_=st_, ct=cres[0]: e.activation(st_[:, :n], ct[:, :n], AF.Silu),
                                  reads=[cres[0].d], writes=[st_.d])
                            cx.op("dve", lambda e, st_=st_, ct=cres[1], yt=yt: e.tensor_tensor(yt[:, j, :n], st_[:, :n], ct[:, :n], ALU.mult),
                                  reads=[st_.d, cres[1].d], writes=[yt.d])
                            itc[0] += 1

                def down_part(si):
                    g, ti = steps[si]
                    s0, n, smp = tl[ti]
                    r = g % RING
                    c0, gs = groups[g]
                    yt = yb[si % 2]
                    if True:
                        for oc in range(KD):
                            pb = bank()
                            for j in range(gs):
                                cx.op("pe", lambda e, j=j, pb=pb, oc=oc, yt=yt: e.matmul(
                                    pb[:, :n], wdn[r][:, j, 128 * oc:128 * oc + 128], yt[:, j, :n], start=(j == 0), stop=(j == gs - 1)),
                                    reads=[wdn[r].d, yt.d], writes=[pb.d], inc=(j == gs - 1))
                            cx.op("dve", lambda e, pb=pb, oc=oc: e.tensor_tensor(x[:, oc, s0:s0 + n], x[:, oc, s0:s0 + n], pb[:, :n], ALU.add),
                                  reads=[pb.d] + xdeps(s0, n), writes=xdeps(s0, n))

                for si in range(len(steps) + 1):
                    if si < len(steps):
                        up_part(si)
                    if si >= 1:
                        down_part(si - 1)
                    if si < len(steps) and steps[si][1] == 0:
                        g = steps[si][0]
                        if g + RING - 1 < NG:
                            load_group(g + RING - 1)
                cx.dma("sp", O["conv_p"][l], tail[:], reads=[tail.d], writes=[outdep])
                cx.dma("sp", O["conv_s"][l], cso[:], reads=[cso.d], writes=[outdep])
                cx.barrier()


        KT16 = [None, None]
        V16 = [None, None]

        def mem_phase(l, after_w=None):
            with contextlib.ExitStack() as ph:
                memT = sb("memT", [P, KD, NMEM], F32, ph)
                nm = sb("nm", [P, 2, KD], F32, ph)
                sq = sb("m_sq", [P, KD, NMEM], BF16, ph)
                rstd = sb("m_rstd", [P, NMEM], F32, ph)
                cx.dma("sp", memT[:], I["memT"].rearrange("k p m -> p k m"), writes=[memT.d])
                cx.dma("sp", nm[:], I["normmem"], writes=[nm.d])
                cx.op("act", lambda e: e.activation(sq[:], memT[:], AF.Square), reads=[memT.d], writes=[sq.d])
                pb = bank()
                for k in range(KD):
                    cx.op("pe", lambda e, k=k: e.matmul(pb[:, :NMEM], ones16[:], sq[:, k, :], start=(k == 0), stop=(k == KD - 1)),
                          reads=[sq.d, ones16.d], writes=[pb.d], inc=(k == KD - 1))
                cx.op("act", lambda e: e.activation(rstd[:], pb[:, :NMEM], AF.Ln, bias=epsb[:, 0:1], scale=1.0 / D),
                      reads=[pb.d, epsb.d], writes=[rstd.d])
                cx.op("act", lambda e: e.activation(rstd[:], rstd[:], AF.Exp, scale=-0.5), reads=[rstd.d], writes=[rstd.d])
                if True:
                    wkv = sb(f"wkv{l}", [P, KD, 2 * D], BF16, ph)
                    memn = sb(f"memn{l}", [P, KD, NMEM], BF16, ph)
                    ko = sb(f"ko{l}", [P, KD, NMEM], F32, ph)
                    vo = sb(f"vo{l}", [P, 2, D], F32, ph)
                    wv = I["wkv"][l].rearrange("k p n -> p k n")
                    for hh in range(2):
                        cx.dma("pool", wkv[:, :, hh * D:(hh + 1) * D], wv[:, :, hh * D:(hh + 1) * D], writes=[wkv.d])
                    if after_w is not None:
                        after_w()
                    for k in range(KD):
                        cx.op("dve", lambda e, k=k: e.scalar_tensor_tensor(memn[:, k, :], memT[:, k, :], nm[:, l, k:k + 1], rstd[:],
                                                                          ALU.mult, ALU.mult),
                              reads=[memT.d, nm.d, rstd.d], writes=[memn.d])
                    for c in range(KD):
                        pb = bank()
                        for k in range(KD):
                            cx.op("pe", lambda e, k=k, c=c, pb=pb: e.matmul(pb[:, :NMEM], wkv[:, k, 128 * c:128 * c + 128], memn[:, k, :],
                                                                           start=(k == 0), stop=(k == KD - 1)),
                                  reads=[wkv.d, memn.d], writes=[pb.d], inc=(k == KD - 1))
                        cx.op("act", lambda e, c=c, pb=pb: e.copy(KT16[l][:, c, :], pb[:, :NMEM]), reads=[pb.d], writes=[KT16[l].d])
                        cx.op("dve", lambda e, c=c, pb=pb: e.tensor_copy(ko[:, c, :], pb[:, :NMEM]), reads=[pb.d], writes=[ko.d])
                    cx.dma("sp", O["mem_kT"][l].rearrange("k p m -> p k m"), ko[:], reads=[ko.d], writes=[outdep])
                    for mc in range(2):
                        for cb_ in range(2):
                            pb = bank()
                            for k in range(KD):
                                cx.op("pe", lambda e, k=k, mc=mc, cb_=cb_, pb=pb: e.matmul(
                                    pb[:, :512], memn[:, k, 128 * mc:128 * mc + 128], wkv[:, k, D + 512 * cb_:D + 512 * cb_ + 512],
                                    start=(k == 0), stop=(k == KD - 1)),
                                    reads=[wkv.d, memn.d], writes=[pb.d], inc=(k == KD - 1))
                            cx.op("act", lambda e, mc=mc, cb_=cb_, pb=pb: e.copy(V16[l][:, mc, 512 * cb_:512 * cb_ + 512], pb[:, :512]),
                                  reads=[pb.d], writes=[V16[l].d])
                            cx.op("dve", lambda e, mc=mc, cb_=cb_, pb=pb: e.tensor_copy(vo[:, mc, 512 * cb_:512 * cb_ + 512], pb[:, :512]),
                                  reads=[pb.d], writes=[vo.d])
                    cx.dma("sp", O["mem_v"][l].rearrange("m p n -> p m n"), vo[:], reads=[vo.d], writes=[outdep])
                cx.barrier()

        def xattn_phase(l):
            with contextlib.ExitStack() as ph:
                KT16[l] = sb(f"KT16_{l}", [P, KD, NMEM], BF16, ph)
                V16[l] = sb(f"V16_{l}", [P, 2, D], BF16, ph)
                wq = sb("wq", [P, KD, D], BF16, ph)
                wo = sb("wo", [P, KD, D], BF16, ph)
                mem_phase(l, after_w=lambda: (cx.dma("pool", wq[:], I["wq"][l].rearrange("k p n -> p k n"), writes=[wq.d]),
                                              cx.dma("pool", wo[:], I["wo"][l].rearrange("k p n -> p k n"), writes=[wo.d])))
                if cfg.xmode == "mem":
                    return
                scr = {"sq16": sb("x_sq16", [P, KD, 512], BF16, ph), "rstd": sb("x_rstd", [P, 512], F32, ph), "eps": epsb}
                h = sb("x_h", [P, KD, 512], BF16, ph)
                qa = sb("x_qa", [P, KD, T], BF16, ph)
                NR = 3
                pTb = [sb(f"x_pT{i}", [P, 2, 512], BF16, ph) for i in range(NR)]
                rsb = [sb(f"x_rs{i}", [P, 512], F32, ph) for i in range(2)]
                ktr = [sb(f"x_kt{i}", [P, KD, NMEM], BF16, ph) for i in range(2)]
                vtr = [sb(f"x_vt{i}", [P, 2, D], BF16, ph) for i in range(2)]
                pTs = sb("x_pTs", [P, 2, 256], BF16, ph)
                pTs_d = [Dep() for _ in range(NSEQ_S)]
                tl = tiles_of(LP, 512)
                qd = [[Dep() for _ in range(4)] for _ in tl]
                order = [len(tl) - 1] + list(range(len(tl) - 1))
                for ti in order:
                    s0, n, smp = tl[ti]
                    rmsnorm(lambda k: x[:, k, s0:s0 + n], xdeps(s0, n), n, 2 + l, lambda k: h[:, k, :n], [h.d], scr)
                    for c in range(KD):
                        pb = bank()
                        for k in range(KD):
                            cx.op("pe", lambda e, k=k, c=c, pb=pb: e.matmul(pb[:, :n], wq[:, k, 128 * c:128 * c + 128], h[:, k, :n],
                                                                           start=(k == 0), stop=(k == KD - 1)),
                                  reads=[wq.d, h.d], writes=[pb.d], inc=(k == KD - 1))
                        if c % 2 == 0:
                            cx.op("act", lambda e, c=c, pb=pb: e.copy(qa[:, c, s0:s0 + n], pb[:, :n]), reads=[pb.d], writes=[qd[ti][c // 2]])
                        else:
                            cx.op("dve", lambda e, c=c, pb=pb: e.tensor_copy(qa[:, c, s0:s0 + n], pb[:, :n]), reads=[pb.d], writes=[qd[ti][c // 2]])
                S_ps = reserve()
                O_ps = reserve()
                Sv = S_ps[:, :].rearrange("p (m c) -> p m c", m=2)
                sN = len(tl) - 1
                sS0 = tl[sN][0]

                def sample_seq(b):
                    kt, vt = ktr[b % 2], vtr[b % 2]
                    cx.dma("pool", kt[:], I["kTc"][l, b].rearrange("k p m -> p k m"), writes=[kt.d])
                    cx.dma("pool", vt[:], I["vc"][l, b].rearrange("m p n -> p m n"), writes=[vt.d])
                    for hd in range(4):
                        for mc in range(2):
                            for dc in range(2):
                                col = mc * 256 + (b * 4 + hd) * 4
                                cx.op("pe", lambda e, mc=mc, dc=dc, hd=hd, col=col: e.matmul(
                                    S_ps[:, col:col + 4], kt[:, 2 * hd + dc, 128 * mc:128 * mc + 128], qa[:, 2 * hd + dc, sS0 + 4 * b:sS0 + 4 * b + 4],
                                    start=(dc == 0), stop=(dc == 1)),
                                    reads=[kt.d, qd[sN][hd]], writes=[S_ps.d], inc=(dc == 1 and mc == 1 and hd == 3))
                    cx.op("act", lambda e: e.activation(pTs[:, :, 16 * b:16 * b + 16], Sv[:, :, 16 * b:16 * b + 16], AF.Exp, scale=1.0 / 16.0),
                          reads=[S_ps.d], writes=[pTs_d[b]])
                    for hd in range(4):
                        for dc in range(2):
                            for mc in range(2):
                                col = (2 * hd + dc) * 64 + 4 * b
                                pc = (b * 4 + hd) * 4
                                cx.op("pe", lambda e, mc=mc, dc=dc, hd=hd, col=col, pc=pc: e.matmul(
                                    O_ps[:, col:col + 4], vt[:, mc, (2 * hd + dc) * 128:(2 * hd + dc) * 128 + 128], pTs[:, mc, pc:pc + 4],
                                    start=(mc == 0), stop=(mc == 1)),
                                    reads=[vt.d, pTs_d[b]], writes=[O_ps.d], inc=(mc == 1 and dc == 1 and hd == 3))

                items = [(ti, hd) for ti in range(len(tl) - 1) for hd in range(4)]
                sc_banks = {}

                def SC(i):
                    ti, hd = items[i]
                    s0, n, smp = tl[ti]
                    pT = pTb[i % NR]
                    pbs = [bank(), bank()]
                    for mc in range(2):
                        for dc in range(2):
                            cx.op("pe", lambda e, mc=mc, dc=dc: e.matmul(
                                pbs[mc][:, :n], KT16[l][:, 2 * hd + dc, 128 * mc:128 * mc + 128], qa[:, 2 * hd + dc, s0:s0 + n],
                                start=(dc == 0), stop=(dc == 1)),
                                reads=[KT16[l].d, qd[ti][hd]], writes=[pbs[mc].d], inc=(dc == 1))
                    for mc in range(2):
                        cx.op("act", lambda e, mc=mc: e.activation(pT[:, mc, :n], pbs[mc][:, :n], AF.Exp, scale=1.0 / 16.0),
                              reads=[pbs[mc].d], writes=[pT.d])

                def REST(i):
                    ti, hd = items[i]
                    s0, n, smp = tl[ti]
                    pT, rs = pTb[i % NR], rsb[i % 2]
                    pbsum = bank()
                    for mc in range(2):
                        cx.op("pe", lambda e, mc=mc: e.matmul(pbsum[:, :n], ones16[:], pT[:, mc, :n], start=(mc == 0), stop=(mc == 1)),
                              reads=[pT.d, ones16.d], writes=[pbsum.d], inc=(mc == 1))
                    cx.op("act", lambda e: e.activation(rs[:, :n], pbsum[:, :n], AF.Ln), reads=[pbsum.d], writes=[rs.d])
                    cx.op("act", lambda e: e.activation(rs[:, :n], rs[:, :n], AF.Exp, scale=-1.0), reads=[rs.d], writes=[rs.d])
                    for dc in range(2):
                        pbo = bank()
                        for mc in range(2):
                            cx.op("pe", lambda e, mc=mc, dc=dc, pbo=pbo: e.matmul(
                                pbo[:, :n], V16[l][:, mc, (2 * hd + dc) * 128:(2 * hd + dc) * 128 + 128], pT[:, mc, :n],
                                start=(mc == 0), stop=(mc == 1)),
                                reads=[V16[l].d, pT.d], writes=[pbo.d], inc=(mc == 1))
                        cx.op("dve", lambda e, dc=dc, pbo=pbo: e.tensor_tensor(qa[:, 2 * hd + dc, s0:s0 + n], pbo[:, :n], rs[:, :n], ALU.mult),
                              reads=[pbo.d, rs.d], writes=[qd[ti][hd]])

                NI = len(items)
                nb_done = 0
                if NI > 0:
                    SC(0)
                for i in range(NI):
                    if i + 1 < NI:
                        SC(i + 1)
                    REST(i)
                    want = ((i + 1) * NSEQ_S + NI - 1) // NI
                    while nb_done < min(want, NSEQ_S):
                        sample_seq(nb_done)
                        nb_done += 1
                while nb_done < NSEQ_S:
                    sample_seq(nb_done)
                    nb_done += 1
                pbsum = bank()
                for mc in range(2):
                    cx.op("pe", lambda e, mc=mc, pbsum=pbsum: e.matmul(pbsum[:, :256], ones16[:], pTs[:, mc, :], start=(mc == 0), stop=(mc == 1)),
                          reads=pTs_d + [ones16.d], writes=[pbsum.d], inc=(mc == 1))
                rs = rsb[0]
                cx.op("act", lambda e: e.activation(rs[:, :256], pbsum[:, :256], AF.Ln), reads=[pbsum.d], writes=[rs.d])
                cx.op("act", lambda e: e.activation(rs[:, :256], rs[:, :256], AF.Exp, scale=-1.0), reads=[rs.d], writes=[rs.d])
                rsv = rs[:, :256].rearrange("p (b h t) -> p b h t", b=NSEQ_S, h=4)
                Ov = O_ps[:, :].rearrange("p (c b t) -> p c b t", c=KD, b=NSEQ_S)
                aov = qa[:, :, sS0:sS0 + NS].rearrange("p c (b t) -> p c b t", b=NSEQ_S)
                for hd in range(4):
                    for dc in range(2):
                        c = 2 * hd + dc
                        cx.op("dve", lambda e, c=c, hd=hd: e.tensor_tensor(aov[:, c], Ov[:, c], rsv[:, :, hd, :], ALU.mult),
                              reads=[O_ps.d, rs.d], writes=[qd[sN][hd]])
                release(S_ps)
                release(O_ps)
                for ti in range(len(tl)):
                    s0, n, smp = tl[ti]
                    for oc in range(KD):
                        pb = bank()
                        for k in range(KD):
                            cx.op("pe", lambda e, k=k, oc=oc, pb=pb: e.matmul(pb[:, :n], wo[:, k, 128 * oc:128 * oc + 128], qa[:, k, s0:s0 + n],
                                                                            start=(k == 0), stop=(k == KD - 1)),
                                  reads=[wo.d] + qd[ti], writes=[pb.d], inc=(k == KD - 1))
                        cx.op("dve", lambda e, pb=pb, oc=oc: e.tensor_tensor(x[:, oc, s0:s0 + n], x[:, oc, s0:s0 + n], pb[:, :n], ALU.add),
                              reads=[pb.d] + xdeps(s0, n), writes=xdeps(s0, n))
                cx.barrier()

        cm = sb("cmask", [P, 464], F32)
        cx.dma("sp", cm[:], I["cmask"], writes=[cm.d])
        maskC16 = sb("maskC16", [64, 64], BF16)
        maskS16 = sb("maskS16", [64, 64], BF16)
        msel16 = sb("msel16", [64, 16], BF16)
        cx.op("dve", lambda e: e.tensor_copy(maskC16[:], cm[:64, 0:64]), reads=[cm.d], writes=[maskC16.d])
        cx.op("dve", lambda e: e.tensor_copy(maskS16[:], cm[:64, 64:128]), reads=[cm.d], writes=[maskS16.d])
        cx.op("dve", lambda e: e.tensor_copy(msel16[:], cm[:64, 128:144]), reads=[cm.d], writes=[msel16.d])

        def gla_alloc(ph, dvc, NT, CK=GC):
            dv = 128 * dvc
            g = {}
            g["ring"] = [[sb(f"g_bt{i}", [P, NT], F32, ph), sb(f"g_eb{i}", [P, NT], F32, ph), sb(f"g_enb{i}", [P, NT], F32, ph)] for i in range(2)]
            g["qt16"] = sb("g_qt16", [P, 4, NT], BF16, ph)
            g["kt16"] = sb("g_kt16", [P, 4, NT], BF16, ph)
            g["ebl"] = sb("g_ebl", [P, 4, 16], F32, ph)
            g["kexp"] = [sb(f"g_kexp{i}", [64, NSEQ_S, 128], BF16, ph) for i in range(1)] * 2
            g["S32a"] = sb("g_S32a", [P, 4, dv], F32, ph)
            g["S16a"] = sb("g_S16a", [P, 4, dv], BF16, ph)
            RA = 1 if dvc == 2 else 2
            g["am4"] = [sb(f"g_am4_{i}", [CK, 4, CK], BF16, ph) for i in range(RA)] * (2 // RA)
            g["ktm4"] = [sb(f"g_ktm4_{i}", [CK, 4, 128], BF16, ph) for i in range(RA)] * (2 // RA)
            R4 = 2 if dvc == 2 else 4
            R2 = 1 if dvc == 2 else 2
            g["s32r"] = [sb(f"g_s32r{i}", [P, dv], F32, ph) for i in range(R4)] * (4 // R4)
            g["s16r"] = [sb(f"g_s16r{i}", [P, dv], BF16, ph) for i in range(R4)] * (4 // R4)
            g["so"] = [sb(f"g_so{i}", [P, dv], F32, ph) for i in range(R4)] * (4 // R4)
            g["o32"] = [sb(f"g_o32{i}", [P, dvc, NT], F32, ph) for i in range(R2)] * (2 // R2)
            g["osq"] = [sb(f"g_osq{i}", [P, dvc, NT], BF16, ph) for i in range(R2)] * (2 // R2)
            g["rstd"] = [sb(f"g_rstd{i}", [P, NT], F32, ph) for i in range(R2)] * (2 // R2)
            cx.op("pool", lambda e: e.memset(g["S32a"][:], 0.0), writes=[g["S32a"].d])
            cx.op("pool", lambda e: e.memset(g["S16a"][:], 0.0), writes=[g["S16a"].d])
            return g

        def gla_prep(g, n, smp, scale, q32, k32, lg32, CK=GC, mres_p=None):
            C = LS if smp else CK
            nch = n // C
            mres = cm[:, 400:464] if smp else (cm[:, 144:144 + n] if mres_p is None else mres_p[:, :n])
            qt16, kt16, ebl = g["qt16"], g["kt16"], g["ebl"]
            for hd in range(4):
                bt_, eb_, enb_ = g["ring"][hd % 2]
                cx.op("dve", lambda e, hd=hd, bt_=bt_: e.tensor_tensor_scan(bt_[:, :n], mres, lg32[:, hd, :n], 0.0, ALU.mult, ALU.add),
                      reads=[lg32.d, cm.d], writes=[bt_.d])
                cx.op("act", lambda e, bt_=bt_, eb_=eb_: e.activation(eb_[:, :n], bt_[:, :n], AF.Exp, scale=scale), reads=[bt_.d], writes=[eb_.d])
                cx.op("act", lambda e, bt_=bt_, enb_=enb_: e.activation(enb_[:, :n], bt_[:, :n], AF.Exp, scale=-scale), reads=[bt_.d], writes=[enb_.d])
                cx.op("dve", lambda e, hd=hd, eb_=eb_: e.tensor_tensor(qt16[:, hd, :n], q32[:, hd, :n], eb_[:, :n], ALU.mult),
                      reads=[q32.d, eb_.d], writes=[qt16.d])
                cx.op("pool", lambda e, hd=hd, enb_=enb_: e.tensor_tensor(kt16[:, hd, :n], k32[:, hd, :n], enb_[:, :n], ALU.mult),
                      reads=[k32.d, enb_.d], writes=[kt16.d])
                ebv = eb_[:, :n].rearrange("p (c j) -> p c j", j=C)
                cx.op("act", lambda e, hd=hd, ebv=ebv: e.copy(ebl[:, hd, :nch], ebv[:, :, C - 1]), reads=[eb_.d], writes=[ebl.d])

        def gla_chunks(g, n, smp, dvc, v_tm, st_in, st_out, CK=GC, maskCK=None):
            dv = 128 * dvc
            if maskCK is None:
                maskCK = maskC16
            C = LS if smp else CK
            nch = n // C
            qt16, kt16, ebl = g["qt16"], g["kt16"], g["ebl"]
            pbo = [reserve() for _ in range(4)]
            if not smp:
                S32a, S16a = g["S32a"], g["S16a"]
                hpb = 512 // dv
                for ci in range(nch):
                    cs = slice(ci * CK, ci * CK + CK)
                    am, ktm = g["am4"][ci % 2], g["ktm4"][ci % 2]
                    pbA, pbT = bank(), bank()
                    pbTv = pbT[:, :].bitcast(BF16)
                    for hd in range(4):
                        cx.op("pe", lambda e, hd=hd: e.matmul(pbA[:CK, hd * CK:(hd + 1) * CK], kt16[:, hd, cs], qt16[:, hd, cs], start=True, stop=True),
                              reads=[kt16.d, qt16.d], writes=[pbA.d], inc=(hd == 3))
                    for hd in range(4):
                        cx.op("pe", lambda e, hd=hd: e.transpose(pbTv[:CK, hd * 128:(hd + 1) * 128], kt16[:, hd, cs], ident16[:]),
                              reads=[kt16.d, ident16.d], writes=[pbT.d], inc=(hd == 3))
                    cx.op("dve", lambda e: e.tensor_tensor(am[:], pbA[:CK, :4 * CK].rearrange("p (h c) -> p h c", h=4),
                                                          maskCK[:, :].unsqueeze(1).broadcast_to([CK, 4, CK]), ALU.mult),
                          reads=[pbA.d, maskCK.d], writes=[am.d])
                    cx.op("act", lambda e: e.copy(ktm[:], pbTv[:CK, :512].rearrange("p (h c) -> p h c", h=4)), reads=[pbT.d], writes=[ktm.d])
                    for hd in range(4):
                        for dc in range(dvc):
                            oc = slice(dc * 256 + ci * CK, dc * 256 + ci * CK + CK)
                            vs = slice(hd * dv + 128 * dc, hd * dv + 128 * dc + 128)
                            cx.op("pe", lambda e, hd=hd, oc=oc, vs=vs: e.matmul(pbo[hd][:, oc], v_tm[:CK, ci, vs], am[:, hd, :], start=True, stop=False),
                                  reads=[v_tm.d, am.d], writes=[pbo[hd].d], inc=False)
                            cx.op("pe", lambda e, hd=hd, oc=oc, dc=dc: e.matmul(pbo[hd][:, oc], S16a[:, hd, 128 * dc:128 * dc + 128], qt16[:, hd, cs],
                                                                                 start=False, stop=True),
                                  reads=[S16a.d, qt16.d], writes=[pbo[hd].d], inc=(dc == dvc - 1))
                    pbS = [bank() for _ in range(4 // hpb)]
                    for hd in range(4):
                        bi, col = hd // hpb, (hd % hpb) * dv
                        cx.op("pe", lambda e, hd=hd, bi=bi, col=col: e.matmul(pbS[bi][:, col:col + dv], ktm[:, hd, :], v_tm[:CK, ci, hd * dv:(hd + 1) * dv],
                                                                             start=True, stop=True),
                              reads=[ktm.d, v_tm.d], writes=[pbS[bi].d], inc=(hd % hpb == hpb - 1))
                    for bi in range(4 // hpb):
                        h0 = bi * hpb
                        S32v = S32a[:, h0:h0 + hpb, :]
                        psv = pbS[bi][:, :].rearrange("p (h d) -> p h d", h=hpb)
                        cx.op("dve", lambda e, S32v=S32v, psv=psv: e.tensor_tensor(S32v, psv, S32v, ALU.add), reads=[pbS[bi].d, S32a.d], writes=[S32a.d])
                        cx.op("dve", lambda e, S32v=S32v, h0=h0: e.tensor_tensor(S32v, S32v, ebl[:, h0:h0 + hpb, ci:ci + 1].broadcast_to([P, hpb, dv]), ALU.mult),
                              reads=[S32a.d, ebl.d], writes=[S32a.d])
                        cx.op("act", lambda e, S32v=S32v, h0=h0: e.copy(S16a[:, h0:h0 + hpb, :], S32v), reads=[S32a.d], writes=[S16a.d])
                return pbo
            mask16 = maskS16 if smp else maskC16
            nchunks = 1 if smp else nch
            CT = n if smp else GC
            for ci in range(nchunks):
                cs = slice(ci * CT, ci * CT + CT)
                for hd in range(4):
                    am, ktm = g["am4"][0][:, hd, :], g["ktm4"][0][:, hd, :]
                    am_d, ktm_d = g["am4"][0].d, g["ktm4"][0].d
                    pbA = bank()
                    cx.op("pe", lambda e, hd=hd, pbA=pbA: e.matmul(pbA[:CT, :CT], kt16[:, hd, cs], qt16[:, hd, cs], start=True, stop=True),
                          reads=[kt16.d, qt16.d], writes=[pbA.d])
                    pbT = bank()
                    pbTv = pbT[:, :].bitcast(BF16)
                    cx.op("pe", lambda e, hd=hd, pbTv=pbTv: e.transpose(pbTv[:CT, :128], kt16[:, hd, cs], ident16[:]),
                          reads=[kt16.d, ident16.d], writes=[pbT.d])
                    cx.op("dve", lambda e, am=am, pbA=pbA: e.tensor_tensor(am[:CT, :CT], pbA[:CT, :CT], mask16[:CT, :CT], ALU.mult),
                          reads=[pbA.d, mask16.d], writes=[am_d])
                    cx.op("act", lambda e, ktm=ktm, pbTv=pbTv: e.copy(ktm[:CT, :], pbTv[:CT, :128]), reads=[pbT.d], writes=[ktm_d])
                    for dc in range(dvc):
                        oc = slice(dc * 256 + ci * CT, dc * 256 + ci * CT + CT)
                        vs = slice(hd * dv + 128 * dc, hd * dv + 128 * dc + 128)
                        if not smp:
                            cx.op("pe", lambda e, hd=hd, oc=oc, vs=vs, am=am: e.matmul(pbo[hd][:, oc], v_tm[:CT, ci, vs], am[:CT, :CT], start=True, stop=False),
                                  reads=[v_tm.d, am_d], writes=[pbo[hd].d], inc=False)
                            cx.op("pe", lambda e, hd=hd, oc=oc, dc=dc: e.matmul(pbo[hd][:, oc], g["S16"][hd][:, 128 * dc:128 * dc + 128], qt16[:, hd, cs],
                                                                                 start=False, stop=True),
                                  reads=[g["S16"][hd].d, qt16.d], writes=[pbo[hd].d], inc=(dc == dvc - 1))
                    if not smp:
                        pbS = bank()
                        cx.op("pe", lambda e, hd=hd, pbS=pbS: e.matmul(pbS[:, :dv], ident[:], g["S32"][hd][:], start=True, stop=False),
                              reads=[ident.d, g["S32"][hd].d], writes=[pbS.d], inc=False)
                        cx.op("pe", lambda e, hd=hd, pbS=pbS, ktm=ktm: e.matmul(pbS[:, :dv], ktm[:CT, :], v_tm[:CT, ci, hd * dv:(hd + 1) * dv], start=False, stop=True),
                              reads=[ktm_d, v_tm.d], writes=[pbS.d])
                        cx.op("act", lambda e, hd=hd, pbS=pbS: e.activation(g["S32"][hd][:], pbS[:, :dv], AF.Copy, scale=ebl[:, hd, ci:ci + 1]),
                              reads=[pbS.d, ebl.d], writes=[g["S32"][hd].d])
                        cx.op("dve", lambda e, hd=hd, pbS=pbS: e.tensor_scalar(g["S16"][hd][:], pbS[:, :dv], ebl[:, hd, ci:ci + 1], None, ALU.mult),
                              reads=[pbS.d, ebl.d], writes=[g["S16"][hd].d])
                    else:
                        kexp = g["kexp"][hd % 2]
                        cx.op("dve", lambda e, kexp=kexp, ktm=ktm: e.tensor_tensor(
                            kexp[:, :, :], ktm[:64, :].unsqueeze(1).broadcast_to([64, NSEQ_S, 128]),
                            msel16[:, :].unsqueeze(2).broadcast_to([64, NSEQ_S, 128]), ALU.mult),
                            reads=[ktm_d, msel16.d], writes=[kexp.d])
                        for b in range(NSEQ_S):
                            r = (hd * NSEQ_S + b) % 4
                            s32, s16, so = g["s32r"][r], g["s16r"][r], g["so"][r]
                            cx.dma("pool", s32[:], st_in[b, hd], writes=[s32.d])
                            cx.dma("pool", s16[:], st_in[b, hd], writes=[s16.d])
                            for dc in range(dvc):
                                oc = slice(dc * 256 + 4 * b, dc * 256 + 4 * b + 4)
                                vs = slice(hd * dv + 128 * dc, hd * dv + 128 * dc + 128)
                                cx.op("pe", lambda e, hd=hd, oc=oc, dc=dc, s16=s16, b=b: e.matmul(
                                    pbo[hd][:, oc], s16[:, 128 * dc:128 * dc + 128], qt16[:, hd, 4 * b:4 * b + 4], start=True, stop=False),
                                    reads=[s16.d, qt16.d], writes=[pbo[hd].d], inc=False)
                                cx.op("pe", lambda e, hd=hd, oc=oc, vs=vs, am=am, b=b: e.matmul(
                                    pbo[hd][:, oc], v_tm[:CT, ci, vs], am[:CT, 4 * b:4 * b + 4], start=False, stop=True),
                                    reads=[v_tm.d, am_d], writes=[pbo[hd].d], inc=(dc == dvc - 1 and b == NSEQ_S - 1))
                            pbS = bank()
                            cx.op("pe", lambda e, hd=hd, pbS=pbS, kexp=kexp, b=b: e.matmul(pbS[:, :dv], kexp[:, b, :], v_tm[:CT, ci, hd * dv:(hd + 1) * dv],
                                                                                        start=True, stop=True),
                                  reads=[kexp.d, v_tm.d], writes=[pbS.d])
                            cx.op("act", lambda e, hd=hd, s32=s32, b=b: e.activation(s32[:], s32[:], AF.Copy, scale=ebl[:, hd, b:b + 1]),
                                  reads=[s32.d, ebl.d], writes=[s32.d])
                            cx.op("dve", lambda e, hd=hd, pbS=pbS, so=so, b=b, s32=s32: e.scalar_tensor_tensor(so[:], pbS[:, :dv], ebl[:, hd, b:b + 1], s32[:],
                                                                                                   ALU.mult, ALU.add),
                                  reads=[pbS.d, ebl.d, s32.d], writes=[so.d])
                            cx.dma("sp", st_out[b, hd], so[:], reads=[so.d], writes=[outdep])
            return pbo

        def gla_post(g, n, dvc, pbo, gate32, gn, on16, on_c0):
            dv = 128 * dvc
            for hd in range(4):
                o32, osq, rstd = g["o32"][hd % 2], g["osq"][hd % 2], g["rstd"][hd % 2]
                for dc in range(dvc):
                    cx.op("act", lambda e, hd=hd, dc=dc, o32=o32: e.copy(o32[:, dc, :n], pbo[hd][:, dc * 256:dc * 256 + n]), reads=[pbo[hd].d], writes=[o32.d])
                    cx.op("act", lambda e, hd=hd, dc=dc, osq=osq: e.activation(osq[:, dc, :n], pbo[hd][:, dc * 256:dc * 256 + n], AF.Square),
                          reads=[pbo[hd].d], writes=[osq.d])
                pb = bank()
                for dc in range(dvc):
                    cx.op("pe", lambda e, dc=dc, pb=pb, osq=osq: e.matmul(pb[:, :n], ones16[:], osq[:, dc, :n], start=(dc == 0), stop=(dc == dvc - 1)),
                          reads=[osq.d, ones16.d], writes=[pb.d], inc=(dc == dvc - 1))
                cx.op("act", lambda e, pb=pb, rstd=rstd: e.activation(rstd[:, :n], pb[:, :n], AF.Ln, bias=epsb[:, 0:1], scale=1.0 / dv),
                      reads=[pb.d, epsb.d], writes=[rstd.d])
                cx.op("act", lambda e, rstd=rstd: e.activation(rstd[:, :n], rstd[:, :n], AF.Exp, scale=-0.5), reads=[rstd.d], writes=[rstd.d])
                for dc in range(dvc):
                    cx.op("pool", lambda e, dc=dc, o32=o32, rstd=rstd: e.tensor_tensor(o32[:, dc, :n], o32[:, dc, :n], rstd[:, :n], ALU.mult),
                          reads=[o32.d, rstd.d], writes=[o32.d])
                    cx.op("dve", lambda e, hd=hd, dc=dc, o32=o32: e.scalar_tensor_tensor(
                        on16[:, on_c0 + hd * dvc + dc, :n], o32[:, dc, :n], gn[:, dc:dc + 1], gate32[:, hd * dvc + dc, :n], ALU.mult, ALU.mult),
                        reads=[o32.d, gn.d, gate32.d], writes=[on16.d])
            for b_ in pbo:
                release(b_)

        def a1_phase():
            NT = 256
            with contextlib.ExitStack() as ph:
                w = sb("w_in_c", [P, KD, 3088], BF16, ph)
                wv = I["w_in_c"].rearrange("k p n -> p k n")
                for c0 in range(0, 3088, 772):
                    cx.dma("pool", w[:, :, c0:c0 + 772], wv[:, :, c0:c0 + 772], writes=[w.d])
                wout = sb("w_out_c", [P, KD, D], BF16, ph)
                cx.dma("pool", wout[:], I["w_out_c"].rearrange("k p n -> p k n"), writes=[wout.d])
                wgu = sb("wgu", [16, 512], BF16, ph)
                cx.dma("pool", wgu[:], I["gla_wgu"], writes=[wgu.d])
                gsm = sb("gsm", [P, 6], F32, ph)
                cx.dma("sp", gsm[:], I["gla_small"], writes=[gsm.d])
                gnb = sb("gnb", [P, 2], F32, ph)
                cx.op("dve", lambda e: e.tensor_copy(gnb[:], gsm[:, 4:6]), reads=[gsm.d], writes=[gnb.d])
                nbg = sb("nbg", [P, 4], F32, ph)
                cx.op("dve", lambda e: e.tensor_scalar(nbg[:], gsm[:, 0:4], -1.0, None, ALU.mult), reads=[gsm.d], writes=[nbg.d])
                scr = {"sq16": sb("a_sq16", [P, KD, NT], BF16, ph), "rstd": sb("a_rstd", [P, NT], F32, ph), "eps": epsb}
                h = sb("a_h", [P, KD, NT], BF16, ph)
                q32 = sb("a_q32", [P, 4, NT], F32, ph)
                k32 = sb("a_k32", [P, 4, NT], F32, ph)
                lg32 = sb("a_lg32", [P, 4, NT], F32, ph)
                sg32 = lg32
                gate32 = sb("a_gate16", [P, 8, NT], BF16, ph)
                gd16 = sb("a_gd16", [16, NT], BF16, ph)
                CK1 = 128
                v_tm = sb("a_vtm", [CK1, NT // CK1, 1024], BF16, ph)
                cm2 = sb("a_cm2", [P, 384], F32, ph)
                cx.dma("sp", cm2[:], I["cmask2"], writes=[cm2.d])
                mask128 = sb("a_mask128", [P, 128], BF16, ph)
                cx.op("dve", lambda e: e.tensor_copy(mask128[:], cm2[:, 0:128]), reads=[cm2.d], writes=[mask128.d])
                mres128 = View(cm2[:, 128:384])
                mres128.d = cm2.d
                on16 = sb("a_on16", [P, 8, NT], BF16, ph)
                g = gla_alloc(ph, 2, NT, CK=CK1)
                ev = [0]

                def evac(fn_act, fn_dve, reads, writes):
                    ev[0] += 1
                    if ev[0] % 2:
                        cx.op("act", fn_act, reads=reads, writes=writes)
                    else:
                        cx.op("dve", fn_dve, reads=reads, writes=writes)

                hb2 = [h, sb("a_h2", [P, KD, NT], BF16, ph)]
                tl = tiles_of(LP, NT)

                def proj(h_, c0, n, M=128):
                    pb = bank()
                    for k in range(KD):
                        cx.op("pe", lambda e, k=k, pb=pb: e.matmul(pb[:M, :n], w[:, k, c0:c0 + M], h_[:, k, :n], start=(k == 0), stop=(k == KD - 1)),
                              reads=[w.d, h_.d], writes=[pb.d], inc=(k == KD - 1))
                    return pb

                def stA(ti):
                    s0, n, smp = tl[ti]
                    h_ = hb2[ti % 2]
                    rmsnorm(lambda k: x[:, k, s0:s0 + n], xdeps(s0, n), n, 1, lambda k: h_[:, k, :n], [h_.d], scr)

                def stB(ti):
                    s0, n, smp = tl[ti]
                    h_ = hb2[ti % 2]
                    for hd in range(4):
                        pb = proj(h_, 128 * hd, n)
                        cx.op("act", lambda e, hd=hd, pb=pb: e.mul(q32[:, hd, :n], pb[:, :n], 128.0 ** -0.5), reads=[pb.d], writes=[q32.d])
                        pb = proj(h_, 512 + 128 * hd, n)
                        cx.op("dve", lambda e, hd=hd, pb=pb: e.tensor_copy(k32[:, hd, :n], pb[:, :n]), reads=[pb.d], writes=[k32.d])
                    pb = proj(h_, 3072, n, M=16)
                    cx.op("dve", lambda e, pb=pb: e.tensor_copy(gd16[:, :n], pb[:16, :n]), reads=[pb.d], writes=[gd16.d])
                    for hd in range(4):
                        pb = bank()
                        cx.op("pe", lambda e, hd=hd, pb=pb: e.matmul(pb[:, :n], wgu[:, 128 * hd:128 * hd + 128], gd16[:, :n], start=True, stop=True),
                              reads=[wgu.d, gd16.d], writes=[pb.d])
                        cx.op("act", lambda e, hd=hd, pb=pb: e.activation(sg32[:, hd, :n], pb[:, :n], AF.Exp, bias=nbg[:, hd:hd + 1], scale=-1.0),
                              reads=[pb.d, nbg.d], writes=[sg32.d])
                        cx.op("act", lambda e, hd=hd: e.activation(lg32[:, hd, :n], sg32[:, hd, :n], AF.Ln, bias=1.0), reads=[sg32.d], writes=[lg32.d])

                def stC(ti):
                    s0, n, smp = tl[ti]
                    gla_prep(g, n, smp, -1.0 / 16.0, q32, k32, lg32, CK=CK1, mres_p=mres128)

                def stDr(ti):
                    s0, n, smp = tl[ti]
                    h_ = hb2[ti % 2]
                    for c in range(8):
                        pb = proj(h_, 2048 + 128 * c, n)
                        cx.op("act", lambda e, c=c, pb=pb: e.activation(gate32[:, c, :n], pb[:, :n], AF.Silu), reads=[pb.d], writes=[gate32.d])

                def stDv(ti):
                    s0, n, smp = tl[ti]
                    h_ = hb2[ti % 2]
                    CT = n if smp else CK1
                    for ci in range(n // CT):
                        for cb_ in range(2):
                            pb = bank()
                            for k in range(KD):
                                cx.op("pe", lambda e, k=k, pb=pb, ci=ci, cb_=cb_: e.matmul(
                                    pb[:CT, :512], h_[:, k, ci * CT:ci * CT + CT], w[:, k, 1024 + 512 * cb_:1024 + 512 * cb_ + 512],
                                    start=(k == 0), stop=(k == KD - 1)),
                                    reads=[w.d, h_.d], writes=[pb.d], inc=(k == KD - 1))
                            cx.op("dve", lambda e, pb=pb, ci=ci, cb_=cb_: e.tensor_copy(v_tm[:CT, ci, 512 * cb_:512 * cb_ + 512], pb[:CT, :512]),
                                  reads=[pb.d], writes=[v_tm.d])

                pbos = {}

                def stE(ti):
                    s0, n, smp = tl[ti]
                    pbos[ti] = gla_chunks(g, n, smp, 2, v_tm, I["st_gla"], O["gla_s"], CK=CK1, maskCK=mask128)
                    if ti == len(tl) - 2:
                        for hd in range(4):
                            cx.dma("sp", O["gla_p"][hd], g["S32a"][:, hd, :], reads=[g["S32a"].d], writes=[outdep])

                def stF(ti):
                    s0, n, smp = tl[ti]
                    gla_post(g, n, 2, pbos.pop(ti), gate32, gnb, on16, 0)

                def stG(ti):
                    s0, n, smp = tl[ti]
                    for oc in range(KD):
                        pb = bank()
                        for k in range(KD):
                            cx.op("pe", lambda e, k=k, oc=oc, pb=pb: e.matmul(pb[:, :n], wout[:, k, 128 * oc:128 * oc + 128], on16[:, k, :n],
                                                                            start=(k == 0), stop=(k == KD - 1)),
                                  reads=[wout.d, on16.d], writes=[pb.d], inc=(k == KD - 1))
                        cx.op("dve", lambda e, pb=pb, oc=oc: e.tensor_tensor(x[:, oc, s0:s0 + n], x[:, oc, s0:s0 + n], pb[:, :n], ALU.add),
                              reads=[pb.d] + xdeps(s0, n), writes=xdeps(s0, n))

                NTL = len(tl)
                stA(0); stB(0); stC(0); stDv(0); stDr(0)
                if NTL > 1:
                    stA(1)
                stE(0)
                for ti in range(1, NTL):
                    stB(ti); stDv(ti); stF(ti - 1); stC(ti); stG(ti - 1); stDr(ti)
                    if ti + 1 < NTL:
                        stA(ti + 1)
                    if tl[ti][2]:
                        cx.barrier()
                        g["s32r"] = [View(q32[:, j, :]) for j in range(4)]
                        g["so"] = [View(k32[:, j, :]) for j in range(4)]
                        g["s16r"] = [View(hb2[0][:, j, :]) for j in range(4)]
                    stE(ti)
                stF(NTL - 1); stG(NTL - 1)
                cx.barrier()


        TWO_PI = 2.0 * math.pi

        def a0_phase(NTA, NTB):
            with contextlib.ExitStack() as ph0:
                hall = sb("hall0", [P, KD, T], BF16, ph0)
                hdp = {s0: Dep() for s0 in range(0, T, 128)}

                def hdeps(s0, n):
                    return [hdp[k] for k in range(s0, s0 + n, 128)]
                with contextlib.ExitStack() as ph:
                    scr = {"sq16": sb("a0_sq16", [P, KD, 256], BF16, ph), "rstd": sb("a0_rstd", [P, 256], F32, ph), "eps": epsb}
                    for (s0, n, smp) in tiles_of(LP, 256):
                        rmsnorm(lambda k: x[:, k, s0:s0 + n], xdeps(s0, n), n, 0, lambda k: hall[:, k, s0:s0 + n], hdeps(s0, n), scr)
                    cx.barrier()
                wv = I["w_in_ab"].rearrange("k p n -> p k n")
                wov = I["w_out_ab"].rearrange("k p n -> p k n")
                if "A0a" in cfg.phases or "A0" in cfg.phases:
                  with contextlib.ExitStack() as ph:
                    NT = NTA
                    w = sb("w_in_a", [P, KD, 2048], BF16, ph)
                    for c0 in range(0, 2048, 512):
                        cx.dma("pool", w[:, :, c0:c0 + 512], wv[:, :, c0:c0 + 512], writes=[w.d])
                    wout = sb("w_out_a", [P, 4, D], BF16, ph)
                    cx.dma("pool", wout[:], wov[:, 0:4, :], writes=[wout.d])
                    hsm = sb("hsm", [P, 13], F32, ph)
                    cx.dma("sp", hsm[:], I["hg_small"], writes=[hsm.d])
                    lbe = sb("lbe", [P, 4, 3], F32, ph)
                    lbs = sb("lbs", [P, 4], F32, ph)
                    lb = sb("lb", [P, 4], F32, ph)
                    oml = sb("oml", [P, 4], F32, ph)
                    gnb = sb("gnb0", [P, 1], F32, ph)
                    cx.op("act", lambda e: e.activation(lbe[:], hsm[:, 0:12].rearrange("p (c j) -> p c j", j=3), AF.Exp), reads=[hsm.d], writes=[lbe.d])
                    cx.op("dve", lambda e: e.tensor_tensor(lbs[:], lbe[:, :, 0], lbe[:, :, 1], ALU.add), reads=[lbe.d], writes=[lbs.d])
                    cx.op("dve", lambda e: e.tensor_tensor(lbs[:], lbs[:], lbe[:, :, 2], ALU.add), reads=[lbe.d, lbs.d], writes=[lbs.d])
                    cx.op("dve", lambda e: e.reciprocal(lbs[:], lbs[:]), reads=[lbs.d], writes=[lbs.d])
                    cx.op("dve", lambda e: e.tensor_tensor(lb[:], lbe[:, :, 0], lbs[:], ALU.mult), reads=[lbe.d, lbs.d], writes=[lb.d])
                    cx.op("dve", lambda e: e.tensor_scalar(oml[:], lb[:], -1.0, 1.0, ALU.mult, ALU.add), reads=[lb.d], writes=[oml.d])
                    cx.op("dve", lambda e: e.tensor_copy(gnb[:], hsm[:, 12:13]), reads=[hsm.d], writes=[gnb.d])
                    q32 = sb("h_q32", [P, 4, NT], F32, ph)
                    k32 = sb("h_k32", [P, 4, NT], F32, ph)
                    fg32 = sb("h_fg32", [P, 4, NT], F32, ph)
                    lg32 = sb("h_lg32", [P, 4, NT], F32, ph)
                    gate32 = sb("h_gate32", [P, 4, NT], F32, ph)
                    v_tm = sb("h_vtm", [64, max(NT // GC, 1), 512], BF16, ph)
                    on16 = sb("h_on16", [P, 4, NT], BF16, ph)
                    g = gla_alloc(ph, 1, NT)

                    def proj(c0, s0, n):
                        pb = bank()
                        for k in range(KD):
                            cx.op("pe", lambda e, k=k, pb=pb: e.matmul(pb[:, :n], w[:, k, c0:c0 + 128], hall[:, k, s0:s0 + n], start=(k == 0), stop=(k == KD - 1)),
                                  reads=[w.d] + hdeps(s0, n), writes=[pb.d], inc=(k == KD - 1))
                        return pb

                    tl = tiles_of(LP, NT)

                    def stB(ti):
                        s0, n, smp = tl[ti]
                        for hd in range(4):
                            pb = proj(128 * hd, s0, n)
                            cx.op("act", lambda e, hd=hd, pb=pb: e.activation(q32[:, hd, :n], pb[:, :n], AF.Silu), reads=[pb.d], writes=[q32.d])
                        for hd in range(4):
                            pb = proj(512 + 128 * hd, s0, n)
                            cx.op("act", lambda e, hd=hd, pb=pb: e.activation(fg32[:, hd, :n], pb[:, :n], AF.Sigmoid), reads=[pb.d], writes=[fg32.d])
                            cx.op("dve", lambda e, hd=hd: e.tensor_scalar(fg32[:, hd, :n], fg32[:, hd, :n], oml[:, hd:hd + 1], lb[:, hd:hd + 1], ALU.mult, ALU.add),
                                  reads=[fg32.d, oml.d, lb.d], writes=[fg32.d])
                        for hd in range(4):
                            cx.op("act", lambda e, hd=hd: e.activation(lg32[:, hd, :n], fg32[:, hd, :n], AF.Ln), reads=[fg32.d], writes=[lg32.d])
                            cx.op("pool", lambda e, hd=hd: e.tensor_scalar(k32[:, hd, :n], fg32[:, hd, :n], -1.0, 1.0, ALU.mult, ALU.add),
                                  reads=[fg32.d], writes=[k32.d])

                    def stC(ti):
                        s0, n, smp = tl[ti]
                        gla_prep(g, n, smp, 1.0, q32, k32, lg32)

                    def stDr(ti):
                        s0, n, smp = tl[ti]
                        for hd in range(4):
                            pb = proj(1536 + 128 * hd, s0, n)
                            cx.op("act", lambda e, hd=hd, pb=pb: e.activation(gate32[:, hd, :n], pb[:, :n], AF.Silu), reads=[pb.d], writes=[gate32.d])

                    def stDv(ti):
                        s0, n, smp = tl[ti]
                        CT = n if smp else GC
                        for ci in range(n // CT):
                            pb = bank()
                            for k in range(KD):
                                cx.op("pe", lambda e, k=k, pb=pb, ci=ci: e.matmul(
                                    pb[:CT, :512], hall[:, k, s0 + ci * CT:s0 + ci * CT + CT], w[:, k, 1024:1536], start=(k == 0), stop=(k == KD - 1)),
                                    reads=[w.d] + hdeps(s0, n), writes=[pb.d], inc=(k == KD - 1))
                            cx.op("dve", lambda e, pb=pb, ci=ci: e.tensor_copy(v_tm[:CT, ci, :], pb[:CT, :512]), reads=[pb.d], writes=[v_tm.d])

                    pbos = {}

                    def stE(ti):
                        s0, n, smp = tl[ti]
                        pbos[ti] = gla_chunks(g, n, smp, 1, v_tm, I["st_hgrn"], O["hgrn_s"])
                        if ti == len(tl) - 2:
                            for hd in range(4):
                                cx.dma("sp", O["hgrn_p"][hd], g["S32a"][:, hd, :], reads=[g["S32a"].d], writes=[outdep])

                    def stF(ti):
                        s0, n, smp = tl[ti]
                        gla_post(g, n, 1, pbos.pop(ti), gate32, gnb, on16, 0)

                    def stG(ti):
                        s0, n, smp = tl[ti]
                        for oc in range(KD):
                            pb = bank()
                            for k in range(4):
                                cx.op("pe", lambda e, k=k, oc=oc, pb=pb: e.matmul(pb[:, :n], wout[:, k, 128 * oc:128 * oc + 128], on16[:, k, :n],
                                                                                start=(k == 0), stop=(k == 3)),
                                      reads=[wout.d, on16.d], writes=[pb.d], inc=(k == 3))
                            cx.op("dve", lambda e, pb=pb, oc=oc: e.tensor_tensor(x[:, oc, s0:s0 + n], x[:, oc, s0:s0 + n], pb[:, :n], ALU.add),
                                  reads=[pb.d] + xdeps(s0, n), writes=xdeps(s0, n))

                    stB(0); stC(0); stDv(0); stDr(0); stE(0)
                    for ti in range(1, len(tl)):
                        stB(ti); stDv(ti); stF(ti - 1); stC(ti); stG(ti - 1); stDr(ti); stE(ti)
                    stF(len(tl) - 1); stG(len(tl) - 1)
                    cx.barrier()
                if "A0b" in cfg.phases or "A0" in cfg.phases:
                  with contextlib.ExitStack() as ph:
                    NT = NTB
                    w = sb("w_in_u", [P, KD, 512], BF16, ph)
                    cx.dma("pool", w[:], wv[:, :, 2048:2560], writes=[w.d])
                    wout = sb("w_out_b", [P, 4, D], BF16, ph)
                    cx.dma("pool", wout[:], wov[:, 4:8, :], writes=[wout.d])
                    wglu = sb("wglu", [P, 4, 512], BF16, ph)
                    cx.dma("pool", wglu[:], I["s5_wglu"].rearrange("k p n -> p k n"), writes=[wglu.d])
                    ssm = sb("ssm", [P, 8], F32, ph)
                    cx.dma("sp", ssm[:], I["s5_small"], writes=[ssm.d])
                    x0 = sb("s5x0", [P, 2, 16, NSEQ_S], F32, ph)
                    cx.dma("sp", x0[:], I["s5_x0"].rearrange("r p s b -> p r s b"), writes=[x0.d])
                    Ere = sb("Ere", [P, 16, NT], F32, ph)
                    Eim = sb("Eim", [P, 16, NT], F32, ph)
                    rho = sb("rho", [P, 16], F32, ph)
                    rhoms = sb("rhoms", [P, 16, NS], F32, ph)
                    BB = [sb(f"BB{r}", [P, 4, 512], BF16, ph) for r in range(2)]
                    CC = [sb(f"CC{r}", [P, 16, 128], BF16, ph) for r in range(2)]
                    xst = sb("xst", [P, 2, 16], F32, ph)
                    xso = sb("xso", [P, 2, 16, NSEQ_S], F32, ph)
                    cx.op("pool", lambda e: e.memset(xst[:], 0.0), writes=[xst.d])
                    ones32 = sb("ones32", [P, P], F32, ph)
                    cx.op("pool", lambda e: e.memset(ones32[:], 1.0), writes=[ones32.d])
                    with contextlib.ExitStack() as ps_:
                        R_S5_MAX_RE = -1e-4

                        def sincos(th, F, tag):
                            outs = []
                            for off, nm_ in ((0.0, "sin"), (0.5 * math.pi, "cos")):
                                a = sb(f"sc_a_{tag}{nm_}", [P, F], F32, ps_)
                                ki = sb(f"sc_k_{tag}{nm_}", [P, F], mybir.dt.int32, ps_)
                                kf = sb(f"sc_kf_{tag}{nm_}", [P, F], F32, ps_)
                                cx.op("dve", lambda e, a=a, off=off: e.tensor_scalar(a[:], th[:], 1.0, off, ALU.mult, ALU.add), reads=[th.d], writes=[a.d])
                                cx.op("dve", lambda e, a=a, kf=kf: e.tensor_scalar(kf[:], a[:], 1.0 / TWO_PI, None, ALU.mult), reads=[a.d], writes=[kf.d])
                                cx.op("dve", lambda e, ki=ki, kf=kf: e.tensor_copy(ki[:], kf[:]), reads=[kf.d], writes=[ki.d])
                                cx.op("dve", lambda e, ki=ki, kf=kf: e.tensor_copy(kf[:], ki[:]), reads=[ki.d], writes=[kf.d])
                                cx.op("dve", lambda e, a=a, kf=kf: e.scalar_tensor_tensor(a[:], kf[:], -TWO_PI, a[:], ALU.mult, ALU.add), reads=[a.d, kf.d], writes=[a.d])
                                cx.op("dve", lambda e, a=a, kf=kf: e.tensor_single_scalar(kf[:], a[:], math.pi, ALU.is_gt), reads=[a.d], writes=[kf.d])
                                cx.op("dve", lambda e, a=a, kf=kf: e.scalar_tensor_tensor(a[:], kf[:], -TWO_PI, a[:], ALU.mult, ALU.add), reads=[a.d, kf.d], writes=[a.d])
                                cx.op("dve", lambda e, a=a, kf=kf: e.tensor_single_scalar(kf[:], a[:], -math.pi, ALU.is_lt), reads=[a.d], writes=[kf.d])
                                cx.op("dve", lambda e, a=a, kf=kf: e.scalar_tensor_tensor(a[:], kf[:], TWO_PI, a[:], ALU.mult, ALU.add), reads=[a.d, kf.d], writes=[a.d])
                                cx.op("dve", lambda e, a=a: e.tensor_scalar(a[:], a[:], 3.14159, -3.14159, ALU.min, ALU.max), reads=[a.d], writes=[a.d])
                                cx.op("act", lambda e, a=a: e.activation(a[:], a[:], AF.Sin), reads=[a.d], writes=[a.d])
                                outs.append(a)
                            return outs

                        sm = sb("s5sm", [P, 3, 16], F32, ps_)
                        cx.dma("sp", sm[:], I["s5_sm"], writes=[sm.d])
                        F_ = 16
                        lr = sb("d_lr", [P, F_], F32, ps_)
                        dt = sb("d_dt", [P, F_], F32, ps_)
                        mag = sb("d_mag", [P, F_], F32, ps_)
                        th = sb("d_th", [P, F_], F32, ps_)
                        li_ = sm[:, 1, :]
                        cx.op("dve", lambda e: e.tensor_scalar_min(lr[:], sm[:, 0, :], R_S5_MAX_RE), reads=[sm.d], writes=[lr.d])
                        cx.op("act", lambda e: e.activation(dt[:], sm[:, 2, :], AF.Exp), reads=[sm.d], writes=[dt.d])
                        cx.op("dve", lambda e: e.tensor_tensor(mag[:], lr[:], dt[:], ALU.mult), reads=[lr.d, dt.d], writes=[mag.d])
                        cx.op("act", lambda e: e.activation(mag[:], mag[:], AF.Exp), reads=[mag.d], writes=[mag.d])
                        cx.op("dve", lambda e: e.tensor_tensor(th[:], li_, dt[:], ALU.mult), reads=[dt.d, sm.d], writes=[th.d])
                        sn, cs_ = sincos(th, F_, "sm")
                        cx.op("dve", lambda e: e.tensor_copy(rho[:], mag[:]), reads=[mag.d], writes=[rho.d])
                        cx.op("dve", lambda e: e.tensor_tensor(rhoms[:], rho[:, :].unsqueeze(2).broadcast_to([P, 16, NS]),
                                                              cm[:, 400:464].unsqueeze(1).broadcast_to([P, 16, NS]), ALU.mult),
                              reads=[rho.d, cm.d], writes=[rhoms.d])
                        Ed = Dep()
                        cx.op("dve", lambda e: e.tensor_copy(Ere[:, :, 0], cs_[:]), reads=[cs_.d], writes=[Ed])
                        cx.op("dve", lambda e: e.tensor_scalar(Eim[:, :, 0], sn[:], -1.0, None, ALU.mult), reads=[sn.d], writes=[Ed])
                        t1 = sb("e_t1", [P, 16, NT // 2], F32, ps_)
                        t2 = sb("e_t2", [P, 16, NT // 2], F32, ps_)
                        m_ = 1
                        while m_ < NT:
                            br = Ere[:, :, m_ - 1:m_].broadcast_to([P, 16, m_])
                            bi = Eim[:, :, m_ - 1:m_].broadcast_to([P, 16, m_])
                            ar, ai = Ere[:, :, 0:m_], Eim[:, :, 0:m_]
                            cx.op("dve", lambda e, ar=ar, br=br, m_=m_: e.tensor_tensor(t1[:, :, :m_], ar, br, ALU.mult), reads=[Ed], writes=[t1.d])
                            cx.op("pool", lambda e, ai=ai, bi=bi, m_=m_: e.tensor_tensor(t2[:, :, :m_], ai, bi, ALU.mult), reads=[Ed], writes=[t2.d])
                            cx.op("dve", lambda e, m_=m_: e.tensor_tensor(Ere[:, :, m_:2 * m_], t1[:, :, :m_], t2[:, :, :m_], ALU.subtract), reads=[t1.d, t2.d], writes=[Ed])
                            cx.op("dve", lambda e, ar=ar, bi=bi, m_=m_: e.tensor_tensor(t1[:, :, :m_], ar, bi, ALU.mult), reads=[Ed], writes=[t1.d])
                            cx.op("pool", lambda e, ai=ai, br=br, m_=m_: e.tensor_tensor(t2[:, :, :m_], ai, br, ALU.mult), reads=[Ed], writes=[t2.d])
                            cx.op("dve", lambda e, m_=m_: e.tensor_tensor(Eim[:, :, m_:2 * m_], t1[:, :, :m_], t2[:, :, :m_], ALU.add), reads=[t1.d, t2.d], writes=[Ed])
                            m_ *= 2
                        Ere.d = Ed
                        Eim.d = Ed
                        are = sb("r_are", [P, F_], F32, ps_)
                        aim = sb("r_aim", [P, F_], F32, ps_)
                        den = sb("r_den", [P, F_], F32, ps_)
                        tz = sb("r_tz", [P, F_], F32, ps_)
                        zz = [sb("r_zre", [P, F_], F32, ps_), sb("r_zim", [P, F_], F32, ps_)]
                        zre, zim = zz
                        cx.op("dve", lambda e: e.tensor_tensor(are[:], mag[:], cs_[:], ALU.mult), reads=[mag.d, cs_.d], writes=[are.d])
                        cx.op("dve", lambda e: e.tensor_scalar_add(are[:], are[:], -1.0), reads=[are.d], writes=[are.d])
                        cx.op("dve", lambda e: e.tensor_tensor(aim[:], mag[:], sn[:], ALU.mult), reads=[mag.d, sn.d], writes=[aim.d])
                        cx.op("dve", lambda e: e.tensor_tensor(den[:], lr[:], lr[:], ALU.mult), reads=[lr.d], writes=[den.d])
                        cx.op("dve", lambda e: e.tensor_tensor(tz[:], li_, li_, ALU.mult), reads=[sm.d], writes=[tz.d])
                        cx.op("dve", lambda e: e.tensor_tensor(den[:], den[:], tz[:], ALU.add), reads=[den.d, tz.d], writes=[den.d])
                        cx.op("dve", lambda e: e.reciprocal(den[:], den[:]), reads=[den.d], writes=[den.d])
                        cx.op("dve", lambda e: e.tensor_tensor(zre[:], are[:], lr[:], ALU.mult), reads=[are.d, lr.d], writes=[zre.d])
                        cx.op("dve", lambda e: e.tensor_tensor(tz[:], aim[:], li_, ALU.mult), reads=[aim.d, sm.d, tz.d], writes=[tz.d])
                        cx.op("dve", lambda e: e.tensor_tensor(zre[:], zre[:], tz[:], ALU.add), reads=[zre.d, tz.d], writes=[zre.d])
                        cx.op("dve", lambda e: e.tensor_tensor(zre[:], zre[:], den[:], ALU.mult), reads=[zre.d, den.d], writes=[zre.d])
                        cx.op("dve", lambda e: e.tensor_tensor(zim[:], aim[:], lr[:], ALU.mult), reads=[aim.d, lr.d], writes=[zim.d])
                        cx.op("dve", lambda e: e.tensor_tensor(tz[:], are[:], li_, ALU.mult), reads=[are.d, sm.d, tz.d], writes=[tz.d])
                        cx.op("dve", lambda e: e.tensor_tensor(zim[:], zim[:], tz[:], ALU.subtract), reads=[zim.d, tz.d], writes=[zim.d])
                        cx.op("dve", lambda e: e.tensor_tensor(zim[:], zim[:], den[:], ALU.mult), reads=[zim.d, den.d], writes=[zim.d])
                        bexp = [sb(f"bexp{r}", [P, 4, 512], F32, ps_) for r in range(2)]
                        for r in range(2):
                            cx.dma("sp", bexp[r][:], I["s5_bexp"][r], writes=[bexp[r].d])
                        dg = [sb(f"dg{i}", [P, P], F32, ps_) for i in range(4)]
                        ta = sb("r_ta", [P, 512], F32, ps_)
                        tb = sb("r_tb", [P, 512], F32, ps_)
                        di = 0
                        for c in range(4):
                            pz = [bank(), bank()]
                            for r in range(2):
                                for m in range(4):
                                    dgt = dg[di % 4]
                                    di += 1
                                    cx.op("dve", lambda e, dgt=dgt, r=r, c=c, m=m: e.tensor_scalar(dgt[:], ident[:], zz[r][:, 4 * c + m:4 * c + m + 1], None, ALU.mult),
                                          reads=[ident.d, zz[r].d], writes=[dgt.d])
                                    cx.op("pe", lambda e, dgt=dgt, r=r, m=m, pz=pz: e.matmul(pz[r][:, 128 * m:128 * m + 128], ones32[:], dgt[:], start=True, stop=True),
                                          reads=[ones32.d, dgt.d], writes=[pz[r].d])
                            cx.op("dve", lambda e, c=c, pz=pz: e.tensor_tensor(ta[:], pz[0][:, :], bexp[0][:, c, :], ALU.mult), reads=[pz[0].d, bexp[0].d], writes=[ta.d])
                            cx.op("dve", lambda e, c=c, pz=pz: e.tensor_tensor(tb[:], pz[1][:, :], bexp[1][:, c, :], ALU.mult), reads=[pz[1].d, bexp[1].d], writes=[tb.d])
                            cx.op("dve", lambda e, c=c: e.tensor_tensor(BB[0][:, c, :], ta[:], tb[:], ALU.subtract), reads=[ta.d, tb.d], writes=[BB[0].d])
                            cx.op("dve", lambda e, c=c, pz=pz: e.tensor_tensor(ta[:], pz[0][:, :], bexp[1][:, c, :], ALU.mult), reads=[pz[0].d, bexp[1].d, ta.d], writes=[ta.d])
                            cx.op("dve", lambda e, c=c, pz=pz: e.tensor_tensor(tb[:], pz[1][:, :], bexp[0][:, c, :], ALU.mult), reads=[pz[1].d, bexp[0].d, tb.d], writes=[tb.d])
                            cx.op("dve", lambda e, c=c: e.tensor_tensor(BB[1][:, c, :], ta[:], tb[:], ALU.add), reads=[ta.d, tb.d], writes=[BB[1].d])
                        cexp = sb("cexp", [P, 16, 128], F32, ps_)
                        cx.dma("sp", cexp[:], I["s5_cexp"][0], writes=[cexp.d])
                        cx.op("act", lambda e: e.copy(CC[0][:], cexp[:]), reads=[cexp.d], writes=[CC[0].d])
                        cexp2 = cexp
                        cx.dma("sp", cexp2[:], I["s5_cexp"][1], writes=[cexp2.d])
                        cx.op("act", lambda e: e.mul(CC[1][:], cexp2[:], -1.0), reads=[cexp2.d], writes=[CC[1].d])
                        cx.barrier()
                    RG = 3
                    u32 = [sb(f"s_u32{i}", [P, 4, NT], F32, ph) for i in range(2)]
                    u16 = [sb(f"s_u16{i}", [P, 4, NT], BF16, ph) for i in range(2)]
                    tt = [[sb(f"s_tt{i}_{j}", [P, 2, NT], F32, ph) for j in range(4)] for i in range(1)] * 2
                    wre = [sb(f"s_wre{i}", [P, 2, NT], F32, ph) for i in range(1)] * 2
                    wim = [sb(f"s_wim{i}", [P, 2, NT], F32, ph) for i in range(1)] * 2
                    zre_ = [sb(f"s_zre{i}", [P, 2, NT], F32, ph) for i in range(RG)]
                    zim_ = [sb(f"s_zim{i}", [P, 2, NT], F32, ph) for i in range(RG)]
                    pp = [[sb(f"s_pp{i}_{j}", [P, 2, NT], F32, ph) for j in range(4)] for i in range(2)]
                    xre32 = [sb(f"s_xre{i}", [P, 2, NT], F32, ph) for i in range(2)]
                    xim32 = [sb(f"s_xim{i}", [P, 2, NT], F32, ph) for i in range(2)]
                    xre16 = [sb(f"s_xre16{i}", [P, 4, NT], BF16, ph) for i in range(2)]
                    xim16 = [sb(f"s_xim16{i}", [P, 4, NT], BF16, ph) for i in range(2)]
                    y32 = sb("s_y32", [P, 4, NT], F32, ph)
                    gt = sb("s_gt", [P, 4, NT], F32, ph)
                    yg32 = sb("s_yg32", [P, 4, NT], F32, ph)
                    yg16 = sb("s_yg16", [P, 4, NT], BF16, ph)
                    sgl = sb("s_sgl", [P, NT], F32, ph)
                    on16 = sb("s_on16", [P, 4, NT], BF16, ph)
                    tmpx = sb("s_tmpx", [P, NSEQ_S], F32, ph)
                    tl = tiles_of(LP, NT)

                    def uproj(ti):
                        s0, n, smp = tl[ti]
                        for c in range(4):
                            pb = bank()
                            for k in range(KD):
                                cx.op("pe", lambda e, k=k, pb=pb, c=c: e.matmul(pb[:, :n], w[:, k, 128 * c:128 * c + 128], hall[:, k, s0:s0 + n],
                                                                               start=(k == 0), stop=(k == KD - 1)),
                                      reads=[w.d] + hdeps(s0, n), writes=[pb.d], inc=(k == KD - 1))
                            cx.op("act", lambda e, pb=pb, c=c: e.copy(u32[ti % 2][:, c, :n], pb[:, :n]), reads=[pb.d], writes=[u32[ti % 2].d])
                            cx.op("act", lambda e, pb=pb, c=c: e.copy(u16[ti % 2][:, c, :n], pb[:, :n]), reads=[pb.d], writes=[u16[ti % 2].d])

                    def geom(ti, s_):
                        s0, n, smp = tl[ti]
                        nb, L = (NSEQ_S, LS) if smp else (1, n)
                        if smp:
                            er = Ere[:, s_:s_ + 2, 0:L].unsqueeze(2).broadcast_to([P, 2, nb, L])
                            ei = Eim[:, s_:s_ + 2, 0:L].unsqueeze(2).broadcast_to([P, 2, nb, L])
                            v3 = lambda ap: ap.rearrange("p c (b j) -> p c b j", b=nb)
                        else:
                            er = Ere[:, s_:s_ + 2, 0:n]
                            ei = Eim[:, s_:s_ + 2, 0:n]
                            v3 = lambda ap: ap
                        return s0, n, smp, nb, L, er, ei, v3

                    def stageA(it, ti, s_):
                        s0, n, smp, nb, L, er, ei, v3 = geom(ti, s_)
                        pbk = bank()
                        ut = u16[ti % 2]
                        for ri in range(2):
                            for q in range(2):
                                s2 = s_ + q
                                c, m = s2 // 4, s2 % 4
                                col = (2 * ri + q) * NT
                                cx.op("pe", lambda e, ri=ri, c=c, m=m, col=col: e.matmul(pbk[:, col:col + n], BB[ri][:, c, 128 * m:128 * m + 128], ut[:, c, :n],
                                                                                        start=True, stop=True),
                                      reads=[BB[ri].d, ut.d], writes=[pbk.d], inc=(ri == 1 and q == 1))
                        pk = pbk[:, :].rearrange("p (r q t) -> p r q t", r=2, q=2)
                        pbr, pbi = pk[:, 0, :, :n], pk[:, 1, :, :n]
                        t_ = tt[it % 2]
                        wr_, wi_ = wre[it % 2], wim[it % 2]
                        cx.op("dve", lambda e: e.tensor_tensor(v3(t_[0][:, :, :n]), er, v3(pbr), ALU.mult), reads=[Ere.d, pbk.d], writes=[t_[0].d])
                        cx.op("dve", lambda e: e.tensor_tensor(v3(t_[1][:, :, :n]), ei, v3(pbi), ALU.mult), reads=[Eim.d, pbk.d], writes=[t_[1].d])
                        cx.op("dve", lambda e: e.tensor_tensor(v3(t_[2][:, :, :n]), er, v3(pbi), ALU.mult), reads=[Ere.d, pbk.d], writes=[t_[2].d])
                        cx.op("dve", lambda e: e.tensor_tensor(v3(t_[3][:, :, :n]), ei, v3(pbr), ALU.mult), reads=[Eim.d, pbk.d], writes=[t_[3].d])

                    def stageA2(it, ti, s_):
                        s0, n, smp, nb, L, er, ei, v3 = geom(ti, s_)
                        t_ = tt[it % 2]
                        wr_, wi_ = wre[it % 2], wim[it % 2]
                        cx.op("dve", lambda e: e.tensor_tensor(wr_[:, :, :n], t_[0][:, :, :n], t_[1][:, :, :n], ALU.subtract), reads=[t_[0].d, t_[1].d], writes=[wr_.d])
                        cx.op("dve", lambda e: e.tensor_tensor(wi_[:, :, :n], t_[2][:, :, :n], t_[3][:, :, :n], ALU.add), reads=[t_[2].d, t_[3].d], writes=[wi_.d])

                    def stageA3(it, ti, s_):
                        s0, n, smp, nb, L, er, ei, v3 = geom(ti, s_)
                        wr_, wi_ = wre[it % 2], wim[it % 2]
                        zr_, zi_ = zre_[it % RG], zim_[it % RG]
                        for q in range(2):
                            s2 = s_ + q
                            if smp:
                                for ri, wt in ((0, wr_), (1, wi_)):
                                    cx.op("dve", lambda e, ri=ri, s2=s2: e.tensor_scalar(tmpx[:], x0[:, ri, s2, :], rho[:, s2:s2 + 1], None, ALU.mult),
                                          reads=[x0.d, rho.d], writes=[tmpx.d])
                                    w3 = wt[:, q, :n].rearrange("p (b j) -> p b j", b=nb)
                                    cx.op("dve", lambda e, w3=w3: e.tensor_tensor(w3[:, :, 0], w3[:, :, 0], tmpx[:], ALU.add), reads=[wt.d, tmpx.d], writes=[wt.d])
                                d0 = rhoms[:, s2, :]
                                cx.op("dve", lambda e, q=q, d0=d0: e.tensor_tensor_scan(zr_[:, q, :n], d0, wr_[:, q, :n], 0.0, ALU.mult, ALU.add),
                                      reads=[rhoms.d, wr_.d], writes=[zr_.d])
                                cx.op("dve", lambda e, q=q, d0=d0: e.tensor_tensor_scan(zi_[:, q, :n], d0, wi_[:, q, :n], 0.0, ALU.mult, ALU.add),
                                      reads=[rhoms.d, wi_.d], writes=[zi_.d])
                            else:
                                d0 = rho[:, s2:s2 + 1].broadcast_to([P, n])
                                cx.op("dve", lambda e, q=q, d0=d0, s2=s2: e.tensor_tensor_scan(zr_[:, q, :n], d0, wr_[:, q, :n], xst[:, 0, s2:s2 + 1], ALU.mult, ALU.add),
                                      reads=[rho.d, wr_.d, xst.d], writes=[zr_.d])
                                cx.op("dve", lambda e, q=q, d0=d0, s2=s2: e.tensor_tensor_scan(zi_[:, q, :n], d0, wi_[:, q, :n], xst[:, 1, s2:s2 + 1], ALU.mult, ALU.add),
                                      reads=[rho.d, wi_.d, xst.d], writes=[zi_.d])

                    def stageB(it, ti, s_):
                        s0, n, smp, nb, L, er, ei, v3 = geom(ti, s_)
                        zr_, zi_ = zre_[it % RG], zim_[it % RG]
                        p_ = pp[it % 2]
                        cx.op("pool", lambda e: e.tensor_tensor(v3(p_[0][:, :, :n]), er, v3(zr_[:, :, :n]), ALU.mult), reads=[Ere.d, zr_.d], writes=[p_[0].d])
                        cx.op("pool", lambda e: e.tensor_tensor(v3(p_[1][:, :, :n]), ei, v3(zi_[:, :, :n]), ALU.mult), reads=[Eim.d, zi_.d], writes=[p_[1].d])
                        cx.op("pool", lambda e: e.tensor_tensor(v3(p_[2][:, :, :n]), er, v3(zi_[:, :, :n]), ALU.mult), reads=[Ere.d, zi_.d], writes=[p_[2].d])
                        cx.op("pool", lambda e: e.tensor_tensor(v3(p_[3][:, :, :n]), ei, v3(zr_[:, :, :n]), ALU.mult), reads=[Eim.d, zr_.d], writes=[p_[3].d])

                    def stageC(it, ti, s_):
                        s0, n, smp, nb, L, er, ei, v3 = geom(ti, s_)
                        c, m = s_ // 4, s_ % 4
                        p_ = pp[it % 2]
                        xr_, xi_ = xre32[it % 2], xim32[it % 2]
                        x16r, x16i = xre16[(ti * 4 + c) % 2], xim16[(ti * 4 + c) % 2]
                        cx.op("pool", lambda e: e.tensor_tensor(xr_[:, :, :n], p_[0][:, :, :n], p_[1][:, :, :n], ALU.add), reads=[p_[0].d, p_[1].d], writes=[xr_.d])
                        cx.op("pool", lambda e: e.tensor_tensor(xi_[:, :, :n], p_[2][:, :, :n], p_[3][:, :, :n], ALU.subtract), reads=[p_[2].d, p_[3].d], writes=[xi_.d])
                        cx.op("act", lambda e: e.copy(x16r[:, m:m + 2, :n], xr_[:, :, :n]), reads=[xr_.d], writes=[x16r.d])
                        cx.op("act", lambda e: e.copy(x16i[:, m:m + 2, :n], xi_[:, :, :n]), reads=[xi_.d], writes=[x16i.d])
                        if smp:
                            for q in range(2):
                                xv = xr_[:, q, :n].rearrange("p (b j) -> p b j", b=nb)
                                cx.op("act", lambda e, q=q, xv=xv: e.copy(xso[:, 0, s_ + q, :], xv[:, :, L - 1]), reads=[xr_.d], writes=[xso.d])
                                xv2 = xi_[:, q, :n].rearrange("p (b j) -> p b j", b=nb)
                                cx.op("act", lambda e, q=q, xv2=xv2: e.copy(xso[:, 1, s_ + q, :], xv2[:, :, L - 1]), reads=[xi_.d], writes=[xso.d])
                        else:
                            cx.op("act", lambda e: e.copy(xst[:, 0, s_:s_ + 2], xr_[:, :, n - 1]), reads=[xr_.d], writes=[xst.d])
                            cx.op("act", lambda e: e.copy(xst[:, 1, s_:s_ + 2], xi_[:, :, n - 1]), reads=[xi_.d], writes=[xst.d])
                        if m == 2:
                            pby = bank()
                            for mm_ in range(4):
                                cx.op("pe", lambda e, mm_=mm_: e.matmul(pby[:, :n], CC[0][:, 4 * c + mm_, :], x16r[:, mm_, :n], start=(mm_ == 0), stop=False),
                                      reads=[CC[0].d, x16r.d], writes=[pby.d], inc=False)
                                cx.op("pe", lambda e, mm_=mm_: e.matmul(pby[:, :n], CC[1][:, 4 * c + mm_, :], x16i[:, mm_, :n], start=False, stop=(mm_ == 3)),
                                      reads=[CC[1].d, x16i.d], writes=[pby.d], inc=(mm_ == 3))
                            ut = u32[ti % 2]
                            cx.op("dve", lambda e: e.scalar_tensor_tensor(y32[:, c, :n], ut[:, c, :n], ssm[:, c:c + 1], pby[:, :n], ALU.mult, ALU.add),
                                  reads=[pby.d, ut.d, ssm.d], writes=[y32.d])
                        if s_ == 14:
                            tile_post(ti)

                    def tile_post(ti):
                        s0, n, smp = tl[ti]
                        if ti == len(tl) - 2:
                            cx.dma("sp", O["s5_p"].rearrange("r p s -> p r s"), xst[:], reads=[xst.d], writes=[outdep])
                        cx.op("pool", lambda e: e.tensor_tensor(gt[:, :, :n], y32[:, :, :n], y32[:, :, :n], ALU.mult), reads=[y32.d], writes=[gt.d])
                        cx.op("pool", lambda e: e.tensor_scalar(gt[:, :, :n], gt[:, :, :n], 0.044715, 1.0, ALU.mult, ALU.add), reads=[gt.d], writes=[gt.d])
                        cx.op("pool", lambda e: e.tensor_tensor(gt[:, :, :n], gt[:, :, :n], y32[:, :, :n], ALU.mult), reads=[gt.d, y32.d], writes=[gt.d])
                        cx.op("act", lambda e: e.activation(gt[:, :, :n], gt[:, :, :n], AF.Sigmoid, scale=2.0 * math.sqrt(2.0 / math.pi)), reads=[gt.d], writes=[gt.d])
                        cx.op("dve", lambda e: e.tensor_tensor(yg32[:, :, :n], y32[:, :, :n], gt[:, :, :n], ALU.mult), reads=[gt.d, y32.d], writes=[yg32.d])
                        cx.op("act", lambda e: e.copy(yg16[:, :, :n], yg32[:, :, :n]), reads=[yg32.d], writes=[yg16.d])
                        for c in range(4):
                            pb = bank()
                            for k in range(4):
                                cx.op("pe", lambda e, k=k, pb=pb, c=c: e.matmul(pb[:, :n], wglu[:, k, 128 * c:128 * c + 128], yg16[:, k, :n], start=(k == 0), stop=(k == 3)),
                                      reads=[wglu.d, yg16.d], writes=[pb.d], inc=(k == 3))
                            cx.op("act", lambda e, pb=pb, c=c: e.activation(sgl[:, :n], pb[:, :n], AF.Sigmoid, bias=ssm[:, 4 + c:5 + c]), reads=[pb.d, ssm.d], writes=[sgl.d])
                            cx.op("dve", lambda e, c=c: e.tensor_tensor(on16[:, c, :n], yg32[:, c, :n], sgl[:, :n], ALU.mult), reads=[yg32.d, sgl.d], writes=[on16.d])
                        for oc in range(KD):
                            pb = bank()
                            for k in range(4):
                                cx.op("pe", lambda e, k=k, oc=oc, pb=pb: e.matmul(pb[:, :n], wout[:, k, 128 * oc:128 * oc + 128], on16[:, k, :n], start=(k == 0), stop=(k == 3)),
                                      reads=[wout.d, on16.d], writes=[pb.d], inc=(k == 3))
                            cx.op("dve", lambda e, pb=pb, oc=oc: e.tensor_tensor(x[:, oc, s0:s0 + n], x[:, oc, s0:s0 + n], pb[:, :n], ALU.add),
                                  reads=[pb.d] + xdeps(s0, n), writes=xdeps(s0, n))

                    items = [(ti, s_) for ti in range(len(tl)) for s_ in range(0, 16, 2)]
                    NI = len(items)
                    uproj(0)
                    for i in range(NI + 3):
                        if i < NI:
                            ti, s_ = items[i]
                            if s_ == 8 and ti + 1 < len(tl):
                                uproj(ti + 1)
                            stageA(i, ti, s_)
                        if 0 <= i - 1 < NI:
                            stageA3(i - 1, *items[i - 1])
                        if i < NI:
                            stageA2(i, *items[i])
                        if 0 <= i - 2 < NI:
                            stageB(i - 2, *items[i - 2])
                        if 0 <= i - 3 < NI:
                            stageC(i - 3, *items[i - 3])
                    cx.dma("sp", O["s5_s"].rearrange("r p s b -> p r s b"), xso[:], reads=[xso.d], writes=[outdep])
                    cx.barrier()

        for l in range(2):
            if l == 0 and any(p in cfg.phases for p in ("A0", "A0a", "A0b")):
                a0_phase(cfg.NTA, cfg.NTB)
            if l == 1 and "A1" in cfg.phases:
                a1_phase()
            if f"X{l}" in cfg.phases:
                xattn_phase(l)
            if f"F{l}" in cfg.phases:
                ffn_phase(l)

        with contextlib.ExitStack() as ph:
            scr = {"sq16": sb("n_sq16", [P, KD, 512], BF16, ph), "rstd": sb("n_rstd", [P, 512], F32, ph), "eps": epsb}
            yo = [sb(f"yo{i}", [P, KD, 512], F32, ph) for i in range(2)]
            for i, (s0, n, smp) in enumerate(tiles_of(LP, 512)):
                yt = yo[i % 2]
                rmsnorm(lambda k: x[:, k, s0:s0 + n], xdeps(s0, n), n, 6, lambda k: yt[:, k, :n], [yt.d], scr)
                cx.dma("sp", O["yT"][:, :, s0:s0 + n].rearrange("k p t -> p k t"), yt[:, :, :n], reads=[yt.d], writes=[outdep])
            cx.barrier()
        cx.barrier(engines=("sp",))
        stats = {k: (e.n_ins, e.cnt) for k, e in cx.E.items()}
    return nc, stats


def fm(v, nchunk):
    v = np.asarray(v)
    lead = v.shape[:-1]
    r = v.reshape(lead + (nchunk, 128))
    return np.ascontiguousarray(np.moveaxis(r, -1, 0))


def prep_core(inp, c, cfg):
    LP, T = cfg.LP, cfg.T
    m = {}
    xp = inp["x_prompt"][c, :LP]
    xs = inp["x_sample"][NSEQ_S * c:NSEQ_S * (c + 1)].reshape(NS, D)
    xT = np.concatenate([xp, xs], axis=0).T
    m["xT"] = np.ascontiguousarray(xT.reshape(KD, 128, T))
    m["ident"] = np.eye(128, dtype=np.float32)
    nl = [inp["norm_mix"][0], inp["norm_mix"][1], inp["norm_cross"][0], inp["norm_cross"][1],
          inp["norm_ffn"][0], inp["norm_ffn"][1], inp["norm_final"]]
    m["norms"] = np.ascontiguousarray(np.stack([v.reshape(KD, 128).T for v in nl], axis=1))
    m["ffn_up"] = np.ascontiguousarray(inp["ffn_w_up"].reshape(2, KD, 128, 2 * FFN))
    m["ffn_dn"] = np.ascontiguousarray(inp["ffn_w_down"].reshape(2, NFC, 128, D))
    cw = inp["ffn_conv_w"].reshape(2, 3, 2 * NFC, 128)
    m["convw"] = np.ascontiguousarray(cw.transpose(0, 3, 2, 1))
    m["convb"] = np.ascontiguousarray(inp["ffn_conv_b"].reshape(2, 2 * NFC, 128).transpose(0, 2, 1))
    cs = inp["state_ffn_conv"][:, NSEQ_S * c:NSEQ_S * (c + 1)]
    cs = cs.reshape(2, NSEQ_S, 2, 2 * NFC, 128)
    m["convst"] = np.ascontiguousarray(cs.transpose(0, 4, 3, 1, 2))
    cmk = np.zeros((128, 464), np.float32)
    jj, ii = np.meshgrid(np.arange(64), np.arange(64), indexing="ij")
    cmk[:64, 0:64] = (jj <= ii)
    cmk[:64, 64:128] = (jj <= ii) & (jj // LS == ii // LS)
    cmk[:64, 128:144] = (np.arange(64)[:, None] // LS == np.arange(NSEQ_S)[None, :])
    cmk[:, 144:400] = (np.arange(256) % GC != 0)[None, :]
    cmk[:, 400:464] = (np.arange(64) % LS != 0)[None, :]
    m["cmask"] = cmk
    cm2 = np.zeros((128, 384), np.float32)
    j2, i2 = np.meshgrid(np.arange(128), np.arange(128), indexing="ij")
    cm2[:, 0:128] = (j2 <= i2)
    cm2[:, 128:384] = (np.arange(256) % 128 != 0)[None, :]
    m["cmask2"] = cm2
    m["w_in_c"] = inp["w_in_c"][0].reshape(KD, 128, 3088)
    m["w_out_c"] = inp["w_out_c"][0].reshape(KD, 128, D)
    m["gla_wgu"] = inp["gla_w_gate_up"][0]
    m["gla_small"] = np.concatenate([inp["gla_b_gate"][0].reshape(4, 128).T, inp["gla_gnorm"][0].reshape(2, 128).T], axis=1)
    m["st_gla"] = inp["state_gla"][0, NSEQ_S * c:NSEQ_S * (c + 1)]
    m["w_in_ab"] = inp["w_in_ab"][0].reshape(KD, 128, 2560)
    m["w_out_ab"] = inp["w_out_ab"][0].reshape(KD, 128, D)
    lbr = inp["hgrn_lb"].reshape(3, 4, 128).transpose(2, 1, 0).reshape(128, 12)
    m["hg_small"] = np.concatenate([lbr, inp["hgrn_gnorm"][0].reshape(128, 1)], axis=1)
    m["st_hgrn"] = inp["state_hgrn"][0, NSEQ_S * c:NSEQ_S * (c + 1)]

    def sm16(a):
        return a.reshape(16, 128).T
    ls_full = np.repeat(inp["s5_log_step"][0][:, None], 64, axis=1)
    m["s5_sm"] = np.stack([sm16(inp["s5_lam_re"][0]), sm16(inp["s5_lam_im"][0]), sm16(ls_full)], axis=1)
    m["s5_row"] = np.stack([inp["s5_lam_re"][0].reshape(2048), inp["s5_lam_im"][0].reshape(2048), ls_full.reshape(2048)], axis=0)
    bexp = np.zeros((2, 128, 4, 512), np.float32)
    cexp = np.zeros((2, 128, 16, 128), np.float32)
    for r, (bsrc, csrc) in enumerate(((inp["s5_b_re"][0], inp["s5_c_re"][0]), (inp["s5_b_im"][0], inp["s5_c_im"][0]))):
        for g_ in range(32):
            cc_, gl = g_ // 8, g_ % 8
            bexp[r, gl * 16:(gl + 1) * 16, cc_, gl * 64:(gl + 1) * 64] = bsrc[g_].T
            s_, two = g_ // 2, g_ % 2
            cexp[r, two * 64:(two + 1) * 64, s_, gl * 16:(gl + 1) * 16] = csrc[g_].T
    m["s5_bexp"] = bexp
    m["s5_cexp"] = cexp
    m["s5_small"] = np.concatenate([inp["s5_d"][0].reshape(4, 128).T, inp["s5_b_glu"][0].reshape(4, 128).T], axis=1)
    m["s5_wglu"] = inp["s5_w_glu"][0].reshape(4, 128, 512)
    x0 = np.stack([inp["state_s5_re"][0, NSEQ_S * c:NSEQ_S * (c + 1)], inp["state_s5_im"][0, NSEQ_S * c:NSEQ_S * (c + 1)]], 0)
    m["s5_x0"] = x0.reshape(2, NSEQ_S, 16, 128).transpose(0, 3, 2, 1)
    m["memT"] = inp["mem_prompt"][c].T.reshape(KD, 128, NMEM)
    m["normmem"] = np.stack([inp["norm_mem"][l].reshape(KD, 128).T for l in range(2)], axis=1)
    m["wkv"] = inp["xa_w_kv"].reshape(2, KD, 128, 2 * D)
    m["wq"] = inp["xa_w_q"].reshape(2, KD, 128, D)
    m["wo"] = inp["xa_w_o"].reshape(2, KD, 128, D)
    ck = inp["cache_mem_k"][:, NSEQ_S * c:NSEQ_S * (c + 1)].reshape(2, NSEQ_S, NMEM, D)
    m["kTc"] = ck.transpose(0, 1, 3, 2).reshape(2, NSEQ_S, KD, 128, NMEM)
    m["vc"] = inp["cache_mem_v"][:, NSEQ_S * c:NSEQ_S * (c + 1)].reshape(2, NSEQ_S, 2, 128, D)
    return {k: np.ascontiguousarray(v, dtype=np.float32) for k, v in m.items()}


_CACHE = {}


def run(inputs, cfg, n_cores=8):
    key = (cfg.LP, tuple(cfg.phases), cfg.xmode)
    if key not in _CACHE:
        _CACHE[key] = build(cfg)
    nc, stats = _CACHE[key]
    in_maps = [prep_core(inputs, c, cfg) for c in range(n_cores)]
    res = run_bass_kernel_spmd(nc, in_maps, core_ids=list(range(n_cores)))
    return res.results, stats


def assemble(results, cfg, n_cores=8):
    LP, T = cfg.LP, cfg.T
    out = {}
    yp, ys, cp, cs = [], [], [], []
    mk, mv = [], []
    glp, gls = [], []
    hgp, hgs, s5p, s5s = [], [], [], []
    for c in range(n_cores):
        r = results[c]
        hgp.append(r["hgrn_p"]); hgs.append(r["hgrn_s"])
        s5p.append(r["s5_p"].transpose(0, 2, 1).reshape(2, 32, 64))
        s5s.append(r["s5_s"].transpose(0, 3, 2, 1).reshape(2, NSEQ_S, 32, 64))
        glp.append(r["gla_p"]); gls.append(r["gla_s"])
        mk.append(r["mem_kT"].reshape(2, D, NMEM).transpose(0, 2, 1).reshape(2, NMEM, 4, 256))
        mv.append(r["mem_v"].reshape(2, NMEM, 4, 256))
        yT = r["yT"].reshape(D, T)
        yp.append(yT[:, :LP].T)
        ys.append(yT[:, LP:].T.reshape(NSEQ_S, LS, D))
        cp.append(r["conv_p"].transpose(0, 3, 2, 1).reshape(2, 2, 2 * FFN))
        cs.append(r["conv_s"].transpose(0, 3, 4, 2, 1).reshape(2, NSEQ_S, 2, 2 * FFN))
    out["hgrn_p"] = np.stack(hgp, 0)[None]
    out["hgrn_s"] = np.concatenate(hgs, 0)[None]
    s5p = np.stack(s5p, 1); s5s = np.concatenate(s5s, 1)
    out["s5_re_p"] = s5p[0][None]; out["s5_im_p"] = s5p[1][None]
    out["s5_re_s"] = s5s[0][None]; out["s5_im_s"] = s5s[1][None]
    out["gla_p"] = np.stack(glp, 0)[None]
    out["gla_s"] = np.concatenate(gls, 0)[None]
    out["mem_k_p"] = np.stack(mk, 1)
    out["mem_v_p"] = np.stack(mv, 1)
    out["y_prompt"] = np.stack(yp, 0)
    out["y_sample"] = np.concatenate(ys, 0)
    out["conv_p"] = np.stack(cp, 1)
    out["conv_s"] = np.concatenate(cs, 1)
    return out


def kernel(**inputs):
    cfg = Cfg()
    inputs = {k: np.asarray(v) for k, v in inputs.items()}
    results, _ = run(inputs, cfg)
    o = assemble(results, cfg)
    names = ["y_prompt", "y_sample", "hgrn_p", "s5_re_p", "s5_im_p", "gla_p", "mem_k_p", "mem_v_p", "conv_p",
             "hgrn_s", "s5_re_s", "s5_im_s", "gla_s", "conv_s"]
    return tuple(np.ascontiguousarray(o[k], dtype=np.float32) for k in names)
```

```python
import contextlib
import math
import numpy as np
import concourse.bass as bass
import concourse.mybir as mybir
from concourse.bass_utils import run_bass_kernel_spmd

F32 = mybir.dt.float32
BF16 = mybir.dt.bfloat16
AF = mybir.ActivationFunctionType
ALU = mybir.AluOpType

D = 1024
KD = 8
NSEQ_S = 16
LS = 4
NS = NSEQ_S * LS
FFN = 2816
NFC = 22
NMEM = 256
EPS = 1e-6
GC = 64


class Dep:
    __slots__ = ("writer", "readers", "excl")

    def __init__(self, excl=False):
        self.writer = None
        self.readers = {}
        self.excl = excl


class Eng:
    def __init__(self, name, h, sem, is_pe=False):
        self.name, self.h, self.sem, self.is_pe = name, h, sem, is_pe
        self.cnt = 0
        self.waited = {}
        self.n_ins = 0

    def wait(self, ev):
        sem, val = ev
        if sem is self.sem and self.is_pe:
            return
        if self.waited.get(sem, 0) >= val:
            return
        self.waited[sem] = val
        self.h.wait_ge(sem, val)


class Ctx:
    def __init__(self, nc, stack, n_dma_sems=(20, 28, 8)):
        self.nc = nc
        mk = lambda n: stack.enter_context(nc.semaphore(n))
        self.E = {
            "pe": Eng("pe", nc.tensor, mk("s_pe"), is_pe=True),
            "act": Eng("act", nc.scalar, mk("s_act")),
            "dve": Eng("dve", nc.vector, mk("s_dve")),
            "pool": Eng("pool", nc.gpsimd, mk("s_pool")),
            "sp": Eng("sp", nc.sync, mk("s_sp")),
        }
        self.dma_sems = {}
        for q, n in zip(("sp", "pool", "act"), n_dma_sems):
            self.dma_sems[q] = [[mk(f"d_{q}{i}"), 0] for i in range(n)]
        self.dma_rr = {"sp": 0, "pool": 0, "act": 0}

    def _pre(self, E, reads, writes):
        for d in reads:
            if d.writer is not None:
                E.wait(d.writer)
            if d.excl:
                for ev in d.readers.values():
                    if ev[0] is not E.sem:
                        E.wait(ev)
        for d in writes:
            if d.writer is not None:
                E.wait(d.writer)
            for ev in d.readers.values():
                E.wait(ev)

    @staticmethod
    def _post(ev, reads, writes):
        for d in reads:
            d.readers[ev[0]] = ev
        for d in writes:
            d.writer = ev
            d.readers = {}

    def op(self, eng, fn, reads=(), writes=(), inc=True):
        E = self.E[eng]
        self._pre(E, reads, writes)
        ins = fn(E.h)
        E.n_ins += 1
        if inc:
            E.cnt += 1
            ins.then_inc(E.sem, 1)
            ev = (E.sem, E.cnt)
        else:
            ev = (E.sem, E.cnt + 1)
        self._post(ev, reads, writes)
        return ins

    def dma(self, q, out, in_, reads=(), writes=(), **kw):
        E = self.E[q]
        self._pre(E, reads, writes)
        pool = self.dma_sems[q]
        i = self.dma_rr[q]
        self.dma_rr[q] = (i + 1) % len(pool)
        slot = pool[i]
        if slot[1] > 0:
            E.wait((slot[0], slot[1]))
        slot[1] += 16
        ins = E.h.dma_start(out=out, in_=in_, **kw)
        ins.then_inc(slot[0], 16)
        E.n_ins += 1
        self._post((slot[0], slot[1]), reads, writes)
        return ins

    def barrier(self, engines=("pe", "act", "dve", "pool", "sp")):
        for en in engines:
            E = self.E[en]
            for F in self.E.values():
                if F is not E and F.cnt > 0:
                    E.wait((F.sem, F.cnt))
            for pool in self.dma_sems.values():
                for slot in pool:
                    if slot[1] > 0:
                        E.wait((slot[0], slot[1]))


class View:
    def __init__(self, ap):
        self.ap = ap
        self.d = Dep()

    def __getitem__(self, k):
        return self.ap[k]


class Buf:
    def __init__(self, t):
        self.t = t
        self.d = Dep()

    def __getitem__(self, k):
        return self.t[k]


class Cfg:
    def __init__(self, LP=2048, phases=("A0", "X0", "F0", "A1", "X1", "F1")):
        self.LP = LP
        self.T = LP + NS
        self.phases = phases
        self.NTA = 256
        self.xmode = "full"
        self.NTB = 128


def tiles_of(LP, n):
    ts = [(i, n, False) for i in range(0, LP, n)]
    ts.append((LP, NS, True))
    return ts


def build(cfg):
    nc = bass.Bass("TRN2", target_bir_lowering=False)
    LP, T = cfg.LP, cfg.T
    P = 128

    def din(name, shape, dt=F32):
        return nc.dram_tensor(name, list(shape), dt, kind="ExternalInput").ap()

    def dout(name, shape):
        return nc.dram_tensor(name, list(shape), F32, kind="ExternalOutput").ap()

    I = {}
    I["xT"] = din("xT", [KD, P, T])
    I["ident"] = din("ident", [P, P])
    I["norms"] = din("norms", [P, 7, KD])
    I["ffn_up"] = din("ffn_up", [2, KD, P, 2 * FFN])
    I["ffn_dn"] = din("ffn_dn", [2, NFC, P, D])
    I["convw"] = din("convw", [2, P, 2 * NFC, 3])
    I["convb"] = din("convb", [2, P, 2 * NFC])
    I["convst"] = din("convst", [2, P, 2 * NFC, NSEQ_S, 2])
    I["cmask"] = din("cmask", [P, 464])
    I["cmask2"] = din("cmask2", [P, 384])
    I["w_in_c"] = din("w_in_c", [KD, P, 3088])
    I["w_out_c"] = din("w_out_c", [KD, P, D])
    I["gla_wgu"] = din("gla_wgu", [16, 512])
    I["gla_small"] = din("gla_small", [P, 6])
    I["st_gla"] = din("st_gla", [NSEQ_S, 4, P, 256])
    I["w_in_ab"] = din("w_in_ab", [KD, P, 2560])
    I["w_out_ab"] = din("w_out_ab", [KD, P, D])
    I["hg_small"] = din("hg_small", [P, 13])
    I["st_hgrn"] = din("st_hgrn", [NSEQ_S, 4, P, 128])
    I["s5_sm"] = din("s5_sm", [P, 3, 16])
    I["s5_row"] = din("s5_row", [3, 2048])
    I["s5_bexp"] = din("s5_bexp", [2, P, 4, 512])
    I["s5_cexp"] = din("s5_cexp", [2, P, 16, 128])
    I["s5_small"] = din("s5_small", [P, 8])
    I["s5_wglu"] = din("s5_wglu", [4, P, 512])
    I["s5_x0"] = din("s5_x0", [2, P, 16, NSEQ_S])
    I["memT"] = din("memT", [KD, P, NMEM])
    I["normmem"] = din("normmem", [P, 2, KD])
    I["wkv"] = din("wkv", [2, KD, P, 2 * D])
    I["wq"] = din("wq", [2, KD, P, D])
    I["wo"] = din("wo", [2, KD, P, D])
    I["kTc"] = din("kTc", [2, NSEQ_S, KD, P, NMEM])
    I["vc"] = din("vc", [2, NSEQ_S, 2, P, D])
    O = {}
    O["mem_kT"] = dout("mem_kT", [2, KD, P, NMEM])
    O["mem_v"] = dout("mem_v", [2, 2, P, D])
    O["hgrn_p"] = dout("hgrn_p", [4, P, 128])
    O["hgrn_s"] = dout("hgrn_s", [NSEQ_S, 4, P, 128])
    O["s5_p"] = dout("s5_p", [2, P, 16])
    O["s5_s"] = dout("s5_s", [2, P, 16, NSEQ_S])
    O["gla_p"] = dout("gla_p", [4, P, 256])
    O["gla_s"] = dout("gla_s", [NSEQ_S, 4, P, 256])
    O["yT"] = dout("yT", [KD, P, T])
    O["conv_p"] = dout("conv_p", [2, P, 2 * NFC, 2])
    O["conv_s"] = dout("conv_s", [2, P, 2 * NFC, NSEQ_S, 2])

    with contextlib.ExitStack() as st:
        cx = Ctx(nc, st)

        uid = [0]

        def sb(name, shape, dt, stack=st):
            uid[0] += 1
            return Buf(stack.enter_context(nc.sbuf_tensor(f"s{uid[0]}_{name}", list(shape), dt)))

        banks = [Buf(st.enter_context(nc.psum_tensor(f"bank{i}", [P, 512], F32))) for i in range(8)]
        for b_ in banks:
            b_.d.excl = True
        bank_rr = [0]

        reserved = set()

        def bank():
            while True:
                i = bank_rr[0]
                bank_rr[0] = (bank_rr[0] + 1) % 8
                if i not in reserved:
                    return banks[i]

        def reserve():
            b = bank()
            reserved.add(banks.index(b))
            return b

        def release(b):
            reserved.discard(banks.index(b))

        x = sb("x", [P, KD, T], F32)
        xd = {}
        for s0 in range(0, T, 256):
            xd[s0] = Dep()

        def xdeps(s0, n):
            return [xd[k] for k in range((s0 // 256) * 256, s0 + n, 256)]

        ident = sb("ident", [P, P], F32)
        ident16 = sb("ident16", [P, P], BF16)
        ones16 = sb("ones16", [P, P], BF16)
        norms = sb("norms", [P, 7, KD], F32)
        outdep = Dep()

        cx.dma("sp", ident[:], I["ident"], writes=[ident.d])
        cx.dma("sp", norms[:], I["norms"], writes=[norms.d])
        cx.op("dve", lambda e: e.tensor_copy(ident16[:], ident[:]), reads=[ident.d], writes=[ident16.d])
        cx.op("dve", lambda e: e.memset(ones16[:], 1.0), writes=[ones16.d])
        for s0 in range(0, T, 256):
            n = min(256, T - s0)
            cx.dma("sp", x[:, :, s0:s0 + n], I["xT"][:, :, s0:s0 + n].rearrange("k p t -> p k t"),
                   writes=[xd[s0]])

        def rmsnorm(src_fn, src_deps, n, gidx, out_fn, out_deps, scr, nk=KD, dim=D, out_eng="dve", src_all=None):
            sq, rstd = scr["sq16"], scr["rstd"]
            if src_all is not None:
                cx.op("act", lambda e: e.activation(sq[:, :nk, :n], src_all, AF.Square), reads=src_deps, writes=[sq.d])
            else:
                for k in range(nk):
                    cx.op("act", lambda e, k=k: e.activation(sq[:, k, :n], src_fn(k), AF.Square),
                          reads=src_deps, writes=[sq.d])
            pb = bank()
            for k in range(nk):
                cx.op("pe", lambda e, k=k: e.matmul(pb[:, :n], ones16[:], sq[:, k, :n], start=(k == 0), stop=(k == nk - 1)),
                      reads=[sq.d, ones16.d], writes=[pb.d], inc=(k == nk - 1))
            cx.op("act", lambda e: e.activation(rstd[:, :n], pb[:, :n], AF.Ln, bias=scr["eps"][:, 0:1], scale=1.0 / dim),
                  reads=[pb.d, scr["eps"].d], writes=[rstd.d])
            cx.op("act", lambda e: e.activation(rstd[:, :n], rstd[:, :n], AF.Exp, scale=-0.5), reads=[rstd.d], writes=[rstd.d])
            for k in range(nk):
                cx.op(out_eng, lambda e, k=k: e.scalar_tensor_tensor(out_fn(k), src_fn(k), norms[:, gidx, k:k + 1], rstd[:, :n],
                                                                      ALU.mult, ALU.mult),
                      reads=list(src_deps) + [rstd.d, norms.d], writes=out_deps)

        epsb = sb("epsb", [P, 1], F32)
        cx.op("dve", lambda e: e.memset(epsb[:], EPS), writes=[epsb.d])

        def ffn_phase(l):
            with contextlib.ExitStack() as ph:
                hall = sb("hall", [P, KD, T], BF16, ph)
                hd = {s0: Dep() for s0 in range(0, T, 512)}
                cw = sb("cw", [P, 2 * NFC, 3], F32, ph)
                cb = sb("cb", [P, 2 * NFC], F32, ph)
                cst = sb("cst", [P, 2 * NFC, NSEQ_S, 2], F32, ph)
                tail = sb("tail", [P, 2 * NFC, 2], F32, ph)
                cso = sb("cso", [P, 2 * NFC, NSEQ_S, 2], F32, ph)
                cx.dma("sp", cw[:], I["convw"][l], writes=[cw.d])
                cx.dma("sp", cb[:], I["convb"][l], writes=[cb.d])
                cx.dma("sp", cst[:], I["convst"][l], writes=[cst.d])
                cx.op("dve", lambda e: e.memset(tail[:], 0.0), writes=[tail.d])
                tl = tiles_of(LP, 512)
                GS = 4
                groups = [(c0, min(GS, NFC - c0)) for c0 in range(0, NFC, GS)]
                NG = len(groups)
                RING = 2
                wup = [sb(f"wup{i}", [P, KD, 2, 128 * GS], BF16, ph) for i in range(RING)]
                wdn = [sb(f"wdn{i}", [P, GS, D], BF16, ph) for i in range(RING)]
                upv = I["ffn_up"][l].rearrange("k p n -> p k n")

                def load_group(g):
                    r = g % RING
                    c0, gs = groups[g]
                    for half in range(2):
                        cstart = half * FFN + 128 * c0
                        cx.dma("pool", wup[r][:, :, half, :128 * gs], upv[:, :, cstart:cstart + 128 * gs], writes=[wup[r].d])
                    cx.dma("pool", wdn[r][:, :gs, :], I["ffn_dn"][l][c0:c0 + gs].rearrange("j p n -> p j n"), writes=[wdn[r].d])

                load_group(0)
                with contextlib.ExitStack() as phn:
                    scr = {"sq16": sb("f_sq16", [P, KD, 512], BF16, phn), "rstd": sb("f_rstd", [P, 512], F32, phn), "eps": epsb}
                    for (s0, n, smp) in tl:
                        rmsnorm(lambda k: x[:, k, s0:s0 + n], xdeps(s0, n), n, 4 + l,
                                lambda k: hall[:, k, s0:s0 + n], [hd[s0]], scr, src_all=x[:, :, s0:s0 + n])
                    cx.barrier()
                ext = [sb(f"ext{i}", [P, 512 + 2 * NSEQ_S], F32, ph) for i in range(4)]
                cc = [sb(f"cc{i}", [P, 512], F32, ph) for i in range(4)]
                sa = [sb(f"sa{i}", [P, 512], F32, ph) for i in range(2)]
                yb = [sb(f"yb{i}", [P, GS, 512], BF16, ph) for i in range(2)]
                itc = [0]
                steps = [(g, ti) for g in range(NG) for ti in range(len(tl))]

                def up_part(si):
                    g, ti = steps[si]
                    s0, n, smp = tl[ti]
                    r = g % RING
                    c0, gs = groups[g]
                    if True:
                        nseq, L = (NSEQ_S, LS) if smp else (1, n)
                        yt = yb[si % 2]
                        for j in range(gs):
                            it = itc[0]
                            ch = [c0 + j, NFC + c0 + j]
                            cres = []
                            for half in range(2):
                                c = ch[half]
                                pb = bank()
                                for k in range(KD):
                                    cx.op("pe", lambda e, k=k, half=half, pb=pb: e.matmul(
                                        pb[:, :n], wup[r][:, k, half, 128 * j:128 * j + 128], hall[:, k, s0:s0 + n],
                                        start=(k == 0), stop=(k == KD - 1)),
                                        reads=[wup[r].d, hd[s0]], writes=[pb.d], inc=(k == KD - 1))
                                et = ext[(2 * it + half) % 4]
                                ct = cc[(2 * it + half) % 4]
                                ev = et[:, :nseq * (L + 2)].rearrange("p (s l) -> p s l", s=nseq)
                                pv = pb[:, :n].rearrange("p (s l) -> p s l", s=nseq)
                                cv = ct[:, :n].rearrange("p (s l) -> p s l", s=nseq)
                                if smp:
                                    cx.op("pool", lambda e, ev=ev, c=c: e.tensor_copy(ev[:, :, 0:2], cst[:, c, :, :]),
                                          reads=[cst.d], writes=[et.d])
                                else:
                                    cx.op("pool", lambda e, ev=ev, c=c: e.tensor_copy(ev[:, :, 0:2], tail[:, c:c + 1, :]),
                                          reads=[tail.d], writes=[et.d])
                                cx.op("act", lambda e, ev=ev, pv=pv: e.copy(ev[:, :, 2:L + 2], pv), reads=[pb.d], writes=[et.d])
                                cx.op("act", lambda e, cv=cv, pv=pv, c=c: e.activation(cv, pv, AF.Identity, bias=cb[:, c:c + 1],
                                                                                      scale=cw[:, c, 2:3]),
                                      reads=[pb.d, cw.d, cb.d], writes=[ct.d])
                                cx.op("dve", lambda e, cv=cv, ev=ev, c=c: e.scalar_tensor_tensor(cv, ev[:, :, 1:L + 1], cw[:, c, 1:2], cv,
                                                                                              ALU.mult, ALU.add),
                                      reads=[et.d, cw.d, ct.d], writes=[ct.d])
                                cx.op("dve", lambda e, cv=cv, ev=ev, c=c: e.scalar_tensor_tensor(cv, ev[:, :, 0:L], cw[:, c, 0:1], cv,
                                                                                              ALU.mult, ALU.add),
                                      reads=[et.d, cw.d, ct.d], writes=[ct.d])
                                if smp:
                                    cx.op("pool", lambda e, ev=ev, c=c: e.tensor_copy(cso[:, c, :, :], ev[:, :, L:L + 2]),
                                          reads=[et.d], writes=[cso.d])
                                else:
                                    cx.op("pool", lambda e, ev=ev, c=c: e.tensor_copy(tail[:, c:c + 1, :], ev[:, :, L:L + 2]),
                                          reads=[et.d], writes=[tail.d])
                                cres.append(ct)
                            st_ = sa[it % 2]
                            cx.op("act", lambda e, st_=st_, ct=cres[0]: e.activation(st_[:, :n], ct[:, :n], AF.Silu),
                                  reads=[cres[0].d], writes=[st_.d])
                            cx.op("dve", lambda e, st_=st_, ct=cres[1], yt=yt: e.tensor_tensor(yt[:, j, :n], st_[:, :n], ct[:, :n], ALU.mult),
                                  reads=[st_.d, cres[1].d], writes=[yt.d])
                            itc[0] += 1

                def down_part(si):
                    g, ti = steps[si]
                    s0, n, smp = tl[ti]
                    r = g % RING
                    c0, gs = groups[g]
                    yt = yb[si % 2]
                    if True:
                        for oc in range(KD):
                            pb = bank()
                            for j in range(gs):
                                cx.op("pe", lambda e, j=j, pb=pb, oc=oc, yt=yt: e.matmul(
                                    pb[:, :n], wdn[r][:, j, 128 * oc:128 * oc + 128], yt[:, j, :n], start=(j == 0), stop=(j == gs - 1)),
                                    reads=[wdn[r].d, yt.d], writes=[pb.d], inc=(j == gs - 1))
                            cx.op("dve", lambda e, pb=pb, oc=oc: e.tensor_tensor(x[:, oc, s0:s0 + n], x[:, oc, s0:s0 + n], pb[:, :n], ALU.add),
                                  reads=[pb.d] + xdeps(s0, n), writes=xdeps(s0, n))

                for si in range(len(steps) + 1):
                    if si < len(steps):
                        up_part(si)
                    if si >= 1:
                        down_part(si - 1)
                    if si < len(steps) and steps[si][1] == 0:
                        g = steps[si][0]
                        if g + RING - 1 < NG:
                            load_group(g + RING - 1)
                cx.dma("sp", O["conv_p"][l], tail[:], reads=[tail.d], writes=[outdep])
                cx.dma("sp", O["conv_s"][l], cso[:], reads=[cso.d], writes=[outdep])
                cx.barrier()


        KT16 = [None, None]
        V16 = [None, None]

        def mem_phase(l):
            with contextlib.ExitStack() as ph:
                memT = sb("memT", [P, KD, NMEM], F32, ph)
                nm = sb("nm", [P, 2, KD], F32, ph)
                sq = sb("m_sq", [P, KD, NMEM], BF16, ph)
                rstd = sb("m_rstd", [P, NMEM], F32, ph)
                cx.dma("sp", memT[:], I["memT"].rearrange("k p m -> p k m"), writes=[memT.d])
                cx.dma("sp", nm[:], I["normmem"], writes=[nm.d])
                cx.op("act", lambda e: e.activation(sq[:], memT[:], AF.Square), reads=[memT.d], writes=[sq.d])
                pb = bank()
                for k in range(KD):
                    cx.op("pe", lambda e, k=k: e.matmul(pb[:, :NMEM], ones16[:], sq[:, k, :], start=(k == 0), stop=(k == KD - 1)),
                          reads=[sq.d, ones16.d], writes=[pb.d], inc=(k == KD - 1))
                cx.op("act", lambda e: e.activation(rstd[:], pb[:, :NMEM], AF.Ln, bias=epsb[:, 0:1], scale=1.0 / D),
                      reads=[pb.d, epsb.d], writes=[rstd.d])
                cx.op("act", lambda e: e.activation(rstd[:], rstd[:], AF.Exp, scale=-0.5), reads=[rstd.d], writes=[rstd.d])
                if True:
                    wkv = sb(f"wkv{l}", [P, KD, 2 * D], BF16, ph)
                    memn = sb(f"memn{l}", [P, KD, NMEM], BF16, ph)
                    ko = sb(f"ko{l}", [P, KD, NMEM], F32, ph)
                    vo = sb(f"vo{l}", [P, 2, D], F32, ph)
                    wv = I["wkv"][l].rearrange("k p n -> p k n")
                    for hh in range(2):
                        cx.dma("pool", wkv[:, :, hh * D:(hh + 1) * D], wv[:, :, hh * D:(hh + 1) * D], writes=[wkv.d])
                    for k in range(KD):
                        cx.op("dve", lambda e, k=k: e.scalar_tensor_tensor(memn[:, k, :], memT[:, k, :], nm[:, l, k:k + 1], rstd[:],
                                                                          ALU.mult, ALU.mult),
                              reads=[memT.d, nm.d, rstd.d], writes=[memn.d])
                    for c in range(KD):
                        pb = bank()
                        for k in range(KD):
                            cx.op("pe", lambda e, k=k, c=c, pb=pb: e.matmul(pb[:, :NMEM], wkv[:, k, 128 * c:128 * c + 128], memn[:, k, :],
                                                                           start=(k == 0), stop=(k == KD - 1)),
                                  reads=[wkv.d, memn.d], writes=[pb.d], inc=(k == KD - 1))
                        cx.op("act", lambda e, c=c, pb=pb: e.copy(KT16[l][:, c, :], pb[:, :NMEM]), reads=[pb.d], writes=[KT16[l].d])
                        cx.op("dve", lambda e, c=c, pb=pb: e.tensor_copy(ko[:, c, :], pb[:, :NMEM]), reads=[pb.d], writes=[ko.d])
                    cx.dma("sp", O["mem_kT"][l].rearrange("k p m -> p k m"), ko[:], reads=[ko.d], writes=[outdep])
                    for mc in range(2):
                        for cb_ in range(2):
                            pb = bank()
                            for k in range(KD):
                                cx.op("pe", lambda e, k=k, mc=mc, cb_=cb_, pb=pb: e.matmul(
                                    pb[:, :512], memn[:, k, 128 * mc:128 * mc + 128], wkv[:, k, D + 512 * cb_:D + 512 * cb_ + 512],
                                    start=(k == 0), stop=(k == KD - 1)),
                                    reads=[wkv.d, memn.d], writes=[pb.d], inc=(k == KD - 1))
                            cx.op("act", lambda e, mc=mc, cb_=cb_, pb=pb: e.copy(V16[l][:, mc, 512 * cb_:512 * cb_ + 512], pb[:, :512]),
                                  reads=[pb.d], writes=[V16[l].d])
                            cx.op("dve", lambda e, mc=mc, cb_=cb_, pb=pb: e.tensor_copy(vo[:, mc, 512 * cb_:512 * cb_ + 512], pb[:, :512]),
                                  reads=[pb.d], writes=[vo.d])
                    cx.dma("sp", O["mem_v"][l].rearrange("m p n -> p m n"), vo[:], reads=[vo.d], writes=[outdep])
                cx.barrier()

        def xattn_phase(l):
            with contextlib.ExitStack() as ph:
                KT16[l] = sb(f"KT16_{l}", [P, KD, NMEM], BF16, ph)
                V16[l] = sb(f"V16_{l}", [P, 2, D], BF16, ph)
                mem_phase(l)
                if cfg.xmode == "mem":
                    return
                wq = sb("wq", [P, KD, D], BF16, ph)
                wo = sb("wo", [P, KD, D], BF16, ph)
                cx.dma("pool", wq[:], I["wq"][l].rearrange("k p n -> p k n"), writes=[wq.d])
                cx.dma("pool", wo[:], I["wo"][l].rearrange("k p n -> p k n"), writes=[wo.d])
                scr = {"sq16": sb("x_sq16", [P, KD, 512], BF16, ph), "rstd": sb("x_rstd", [P, 512], F32, ph), "eps": epsb}
                h = sb("x_h", [P, KD, 512], BF16, ph)
                qa = sb("x_qa", [P, KD, T], BF16, ph)
                NR = 3
                pTb = [sb(f"x_pT{i}", [P, 2, 512], BF16, ph) for i in range(NR)]
                rsb = [sb(f"x_rs{i}", [P, 512], F32, ph) for i in range(2)]
                ktr = [sb(f"x_kt{i}", [P, KD, NMEM], BF16, ph) for i in range(2)]
                vtr = [sb(f"x_vt{i}", [P, 2, D], BF16, ph) for i in range(2)]
                pTs = sb("x_pTs", [P, 2, 256], BF16, ph)
                pTs_d = [Dep() for _ in range(NSEQ_S)]
                tl = tiles_of(LP, 512)
                qd = [[Dep() for _ in range(4)] for _ in tl]
                order = [len(tl) - 1] + list(range(len(tl) - 1))
                for ti in order:
                    s0, n, smp = tl[ti]
                    rmsnorm(lambda k: x[:, k, s0:s0 + n], xdeps(s0, n), n, 2 + l, lambda k: h[:, k, :n], [h.d], scr, src_all=x[:, :, s0:s0 + n])
                    for c in range(KD):
                        pb = bank()
                        for k in range(KD):
                            cx.op("pe", lambda e, k=k, c=c, pb=pb: e.matmul(pb[:, :n], wq[:, k, 128 * c:128 * c + 128], h[:, k, :n],
                                                                           start=(k == 0), stop=(k == KD - 1)),
                                  reads=[wq.d, h.d], writes=[pb.d], inc=(k == KD - 1))
                        if c % 2 == 0:
                            cx.op("act", lambda e, c=c, pb=pb: e.copy(qa[:, c, s0:s0 + n], pb[:, :n]), reads=[pb.d], writes=[qd[ti][c // 2]])
                        else:
                            cx.op("dve", lambda e, c=c, pb=pb: e.tensor_copy(qa[:, c, s0:s0 + n], pb[:, :n]), reads=[pb.d], writes=[qd[ti][c // 2]])
                S_ps = reserve()
                O_ps = reserve()
                Sv = S_ps[:, :].rearrange("p (m c) -> p m c", m=2)
                sN = len(tl) - 1
                sS0 = tl[sN][0]

                def sample_seq(b):
                    kt, vt = ktr[b % 2], vtr[b % 2]
                    cx.dma("pool", kt[:], I["kTc"][l, b].rearrange("k p m -> p k m"), writes=[kt.d])
                    cx.dma("pool", vt[:], I["vc"][l, b].rearrange("m p n -> p m n"), writes=[vt.d])
                    for hd in range(4):
                        for mc in range(2):
                            for dc in range(2):
                                col = mc * 256 + (b * 4 + hd) * 4
                                cx.op("pe", lambda e, mc=mc, dc=dc, hd=hd, col=col: e.matmul(
                                    S_ps[:, col:col + 4], kt[:, 2 * hd + dc, 128 * mc:128 * mc + 128], qa[:, 2 * hd + dc, sS0 + 4 * b:sS0 + 4 * b + 4],
                                    start=(dc == 0), stop=(dc == 1)),
                                    reads=[kt.d, qd[sN][hd]], writes=[S_ps.d], inc=(dc == 1 and mc == 1 and hd == 3))
                    cx.op("act", lambda e: e.activation(pTs[:, :, 16 * b:16 * b + 16], Sv[:, :, 16 * b:16 * b + 16], AF.Exp, scale=1.0 / 16.0),
                          reads=[S_ps.d], writes=[pTs_d[b]])
                    for hd in range(4):
                        for dc in range(2):
                            for mc in range(2):
                                col = (2 * hd + dc) * 64 + 4 * b
                                pc = (b * 4 + hd) * 4
                                cx.op("pe", lambda e, mc=mc, dc=dc, hd=hd, col=col, pc=pc: e.matmul(
                                    O_ps[:, col:col + 4], vt[:, mc, (2 * hd + dc) * 128:(2 * hd + dc) * 128 + 128], pTs[:, mc, pc:pc + 4],
                                    start=(mc == 0), stop=(mc == 1)),
                                    reads=[vt.d, pTs_d[b]], writes=[O_ps.d], inc=(mc == 1 and dc == 1 and hd == 3))

                items = [(ti, hd) for ti in range(len(tl) - 1) for hd in range(4)]
                sc_banks = {}

                def SC(i):
                    ti, hd = items[i]
                    s0, n, smp = tl[ti]
                    pT = pTb[i % NR]
                    pbs = [bank(), bank()]
                    for mc in range(2):
                        for dc in range(2):
                            cx.op("pe", lambda e, mc=mc, dc=dc: e.matmul(
                                pbs[mc][:, :n], KT16[l][:, 2 * hd + dc, 128 * mc:128 * mc + 128], qa[:, 2 * hd + dc, s0:s0 + n],
                                start=(dc == 0), stop=(dc == 1)),
                                reads=[KT16[l].d, qd[ti][hd]], writes=[pbs[mc].d], inc=(dc == 1))
                    for mc in range(2):
                        cx.op("act", lambda e, mc=mc: e.activation(pT[:, mc, :n], pbs[mc][:, :n], AF.Exp, scale=1.0 / 16.0),
                              reads=[pbs[mc].d], writes=[pT.d])

                def REST(i):
                    ti, hd = items[i]
                    s0, n, smp = tl[ti]
                    pT, rs = pTb[i % NR], rsb[i % 2]
                    pbsum = bank()
                    for mc in range(2):
                        cx.op("pe", lambda e, mc=mc: e.matmul(pbsum[:, :n], ones16[:], pT[:, mc, :n], start=(mc == 0), stop=(mc == 1)),
                              reads=[pT.d, ones16.d], writes=[pbsum.d], inc=(mc == 1))
                    cx.op("act", lambda e: e.activation(rs[:, :n], pbsum[:, :n], AF.Ln), reads=[pbsum.d], writes=[rs.d])
                    cx.op("act", lambda e: e.activation(rs[:, :n], rs[:, :n], AF.Exp, scale=-1.0), reads=[rs.d], writes=[rs.d])
                    for dc in range(2):
                        pbo = bank()
                        for mc in range(2):
                            cx.op("pe", lambda e, mc=mc, dc=dc, pbo=pbo: e.matmul(
                                pbo[:, :n], V16[l][:, mc, (2 * hd + dc) * 128:(2 * hd + dc) * 128 + 128], pT[:, mc, :n],
                                start=(mc == 0), stop=(mc == 1)),
                                reads=[V16[l].d, pT.d], writes=[pbo.d], inc=(mc == 1))
                        cx.op("dve", lambda e, dc=dc, pbo=pbo: e.tensor_tensor(qa[:, 2 * hd + dc, s0:s0 + n], pbo[:, :n], rs[:, :n], ALU.mult),
                              reads=[pbo.d, rs.d], writes=[qd[ti][hd]])

                NI = len(items)
                nb_done = 0
                if NI > 0:
                    SC(0)
                for i in range(NI):
                    if i + 1 < NI:
                        SC(i + 1)
                    REST(i)
                    want = ((i + 1) * NSEQ_S + NI - 1) // NI
                    while nb_done < min(want, NSEQ_S):
                        sample_seq(nb_done)
                        nb_done += 1
                while nb_done < NSEQ_S:
                    sample_seq(nb_done)
                    nb_done += 1
                pbsum = bank()
                for mc in range(2):
                    cx.op("pe", lambda e, mc=mc, pbsum=pbsum: e.matmul(pbsum[:, :256], ones16[:], pTs[:, mc, :], start=(mc == 0), stop=(mc == 1)),
                          reads=pTs_d + [ones16.d], writes=[pbsum.d], inc=(mc == 1))
                rs = rsb[0]
                cx.op("act", lambda e: e.activation(rs[:, :256], pbsum[:, :256], AF.Ln), reads=[pbsum.d], writes=[rs.d])
                cx.op("act", lambda e: e.activation(rs[:, :256], rs[:, :256], AF.Exp, scale=-1.0), reads=[rs.d], writes=[rs.d])
                rsv = rs[:, :256].rearrange("p (b h t) -> p b h t", b=NSEQ_S, h=4)
                Ov = O_ps[:, :].rearrange("p (c b t) -> p c b t", c=KD, b=NSEQ_S)
                aov = qa[:, :, sS0:sS0 + NS].rearrange("p c (b t) -> p c b t", b=NSEQ_S)
                for hd in range(4):
                    for dc in range(2):
                        c = 2 * hd + dc
                        cx.op("dve", lambda e, c=c, hd=hd: e.tensor_tensor(aov[:, c], Ov[:, c], rsv[:, :, hd, :], ALU.mult),
                              reads=[O_ps.d, rs.d], writes=[qd[sN][hd]])
                release(S_ps)
                release(O_ps)
                for ti in range(len(tl)):
                    s0, n, smp = tl[ti]
                    for oc in range(KD):
                        pb = bank()
                        for k in range(KD):
                            cx.op("pe", lambda e, k=k, oc=oc, pb=pb: e.matmul(pb[:, :n], wo[:, k, 128 * oc:128 * oc + 128], qa[:, k, s0:s0 + n],
                                                                            start=(k == 0), stop=(k == KD - 1)),
                                  reads=[wo.d] + qd[ti], writes=[pb.d], inc=(k == KD - 1))
                        cx.op("dve", lambda e, pb=pb, oc=oc: e.tensor_tensor(x[:, oc, s0:s0 + n], x[:, oc, s0:s0 + n], pb[:, :n], ALU.add),
                              reads=[pb.d] + xdeps(s0, n), writes=xdeps(s0, n))
                cx.barrier()

        cm = sb("cmask", [P, 464], F32)
        cx.dma("sp", cm[:], I["cmask"], writes=[cm.d])
        maskC16 = sb("maskC16", [64, 64], BF16)
        maskS16 = sb("maskS16", [64, 64], BF16)
        msel16 = sb("msel16", [64, 16], BF16)
        cx.op("dve", lambda e: e.tensor_copy(maskC16[:], cm[:64, 0:64]), reads=[cm.d], writes=[maskC16.d])
        cx.op("dve", lambda e: e.tensor_copy(maskS16[:], cm[:64, 64:128]), reads=[cm.d], writes=[maskS16.d])
        cx.op("dve", lambda e: e.tensor_copy(msel16[:], cm[:64, 128:144]), reads=[cm.d], writes=[msel16.d])

        def gla_alloc(ph, dvc, NT, CK=GC):
            dv = 128 * dvc
            g = {}
            g["ring"] = [[sb(f"g_bt{i}", [P, NT], F32, ph), sb(f"g_eb{i}", [P, NT], F32, ph), sb(f"g_enb{i}", [P, NT], F32, ph)] for i in range(2)]
            g["qt16"] = sb("g_qt16", [P, 4, NT], BF16, ph)
            g["kt16"] = sb("g_kt16", [P, 4, NT], BF16, ph)
            g["ebl"] = sb("g_ebl", [P, 4, 16], F32, ph)
            g["kexp"] = [sb(f"g_kexp{i}", [64, NSEQ_S, 128], BF16, ph) for i in range(1)] * 2
            g["S32a"] = sb("g_S32a", [P, 4, dv], F32, ph)
            g["S16a"] = sb("g_S16a", [P, 4, dv], BF16, ph)
            RA = 1 if dvc == 2 else 2
            g["am4"] = [sb(f"g_am4_{i}", [CK, 4, CK], BF16, ph) for i in range(RA)] * (2 // RA)
            g["ktm4"] = [sb(f"g_ktm4_{i}", [CK, 4, 128], BF16, ph) for i in range(RA)] * (2 // RA)
            R4 = 2 if dvc == 2 else 4
            R2 = 1 if dvc == 2 else 2
            g["s32r"] = [sb(f"g_s32r{i}", [P, dv], F32, ph) for i in range(R4)] * (4 // R4)
            g["s16r"] = [sb(f"g_s16r{i}", [P, dv], BF16, ph) for i in range(R4)] * (4 // R4)
            g["so"] = [sb(f"g_so{i}", [P, dv], F32, ph) for i in range(R4)] * (4 // R4)
            g["o32"] = [sb(f"g_o32{i}", [P, dvc, NT], F32, ph) for i in range(R2)] * (2 // R2)
            g["osq"] = [sb(f"g_osq{i}", [P, dvc, NT], BF16, ph) for i in range(R2)] * (2 // R2)
            g["rstd"] = [sb(f"g_rstd{i}", [P, NT], F32, ph) for i in range(R2)] * (2 // R2)
            cx.op("pool", lambda e: e.memset(g["S32a"][:], 0.0), writes=[g["S32a"].d])
            cx.op("pool", lambda e: e.memset(g["S16a"][:], 0.0), writes=[g["S16a"].d])
            return g

        def gla_prep(g, n, smp, scale, q32, k32, lg32, CK=GC, mres_p=None):
            C = LS if smp else CK
            nch = n // C
            mres = cm[:, 400:464] if smp else (cm[:, 144:144 + n] if mres_p is None else mres_p[:, :n])
            qt16, kt16, ebl = g["qt16"], g["kt16"], g["ebl"]
            for hd in range(4):
                bt_, eb_, enb_ = g["ring"][hd % 2]
                cx.op("dve", lambda e, hd=hd, bt_=bt_: e.tensor_tensor_scan(bt_[:, :n], mres, lg32[:, hd, :n], 0.0, ALU.mult, ALU.add),
                      reads=[lg32.d, cm.d], writes=[bt_.d])
                cx.op("act", lambda e, bt_=bt_, eb_=eb_: e.activation(eb_[:, :n], bt_[:, :n], AF.Exp, scale=scale), reads=[bt_.d], writes=[eb_.d])
                cx.op("act", lambda e, bt_=bt_, enb_=enb_: e.activation(enb_[:, :n], bt_[:, :n], AF.Exp, scale=-scale), reads=[bt_.d], writes=[enb_.d])
                cx.op("dve", lambda e, hd=hd, eb_=eb_: e.tensor_tensor(qt16[:, hd, :n], q32[:, hd, :n], eb_[:, :n], ALU.mult),
                      reads=[q32.d, eb_.d], writes=[qt16.d])
                cx.op("pool", lambda e, hd=hd, enb_=enb_: e.tensor_tensor(kt16[:, hd, :n], k32[:, hd, :n], enb_[:, :n], ALU.mult),
                      reads=[k32.d, enb_.d], writes=[kt16.d])
                ebv = eb_[:, :n].rearrange("p (c j) -> p c j", j=C)
                cx.op("act", lambda e, hd=hd, ebv=ebv: e.copy(ebl[:, hd, :nch], ebv[:, :, C - 1]), reads=[eb_.d], writes=[ebl.d])

        def gla_chunks(g, n, smp, dvc, v_tm, st_in, st_out, CK=GC, maskCK=None):
            dv = 128 * dvc
            if maskCK is None:
                maskCK = maskC16
            C = LS if smp else CK
            nch = n // C
            qt16, kt16, ebl = g["qt16"], g["kt16"], g["ebl"]
            pbo = [reserve() for _ in range(4)]
            if not smp:
                S32a, S16a = g["S32a"], g["S16a"]
                hpb = 512 // dv
                for ci in range(nch):
                    cs = slice(ci * CK, ci * CK + CK)
                    am, ktm = g["am4"][ci % 2], g["ktm4"][ci % 2]
                    pbA, pbT = bank(), bank()
                    pbTv = pbT[:, :].bitcast(BF16)
                    for hd in range(4):
                        cx.op("pe", lambda e, hd=hd: e.matmul(pbA[:CK, hd * CK:(hd + 1) * CK], kt16[:, hd, cs], qt16[:, hd, cs], start=True, stop=True),
                              reads=[kt16.d, qt16.d], writes=[pbA.d], inc=(hd == 3))
                    for hd in range(4):
                        cx.op("pe", lambda e, hd=hd: e.transpose(pbTv[:CK, hd * 128:(hd + 1) * 128], kt16[:, hd, cs], ident16[:]),
                              reads=[kt16.d, ident16.d], writes=[pbT.d], inc=(hd == 3))
                    cx.op("dve", lambda e: e.tensor_tensor(am[:], pbA[:CK, :4 * CK].rearrange("p (h c) -> p h c", h=4),
                                                          maskCK[:, :].unsqueeze(1).broadcast_to([CK, 4, CK]), ALU.mult),
                          reads=[pbA.d, maskCK.d], writes=[am.d])
                    cx.op("act", lambda e: e.copy(ktm[:], pbTv[:CK, :512].rearrange("p (h c) -> p h c", h=4)), reads=[pbT.d], writes=[ktm.d])
                    for hd in range(4):
                        for dc in range(dvc):
                            oc = slice(dc * 256 + ci * CK, dc * 256 + ci * CK + CK)
                            vs = slice(hd * dv + 128 * dc, hd * dv + 128 * dc + 128)
                            cx.op("pe", lambda e, hd=hd, oc=oc, vs=vs: e.matmul(pbo[hd][:, oc], v_tm[:CK, ci, vs], am[:, hd, :], start=True, stop=False),
                                  reads=[v_tm.d, am.d], writes=[pbo[hd].d], inc=False)
                            cx.op("pe", lambda e, hd=hd, oc=oc, dc=dc: e.matmul(pbo[hd][:, oc], S16a[:, hd, 128 * dc:128 * dc + 128], qt16[:, hd, cs],
                                                                                 start=False, stop=True),
                                  reads=[S16a.d, qt16.d], writes=[pbo[hd].d], inc=(dc == dvc - 1))
                    pbS = [bank() for _ in range(4 // hpb)]
                    for hd in range(4):
                        bi, col = hd // hpb, (hd % hpb) * dv
                        cx.op("pe", lambda e, hd=hd, bi=bi, col=col: e.matmul(pbS[bi][:, col:col + dv], ktm[:, hd, :], v_tm[:CK, ci, hd * dv:(hd + 1) * dv],
                                                                             start=True, stop=True),
                              reads=[ktm.d, v_tm.d], writes=[pbS[bi].d], inc=(hd % hpb == hpb - 1))
                    for bi in range(4 // hpb):
                        h0 = bi * hpb
                        S32v = S32a[:, h0:h0 + hpb, :]
                        psv = pbS[bi][:, :].rearrange("p (h d) -> p h d", h=hpb)
                        cx.op("dve", lambda e, S32v=S32v, psv=psv: e.tensor_tensor(S32v, psv, S32v, ALU.add), reads=[pbS[bi].d, S32a.d], writes=[S32a.d])
                        cx.op("dve", lambda e, S32v=S32v, h0=h0: e.tensor_tensor(S32v, S32v, ebl[:, h0:h0 + hpb, ci:ci + 1].broadcast_to([P, hpb, dv]), ALU.mult),
                              reads=[S32a.d, ebl.d], writes=[S32a.d])
                        cx.op("act", lambda e, S32v=S32v, h0=h0: e.copy(S16a[:, h0:h0 + hpb, :], S32v), reads=[S32a.d], writes=[S16a.d])
                return pbo
            mask16 = maskS16 if smp else maskC16
            nchunks = 1 if smp else nch
            CT = n if smp else GC
            for ci in range(nchunks):
                cs = slice(ci * CT, ci * CT + CT)
                for hd in range(4):
                    am, ktm = g["am4"][0][:, hd, :], g["ktm4"][0][:, hd, :]
                    am_d, ktm_d = g["am4"][0].d, g["ktm4"][0].d
                    pbA = bank()
                    cx.op("pe", lambda e, hd=hd, pbA=pbA: e.matmul(pbA[:CT, :CT], kt16[:, hd, cs], qt16[:, hd, cs], start=True, stop=True),
                          reads=[kt16.d, qt16.d], writes=[pbA.d])
                    pbT = bank()
                    pbTv = pbT[:, :].bitcast(BF16)
                    cx.op("pe", lambda e, hd=hd, pbTv=pbTv: e.transpose(pbTv[:CT, :128], kt16[:, hd, cs], ident16[:]),
                          reads=[kt16.d, ident16.d], writes=[pbT.d])
                    cx.op("dve", lambda e, am=am, pbA=pbA: e.tensor_tensor(am[:CT, :CT], pbA[:CT, :CT], mask16[:CT, :CT], ALU.mult),
                          reads=[pbA.d, mask16.d], writes=[am_d])
                    cx.op("act", lambda e, ktm=ktm, pbTv=pbTv: e.copy(ktm[:CT, :], pbTv[:CT, :128]), reads=[pbT.d], writes=[ktm_d])
                    for dc in range(dvc):
                        oc = slice(dc * 256 + ci * CT, dc * 256 + ci * CT + CT)
                        vs = slice(hd * dv + 128 * dc, hd * dv + 128 * dc + 128)
                        if not smp:
                            cx.op("pe", lambda e, hd=hd, oc=oc, vs=vs, am=am: e.matmul(pbo[hd][:, oc], v_tm[:CT, ci, vs], am[:CT, :CT], start=True, stop=False),
                                  reads=[v_tm.d, am_d], writes=[pbo[hd].d], inc=False)
                            cx.op("pe", lambda e, hd=hd, oc=oc, dc=dc: e.matmul(pbo[hd][:, oc], g["S16"][hd][:, 128 * dc:128 * dc + 128], qt16[:, hd, cs],
                                                                                 start=False, stop=True),
                                  reads=[g["S16"][hd].d, qt16.d], writes=[pbo[hd].d], inc=(dc == dvc - 1))
                    if not smp:
                        pbS = bank()
                        cx.op("pe", lambda e, hd=hd, pbS=pbS: e.matmul(pbS[:, :dv], ident[:], g["S32"][hd][:], start=True, stop=False),
                              reads=[ident.d, g["S32"][hd].d], writes=[pbS.d], inc=False)
                        cx.op("pe", lambda e, hd=hd, pbS=pbS, ktm=ktm: e.matmul(pbS[:, :dv], ktm[:CT, :], v_tm[:CT, ci, hd * dv:(hd + 1) * dv], start=False, stop=True),
                              reads=[ktm_d, v_tm.d], writes=[pbS.d])
                        cx.op("act", lambda e, hd=hd, pbS=pbS: e.activation(g["S32"][hd][:], pbS[:, :dv], AF.Copy, scale=ebl[:, hd, ci:ci + 1]),
                              reads=[pbS.d, ebl.d], writes=[g["S32"][hd].d])
                        cx.op("dve", lambda e, hd=hd, pbS=pbS: e.tensor_scalar(g["S16"][hd][:], pbS[:, :dv], ebl[:, hd, ci:ci + 1], None, ALU.mult),
                              reads=[pbS.d, ebl.d], writes=[g["S16"][hd].d])
                    else:
                        kexp = g["kexp"][hd % 2]
                        cx.op("dve", lambda e, kexp=kexp, ktm=ktm: e.tensor_tensor(
                            kexp[:, :, :], ktm[:64, :].unsqueeze(1).broadcast_to([64, NSEQ_S, 128]),
                            msel16[:, :].unsqueeze(2).broadcast_to([64, NSEQ_S, 128]), ALU.mult),
                            reads=[ktm_d, msel16.d], writes=[kexp.d])
                        for b in range(NSEQ_S):
                            r = (hd * NSEQ_S + b) % 4
                            s32, s16, so = g["s32r"][r], g["s16r"][r], g["so"][r]
                            cx.dma("pool", s32[:], st_in[b, hd], writes=[s32.d])
                            cx.dma("pool", s16[:], st_in[b, hd], writes=[s16.d])
                            for dc in range(dvc):
                                oc = slice(dc * 256 + 4 * b, dc * 256 + 4 * b + 4)
                                vs = slice(hd * dv + 128 * dc, hd * dv + 128 * dc + 128)
                                cx.op("pe", lambda e, hd=hd, oc=oc, dc=dc, s16=s16, b=b: e.matmul(
                                    pbo[hd][:, oc], s16[:, 128 * dc:128 * dc + 128], qt16[:, hd, 4 * b:4 * b + 4], start=True, stop=False),
                                    reads=[s16.d, qt16.d], writes=[pbo[hd].d], inc=False)
                                cx.op("pe", lambda e, hd=hd, oc=oc, vs=vs, am=am, b=b: e.matmul(
                                    pbo[hd][:, oc], v_tm[:CT, ci, vs], am[:CT, 4 * b:4 * b + 4], start=False, stop=True),
                                    reads=[v_tm.d, am_d], writes=[pbo[hd].d], inc=(dc == dvc - 1 and b == NSEQ_S - 1))
                            pbS = bank()
                            cx.op("pe", lambda e, hd=hd, pbS=pbS, kexp=kexp, b=b: e.matmul(pbS[:, :dv], kexp[:, b, :], v_tm[:CT, ci, hd * dv:(hd + 1) * dv],
                                                                                        start=True, stop=True),
                                  reads=[kexp.d, v_tm.d], writes=[pbS.d])
                            cx.op("act", lambda e, hd=hd, s32=s32, b=b: e.activation(s32[:], s32[:], AF.Copy, scale=ebl[:, hd, b:b + 1]),
                                  reads=[s32.d, ebl.d], writes=[s32.d])
                            cx.op("dve", lambda e, hd=hd, pbS=pbS, so=so, b=b, s32=s32: e.scalar_tensor_tensor(so[:], pbS[:, :dv], ebl[:, hd, b:b + 1], s32[:],
                                                                                                   ALU.mult, ALU.add),
                                  reads=[pbS.d, ebl.d, s32.d], writes=[so.d])
                            cx.dma("sp", st_out[b, hd], so[:], reads=[so.d], writes=[outdep])
            return pbo

        def gla_post(g, n, dvc, pbo, gate32, gn, on16, on_c0):
            dv = 128 * dvc
            for hd in range(4):
                o32, osq, rstd = g["o32"][hd % 2], g["osq"][hd % 2], g["rstd"][hd % 2]
                for dc in range(dvc):
                    cx.op("act", lambda e, hd=hd, dc=dc, o32=o32: e.copy(o32[:, dc, :n], pbo[hd][:, dc * 256:dc * 256 + n]), reads=[pbo[hd].d], writes=[o32.d])
                    cx.op("act", lambda e, hd=hd, dc=dc, osq=osq: e.activation(osq[:, dc, :n], pbo[hd][:, dc * 256:dc * 256 + n], AF.Square),
                          reads=[pbo[hd].d], writes=[osq.d])
                pb = bank()
                for dc in range(dvc):
                    cx.op("pe", lambda e, dc=dc, pb=pb, osq=osq: e.matmul(pb[:, :n], ones16[:], osq[:, dc, :n], start=(dc == 0), stop=(dc == dvc - 1)),
                          reads=[osq.d, ones16.d], writes=[pb.d], inc=(dc == dvc - 1))
                cx.op("act", lambda e, pb=pb, rstd=rstd: e.activation(rstd[:, :n], pb[:, :n], AF.Ln, bias=epsb[:, 0:1], scale=1.0 / dv),
                      reads=[pb.d, epsb.d], writes=[rstd.d])
                cx.op("act", lambda e, rstd=rstd: e.activation(rstd[:, :n], rstd[:, :n], AF.Exp, scale=-0.5), reads=[rstd.d], writes=[rstd.d])
                for dc in range(dvc):
                    cx.op("pool", lambda e, dc=dc, o32=o32, rstd=rstd: e.tensor_tensor(o32[:, dc, :n], o32[:, dc, :n], rstd[:, :n], ALU.mult),
                          reads=[o32.d, rstd.d], writes=[o32.d])
                    cx.op("dve", lambda e, hd=hd, dc=dc, o32=o32: e.scalar_tensor_tensor(
                        on16[:, on_c0 + hd * dvc + dc, :n], o32[:, dc, :n], gn[:, dc:dc + 1], gate32[:, hd * dvc + dc, :n], ALU.mult, ALU.mult),
                        reads=[o32.d, gn.d, gate32.d], writes=[on16.d])
            for b_ in pbo:
                release(b_)

        def a1_phase():
            NT = 256
            with contextlib.ExitStack() as ph:
                w = sb("w_in_c", [P, KD, 3088], BF16, ph)
                wv = I["w_in_c"].rearrange("k p n -> p k n")
                for c0 in range(0, 3088, 772):
                    cx.dma("pool", w[:, :, c0:c0 + 772], wv[:, :, c0:c0 + 772], writes=[w.d])
                wout = sb("w_out_c", [P, KD, D], BF16, ph)
                cx.dma("pool", wout[:], I["w_out_c"].rearrange("k p n -> p k n"), writes=[wout.d])
                wgu = sb("wgu", [16, 512], BF16, ph)
                cx.dma("pool", wgu[:], I["gla_wgu"], writes=[wgu.d])
                gsm = sb("gsm", [P, 6], F32, ph)
                cx.dma("sp", gsm[:], I["gla_small"], writes=[gsm.d])
                gnb = sb("gnb", [P, 2], F32, ph)
                cx.op("dve", lambda e: e.tensor_copy(gnb[:], gsm[:, 4:6]), reads=[gsm.d], writes=[gnb.d])
                nbg = sb("nbg", [P, 4], F32, ph)
                cx.op("dve", lambda e: e.tensor_scalar(nbg[:], gsm[:, 0:4], -1.0, None, ALU.mult), reads=[gsm.d], writes=[nbg.d])
                scr = {"sq16": sb("a_sq16", [P, KD, NT], BF16, ph), "rstd": sb("a_rstd", [P, NT], F32, ph), "eps": epsb}
                h = sb("a_h", [P, KD, NT], BF16, ph)
                q32 = sb("a_q32", [P, 4, NT], F32, ph)
                k32 = sb("a_k32", [P, 4, NT], F32, ph)
                lg32 = sb("a_lg32", [P, 4, NT], F32, ph)
                sg32 = lg32
                gate32 = sb("a_gate16", [P, 8, NT], BF16, ph)
                gd16 = sb("a_gd16", [16, NT], BF16, ph)
                CK1 = 128
                v_tm = sb("a_vtm", [CK1, NT // CK1, 1024], BF16, ph)
                cm2 = sb("a_cm2", [P, 384], F32, ph)
                cx.dma("sp", cm2[:], I["cmask2"], writes=[cm2.d])
                mask128 = sb("a_mask128", [P, 128], BF16, ph)
                cx.op("dve", lambda e: e.tensor_copy(mask128[:], cm2[:, 0:128]), reads=[cm2.d], writes=[mask128.d])
                mres128 = View(cm2[:, 128:384])
                mres128.d = cm2.d
                on16 = sb("a_on16", [P, 8, NT], BF16, ph)
                g = gla_alloc(ph, 2, NT, CK=CK1)
                ev = [0]

                def evac(fn_act, fn_dve, reads, writes):
                    ev[0] += 1
                    if ev[0] % 2:
                        cx.op("act", fn_act, reads=reads, writes=writes)
                    else:
                        cx.op("dve", fn_dve, reads=reads, writes=writes)

                hb2 = [h, sb("a_h2", [P, KD, NT], BF16, ph)]
                tl = tiles_of(LP, NT)

                def proj(h_, c0, n, M=128):
                    pb = bank()
                    for k in range(KD):
                        cx.op("pe", lambda e, k=k, pb=pb: e.matmul(pb[:M, :n], w[:, k, c0:c0 + M], h_[:, k, :n], start=(k == 0), stop=(k == KD - 1)),
                              reads=[w.d, h_.d], writes=[pb.d], inc=(k == KD - 1))
                    return pb

                def stA(ti):
                    s0, n, smp = tl[ti]
                    h_ = hb2[ti % 2]
                    rmsnorm(lambda k: x[:, k, s0:s0 + n], xdeps(s0, n), n, 1, lambda k: h_[:, k, :n], [h_.d], scr, src_all=x[:, :, s0:s0 + n])

                def stB(ti):
                    s0, n, smp = tl[ti]
                    h_ = hb2[ti % 2]
                    for hd in range(4):
                        pb = proj(h_, 128 * hd, n)
                        cx.op("act", lambda e, hd=hd, pb=pb: e.mul(q32[:, hd, :n], pb[:, :n], 128.0 ** -0.5), reads=[pb.d], writes=[q32.d])
                        pb = proj(h_, 512 + 128 * hd, n)
                        cx.op("dve", lambda e, hd=hd, pb=pb: e.tensor_copy(k32[:, hd, :n], pb[:, :n]), reads=[pb.d], writes=[k32.d])
                    pb = proj(h_, 3072, n, M=16)
                    cx.op("dve", lambda e, pb=pb: e.tensor_copy(gd16[:, :n], pb[:16, :n]), reads=[pb.d], writes=[gd16.d])
                    for hd in range(4):
                        pb = bank()
                        cx.op("pe", lambda e, hd=hd, pb=pb: e.matmul(pb[:, :n], wgu[:, 128 * hd:128 * hd + 128], gd16[:, :n], start=True, stop=True),
                              reads=[wgu.d, gd16.d], writes=[pb.d])
                        cx.op("act", lambda e, hd=hd, pb=pb: e.activation(sg32[:, hd, :n], pb[:, :n], AF.Exp, bias=nbg[:, hd:hd + 1], scale=-1.0),
                              reads=[pb.d, nbg.d], writes=[sg32.d])
                        cx.op("act", lambda e, hd=hd: e.activation(lg32[:, hd, :n], sg32[:, hd, :n], AF.Ln, bias=1.0), reads=[sg32.d], writes=[lg32.d])

                def stC(ti):
                    s0, n, smp = tl[ti]
                    gla_prep(g, n, smp, -1.0 / 16.0, q32, k32, lg32, CK=CK1, mres_p=mres128)

                def stDr(ti):
                    s0, n, smp = tl[ti]
                    h_ = hb2[ti % 2]
                    for c in range(8):
                        pb = proj(h_, 2048 + 128 * c, n)
                        cx.op("act", lambda e, c=c, pb=pb: e.activation(gate32[:, c, :n], pb[:, :n], AF.Silu), reads=[pb.d], writes=[gate32.d])

                def stDv(ti):
                    s0, n, smp = tl[ti]
                    h_ = hb2[ti % 2]
                    CT = n if smp else CK1
                    for ci in range(n // CT):
                        for cb_ in range(2):
                            pb = bank()
                            for k in range(KD):
                                cx.op("pe", lambda e, k=k, pb=pb, ci=ci, cb_=cb_: e.matmul(
                                    pb[:CT, :512], h_[:, k, ci * CT:ci * CT + CT], w[:, k, 1024 + 512 * cb_:1024 + 512 * cb_ + 512],
                                    start=(k == 0), stop=(k == KD - 1)),
                                    reads=[w.d, h_.d], writes=[pb.d], inc=(k == KD - 1))
                            cx.op("dve", lambda e, pb=pb, ci=ci, cb_=cb_: e.tensor_copy(v_tm[:CT, ci, 512 * cb_:512 * cb_ + 512], pb[:CT, :512]),
                                  reads=[pb.d], writes=[v_tm.d])

                pbos = {}

                def stE(ti):
                    s0, n, smp = tl[ti]
                    pbos[ti] = gla_chunks(g, n, smp, 2, v_tm, I["st_gla"], O["gla_s"], CK=CK1, maskCK=mask128)
                    if ti == len(tl) - 2:
                        for hd in range(4):
                            cx.dma("sp", O["gla_p"][hd], g["S32a"][:, hd, :], reads=[g["S32a"].d], writes=[outdep])

                def stF(ti):
                    s0, n, smp = tl[ti]
                    gla_post(g, n, 2, pbos.pop(ti), gate32, gnb, on16, 0)

                def stG(ti):
                    s0, n, smp = tl[ti]
                    for oc in range(KD):
                        pb = bank()
                        for k in range(KD):
                            cx.op("pe", lambda e, k=k, oc=oc, pb=pb: e.matmul(pb[:, :n], wout[:, k, 128 * oc:128 * oc + 128], on16[:, k, :n],
                                                                            start=(k == 0), stop=(k == KD - 1)),
                                  reads=[wout.d, on16.d], writes=[pb.d], inc=(k == KD - 1))
                        cx.op("dve", lambda e, pb=pb, oc=oc: e.tensor_tensor(x[:, oc, s0:s0 + n], x[:, oc, s0:s0 + n], pb[:, :n], ALU.add),
                              reads=[pb.d] + xdeps(s0, n), writes=xdeps(s0, n))

                NTL = len(tl)
                stA(0); stB(0); stC(0); stDv(0); stDr(0)
                if NTL > 1:
                    stA(1)
                stE(0)
                for ti in range(1, NTL):
                    stB(ti); stDv(ti); stF(ti - 1); stC(ti); stG(ti - 1); stDr(ti)
                    if ti + 1 < NTL:
                        stA(ti + 1)
                    if tl[ti][2]:
                        cx.barrier()
                        g["s32r"] = [View(q32[:, j, :]) for j in range(4)]
                        g["so"] = [View(k32[:, j, :]) for j in range(4)]
                        g["s16r"] = [View(hb2[0][:, j, :]) for j in range(4)]
                    stE(ti)
                stF(NTL - 1); stG(NTL - 1)
                cx.barrier()


        TWO_PI = 2.0 * math.pi

        def a0_phase(NTA, NTB):
            with contextlib.ExitStack() as ph0:
                hall = sb("hall0", [P, KD, T], BF16, ph0)
                hdp = {s0: Dep() for s0 in range(0, T, 128)}

                def hdeps(s0, n):
                    return [hdp[k] for k in range(s0, s0 + n, 128)]
                with contextlib.ExitStack() as ph:
                    scr = {"sq16": sb("a0_sq16", [P, KD, 256], BF16, ph), "rstd": sb("a0_rstd", [P, 256], F32, ph), "eps": epsb}
                    for (s0, n, smp) in tiles_of(LP, 256):
                        rmsnorm(lambda k: x[:, k, s0:s0 + n], xdeps(s0, n), n, 0, lambda k: hall[:, k, s0:s0 + n], hdeps(s0, n), scr, src_all=x[:, :, s0:s0 + n])
                    cx.barrier()
                wv = I["w_in_ab"].rearrange("k p n -> p k n")
                wov = I["w_out_ab"].rearrange("k p n -> p k n")
                if "A0a" in cfg.phases or "A0" in cfg.phases:
                  with contextlib.ExitStack() as ph:
                    NT = NTA
                    w = sb("w_in_a", [P, KD, 2048], BF16, ph)
                    for c0 in range(0, 2048, 512):
                        cx.dma("pool", w[:, :, c0:c0 + 512], wv[:, :, c0:c0 + 512], writes=[w.d])
                    wout = sb("w_out_a", [P, 4, D], BF16, ph)
                    cx.dma("pool", wout[:], wov[:, 0:4, :], writes=[wout.d])
                    hsm = sb("hsm", [P, 13], F32, ph)
                    cx.dma("sp", hsm[:], I["hg_small"], writes=[hsm.d])
                    lbe = sb("lbe", [P, 4, 3], F32, ph)
                    lbs = sb("lbs", [P, 4], F32, ph)
                    lb = sb("lb", [P, 4], F32, ph)
                    oml = sb("oml", [P, 4], F32, ph)
                    gnb = sb("gnb0", [P, 1], F32, ph)
                    cx.op("act", lambda e: e.activation(lbe[:], hsm[:, 0:12].rearrange("p (c j) -> p c j", j=3), AF.Exp), reads=[hsm.d], writes=[lbe.d])
                    cx.op("dve", lambda e: e.tensor_tensor(lbs[:], lbe[:, :, 0], lbe[:, :, 1], ALU.add), reads=[lbe.d], writes=[lbs.d])
                    cx.op("dve", lambda e: e.tensor_tensor(lbs[:], lbs[:], lbe[:, :, 2], ALU.add), reads=[lbe.d, lbs.d], writes=[lbs.d])
                    cx.op("dve", lambda e: e.reciprocal(lbs[:], lbs[:]), reads=[lbs.d], writes=[lbs.d])
                    cx.op("dve", lambda e: e.tensor_tensor(lb[:], lbe[:, :, 0], lbs[:], ALU.mult), reads=[lbe.d, lbs.d], writes=[lb.d])
                    cx.op("dve", lambda e: e.tensor_scalar(oml[:], lb[:], -1.0, 1.0, ALU.mult, ALU.add), reads=[lb.d], writes=[oml.d])
                    cx.op("dve", lambda e: e.tensor_copy(gnb[:], hsm[:, 12:13]), reads=[hsm.d], writes=[gnb.d])
                    q32 = sb("h_q32", [P, 4, NT], F32, ph)
                    k32 = sb("h_k32", [P, 4, NT], F32, ph)
                    fg32 = sb("h_fg32", [P, 4, NT], F32, ph)
                    lg32 = sb("h_lg32", [P, 4, NT], F32, ph)
                    gate32 = sb("h_gate32", [P, 4, NT], F32, ph)
                    v_tm = sb("h_vtm", [64, max(NT // GC, 1), 512], BF16, ph)
                    on16 = sb("h_on16", [P, 4, NT], BF16, ph)
                    g = gla_alloc(ph, 1, NT)

                    def proj(c0, s0, n):
                        pb = bank()
                        for k in range(KD):
                            cx.op("pe", lambda e, k=k, pb=pb: e.matmul(pb[:, :n], w[:, k, c0:c0 + 128], hall[:, k, s0:s0 + n], start=(k == 0), stop=(k == KD - 1)),
                                  reads=[w.d] + hdeps(s0, n), writes=[pb.d], inc=(k == KD - 1))
                        return pb

                    tl = tiles_of(LP, NT)

                    def stB(ti):
                        s0, n, smp = tl[ti]
                        for hd in range(4):
                            pb = proj(128 * hd, s0, n)
                            cx.op("act", lambda e, hd=hd, pb=pb: e.activation(q32[:, hd, :n], pb[:, :n], AF.Silu), reads=[pb.d], writes=[q32.d])
                        for hd in range(4):
                            pb = proj(512 + 128 * hd, s0, n)
                            cx.op("act", lambda e, hd=hd, pb=pb: e.activation(fg32[:, hd, :n], pb[:, :n], AF.Sigmoid), reads=[pb.d], writes=[fg32.d])
                            cx.op("dve", lambda e, hd=hd: e.tensor_scalar(fg32[:, hd, :n], fg32[:, hd, :n], oml[:, hd:hd + 1], lb[:, hd:hd + 1], ALU.mult, ALU.add),
                                  reads=[fg32.d, oml.d, lb.d], writes=[fg32.d])
                        for hd in range(4):
                            cx.op("act", lambda e, hd=hd: e.activation(lg32[:, hd, :n], fg32[:, hd, :n], AF.Ln), reads=[fg32.d], writes=[lg32.d])
                            cx.op("pool", lambda e, hd=hd: e.tensor_scalar(k32[:, hd, :n], fg32[:, hd, :n], -1.0, 1.0, ALU.mult, ALU.add),
                                  reads=[fg32.d], writes=[k32.d])

                    def stC(ti):
                        s0, n, smp = tl[ti]
                        gla_prep(g, n, smp, 1.0, q32, k32, lg32)

                    def stDr(ti):
                        s0, n, smp = tl[ti]
                        for hd in range(4):
                            pb = proj(1536 + 128 * hd, s0, n)
                            cx.op("act", lambda e, hd=hd, pb=pb: e.activation(gate32[:, hd, :n], pb[:, :n], AF.Silu), reads=[pb.d], writes=[gate32.d])

                    def stDv(ti):
                        s0, n, smp = tl[ti]
                        CT = n if smp else GC
                        for ci in range(n // CT):
                            pb = bank()
                            for k in range(KD):
                                cx.op("pe", lambda e, k=k, pb=pb, ci=ci: e.matmul(
                                    pb[:CT, :512], hall[:, k, s0 + ci * CT:s0 + ci * CT + CT], w[:, k, 1024:1536], start=(k == 0), stop=(k == KD - 1)),
                                    reads=[w.d] + hdeps(s0, n), writes=[pb.d], inc=(k == KD - 1))
                            cx.op("dve", lambda e, pb=pb, ci=ci: e.tensor_copy(v_tm[:CT, ci, :], pb[:CT, :512]), reads=[pb.d], writes=[v_tm.d])

                    pbos = {}

                    def stE(ti):
                        s0, n, smp = tl[ti]
                        pbos[ti] = gla_chunks(g, n, smp, 1, v_tm, I["st_hgrn"], O["hgrn_s"])
                        if ti == len(tl) - 2:
                            for hd in range(4):
                                cx.dma("sp", O["hgrn_p"][hd], g["S32a"][:, hd, :], reads=[g["S32a"].d], writes=[outdep])

                    def stF(ti):
                        s0, n, smp = tl[ti]
                        gla_post(g, n, 1, pbos.pop(ti), gate32, gnb, on16, 0)

                    def stG(ti):
                        s0, n, smp = tl[ti]
                        for oc in range(KD):
                            pb = bank()
                            for k in range(4):
                                cx.op("pe", lambda e, k=k, oc=oc, pb=pb: e.matmul(pb[:, :n], wout[:, k, 128 * oc:128 * oc + 128], on16[:, k, :n],
                                                                                start=(k == 0), stop=(k == 3)),
                                      reads=[wout.d, on16.d], writes=[pb.d], inc=(k == 3))
                            cx.op("dve", lambda e, pb=pb, oc=oc: e.tensor_tensor(x[:, oc, s0:s0 + n], x[:, oc, s0:s0 + n], pb[:, :n], ALU.add),
                                  reads=[pb.d] + xdeps(s0, n), writes=xdeps(s0, n))

                    stB(0); stC(0); stDv(0); stDr(0); stE(0)
                    for ti in range(1, len(tl)):
                        stB(ti); stDv(ti); stF(ti - 1); stC(ti); stG(ti - 1); stDr(ti); stE(ti)
                    stF(len(tl) - 1); stG(len(tl) - 1)
                    cx.barrier()
                if "A0b" in cfg.phases or "A0" in cfg.phases:
                  with contextlib.ExitStack() as ph:
                    NT = NTB
                    w = sb("w_in_u", [P, KD, 512], BF16, ph)
                    cx.dma("pool", w[:], wv[:, :, 2048:2560], writes=[w.d])
                    wout = sb("w_out_b", [P, 4, D], BF16, ph)
                    cx.dma("pool", wout[:], wov[:, 4:8, :], writes=[wout.d])
                    wglu = sb("wglu", [P, 4, 512], BF16, ph)
                    cx.dma("pool", wglu[:], I["s5_wglu"].rearrange("k p n -> p k n"), writes=[wglu.d])
                    ssm = sb("ssm", [P, 8], F32, ph)
                    cx.dma("sp", ssm[:], I["s5_small"], writes=[ssm.d])
                    x0 = sb("s5x0", [P, 2, 16, NSEQ_S], F32, ph)
                    cx.dma("sp", x0[:], I["s5_x0"].rearrange("r p s b -> p r s b"), writes=[x0.d])
                    Ere = sb("Ere", [P, 16, NT], F32, ph)
                    Eim = sb("Eim", [P, 16, NT], F32, ph)
                    rho = sb("rho", [P, 16], F32, ph)
                    rhoms = sb("rhoms", [P, 16, NS], F32, ph)
                    BB = [sb(f"BB{r}", [P, 4, 512], BF16, ph) for r in range(2)]
                    CC = [sb(f"CC{r}", [P, 16, 128], BF16, ph) for r in range(2)]
                    xst = sb("xst", [P, 2, 16], F32, ph)
                    xso = sb("xso", [P, 2, 16, NSEQ_S], F32, ph)
                    cx.op("pool", lambda e: e.memset(xst[:], 0.0), writes=[xst.d])
                    ones32 = sb("ones32", [P, P], F32, ph)
                    cx.op("pool", lambda e: e.memset(ones32[:], 1.0), writes=[ones32.d])
                    with contextlib.ExitStack() as ps_:
                        R_S5_MAX_RE = -1e-4

                        def sincos(th, F, tag):
                            outs = []
                            for off, nm_ in ((0.0, "sin"), (0.5 * math.pi, "cos")):
                                a = sb(f"sc_a_{tag}{nm_}", [P, F], F32, ps_)
                                ki = sb(f"sc_k_{tag}{nm_}", [P, F], mybir.dt.int32, ps_)
                                kf = sb(f"sc_kf_{tag}{nm_}", [P, F], F32, ps_)
                                cx.op("dve", lambda e, a=a, off=off: e.tensor_scalar(a[:], th[:], 1.0, off, ALU.mult, ALU.add), reads=[th.d], writes=[a.d])
                                cx.op("dve", lambda e, a=a, kf=kf: e.tensor_scalar(kf[:], a[:], 1.0 / TWO_PI, None, ALU.mult), reads=[a.d], writes=[kf.d])
                                cx.op("dve", lambda e, ki=ki, kf=kf: e.tensor_copy(ki[:], kf[:]), reads=[kf.d], writes=[ki.d])
                                cx.op("dve", lambda e, ki=ki, kf=kf: e.tensor_copy(kf[:], ki[:]), reads=[ki.d], writes=[kf.d])
                                cx.op("dve", lambda e, a=a, kf=kf: e.scalar_tensor_tensor(a[:], kf[:], -TWO_PI, a[:], ALU.mult, ALU.add), reads=[a.d, kf.d], writes=[a.d])
                                cx.op("dve", lambda e, a=a, kf=kf: e.tensor_single_scalar(kf[:], a[:], math.pi, ALU.is_gt), reads=[a.d], writes=[kf.d])
                                cx.op("dve", lambda e, a=a, kf=kf: e.scalar_tensor_tensor(a[:], kf[:], -TWO_PI, a[:], ALU.mult, ALU.add), reads=[a.d, kf.d], writes=[a.d])
                                cx.op("dve", lambda e, a=a, kf=kf: e.tensor_single_scalar(kf[:], a[:], -math.pi, ALU.is_lt), reads=[a.d], writes=[kf.d])
                                cx.op("dve", lambda e, a=a, kf=kf: e.scalar_tensor_tensor(a[:], kf[:], TWO_PI, a[:], ALU.mult, ALU.add), reads=[a.d, kf.d], writes=[a.d])
                                cx.op("dve", lambda e, a=a: e.tensor_scalar(a[:], a[:], 3.14159, -3.14159, ALU.min, ALU.max), reads=[a.d], writes=[a.d])
                                cx.op("act", lambda e, a=a: e.activation(a[:], a[:], AF.Sin), reads=[a.d], writes=[a.d])
                                outs.append(a)
                            return outs

                        sm = sb("s5sm", [P, 3, 16], F32, ps_)
                        cx.dma("sp", sm[:], I["s5_sm"], writes=[sm.d])
                        F_ = 16
                        lr = sb("d_lr", [P, F_], F32, ps_)
                        dt = sb("d_dt", [P, F_], F32, ps_)
                        mag = sb("d_mag", [P, F_], F32, ps_)
                        th = sb("d_th", [P, F_], F32, ps_)
                        li_ = sm[:, 1, :]
                        cx.op("dve", lambda e: e.tensor_scalar_min(lr[:], sm[:, 0, :], R_S5_MAX_RE), reads=[sm.d], writes=[lr.d])
                        cx.op("act", lambda e: e.activation(dt[:], sm[:, 2, :], AF.Exp), reads=[sm.d], writes=[dt.d])
                        cx.op("dve", lambda e: e.tensor_tensor(mag[:], lr[:], dt[:], ALU.mult), reads=[lr.d, dt.d], writes=[mag.d])
                        cx.op("act", lambda e: e.activation(mag[:], mag[:], AF.Exp), reads=[mag.d], writes=[mag.d])
                        cx.op("dve", lambda e: e.tensor_tensor(th[:], li_, dt[:], ALU.mult), reads=[dt.d, sm.d], writes=[th.d])
                        sn, cs_ = sincos(th, F_, "sm")
                        cx.op("dve", lambda e: e.tensor_copy(rho[:], mag[:]), reads=[mag.d], writes=[rho.d])
                        cx.op("dve", lambda e: e.tensor_tensor(rhoms[:], rho[:, :].unsqueeze(2).broadcast_to([P, 16, NS]),
                                                              cm[:, 400:464].unsqueeze(1).broadcast_to([P, 16, NS]), ALU.mult),
                              reads=[rho.d, cm.d], writes=[rhoms.d])
                        Ed = Dep()
                        cx.op("dve", lambda e: e.tensor_copy(Ere[:, :, 0], cs_[:]), reads=[cs_.d], writes=[Ed])
                        cx.op("dve", lambda e: e.tensor_scalar(Eim[:, :, 0], sn[:], -1.0, None, ALU.mult), reads=[sn.d], writes=[Ed])
                        t1 = sb("e_t1", [P, 16, NT // 2], F32, ps_)
                        t2 = sb("e_t2", [P, 16, NT // 2], F32, ps_)
                        m_ = 1
                        while m_ < NT:
                            br = Ere[:, :, m_ - 1:m_].broadcast_to([P, 16, m_])
                            bi = Eim[:, :, m_ - 1:m_].broadcast_to([P, 16, m_])
                            ar, ai = Ere[:, :, 0:m_], Eim[:, :, 0:m_]
                            cx.op("dve", lambda e, ar=ar, br=br, m_=m_: e.tensor_tensor(t1[:, :, :m_], ar, br, ALU.mult), reads=[Ed], writes=[t1.d])
                            cx.op("pool", lambda e, ai=ai, bi=bi, m_=m_: e.tensor_tensor(t2[:, :, :m_], ai, bi, ALU.mult), reads=[Ed], writes=[t2.d])
                            cx.op("dve", lambda e, m_=m_: e.tensor_tensor(Ere[:, :, m_:2 * m_], t1[:, :, :m_], t2[:, :, :m_], ALU.subtract), reads=[t1.d, t2.d], writes=[Ed])
                            cx.op("dve", lambda e, ar=ar, bi=bi, m_=m_: e.tensor_tensor(t1[:, :, :m_], ar, bi, ALU.mult), reads=[Ed], writes=[t1.d])
                            cx.op("pool", lambda e, ai=ai, br=br, m_=m_: e.tensor_tensor(t2[:, :, :m_], ai, br, ALU.mult), reads=[Ed], writes=[t2.d])
                            cx.op("dve", lambda e, m_=m_: e.tensor_tensor(Eim[:, :, m_:2 * m_], t1[:, :, :m_], t2[:, :, :m_], ALU.add), reads=[t1.d, t2.d], writes=[Ed])
                            m_ *= 2
                        Ere.d = Ed
                        Eim.d = Ed
                        are = sb("r_are", [P, F_], F32, ps_)
                        aim = sb("r_aim", [P, F_], F32, ps_)
                        den = sb("r_den", [P, F_], F32, ps_)
                        tz = sb("r_tz", [P, F_], F32, ps_)
                        zz = [sb("r_zre", [P, F_], F32, ps_), sb("r_zim", [P, F_], F32, ps_)]
                        zre, zim = zz
                        cx.op("dve", lambda e: e.tensor_tensor(are[:], mag[:], cs_[:], ALU.mult), reads=[mag.d, cs_.d], writes=[are.d])
                        cx.op("dve", lambda e: e.tensor_scalar_add(are[:], are[:], -1.0), reads=[are.d], writes=[are.d])
                        cx.op("dve", lambda e: e.tensor_tensor(aim[:], mag[:], sn[:], ALU.mult), reads=[mag.d, sn.d], writes=[aim.d])
                        cx.op("dve", lambda e: e.tensor_tensor(den[:], lr[:], lr[:], ALU.mult), reads=[lr.d], writes=[den.d])
                        cx.op("dve", lambda e: e.tensor_tensor(tz[:], li_, li_, ALU.mult), reads=[sm.d], writes=[tz.d])
                        cx.op("dve", lambda e: e.tensor_tensor(den[:], den[:], tz[:], ALU.add), reads=[den.d, tz.d], writes=[den.d])
                        cx.op("dve", lambda e: e.reciprocal(den[:], den[:]), reads=[den.d], writes=[den.d])
                        cx.op("dve", lambda e: e.tensor_tensor(zre[:], are[:], lr[:], ALU.mult), reads=[are.d, lr.d], writes=[zre.d])
                        cx.op("dve", lambda e: e.tensor_tensor(tz[:], aim[:], li_, ALU.mult), reads=[aim.d, sm.d, tz.d], writes=[tz.d])
                        cx.op("dve", lambda e: e.tensor_tensor(zre[:], zre[:], tz[:], ALU.add), reads=[zre.d, tz.d], writes=[zre.d])
                        cx.op("dve", lambda e: e.tensor_tensor(zre[:], zre[:], den[:], ALU.mult), reads=[zre.d, den.d], writes=[zre.d])
                        cx.op("dve", lambda e: e.tensor_tensor(zim[:], aim[:], lr[:], ALU.mult), reads=[aim.d, lr.d], writes=[zim.d])
                        cx.op("dve", lambda e: e.tensor_tensor(tz[:], are[:], li_, ALU.mult), reads=[are.d, sm.d, tz.d], writes=[tz.d])
                        cx.op("dve", lambda e: e.tensor_tensor(zim[:], zim[:], tz[:], ALU.subtract), reads=[zim.d, tz.d], writes=[zim.d])
                        cx.op("dve", lambda e: e.tensor_tensor(zim[:], zim[:], den[:], ALU.mult), reads=[zim.d, den.d], writes=[zim.d])
                        bexp = [sb(f"bexp{r}", [P, 4, 512], F32, ps_) for r in range(2)]
                        for r in range(2):
                            cx.dma("sp", bexp[r][:], I["s5_bexp"][r], writes=[bexp[r].d])
                        dg = [sb(f"dg{i}", [P, P], F32, ps_) for i in range(4)]
                        ta = sb("r_ta", [P, 512], F32, ps_)
                        tb = sb("r_tb", [P, 512], F32, ps_)
                        di = 0
                        for c in range(4):
                            pz = [bank(), bank()]
                            for r in range(2):
                                for m in range(4):
                                    dgt = dg[di % 4]
                                    di += 1
                                    cx.op("dve", lambda e, dgt=dgt, r=r, c=c, m=m: e.tensor_scalar(dgt[:], ident[:], zz[r][:, 4 * c + m:4 * c + m + 1], None, ALU.mult),
                                          reads=[ident.d, zz[r].d], writes=[dgt.d])
                                    cx.op("pe", lambda e, dgt=dgt, r=r, m=m, pz=pz: e.matmul(pz[r][:, 128 * m:128 * m + 128], ones32[:], dgt[:], start=True, stop=True),
                                          reads=[ones32.d, dgt.d], writes=[pz[r].d])
                            cx.op("dve", lambda e, c=c, pz=pz: e.tensor_tensor(ta[:], pz[0][:, :], bexp[0][:, c, :], ALU.mult), reads=[pz[0].d, bexp[0].d], writes=[ta.d])
                            cx.op("dve", lambda e, c=c, pz=pz: e.tensor_tensor(tb[:], pz[1][:, :], bexp[1][:, c, :], ALU.mult), reads=[pz[1].d, bexp[1].d], writes=[tb.d])
                            cx.op("dve", lambda e, c=c: e.tensor_tensor(BB[0][:, c, :], ta[:], tb[:], ALU.subtract), reads=[ta.d, tb.d], writes=[BB[0].d])
                            cx.op("dve", lambda e, c=c, pz=pz: e.tensor_tensor(ta[:], pz[0][:, :], bexp[1][:, c, :], ALU.mult), reads=[pz[0].d, bexp[1].d, ta.d], writes=[ta.d])
                            cx.op("dve", lambda e, c=c, pz=pz: e.tensor_tensor(tb[:], pz[1][:, :], bexp[0][:, c, :], ALU.mult), reads=[pz[1].d, bexp[0].d, tb.d], writes=[tb.d])
                            cx.op("dve", lambda e, c=c: e.tensor_tensor(BB[1][:, c, :], ta[:], tb[:], ALU.add), reads=[ta.d, tb.d], writes=[BB[1].d])
                        cexp = sb("cexp", [P, 16, 128], F32, ps_)
                        cx.dma("sp", cexp[:], I["s5_cexp"][0], writes=[cexp.d])
                        cx.op("act", lambda e: e.copy(CC[0][:], cexp[:]), reads=[cexp.d], writes=[CC[0].d])
                        cexp2 = cexp
                        cx.dma("sp", cexp2[:], I["s5_cexp"][1], writes=[cexp2.d])
                        cx.op("act", lambda e: e.mul(CC[1][:], cexp2[:], -1.0), reads=[cexp2.d], writes=[CC[1].d])
                        cx.barrier()
                    RG = 3
                    u32 = [sb(f"s_u32{i}", [P, 4, NT], F32, ph) for i in range(2)]
                    u16 = [sb(f"s_u16{i}", [P, 4, NT], BF16, ph) for i in range(2)]
                    tt = [[sb(f"s_tt{i}_{j}", [P, 2, NT], F32, ph) for j in range(4)] for i in range(1)] * 2
                    wre = [sb(f"s_wre{i}", [P, 2, NT], F32, ph) for i in range(1)] * 2
                    wim = [sb(f"s_wim{i}", [P, 2, NT], F32, ph) for i in range(1)] * 2
                    zre_ = [sb(f"s_zre{i}", [P, 2, NT], F32, ph) for i in range(RG)]
                    zim_ = [sb(f"s_zim{i}", [P, 2, NT], F32, ph) for i in range(RG)]
                    pp = [[sb(f"s_pp{i}_{j}", [P, 2, NT], F32, ph) for j in range(4)] for i in range(2)]
                    xre32 = [sb(f"s_xre{i}", [P, 2, NT], F32, ph) for i in range(2)]
                    xim32 = [sb(f"s_xim{i}", [P, 2, NT], F32, ph) for i in range(2)]
                    xre16 = [sb(f"s_xre16{i}", [P, 4, NT], BF16, ph) for i in range(2)]
                    xim16 = [sb(f"s_xim16{i}", [P, 4, NT], BF16, ph) for i in range(2)]
                    y32 = sb("s_y32", [P, 4, NT], F32, ph)
                    gt = sb("s_gt", [P, 4, NT], F32, ph)
                    yg32 = sb("s_yg32", [P, 4, NT], F32, ph)
                    yg16 = sb("s_yg16", [P, 4, NT], BF16, ph)
                    sgl = sb("s_sgl", [P, NT], F32, ph)
                    on16 = sb("s_on16", [P, 4, NT], BF16, ph)
                    tmpx = sb("s_tmpx", [P, NSEQ_S], F32, ph)
                    tl = tiles_of(LP, NT)

                    def uproj(ti):
                        s0, n, smp = tl[ti]
                        for c in range(4):
                            pb = bank()
                            for k in range(KD):
                                cx.op("pe", lambda e, k=k, pb=pb, c=c: e.matmul(pb[:, :n], w[:, k, 128 * c:128 * c + 128], hall[:, k, s0:s0 + n],
                                                                               start=(k == 0), stop=(k == KD - 1)),
                                      reads=[w.d] + hdeps(s0, n), writes=[pb.d], inc=(k == KD - 1))
                            cx.op("act", lambda e, pb=pb, c=c: e.copy(u32[ti % 2][:, c, :n], pb[:, :n]), reads=[pb.d], writes=[u32[ti % 2].d])
                            cx.op("act", lambda e, pb=pb, c=c: e.copy(u16[ti % 2][:, c, :n], pb[:, :n]), reads=[pb.d], writes=[u16[ti % 2].d])

                    def geom(ti, s_):
                        s0, n, smp = tl[ti]
                        nb, L = (NSEQ_S, LS) if smp else (1, n)
                        if smp:
                            er = Ere[:, s_:s_ + 2, 0:L].unsqueeze(2).broadcast_to([P, 2, nb, L])
                            ei = Eim[:, s_:s_ + 2, 0:L].unsqueeze(2).broadcast_to([P, 2, nb, L])
                            v3 = lambda ap: ap.rearrange("p c (b j) -> p c b j", b=nb)
                        else:
                            er = Ere[:, s_:s_ + 2, 0:n]
                            ei = Eim[:, s_:s_ + 2, 0:n]
                            v3 = lambda ap: ap
                        return s0, n, smp, nb, L, er, ei, v3

                    def stageA(it, ti, s_):
                        s0, n, smp, nb, L, er, ei, v3 = geom(ti, s_)
                        pbk = bank()
                        ut = u16[ti % 2]
                        for ri in range(2):
                            for q in range(2):
                                s2 = s_ + q
                                c, m = s2 // 4, s2 % 4
                                col = (2 * ri + q) * NT
                                cx.op("pe", lambda e, ri=ri, c=c, m=m, col=col: e.matmul(pbk[:, col:col + n], BB[ri][:, c, 128 * m:128 * m + 128], ut[:, c, :n],
                                                                                        start=True, stop=True),
                                      reads=[BB[ri].d, ut.d], writes=[pbk.d], inc=(ri == 1 and q == 1))
                        pk = pbk[:, :].rearrange("p (r q t) -> p r q t", r=2, q=2)
                        pbr, pbi = pk[:, 0, :, :n], pk[:, 1, :, :n]
                        t_ = tt[it % 2]
                        wr_, wi_ = wre[it % 2], wim[it % 2]
                        cx.op("dve", lambda e: e.tensor_tensor(v3(t_[0][:, :, :n]), er, v3(pbr), ALU.mult), reads=[Ere.d, pbk.d], writes=[t_[0].d])
                        cx.op("dve", lambda e: e.tensor_tensor(v3(t_[1][:, :, :n]), ei, v3(pbi), ALU.mult), reads=[Eim.d, pbk.d], writes=[t_[1].d])
                        cx.op("dve", lambda e: e.tensor_tensor(v3(t_[2][:, :, :n]), er, v3(pbi), ALU.mult), reads=[Ere.d, pbk.d], writes=[t_[2].d])
                        cx.op("dve", lambda e: e.tensor_tensor(v3(t_[3][:, :, :n]), ei, v3(pbr), ALU.mult), reads=[Eim.d, pbk.d], writes=[t_[3].d])

                    def stageA2(it, ti, s_):
                        s0, n, smp, nb, L, er, ei, v3 = geom(ti, s_)
                        t_ = tt[it % 2]
                        wr_, wi_ = wre[it % 2], wim[it % 2]
                        cx.op("dve", lambda e: e.tensor_tensor(wr_[:, :, :n], t_[0][:, :, :n], t_[1][:, :, :n], ALU.subtract), reads=[t_[0].d, t_[1].d], writes=[wr_.d])
                        cx.op("dve", lambda e: e.tensor_tensor(wi_[:, :, :n], t_[2][:, :, :n], t_[3][:, :, :n], ALU.add), reads=[t_[2].d, t_[3].d], writes=[wi_.d])

                    def stageA3(it, ti, s_):
                        s0, n, smp, nb, L, er, ei, v3 = geom(ti, s_)
                        wr_, wi_ = wre[it % 2], wim[it % 2]
                        zr_, zi_ = zre_[it % RG], zim_[it % RG]
                        for q in range(2):
                            s2 = s_ + q
                            if smp:
                                for ri, wt in ((0, wr_), (1, wi_)):
                                    cx.op("dve", lambda e, ri=ri, s2=s2: e.tensor_scalar(tmpx[:], x0[:, ri, s2, :], rho[:, s2:s2 + 1], None, ALU.mult),
                                          reads=[x0.d, rho.d], writes=[tmpx.d])
                                    w3 = wt[:, q, :n].rearrange("p (b j) -> p b j", b=nb)
                                    cx.op("dve", lambda e, w3=w3: e.tensor_tensor(w3[:, :, 0], w3[:, :, 0], tmpx[:], ALU.add), reads=[wt.d, tmpx.d], writes=[wt.d])
                                d0 = rhoms[:, s2, :]
                                cx.op("dve", lambda e, q=q, d0=d0: e.tensor_tensor_scan(zr_[:, q, :n], d0, wr_[:, q, :n], 0.0, ALU.mult, ALU.add),
                                      reads=[rhoms.d, wr_.d], writes=[zr_.d])
                                cx.op("dve", lambda e, q=q, d0=d0: e.tensor_tensor_scan(zi_[:, q, :n], d0, wi_[:, q, :n], 0.0, ALU.mult, ALU.add),
                                      reads=[rhoms.d, wi_.d], writes=[zi_.d])
                            else:
                                d0 = rho[:, s2:s2 + 1].broadcast_to([P, n])
                                cx.op("dve", lambda e, q=q, d0=d0, s2=s2: e.tensor_tensor_scan(zr_[:, q, :n], d0, wr_[:, q, :n], xst[:, 0, s2:s2 + 1], ALU.mult, ALU.add),
                                      reads=[rho.d, wr_.d, xst.d], writes=[zr_.d])
                                cx.op("dve", lambda e, q=q, d0=d0, s2=s2: e.tensor_tensor_scan(zi_[:, q, :n], d0, wi_[:, q, :n], xst[:, 1, s2:s2 + 1], ALU.mult, ALU.add),
                                      reads=[rho.d, wi_.d, xst.d], writes=[zi_.d])

                    def stageB(it, ti, s_):
                        s0, n, smp, nb, L, er, ei, v3 = geom(ti, s_)
                        zr_, zi_ = zre_[it % RG], zim_[it % RG]
                        p_ = pp[it % 2]
                        cx.op("pool", lambda e: e.tensor_tensor(v3(p_[0][:, :, :n]), er, v3(zr_[:, :, :n]), ALU.mult), reads=[Ere.d, zr_.d], writes=[p_[0].d])
                        cx.op("pool", lambda e: e.tensor_tensor(v3(p_[1][:, :, :n]), ei, v3(zi_[:, :, :n]), ALU.mult), reads=[Eim.d, zi_.d], writes=[p_[1].d])
                        cx.op("pool", lambda e: e.tensor_tensor(v3(p_[2][:, :, :n]), er, v3(zi_[:, :, :n]), ALU.mult), reads=[Ere.d, zi_.d], writes=[p_[2].d])
                        cx.op("pool", lambda e: e.tensor_tensor(v3(p_[3][:, :, :n]), ei, v3(zr_[:, :, :n]), ALU.mult), reads=[Eim.d, zr_.d], writes=[p_[3].d])

                    def stageC(it, ti, s_):
                        s0, n, smp, nb, L, er, ei, v3 = geom(ti, s_)
                        c, m = s_ // 4, s_ % 4
                        p_ = pp[it % 2]
                        xr_, xi_ = xre32[it % 2], xim32[it % 2]
                        x16r, x16i = xre16[(ti * 4 + c) % 2], xim16[(ti * 4 + c) % 2]
                        cx.op("pool", lambda e: e.tensor_tensor(xr_[:, :, :n], p_[0][:, :, :n], p_[1][:, :, :n], ALU.add), reads=[p_[0].d, p_[1].d], writes=[xr_.d])
                        cx.op("pool", lambda e: e.tensor_tensor(xi_[:, :, :n], p_[2][:, :, :n], p_[3][:, :, :n], ALU.subtract), reads=[p_[2].d, p_[3].d], writes=[xi_.d])
                        cx.op("act", lambda e: e.copy(x16r[:, m:m + 2, :n], xr_[:, :, :n]), reads=[xr_.d], writes=[x16r.d])
                        cx.op("act", lambda e: e.copy(x16i[:, m:m + 2, :n], xi_[:, :, :n]), reads=[xi_.d], writes=[x16i.d])
                        if smp:
                            for q in range(2):
                                xv = xr_[:, q, :n].rearrange("p (b j) -> p b j", b=nb)
                                cx.op("act", lambda e, q=q, xv=xv: e.copy(xso[:, 0, s_ + q, :], xv[:, :, L - 1]), reads=[xr_.d], writes=[xso.d])
                                xv2 = xi_[:, q, :n].rearrange("p (b j) -> p b j", b=nb)
                                cx.op("act", lambda e, q=q, xv2=xv2: e.copy(xso[:, 1, s_ + q, :], xv2[:, :, L - 1]), reads=[xi_.d], writes=[xso.d])
                        else:
                            cx.op("act", lambda e: e.copy(xst[:, 0, s_:s_ + 2], xr_[:, :, n - 1]), reads=[xr_.d], writes=[xst.d])
                            cx.op("act", lambda e: e.copy(xst[:, 1, s_:s_ + 2], xi_[:, :, n - 1]), reads=[xi_.d], writes=[xst.d])
                        if m == 2:
                            pby = bank()
                            for mm_ in range(4):
                                cx.op("pe", lambda e, mm_=mm_: e.matmul(pby[:, :n], CC[0][:, 4 * c + mm_, :], x16r[:, mm_, :n], start=(mm_ == 0), stop=False),
                                      reads=[CC[0].d, x16r.d], writes=[pby.d], inc=False)
                                cx.op("pe", lambda e, mm_=mm_: e.matmul(pby[:, :n], CC[1][:, 4 * c + mm_, :], x16i[:, mm_, :n], start=False, stop=(mm_ == 3)),
                                      reads=[CC[1].d, x16i.d], writes=[pby.d], inc=(mm_ == 3))
                            ut = u32[ti % 2]
                            cx.op("dve", lambda e: e.scalar_tensor_tensor(y32[:, c, :n], ut[:, c, :n], ssm[:, c:c + 1], pby[:, :n], ALU.mult, ALU.add),
                                  reads=[pby.d, ut.d, ssm.d], writes=[y32.d])
                        if s_ == 14:
                            tile_post(ti)

                    def tile_post(ti):
                        s0, n, smp = tl[ti]
                        if ti == len(tl) - 2:
                            cx.dma("sp", O["s5_p"].rearrange("r p s -> p r s"), xst[:], reads=[xst.d], writes=[outdep])
                        cx.op("pool", lambda e: e.tensor_tensor(gt[:, :, :n], y32[:, :, :n], y32[:, :, :n], ALU.mult), reads=[y32.d], writes=[gt.d])
                        cx.op("pool", lambda e: e.tensor_scalar(gt[:, :, :n], gt[:, :, :n], 0.044715, 1.0, ALU.mult, ALU.add), reads=[gt.d], writes=[gt.d])
                        cx.op("pool", lambda e: e.tensor_tensor(gt[:, :, :n], gt[:, :, :n], y32[:, :, :n], ALU.mult), reads=[gt.d, y32.d], writes=[gt.d])
                        cx.op("act", lambda e: e.activation(gt[:, :, :n], gt[:, :, :n], AF.Sigmoid, scale=2.0 * math.sqrt(2.0 / math.pi)), reads=[gt.d], writes=[gt.d])
                        cx.op("dve", lambda e: e.tensor_tensor(yg32[:, :, :n], y32[:, :, :n], gt[:, :, :n], ALU.mult), reads=[gt.d, y32.d], writes=[yg32.d])
                        cx.op("act", lambda e: e.copy(yg16[:, :, :n], yg32[:, :, :n]), reads=[yg32.d], writes=[yg16.d])
                        for c in range(4):
                            pb = bank()
                            for k in range(4):
                                cx.op("pe", lambda e, k=k, pb=pb, c=c: e.matmul(pb[:, :n], wglu[:, k, 128 * c:128 * c + 128], yg16[:, k, :n], start=(k == 0), stop=(k == 3)),
                                      reads=[wglu.d, yg16.d], writes=[pb.d], inc=(k == 3))
                            cx.op("act", lambda e, pb=pb, c=c: e.activation(sgl[:, :n], pb[:, :n], AF.Sigmoid, bias=ssm[:, 4 + c:5 + c]), reads=[pb.d, ssm.d], writes=[sgl.d])
                            cx.op("dve", lambda e, c=c: e.tensor_tensor(on16[:, c, :n], yg32[:, c, :n], sgl[:, :n], ALU.mult), reads=[yg32.d, sgl.d], writes=[on16.d])
                        for oc in range(KD):
                            pb = bank()
                            for k in range(4):
                                cx.op("pe", lambda e, k=k, oc=oc, pb=pb: e.matmul(pb[:, :n], wout[:, k, 128 * oc:128 * oc + 128], on16[:, k, :n], start=(k == 0), stop=(k == 3)),
                                      reads=[wout.d, on16.d], writes=[pb.d], inc=(k == 3))
                            cx.op("dve", lambda e, pb=pb, oc=oc: e.tensor_tensor(x[:, oc, s0:s0 + n], x[:, oc, s0:s0 + n], pb[:, :n], ALU.add),
                                  reads=[pb.d] + xdeps(s0, n), writes=xdeps(s0, n))

                    items = [(ti, s_) for ti in range(len(tl)) for s_ in range(0, 16, 2)]
                    NI = len(items)
                    uproj(0)
                    for i in range(NI + 3):
                        if i < NI:
                            ti, s_ = items[i]
                            if s_ == 8 and ti + 1 < len(tl):
                                uproj(ti + 1)
                            stageA(i, ti, s_)
                        if 0 <= i - 1 < NI:
                            stageA3(i - 1, *items[i - 1])
                        if i < NI:
                            stageA2(i, *items[i])
                        if 0 <= i - 2 < NI:
                            stageB(i - 2, *items[i - 2])
                        if 0 <= i - 3 < NI:
                            stageC(i - 3, *items[i - 3])
                    cx.dma("sp", O["s5_s"].rearrange("r p s b -> p r s b"), xso[:], reads=[xso.d], writes=[outdep])
                    cx.barrier()

        for l in range(2):
            if l == 0 and any(p in cfg.phases for p in ("A0", "A0a", "A0b")):
                a0_phase(cfg.NTA, cfg.NTB)
            if l == 1 and "A1" in cfg.phases:
                a1_phase()
            if f"X{l}" in cfg.phases:
                xattn_phase(l)
            if f"F{l}" in cfg.phases:
                ffn_phase(l)

        with contextlib.ExitStack() as ph:
            scr = {"sq16": sb("n_sq16", [P, KD, 512], BF16, ph), "rstd": sb("n_rstd", [P, 512], F32, ph), "eps": epsb}
            yo = [sb(f"yo{i}", [P, KD, 512], F32, ph) for i in range(2)]
            for i, (s0, n, smp) in enumerate(tiles_of(LP, 512)):
                yt = yo[i % 2]
                rmsnorm(lambda k: x[:, k, s0:s0 + n], xdeps(s0, n), n, 6, lambda k: yt[:, k, :n], [yt.d], scr, src_all=x[:, :, s0:s0 + n])
                cx.dma("sp", O["yT"][:, :, s0:s0 + n].rearrange("k p t -> p k t"), yt[:, :, :n], reads=[yt.d], writes=[outdep])
            cx.barrier()
        cx.barrier(engines=("sp",))
        stats = {k: (e.n_ins, e.cnt) for k, e in cx.E.items()}
    return nc, stats


def fm(v, nchunk):
    v = np.asarray(v)
    lead = v.shape[:-1]
    r = v.reshape(lead + (nchunk, 128))
    return np.ascontiguousarray(np.moveaxis(r, -1, 0))


def prep_core(inp, c, cfg):
    LP, T = cfg.LP, cfg.T
    m = {}
    xp = inp["x_prompt"][c, :LP]
    xs = inp["x_sample"][NSEQ_S * c:NSEQ_S * (c + 1)].reshape(NS, D)
    xT = np.concatenate([xp, xs], axis=0).T
    m["xT"] = np.ascontiguousarray(xT.reshape(KD, 128, T))
    m["ident"] = np.eye(128, dtype=np.float32)
    nl = [inp["norm_mix"][0], inp["norm_mix"][1], inp["norm_cross"][0], inp["norm_cross"][1],
          inp["norm_ffn"][0], inp["norm_ffn"][1], inp["norm_final"]]
    m["norms"] = np.ascontiguousarray(np.stack([v.reshape(KD, 128).T for v in nl], axis=1))
    m["ffn_up"] = np.ascontiguousarray(inp["ffn_w_up"].reshape(2, KD, 128, 2 * FFN))
    m["ffn_dn"] = np.ascontiguousarray(inp["ffn_w_down"].reshape(2, NFC, 128, D))
    cw = inp["ffn_conv_w"].reshape(2, 3, 2 * NFC, 128)
    m["convw"] = np.ascontiguousarray(cw.transpose(0, 3, 2, 1))
    m["convb"] = np.ascontiguousarray(inp["ffn_conv_b"].reshape(2, 2 * NFC, 128).transpose(0, 2, 1))
    cs = inp["state_ffn_conv"][:, NSEQ_S * c:NSEQ_S * (c + 1)]
    cs = cs.reshape(2, NSEQ_S, 2, 2 * NFC, 128)
    m["convst"] = np.ascontiguousarray(cs.transpose(0, 4, 3, 1, 2))
    cmk = np.zeros((128, 464), np.float32)
    jj, ii = np.meshgrid(np.arange(64), np.arange(64), indexing="ij")
    cmk[:64, 0:64] = (jj <= ii)
    cmk[:64, 64:128] = (jj <= ii) & (jj // LS == ii // LS)
    cmk[:64, 128:144] = (np.arange(64)[:, None] // LS == np.arange(NSEQ_S)[None, :])
    cmk[:, 144:400] = (np.arange(256) % GC != 0)[None, :]
    cmk[:, 400:464] = (np.arange(64) % LS != 0)[None, :]
    m["cmask"] = cmk
    cm2 = np.zeros((128, 384), np.float32)
    j2, i2 = np.meshgrid(np.arange(128), np.arange(128), indexing="ij")
    cm2[:, 0:128] = (j2 <= i2)
    cm2[:, 128:384] = (np.arange(256) % 128 != 0)[None, :]
    m["cmask2"] = cm2
    m["w_in_c"] = inp["w_in_c"][0].reshape(KD, 128, 3088)
    m["w_out_c"] = inp["w_out_c"][0].reshape(KD, 128, D)
    m["gla_wgu"] = inp["gla_w_gate_up"][0]
    m["gla_small"] = np.concatenate([inp["gla_b_gate"][0].reshape(4, 128).T, inp["gla_gnorm"][0].reshape(2, 128).T], axis=1)
    m["st_gla"] = inp["state_gla"][0, NSEQ_S * c:NSEQ_S * (c + 1)]
    m["w_in_ab"] = inp["w_in_ab"][0].reshape(KD, 128, 2560)
    m["w_out_ab"] = inp["w_out_ab"][0].reshape(KD, 128, D)
    lbr = inp["hgrn_lb"].reshape(3, 4, 128).transpose(2, 1, 0).reshape(128, 12)
    m["hg_small"] = np.concatenate([lbr, inp["hgrn_gnorm"][0].reshape(128, 1)], axis=1)
    m["st_hgrn"] = inp["state_hgrn"][0, NSEQ_S * c:NSEQ_S * (c + 1)]

    def sm16(a):
        return a.reshape(16, 128).T
    ls_full = np.repeat(inp["s5_log_step"][0][:, None], 64, axis=1)
    m["s5_sm"] = np.stack([sm16(inp["s5_lam_re"][0]), sm16(inp["s5_lam_im"][0]), sm16(ls_full)], axis=1)
    m["s5_row"] = np.stack([inp["s5_lam_re"][0].reshape(2048), inp["s5_lam_im"][0].reshape(2048), ls_full.reshape(2048)], axis=0)
    bexp = np.zeros((2, 128, 4, 512), np.float32)
    cexp = np.zeros((2, 128, 16, 128), np.float32)
    for r, (bsrc, csrc) in enumerate(((inp["s5_b_re"][0], inp["s5_c_re"][0]), (inp["s5_b_im"][0], inp["s5_c_im"][0]))):
        for g_ in range(32):
            cc_, gl = g_ // 8, g_ % 8
            bexp[r, gl * 16:(gl + 1) * 16, cc_, gl * 64:(gl + 1) * 64] = bsrc[g_].T
            s_, two = g_ // 2, g_ % 2
            cexp[r, two * 64:(two + 1) * 64, s_, gl * 16:(gl + 1) * 16] = csrc[g_].T
    m["s5_bexp"] = bexp
    m["s5_cexp"] = cexp
    m["s5_small"] = np.concatenate([inp["s5_d"][0].reshape(4, 128).T, inp["s5_b_glu"][0].reshape(4, 128).T], axis=1)
    m["s5_wglu"] = inp["s5_w_glu"][0].reshape(4, 128, 512)
    x0 = np.stack([inp["state_s5_re"][0, NSEQ_S * c:NSEQ_S * (c + 1)], inp["state_s5_im"][0, NSEQ_S * c:NSEQ_S * (c + 1)]], 0)
    m["s5_x0"] = x0.reshape(2, NSEQ_S, 16, 128).transpose(0, 3, 2, 1)
    m["memT"] = inp["mem_prompt"][c].T.reshape(KD, 128, NMEM)
    m["normmem"] = np.stack([inp["norm_mem"][l].reshape(KD, 128).T for l in range(2)], axis=1)
    m["wkv"] = inp["xa_w_kv"].reshape(2, KD, 128, 2 * D)
    m["wq"] = inp["xa_w_q"].reshape(2, KD, 128, D)
    m["wo"] = inp["xa_w_o"].reshape(2, KD, 128, D)
    ck = inp["cache_mem_k"][:, NSEQ_S * c:NSEQ_S * (c + 1)].reshape(2, NSEQ_S, NMEM, D)
    m["kTc"] = ck.transpose(0, 1, 3, 2).reshape(2, NSEQ_S, KD, 128, NMEM)
    m["vc"] = inp["cache_mem_v"][:, NSEQ_S * c:NSEQ_S * (c + 1)].reshape(2, NSEQ_S, 2, 128, D)
    return {k: np.ascontiguousarray(v, dtype=np.float32) for k, v in m.items()}


_CACHE = {}


def run(inputs, cfg, n_cores=8):
    key = (cfg.LP, tuple(cfg.phases), cfg.xmode)
    if key not in _CACHE:
        _CACHE[key] = build(cfg)
    nc, stats = _CACHE[key]
    in_maps = [prep_core(inputs, c, cfg) for c in range(n_cores)]
    res = run_bass_kernel_spmd(nc, in_maps, core_ids=list(range(n_cores)))
    return res.results, stats


def assemble(results, cfg, n_cores=8):
    LP, T = cfg.LP, cfg.T
    out = {}
    yp, ys, cp, cs = [], [], [], []
    mk, mv = [], []
    glp, gls = [], []
    hgp, hgs, s5p, s5s = [], [], [], []
    for c in range(n_cores):
        r = results[c]
        hgp.append(r["hgrn_p"]); hgs.append(r["hgrn_s"])
        s5p.append(r["s5_p"].transpose(0, 2, 1).reshape(2, 32, 64))
        s5s.append(r["s5_s"].transpose(0, 3, 2, 1).reshape(2, NSEQ_S, 32, 64))
        glp.append(r["gla_p"]); gls.append(r["gla_s"])
        mk.append(r["mem_kT"].reshape(2, D, NMEM).transpose(0, 2, 1).reshape(2, NMEM, 4, 256))
        mv.append(r["mem_v"].reshape(2, NMEM, 4, 256))
        yT = r["yT"].reshape(D, T)
        yp.append(yT[:, :LP].T)
        ys.append(yT[:, LP:].T.reshape(NSEQ_S, LS, D))
        cp.append(r["conv_p"].transpose(0, 3, 2, 1).reshape(2, 2, 2 * FFN))
        cs.append(r["conv_s"].transpose(0, 3, 4, 2, 1).reshape(2, NSEQ_S, 2, 2 * FFN))
    out["hgrn_p"] = np.stack(hgp, 0)[None]
    out["hgrn_s"] = np.concatenate(hgs, 0)[None]
    s5p = np.stack(s5p, 1); s5s = np.concatenate(s5s, 1)
    out["s5_re_p"] = s5p[0][None]; out["s5_im_p"] = s5p[1][None]
    out["s5_re_s"] = s5s[0][None]; out["s5_im_s"] = s5s[1][None]
    out["gla_p"] = np.stack(glp, 0)[None]
    out["gla_s"] = np.concatenate(gls, 0)[None]
    out["mem_k_p"] = np.stack(mk, 1)
    out["mem_v_p"] = np.stack(mv, 1)
    out["y_prompt"] = np.stack(yp, 0)
    out["y_sample"] = np.concatenate(ys, 0)
    out["conv_p"] = np.stack(cp, 1)
    out["conv_s"] = np.concatenate(cs, 1)
    return out


def kernel(**inputs):
    cfg = Cfg()
    inputs = {k: np.asarray(v) for k, v in inputs.items()}
    results, _ = run(inputs, cfg)
    o = assemble(results, cfg)
    names = ["y_prompt", "y_sample", "hgrn_p", "s5_re_p", "s5_im_p", "gla_p", "mem_k_p", "mem_v_p", "conv_p",
             "hgrn_s", "s5_re_s", "s5_im_s", "gla_s", "conv_s"]
    return tuple(np.ascontiguousarray(o[k], dtype=np.float32) for k in names)
```

```python
import contextlib
import math
import numpy as np
import concourse.bass as bass
import concourse.mybir as mybir
from concourse.bass_utils import run_bass_kernel_spmd

F32 = mybir.dt.float32
BF16 = mybir.dt.bfloat16
AF = mybir.ActivationFunctionType
ALU = mybir.AluOpType

D = 1024
KD = 8
NSEQ_S = 16
LS = 4
NS = NSEQ_S * LS
FFN = 2816
NFC = 22
NMEM = 256
EPS = 1e-6
GC = 64


class Dep:
    __slots__ = ("writer", "readers", "excl")

    def __init__(self, excl=False):
        self.writer = None
        self.readers = {}
        self.excl = excl


class Eng:
    def __init__(self, name, h, sem, is_pe=False):
        self.name, self.h, self.sem, self.is_pe = name, h, sem, is_pe
        self.cnt = 0
        self.waited = {}
        self.n_ins = 0

    def wait(self, ev):
        sem, val = ev
        if sem is self.sem and self.is_pe:
            return
        if self.waited.get(sem, 0) >= val:
            return
        self.waited[sem] = val
        self.h.wait_ge(sem, val)


class Ctx:
    def __init__(self, nc, stack, n_dma_sems=(20, 28, 8)):
        self.nc = nc
        mk = lambda n: stack.enter_context(nc.semaphore(n))
        self.E = {
            "pe": Eng("pe", nc.tensor, mk("s_pe"), is_pe=True),
            "act": Eng("act", nc.scalar, mk("s_act")),
            "dve": Eng("dve", nc.vector, mk("s_dve")),
            "pool": Eng("pool", nc.gpsimd, mk("s_pool")),
            "sp": Eng("sp", nc.sync, mk("s_sp")),
        }
        self.dma_sems = {}
        for q, n in zip(("sp", "pool", "act"), n_dma_sems):
            self.dma_sems[q] = [[mk(f"d_{q}{i}"), 0] for i in range(n)]
        self.dma_rr = {"sp": 0, "pool": 0, "act": 0}

    def _pre(self, E, reads, writes):
        for d in reads:
            if d.writer is not None:
                E.wait(d.writer)
            if d.excl:
                for ev in d.readers.values():
                    if ev[0] is not E.sem:
                        E.wait(ev)
        for d in writes:
            if d.writer is not None:
                E.wait(d.writer)
            for ev in d.readers.values():
                E.wait(ev)

    @staticmethod
    def _post(ev, reads, writes):
        for d in reads:
            d.readers[ev[0]] = ev
        for d in writes:
            d.writer = ev
            d.readers = {}

    def op(self, eng, fn, reads=(), writes=(), inc=True):
        E = self.E[eng]
        self._pre(E, reads, writes)
        ins = fn(E.h)
        E.n_ins += 1
        if inc:
            E.cnt += 1
            ins.then_inc(E.sem, 1)
            ev = (E.sem, E.cnt)
        else:
            ev = (E.sem, E.cnt + 1)
        self._post(ev, reads, writes)
        return ins

    def dma(self, q, out, in_, reads=(), writes=(), **kw):
        E = self.E[q]
        self._pre(E, reads, writes)
        pool = self.dma_sems[q]
        i = self.dma_rr[q]
        self.dma_rr[q] = (i + 1) % len(pool)
        slot = pool[i]
        if slot[1] > 0:
            E.wait((slot[0], slot[1]))
        slot[1] += 16
        ins = E.h.dma_start(out=out, in_=in_, **kw)
        ins.then_inc(slot[0], 16)
        E.n_ins += 1
        self._post((slot[0], slot[1]), reads, writes)
        return ins

    def barrier(self, engines=("pe", "act", "dve", "pool", "sp")):
        for en in engines:
            E = self.E[en]
            for F in self.E.values():
                if F is not E and F.cnt > 0:
                    E.wait((F.sem, F.cnt))
            for pool in self.dma_sems.values():
                for slot in pool:
                    if slot[1] > 0:
                        E.wait((slot[0], slot[1]))


class View:
    def __init__(self, ap):
        self.ap = ap
        self.d = Dep()

    def __getitem__(self, k):
        return self.ap[k]


class Buf:
    def __init__(self, t):
        self.t = t
        self.d = Dep()

    def __getitem__(self, k):
        return self.t[k]


class Cfg:
    def __init__(self, LP=2048, phases=("A0", "X0", "F0", "A1", "X1", "F1")):
        self.LP = LP
        self.T = LP + NS
        self.phases = phases
        self.NTA = 256
        self.xmode = "full"
        self.NTB = 128


def tiles_of(LP, n):
    ts = [(i, n, False) for i in range(0, LP, n)]
    ts.append((LP, NS, True))
    return ts


def build(cfg):
    nc = bass.Bass("TRN2", target_bir_lowering=False)
    LP, T = cfg.LP, cfg.T
    P = 128

    def din(name, shape, dt=F32):
        return nc.dram_tensor(name, list(shape), dt, kind="ExternalInput").ap()

    def dout(name, shape):
        return nc.dram_tensor(name, list(shape), F32, kind="ExternalOutput").ap()

    I = {}
    I["xT"] = din("xT", [KD, P, T])
    I["ident"] = din("ident", [P, P])
    I["norms"] = din("norms", [P, 7, KD])
    I["ffn_up"] = din("ffn_up", [2, KD, P, 2 * FFN])
    I["ffn_dn"] = din("ffn_dn", [2, NFC, P, D])
    I["convw"] = din("convw", [2, P, 2 * NFC, 3])
    I["convb"] = din("convb", [2, P, 2 * NFC])
    I["convst"] = din("convst", [2, P, 2 * NFC, NSEQ_S, 2])
    I["cmask"] = din("cmask", [P, 464])
    I["cmask2"] = din("cmask2", [P, 384])
    I["w_in_c"] = din("w_in_c", [KD, P, 3088])
    I["w_out_c"] = din("w_out_c", [KD, P, D])
    I["gla_wgu"] = din("gla_wgu", [16, 512])
    I["gla_small"] = din("gla_small", [P, 6])
    I["st_gla"] = din("st_gla", [NSEQ_S, 4, P, 256])
    I["w_in_ab"] = din("w_in_ab", [KD, P, 2560])
    I["w_out_ab"] = din("w_out_ab", [KD, P, D])
    I["hg_small"] = din("hg_small", [P, 13])
    I["st_hgrn"] = din("st_hgrn", [NSEQ_S, 4, P, 128])
    I["s5_sm"] = din("s5_sm", [P, 3, 16])
    I["s5_row"] = din("s5_row", [3, 2048])
    I["s5_bexp"] = din("s5_bexp", [2, P, 4, 512])
    I["s5_cexp"] = din("s5_cexp", [2, P, 16, 128])
    I["s5_small"] = din("s5_small", [P, 8])
    I["s5_wglu"] = din("s5_wglu", [4, P, 512])
    I["s5_x0"] = din("s5_x0", [2, P, 16, NSEQ_S])
    I["memT"] = din("memT", [KD, P, NMEM])
    I["normmem"] = din("normmem", [P, 2, KD])
    I["wkv"] = din("wkv", [2, KD, P, 2 * D])
    I["wq"] = din("wq", [2, KD, P, D])
    I["wo"] = din("wo", [2, KD, P, D])
    I["kTc"] = din("kTc", [2, NSEQ_S, KD, P, NMEM])
    I["vc"] = din("vc", [2, NSEQ_S, 2, P, D])
    O = {}
    O["mem_kT"] = dout("mem_kT", [2, KD, P, NMEM])
    O["mem_v"] = dout("mem_v", [2, 2, P, D])
    O["hgrn_p"] = dout("hgrn_p", [4, P, 128])
    O["hgrn_s"] = dout("hgrn_s", [NSEQ_S, 4, P, 128])
    O["s5_p"] = dout("s5_p", [2, P, 16])
    O["s5_s"] = dout("s5_s", [2, P, 16, NSEQ_S])
    O["gla_p"] = dout("gla_p", [4, P, 256])
    O["gla_s"] = dout("gla_s", [NSEQ_S, 4, P, 256])
    O["yT"] = dout("yT", [KD, P, T])
    O["conv_p"] = dout("conv_p", [2, P, 2 * NFC, 2])
    O["conv_s"] = dout("conv_s", [2, P, 2 * NFC, NSEQ_S, 2])

    with contextlib.ExitStack() as st:
        cx = Ctx(nc, st)

        uid = [0]

        def sb(name, shape, dt, stack=st):
            uid[0] += 1
            return Buf(stack.enter_context(nc.sbuf_tensor(f"s{uid[0]}_{name}", list(shape), dt)))

        banks = [Buf(st.enter_context(nc.psum_tensor(f"bank{i}", [P, 512], F32))) for i in range(8)]
        for b_ in banks:
            b_.d.excl = True
        bank_rr = [0]

        reserved = set()

        def bank():
            while True:
                i = bank_rr[0]
                bank_rr[0] = (bank_rr[0] + 1) % 8
                if i not in reserved:
                    return banks[i]

        def reserve():
            b = bank()
            reserved.add(banks.index(b))
            return b

        def release(b):
            reserved.discard(banks.index(b))

        x = sb("x", [P, KD, T], F32)
        xd = {}
        for s0 in range(0, T, 256):
            xd[s0] = Dep()

        def xdeps(s0, n):
            return [xd[k] for k in range((s0 // 256) * 256, s0 + n, 256)]

        ident = sb("ident", [P, P], F32)
        ident16 = sb("ident16", [P, P], BF16)
        ones16 = sb("ones16", [P, P], BF16)
        norms = sb("norms", [P, 7, KD], F32)
        outdep = Dep()

        cx.dma("sp", ident[:], I["ident"], writes=[ident.d])
        cx.dma("sp", norms[:], I["norms"], writes=[norms.d])
        cx.op("dve", lambda e: e.tensor_copy(ident16[:], ident[:]), reads=[ident.d], writes=[ident16.d])
        cx.op("dve", lambda e: e.memset(ones16[:], 1.0), writes=[ones16.d])
        for s0 in range(0, T, 256):
            n = min(256, T - s0)
            cx.dma("sp", x[:, :, s0:s0 + n], I["xT"][:, :, s0:s0 + n].rearrange("k p t -> p k t"),
                   writes=[xd[s0]])

        def rmsnorm(src_fn, src_deps, n, gidx, out_fn, out_deps, scr, nk=KD, dim=D, out_eng="dve", src_all=None):
            sq, rstd = scr["sq16"], scr["rstd"]
            if src_all is not None:
                cx.op("act", lambda e: e.activation(sq[:, :nk, :n], src_all, AF.Square), reads=src_deps, writes=[sq.d])
            else:
                for k in range(nk):
                    cx.op("act", lambda e, k=k: e.activation(sq[:, k, :n], src_fn(k), AF.Square),
                          reads=src_deps, writes=[sq.d])
            pb = bank()
            for k in range(nk):
                cx.op("pe", lambda e, k=k: e.matmul(pb[:, :n], ones16[:], sq[:, k, :n], start=(k == 0), stop=(k == nk - 1)),
                      reads=[sq.d, ones16.d], writes=[pb.d], inc=(k == nk - 1))
            cx.op("act", lambda e: e.activation(rstd[:, :n], pb[:, :n], AF.Ln, bias=scr["eps"][:, 0:1], scale=1.0 / dim),
                  reads=[pb.d, scr["eps"].d], writes=[rstd.d])
            cx.op("act", lambda e: e.activation(rstd[:, :n], rstd[:, :n], AF.Exp, scale=-0.5), reads=[rstd.d], writes=[rstd.d])
            for k in range(nk):
                cx.op(out_eng, lambda e, k=k: e.scalar_tensor_tensor(out_fn(k), src_fn(k), norms[:, gidx, k:k + 1], rstd[:, :n],
                                                                      ALU.mult, ALU.mult),
                      reads=list(src_deps) + [rstd.d, norms.d], writes=out_deps)

        epsb = sb("epsb", [P, 1], F32)
        cx.op("dve", lambda e: e.memset(epsb[:], EPS), writes=[epsb.d])

        def ffn_phase(l):
            with contextlib.ExitStack() as ph:
                hall = sb("hall", [P, KD, T], BF16, ph)
                hd = {s0: Dep() for s0 in range(0, T, 512)}
                cw = sb("cw", [P, 2 * NFC, 3], F32, ph)
                cb = sb("cb", [P, 2 * NFC], F32, ph)
                cst = sb("cst", [P, 2 * NFC, NSEQ_S, 2], F32, ph)
                tail = sb("tail", [P, 2 * NFC, 2], F32, ph)
                cso = sb("cso", [P, 2 * NFC, NSEQ_S, 2], F32, ph)
                cx.dma("sp", cw[:], I["convw"][l], writes=[cw.d])
                cx.dma("sp", cb[:], I["convb"][l], writes=[cb.d])
                cx.dma("sp", cst[:], I["convst"][l], writes=[cst.d])
                cx.op("dve", lambda e: e.memset(tail[:], 0.0), writes=[tail.d])
                tl = tiles_of(LP, 512)
                GS = 4
                groups = [(c0, min(GS, NFC - c0)) for c0 in range(0, NFC, GS)]
                NG = len(groups)
                RING = 2
                wup = [sb(f"wup{i}", [P, KD, 2, 128 * GS], BF16, ph) for i in range(RING)]
                wdn = [sb(f"wdn{i}", [P, GS, D], BF16, ph) for i in range(RING)]
                upv = I["ffn_up"][l].rearrange("k p n -> p k n")

                def load_group(g):
                    r = g % RING
                    c0, gs = groups[g]
                    for half in range(2):
                        cstart = half * FFN + 128 * c0
                        cx.dma("pool", wup[r][:, :, half, :128 * gs], upv[:, :, cstart:cstart + 128 * gs], writes=[wup[r].d])
                    cx.dma("pool", wdn[r][:, :gs, :], I["ffn_dn"][l][c0:c0 + gs].rearrange("j p n -> p j n"), writes=[wdn[r].d])

                load_group(0)
                with contextlib.ExitStack() as phn:
                    scr = {"sq16": sb("f_sq16", [P, KD, 512], BF16, phn), "rstd": sb("f_rstd", [P, 512], F32, phn), "eps": epsb}
                    for (s0, n, smp) in tl:
                        rmsnorm(lambda k: x[:, k, s0:s0 + n], xdeps(s0, n), n, 4 + l,
                                lambda k: hall[:, k, s0:s0 + n], [hd[s0]], scr, src_all=x[:, :, s0:s0 + n])
                    cx.barrier()
                ext = [sb(f"ext{i}", [P, 512 + 2 * NSEQ_S], F32, ph) for i in range(4)]
                cc = [sb(f"cc{i}", [P, 512], F32, ph) for i in range(4)]
                sa = [sb(f"sa{i}", [P, 512], F32, ph) for i in range(2)]
                yb = [sb(f"yb{i}", [P, GS, 512], BF16, ph) for i in range(2)]
                itc = [0]
                steps = [(g, ti) for g in range(NG) for ti in range(len(tl))]

                def up_part(si):
                    g, ti = steps[si]
                    s0, n, smp = tl[ti]
                    r = g % RING
                    c0, gs = groups[g]
                    if True:
                        nseq, L = (NSEQ_S, LS) if smp else (1, n)
                        yt = yb[si % 2]
                        for j in range(gs):
                            it = itc[0]
                            ch = [c0 + j, NFC + c0 + j]
                            cres = []
                            for half in range(2):
                                c = ch[half]
                                pb = bank()
                                for k in range(KD):
                                    cx.op("pe", lambda e, k=k, half=half, pb=pb: e.matmul(
                                        pb[:, :n], wup[r][:, k, half, 128 * j:128 * j + 128], hall[:, k, s0:s0 + n],
                                        start=(k == 0), stop=(k == KD - 1)),
                                        reads=[wup[r].d, hd[s0]], writes=[pb.d], inc=(k == KD - 1))
                                et = ext[(2 * it + half) % 4]
                                ct = cc[(2 * it + half) % 4]
                                ev = et[:, :nseq * (L + 2)].rearrange("p (s l) -> p s l", s=nseq)
                                pv = pb[:, :n].rearrange("p (s l) -> p s l", s=nseq)
                                cv = ct[:, :n].rearrange("p (s l) -> p s l", s=nseq)
                                if smp:
                                    cx.op("pool", lambda e, ev=ev, c=c: e.tensor_copy(ev[:, :, 0:2], cst[:, c, :, :]),
                                          reads=[cst.d], writes=[et.d])
                                else:
                                    cx.op("pool", lambda e, ev=ev, c=c: e.tensor_copy(ev[:, :, 0:2], tail[:, c:c + 1, :]),
                                          reads=[tail.d], writes=[et.d])
                                cx.op("act", lambda e, ev=ev, pv=pv: e.copy(ev[:, :, 2:L + 2], pv), reads=[pb.d], writes=[et.d])
                                cx.op("act", lambda e, cv=cv, pv=pv, c=c: e.activation(cv, pv, AF.Identity, bias=cb[:, c:c + 1],
                                                                                      scale=cw[:, c, 2:3]),
                                      reads=[pb.d, cw.d, cb.d], writes=[ct.d])
                                cx.op("dve", lambda e, cv=cv, ev=ev, c=c: e.scalar_tensor_tensor(cv, ev[:, :, 1:L + 1], cw[:, c, 1:2], cv,
                                                                                              ALU.mult, ALU.add),
                                      reads=[et.d, cw.d, ct.d], writes=[ct.d])
                                cx.op("dve", lambda e, cv=cv, ev=ev, c=c: e.scalar_tensor_tensor(cv, ev[:, :, 0:L], cw[:, c, 0:1], cv,
                                                                                              ALU.mult, ALU.add),
                                      reads=[et.d, cw.d, ct.d], writes=[ct.d])
                                if smp:
                                    cx.op("pool", lambda e, ev=ev, c=c: e.tensor_copy(cso[:, c, :, :], ev[:, :, L:L + 2]),
                                          reads=[et.d], writes=[cso.d])
                                else:
                                    cx.op("pool", lambda e, ev=ev, c=c: e.tensor_copy(tail[:, c:c + 1, :], ev[:, :, L:L + 2]),
                                          reads=[et.d], writes=[tail.d])
                                cres.append(ct)
                            st_ = sa[it % 2]
                            cx.op("act", lambda e, st_=st_, ct=cres[0]: e.activation(st_[:, :n], ct[:, :n], AF.Silu),
                                  reads=[cres[0].d], writes=[st_.d])
                            cx.op("dve", lambda e, st_=st_, ct=cres[1], yt=yt: e.tensor_tensor(yt[:, j, :n], st_[:, :n], ct[:, :n], ALU.mult),
                                  reads=[st_.d, cres[1].d], writes=[yt.d])
                            itc[0] += 1

                def down_part(si):
                    g, ti = steps[si]
                    s0, n, smp = tl[ti]
                    r = g % RING
                    c0, gs = groups[g]
                    yt = yb[si % 2]
                    if True:
                        for oc in range(KD):
                            pb = bank()
                            for j in range(gs):
                                cx.op("pe", lambda e, j=j, pb=pb, oc=oc, yt=yt: e.matmul(
                                    pb[:, :n], wdn[r][:, j, 128 * oc:128 * oc + 128], yt[:, j, :n], start=(j == 0), stop=(j == gs - 1)),
                                    reads=[wdn[r].d, yt.d], writes=[pb.d], inc=(j == gs - 1))
                            cx.op("dve", lambda e, pb=pb, oc=oc: e.tensor_tensor(x[:, oc, s0:s0 + n], x[:, oc, s0:s0 + n], pb[:, :n], ALU.add),
                                  reads=[pb.d] + xdeps(s0, n), writes=xdeps(s0, n))

                for si in range(len(steps) + 1):
                    if si < len(steps):
                        up_part(si)
                    if si >= 1:
                        down_part(si - 1)
                    if si < len(steps) and steps[si][1] == 0:
                        g = steps[si][0]
                        if g + RING - 1 < NG:
                            load_group(g + RING - 1)
                cx.dma("sp", O["conv_p"][l], tail[:], reads=[tail.d], writes=[outdep])
                cx.dma("sp", O["conv_s"][l], cso[:], reads=[cso.d], writes=[outdep])
                cx.barrier()


        KT16 = [None, None]
        V16 = [None, None]

        def mem_phase(l):
            with contextlib.ExitStack() as ph:
                memT = sb("memT", [P, KD, NMEM], F32, ph)
                nm = sb("nm", [P, 2, KD], F32, ph)
                sq = sb("m_sq", [P, KD, NMEM], BF16, ph)
                rstd = sb("m_rstd", [P, NMEM], F32, ph)
                cx.dma("sp", memT[:], I["memT"].rearrange("k p m -> p k m"), writes=[memT.d])
                cx.dma("sp", nm[:], I["normmem"], writes=[nm.d])
                cx.op("act", lambda e: e.activation(sq[:], memT[:], AF.Square), reads=[memT.d], writes=[sq.d])
                pb = bank()
                for k in range(KD):
                    cx.op("pe", lambda e, k=k: e.matmul(pb[:, :NMEM], ones16[:], sq[:, k, :], start=(k == 0), stop=(k == KD - 1)),
                          reads=[sq.d, ones16.d], writes=[pb.d], inc=(k == KD - 1))
                cx.op("act", lambda e: e.activation(rstd[:], pb[:, :NMEM], AF.Ln, bias=epsb[:, 0:1], scale=1.0 / D),
                      reads=[pb.d, epsb.d], writes=[rstd.d])
                cx.op("act", lambda e: e.activation(rstd[:], rstd[:], AF.Exp, scale=-0.5), reads=[rstd.d], writes=[rstd.d])
                if True:
                    wkv = sb(f"wkv{l}", [P, KD, 2 * D], BF16, ph)
                    memn = sb(f"memn{l}", [P, KD, NMEM], BF16, ph)
                    ko = sb(f"ko{l}", [P, KD, NMEM], F32, ph)
                    vo = sb(f"vo{l}", [P, 2, D], F32, ph)
                    wv = I["wkv"][l].rearrange("k p n -> p k n")
                    for hh in range(2):
                        cx.dma("pool", wkv[:, :, hh * D:(hh + 1) * D], wv[:, :, hh * D:(hh + 1) * D], writes=[wkv.d])
                    for k in range(KD):
                        cx.op("dve", lambda e, k=k: e.scalar_tensor_tensor(memn[:, k, :], memT[:, k, :], nm[:, l, k:k + 1], rstd[:],
                                                                          ALU.mult, ALU.mult),
                              reads=[memT.d, nm.d, rstd.d], writes=[memn.d])
                    for c in range(KD):
                        pb = bank()
                        for k in range(KD):
                            cx.op("pe", lambda e, k=k, c=c, pb=pb: e.matmul(pb[:, :NMEM], wkv[:, k, 128 * c:128 * c + 128], memn[:, k, :],
                                                                           start=(k == 0), stop=(k == KD - 1)),
                                  reads=[wkv.d, memn.d], writes=[pb.d], inc=(k == KD - 1))
                        cx.op("act", lambda e, c=c, pb=pb: e.copy(KT16[l][:, c, :], pb[:, :NMEM]), reads=[pb.d], writes=[KT16[l].d])
                        cx.op("dve", lambda e, c=c, pb=pb: e.tensor_copy(ko[:, c, :], pb[:, :NMEM]), reads=[pb.d], writes=[ko.d])
                    cx.dma("sp", O["mem_kT"][l].rearrange("k p m -> p k m"), ko[:], reads=[ko.d], writes=[outdep])
                    for mc in range(2):
                        for cb_ in range(2):
                            pb = bank()
                            for k in range(KD):
                                cx.op("pe", lambda e, k=k, mc=mc, cb_=cb_, pb=pb: e.matmul(
                                    pb[:, :512], memn[:, k, 128 * mc:128 * mc + 128], wkv[:, k, D + 512 * cb_:D + 512 * cb_ + 512],
                                    start=(k == 0), stop=(k == KD - 1)),
                                    reads=[wkv.d, memn.d], writes=[pb.d], inc=(k == KD - 1))
                            cx.op("act", lambda e, mc=mc, cb_=cb_, pb=pb: e.copy(V16[l][:, mc, 512 * cb_:512 * cb_ + 512], pb[:, :512]),
                                  reads=[pb.d], writes=[V16[l].d])
                            cx.op("dve", lambda e, mc=mc, cb_=cb_, pb=pb: e.tensor_copy(vo[:, mc, 512 * cb_:512 * cb_ + 512], pb[:, :512]),
                                  reads=[pb.d], writes=[vo.d])
                    cx.dma("sp", O["mem_v"][l].rearrange("m p n -> p m n"), vo[:], reads=[vo.d], writes=[outdep])
                cx.barrier()

        def xattn_phase(l):
            with contextlib.ExitStack() as ph:
                KT16[l] = sb(f"KT16_{l}", [P, KD, NMEM], BF16, ph)
                V16[l] = sb(f"V16_{l}", [P, 2, D], BF16, ph)
                mem_phase(l)
                if cfg.xmode == "mem":
                    return
                wq = sb("wq", [P, KD, D], BF16, ph)
                wo = sb("wo", [P, KD, D], BF16, ph)
                cx.dma("pool", wq[:], I["wq"][l].rearrange("k p n -> p k n"), writes=[wq.d])
                cx.dma("pool", wo[:], I["wo"][l].rearrange("k p n -> p k n"), writes=[wo.d])
                scr = {"sq16": sb("x_sq16", [P, KD, 512], BF16, ph), "rstd": sb("x_rstd", [P, 512], F32, ph), "eps": epsb}
                h = sb("x_h", [P, KD, 512], BF16, ph)
                qa = sb("x_qa", [P, KD, T], BF16, ph)
                NR = 3
                pTb = [sb(f"x_pT{i}", [P, 2, 512], BF16, ph) for i in range(NR)]
                rsb = [sb(f"x_rs{i}", [P, 512], F32, ph) for i in range(2)]
                ktr = [sb(f"x_kt{i}", [P, KD, NMEM], BF16, ph) for i in range(2)]
                vtr = [sb(f"x_vt{i}", [P, 2, D], BF16, ph) for i in range(2)]
                pTs = sb("x_pTs", [P, 2, 256], BF16, ph)
                pTs_d = [Dep() for _ in range(NSEQ_S)]
                tl = tiles_of(LP, 512)
                qd = [[Dep() for _ in range(4)] for _ in tl]
                order = [len(tl) - 1] + list(range(len(tl) - 1))
                for ti in order:
                    s0, n, smp = tl[ti]
                    rmsnorm(lambda k: x[:, k, s0:s0 + n], xdeps(s0, n), n, 2 + l, lambda k: h[:, k, :n], [h.d], scr, src_all=x[:, :, s0:s0 + n])
                    for c in range(KD):
                        pb = bank()
                        for k in range(KD):
                            cx.op("pe", lambda e, k=k, c=c, pb=pb: e.matmul(pb[:, :n], wq[:, k, 128 * c:128 * c + 128], h[:, k, :n],
                                                                           start=(k == 0), stop=(k == KD - 1)),
                                  reads=[wq.d, h.d], writes=[pb.d], inc=(k == KD - 1))
                        if c % 2 == 0:
                            cx.op("act", lambda e, c=c, pb=pb: e.copy(qa[:, c, s0:s0 + n], pb[:, :n]), reads=[pb.d], writes=[qd[ti][c // 2]])
                        else:
                            cx.op("dve", lambda e, c=c, pb=pb: e.tensor_copy(qa[:, c, s0:s0 + n], pb[:, :n]), reads=[pb.d], writes=[qd[ti][c // 2]])
                S_ps = reserve()
                O_ps = reserve()
                Sv = S_ps[:, :].rearrange("p (m c) -> p m c", m=2)
                sN = len(tl) - 1
                sS0 = tl[sN][0]

                def sample_seq(b):
                    kt, vt = ktr[b % 2], vtr[b % 2]
                    cx.dma("pool", kt[:], I["kTc"][l, b].rearrange("k p m -> p k m"), writes=[kt.d])
                    cx.dma("pool", vt[:], I["vc"][l, b].rearrange("m p n -> p m n"), writes=[vt.d])
                    for hd in range(4):
                        for mc in range(2):
                            for dc in range(2):
                                col = mc * 256 + (b * 4 + hd) * 4
                                cx.op("pe", lambda e, mc=mc, dc=dc, hd=hd, col=col: e.matmul(
                                    S_ps[:, col:col + 4], kt[:, 2 * hd + dc, 128 * mc:128 * mc + 128], qa[:, 2 * hd + dc, sS0 + 4 * b:sS0 + 4 * b + 4],
                                    start=(dc == 0), stop=(dc == 1)),
                                    reads=[kt.d, qd[sN][hd]], writes=[S_ps.d], inc=(dc == 1 and mc == 1 and hd == 3))
                    cx.op("act", lambda e: e.activation(pTs[:, :, 16 * b:16 * b + 16], Sv[:, :, 16 * b:16 * b + 16], AF.Exp, scale=1.0 / 16.0),
                          reads=[S_ps.d], writes=[pTs_d[b]])
                    for hd in range(4):
                        for dc in range(2):
                            for mc in range(2):
                                col = (2 * hd + dc) * 64 + 4 * b
                                pc = (b * 4 + hd) * 4
                                cx.op("pe", lambda e, mc=mc, dc=dc, hd=hd, col=col, pc=pc: e.matmul(
                                    O_ps[:, col:col + 4], vt[:, mc, (2 * hd + dc) * 128:(2 * hd + dc) * 128 + 128], pTs[:, mc, pc:pc + 4],
                                    start=(mc == 0), stop=(mc == 1)),
                                    reads=[vt.d, pTs_d[b]], writes=[O_ps.d], inc=(mc == 1 and dc == 1 and hd == 3))

                items = [(ti, hd) for ti in range(len(tl) - 1) for hd in range(4)]
                sc_banks = {}

                def SC(i):
                    ti, hd = items[i]
                    s0, n, smp = tl[ti]
                    pT = pTb[i % NR]
                    pbs = [bank(), bank()]
                    for mc in range(2):
                        for dc in range(2):
                            cx.op("pe", lambda e, mc=mc, dc=dc: e.matmul(
                                pbs[mc][:, :n], KT16[l][:, 2 * hd + dc, 128 * mc:128 * mc + 128], qa[:, 2 * hd + dc, s0:s0 + n],
                                start=(dc == 0), stop=(dc == 1)),
                                reads=[KT16[l].d, qd[ti][hd]], writes=[pbs[mc].d], inc=(dc == 1))
                    for mc in range(2):
                        cx.op("act", lambda e, mc=mc: e.activation(pT[:, mc, :n], pbs[mc][:, :n], AF.Exp, scale=1.0 / 16.0),
                              reads=[pbs[mc].d], writes=[pT.d])

                def REST(i):
                    ti, hd = items[i]
                    s0, n, smp = tl[ti]
                    pT, rs = pTb[i % NR], rsb[i % 2]
                    pbsum = bank()
                    for mc in range(2):
                        cx.op("pe", lambda e, mc=mc: e.matmul(pbsum[:, :n], ones16[:], pT[:, mc, :n], start=(mc == 0), stop=(mc == 1)),
                              reads=[pT.d, ones16.d], writes=[pbsum.d], inc=(mc == 1))
                    cx.op("act", lambda e: e.activation(rs[:, :n], pbsum[:, :n], AF.Ln), reads=[pbsum.d], writes=[rs.d])
                    cx.op("act", lambda e: e.activation(rs[:, :n], rs[:, :n], AF.Exp, scale=-1.0), reads=[rs.d], writes=[rs.d])
                    for dc in range(2):
                        pbo = bank()
                        for mc in range(2):
                            cx.op("pe", lambda e, mc=mc, dc=dc, pbo=pbo: e.matmul(
                                pbo[:, :n], V16[l][:, mc, (2 * hd + dc) * 128:(2 * hd + dc) * 128 + 128], pT[:, mc, :n],
                                start=(mc == 0), stop=(mc == 1)),
                                reads=[V16[l].d, pT.d], writes=[pbo.d], inc=(mc == 1))
                        cx.op("dve", lambda e, dc=dc, pbo=pbo: e.tensor_tensor(qa[:, 2 * hd + dc, s0:s0 + n], pbo[:, :n], rs[:, :n], ALU.mult),
                              reads=[pbo.d, rs.d], writes=[qd[ti][hd]])

                NI = len(items)
                nb_done = 0
                if NI > 0:
                    SC(0)
                for i in range(NI):
                    if i + 1 < NI:
                        SC(i + 1)
                    REST(i)
                    want = ((i + 1) * NSEQ_S + NI - 1) // NI
                    while nb_done < min(want, NSEQ_S):
                        sample_seq(nb_done)
                        nb_done += 1
                while nb_done < NSEQ_S:
                    sample_seq(nb_done)
                    nb_done += 1
                pbsum = bank()
                for mc in range(2):
                    cx.op("pe", lambda e, mc=mc, pbsum=pbsum: e.matmul(pbsum[:, :256], ones16[:], pTs[:, mc, :], start=(mc == 0), stop=(mc == 1)),
                          reads=pTs_d + [ones16.d], writes=[pbsum.d], inc=(mc == 1))
                rs = rsb[0]
                cx.op("act", lambda e: e.activation(rs[:, :256], pbsum[:, :256], AF.Ln), reads=[pbsum.d], writes=[rs.d])
                cx.op("act", lambda e: e.activation(rs[:, :256], rs[:, :256], AF.Exp, scale=-1.0), reads=[rs.d], writes=[rs.d])
                rsv = rs[:, :256].rearrange("p (b h t) -> p b h t", b=NSEQ_S, h=4)
                Ov = O_ps[:, :].rearrange("p (c b t) -> p c b t", c=KD, b=NSEQ_S)
                aov = qa[:, :, sS0:sS0 + NS].rearrange("p c (b t) -> p c b t", b=NSEQ_S)
                for hd in range(4):
                    for dc in range(2):
                        c = 2 * hd + dc
                        cx.op("dve", lambda e, c=c, hd=hd: e.tensor_tensor(aov[:, c], Ov[:, c], rsv[:, :, hd, :], ALU.mult),
                              reads=[O_ps.d, rs.d], writes=[qd[sN][hd]])
                release(S_ps)
                release(O_ps)
                for ti in range(len(tl)):
                    s0, n, smp = tl[ti]
                    for oc in range(KD):
                        pb = bank()
                        for k in range(KD):
                            cx.op("pe", lambda e, k=k, oc=oc, pb=pb: e.matmul(pb[:, :n], wo[:, k, 128 * oc:128 * oc + 128], qa[:, k, s0:s0 + n],
                                                                            start=(k == 0), stop=(k == KD - 1)),
                                  reads=[wo.d] + qd[ti], writes=[pb.d], inc=(k == KD - 1))
                        cx.op("dve", lambda e, pb=pb, oc=oc: e.tensor_tensor(x[:, oc, s0:s0 + n], x[:, oc, s0:s0 + n], pb[:, :n], ALU.add),
                              reads=[pb.d] + xdeps(s0, n), writes=xdeps(s0, n))
                cx.barrier()

        cm = sb("cmask", [P, 464], F32)
        cx.dma("sp", cm[:], I["cmask"], writes=[cm.d])
        maskC16 = sb("maskC16", [64, 64], BF16)
        maskS16 = sb("maskS16", [64, 64], BF16)
        msel16 = sb("msel16", [64, 16], BF16)
        cx.op("dve", lambda e: e.tensor_copy(maskC16[:], cm[:64, 0:64]), reads=[cm.d], writes=[maskC16.d])
        cx.op("dve", lambda e: e.tensor_copy(maskS16[:], cm[:64, 64:128]), reads=[cm.d], writes=[maskS16.d])
        cx.op("dve", lambda e: e.tensor_copy(msel16[:], cm[:64, 128:144]), reads=[cm.d], writes=[msel16.d])

        def gla_alloc(ph, dvc, NT, CK=GC):
            dv = 128 * dvc
            g = {}
            g["ring"] = [[sb(f"g_bt{i}", [P, NT], F32, ph), sb(f"g_eb{i}", [P, NT], F32, ph), sb(f"g_enb{i}", [P, NT], F32, ph)] for i in range(2)]
            g["qt16"] = sb("g_qt16", [P, 4, NT], BF16, ph)
            g["kt16"] = sb("g_kt16", [P, 4, NT], BF16, ph)
            g["ebl"] = sb("g_ebl", [P, 4, 16], F32, ph)
            g["kexp"] = [sb(f"g_kexp{i}", [64, NSEQ_S, 128], BF16, ph) for i in range(1)] * 2
            g["S32a"] = sb("g_S32a", [P, 4, dv], F32, ph)
            g["S16a"] = sb("g_S16a", [P, 4, dv], BF16, ph)
            RA = 1 if dvc == 2 else 2
            g["am4"] = [sb(f"g_am4_{i}", [CK, 4, CK], BF16, ph) for i in range(RA)] * (2 // RA)
            g["ktm4"] = [sb(f"g_ktm4_{i}", [CK, 4, 128], BF16, ph) for i in range(RA)] * (2 // RA)
            R4 = 2 if dvc == 2 else 4
            R2 = 1 if dvc == 2 else 2
            g["s32r"] = [sb(f"g_s32r{i}", [P, dv], F32, ph) for i in range(R4)] * (4 // R4)
            g["s16r"] = [sb(f"g_s16r{i}", [P, dv], BF16, ph) for i in range(R4)] * (4 // R4)
            g["so"] = [sb(f"g_so{i}", [P, dv], F32, ph) for i in range(R4)] * (4 // R4)
            g["o32"] = [sb(f"g_o32{i}", [P, dvc, NT], F32, ph) for i in range(R2)] * (2 // R2)
            g["osq"] = [sb(f"g_osq{i}", [P, dvc, NT], BF16, ph) for i in range(R2)] * (2 // R2)
            g["rstd"] = [sb(f"g_rstd{i}", [P, NT], F32, ph) for i in range(R2)] * (2 // R2)
            cx.op("pool", lambda e: e.memset(g["S32a"][:], 0.0), writes=[g["S32a"].d])
            cx.op("pool", lambda e: e.memset(g["S16a"][:], 0.0), writes=[g["S16a"].d])
            return g

        def gla_prep(g, n, smp, scale, q32, k32, lg32, CK=GC, mres_p=None):
            C = LS if smp else CK
            nch = n // C
            mres = cm[:, 400:464] if smp else (cm[:, 144:144 + n] if mres_p is None else mres_p[:, :n])
            qt16, kt16, ebl = g["qt16"], g["kt16"], g["ebl"]
            for hd in range(4):
                bt_, eb_, enb_ = g["ring"][hd % 2]
                cx.op("dve", lambda e, hd=hd, bt_=bt_: e.tensor_tensor_scan(bt_[:, :n], mres, lg32[:, hd, :n], 0.0, ALU.mult, ALU.add),
                      reads=[lg32.d, cm.d], writes=[bt_.d])
                cx.op("act", lambda e, bt_=bt_, eb_=eb_: e.activation(eb_[:, :n], bt_[:, :n], AF.Exp, scale=scale), reads=[bt_.d], writes=[eb_.d])
                cx.op("act", lambda e, bt_=bt_, enb_=enb_: e.activation(enb_[:, :n], bt_[:, :n], AF.Exp, scale=-scale), reads=[bt_.d], writes=[enb_.d])
                cx.op("dve", lambda e, hd=hd, eb_=eb_: e.tensor_tensor(qt16[:, hd, :n], q32[:, hd, :n], eb_[:, :n], ALU.mult),
                      reads=[q32.d, eb_.d], writes=[qt16.d])
                cx.op("pool", lambda e, hd=hd, enb_=enb_: e.tensor_tensor(kt16[:, hd, :n], k32[:, hd, :n], enb_[:, :n], ALU.mult),
                      reads=[k32.d, enb_.d], writes=[kt16.d])
                ebv = eb_[:, :n].rearrange("p (c j) -> p c j", j=C)
                cx.op("act", lambda e, hd=hd, ebv=ebv: e.copy(ebl[:, hd, :nch], ebv[:, :, C - 1]), reads=[eb_.d], writes=[ebl.d])

        def gla_chunks(g, n, smp, dvc, v_tm, st_in, st_out, CK=GC, maskCK=None):
            dv = 128 * dvc
            if maskCK is None:
                maskCK = maskC16
            C = LS if smp else CK
            nch = n // C
            qt16, kt16, ebl = g["qt16"], g["kt16"], g["ebl"]
            pbo = [reserve() for _ in range(4)]
            if not smp:
                S32a, S16a = g["S32a"], g["S16a"]
                hpb = 512 // dv
                for ci in range(nch):
                    cs = slice(ci * CK, ci * CK + CK)
                    am, ktm = g["am4"][ci % 2], g["ktm4"][ci % 2]
                    pbA, pbT = bank(), bank()
                    pbTv = pbT[:, :].bitcast(BF16)
                    for hd in range(4):
                        cx.op("pe", lambda e, hd=hd: e.matmul(pbA[:CK, hd * CK:(hd + 1) * CK], kt16[:, hd, cs], qt16[:, hd, cs], start=True, stop=True),
                              reads=[kt16.d, qt16.d], writes=[pbA.d], inc=(hd == 3))
                    for hd in range(4):
                        cx.op("pe", lambda e, hd=hd: e.transpose(pbTv[:CK, hd * 128:(hd + 1) * 128], kt16[:, hd, cs], ident16[:]),
                              reads=[kt16.d, ident16.d], writes=[pbT.d], inc=(hd == 3))
                    cx.op("dve", lambda e: e.tensor_tensor(am[:], pbA[:CK, :4 * CK].rearrange("p (h c) -> p h c", h=4),
                                                          maskCK[:, :].unsqueeze(1).broadcast_to([CK, 4, CK]), ALU.mult),
                          reads=[pbA.d, maskCK.d], writes=[am.d])
                    cx.op("act", lambda e: e.copy(ktm[:], pbTv[:CK, :512].rearrange("p (h c) -> p h c", h=4)), reads=[pbT.d], writes=[ktm.d])
                    for hd in range(4):
                        for dc in range(dvc):
                            oc = slice(dc * 256 + ci * CK, dc * 256 + ci * CK + CK)
                            vs = slice(hd * dv + 128 * dc, hd * dv + 128 * dc + 128)
                            cx.op("pe", lambda e, hd=hd, oc=oc, vs=vs: e.matmul(pbo[hd][:, oc], v_tm[:CK, ci, vs], am[:, hd, :], start=True, stop=False),
                                  reads=[v_tm.d, am.d], writes=[pbo[hd].d], inc=False)
                            cx.op("pe", lambda e, hd=hd, oc=oc, dc=dc: e.matmul(pbo[hd][:, oc], S16a[:, hd, 128 * dc:128 * dc + 128], qt16[:, hd, cs],
                                                                                 start=False, stop=True),
                                  reads=[S16a.d, qt16.d], writes=[pbo[hd].d], inc=(dc == dvc - 1))
                    pbS = [bank() for _ in range(4 // hpb)]
                    for hd in range(4):
                        bi, col = hd // hpb, (hd % hpb) * dv
                        cx.op("pe", lambda e, hd=hd, bi=bi, col=col: e.matmul(pbS[bi][:, col:col + dv], ktm[:, hd, :], v_tm[:CK, ci, hd * dv:(hd + 1) * dv],
                                                                             start=True, stop=True),
                              reads=[ktm.d, v_tm.d], writes=[pbS[bi].d], inc=(hd % hpb == hpb - 1))
                    for bi in range(4 // hpb):
                        h0 = bi * hpb
                        S32v = S32a[:, h0:h0 + hpb, :]
                        psv = pbS[bi][:, :].rearrange("p (h d) -> p h d", h=hpb)
                        cx.op("dve", lambda e, S32v=S32v, psv=psv: e.tensor_tensor(S32v, psv, S32v, ALU.add), reads=[pbS[bi].d, S32a.d], writes=[S32a.d])
                        cx.op("dve", lambda e, S32v=S32v, h0=h0: e.tensor_tensor(S32v, S32v, ebl[:, h0:h0 + hpb, ci:ci + 1].broadcast_to([P, hpb, dv]), ALU.mult),
                              reads=[S32a.d, ebl.d], writes=[S32a.d])
                        cx.op("act", lambda e, S32v=S32v, h0=h0: e.copy(S16a[:, h0:h0 + hpb, :], S32v), reads=[S32a.d], writes=[S16a.d])
                return pbo
            mask16 = maskS16 if smp else maskC16
            nchunks = 1 if smp else nch
            CT = n if smp else GC
            for ci in range(nchunks):
                cs = slice(ci * CT, ci * CT + CT)
                for hd in range(4):
                    am, ktm = g["am4"][0][:, hd, :], g["ktm4"][0][:, hd, :]
                    am_d, ktm_d = g["am4"][0].d, g["ktm4"][0].d
                    pbA = bank()
                    cx.op("pe", lambda e, hd=hd, pbA=pbA: e.matmul(pbA[:CT, :CT], kt16[:, hd, cs], qt16[:, hd, cs], start=True, stop=True),
                          reads=[kt16.d, qt16.d], writes=[pbA.d])
                    pbT = bank()
                    pbTv = pbT[:, :].bitcast(BF16)
                    cx.op("pe", lambda e, hd=hd, pbTv=pbTv: e.transpose(pbTv[:CT, :128], kt16[:, hd, cs], ident16[:]),
                          reads=[kt16.d, ident16.d], writes=[pbT.d])
                    cx.op("dve", lambda e, am=am, pbA=pbA: e.tensor_tensor(am[:CT, :CT], pbA[:CT, :CT], mask16[:CT, :CT], ALU.mult),
                          reads=[pbA.d, mask16.d], writes=[am_d])
                    cx.op("act", lambda e, ktm=ktm, pbTv=pbTv: e.copy(ktm[:CT, :], pbTv[:CT, :128]), reads=[pbT.d], writes=[ktm_d])
                    for dc in range(dvc):
                        oc = slice(dc * 256 + ci * CT, dc * 256 + ci * CT + CT)
                        vs = slice(hd * dv + 128 * dc, hd * dv + 128 * dc + 128)
                        if not smp:
                            cx.op("pe", lambda e, hd=hd, oc=oc, vs=vs, am=am: e.matmul(pbo[hd][:, oc], v_tm[:CT, ci, vs], am[:CT, :CT], start=True, stop=False),
                                  reads=[v_tm.d, am_d], writes=[pbo[hd].d], inc=False)
                            cx.op("pe", lambda e, hd=hd, oc=oc, dc=dc: e.matmul(pbo[hd][:, oc], g["S16"][hd][:, 128 * dc:128 * dc + 128], qt16[:, hd, cs],
                                                                                 start=False, stop=True),
                                  reads=[g["S16"][hd].d, qt16.d], writes=[pbo[hd].d], inc=(dc == dvc - 1))
                    if not smp:
                        pbS = bank()
                        cx.op("pe", lambda e, hd=hd, pbS=pbS: e.matmul(pbS[:, :dv], ident[:], g["S32"][hd][:], start=True, stop=False),
                              reads=[ident.d, g["S32"][hd].d], writes=[pbS.d], inc=False)
                        cx.op("pe", lambda e, hd=hd, pbS=pbS, ktm=ktm: e.matmul(pbS[:, :dv], ktm[:CT, :], v_tm[:CT, ci, hd * dv:(hd + 1) * dv], start=False, stop=True),
                              reads=[ktm_d, v_tm.d], writes=[pbS.d])
                        cx.op("act", lambda e, hd=hd, pbS=pbS: e.activation(g["S32"][hd][:], pbS[:, :dv], AF.Copy, scale=ebl[:, hd, ci:ci + 1]),
                              reads=[pbS.d, ebl.d], writes=[g["S32"][hd].d])
                        cx.op("dve", lambda e, hd=hd, pbS=pbS: e.tensor_scalar(g["S16"][hd][:], pbS[:, :dv], ebl[:, hd, ci:ci + 1], None, ALU.mult),
                              reads=[pbS.d, ebl.d], writes=[g["S16"][hd].d])
                    else:
                        kexp = g["kexp"][hd % 2]
                        cx.op("dve", lambda e, kexp=kexp, ktm=ktm: e.tensor_tensor(
                            kexp[:, :, :], ktm[:64, :].unsqueeze(1).broadcast_to([64, NSEQ_S, 128]),
                            msel16[:, :].unsqueeze(2).broadcast_to([64, NSEQ_S, 128]), ALU.mult),
                            reads=[ktm_d, msel16.d], writes=[kexp.d])
                        for b in range(NSEQ_S):
                            r = (hd * NSEQ_S + b) % 4
                            s32, s16, so = g["s32r"][r], g["s16r"][r], g["so"][r]
                            cx.dma("pool", s32[:], st_in[b, hd], writes=[s32.d])
                            cx.dma("pool", s16[:], st_in[b, hd], writes=[s16.d])
                            for dc in range(dvc):
                                oc = slice(dc * 256 + 4 * b, dc * 256 + 4 * b + 4)
                                vs = slice(hd * dv + 128 * dc, hd * dv + 128 * dc + 128)
                                cx.op("pe", lambda e, hd=hd, oc=oc, dc=dc, s16=s16, b=b: e.matmul(
                                    pbo[hd][:, oc], s16[:, 128 * dc:128 * dc + 128], qt16[:, hd, 4 * b:4 * b + 4], start=True, stop=False),
                                    reads=[s16.d, qt16.d], writes=[pbo[hd].d], inc=False)
                                cx.op("pe", lambda e, hd=hd, oc=oc, vs=vs, am=am, b=b: e.matmul(
                                    pbo[hd][:, oc], v_tm[:CT, ci, vs], am[:CT, 4 * b:4 * b + 4], start=False, stop=True),
                                    reads=[v_tm.d, am_d], writes=[pbo[hd].d], inc=(dc == dvc - 1 and b == NSEQ_S - 1))
                            pbS = bank()
                            cx.op("pe", lambda e, hd=hd, pbS=pbS, kexp=kexp, b=b: e.matmul(pbS[:, :dv], kexp[:, b, :], v_tm[:CT, ci, hd * dv:(hd + 1) * dv],
                                                                                        start=True, stop=True),
                                  reads=[kexp.d, v_tm.d], writes=[pbS.d])
                            cx.op("act", lambda e, hd=hd, s32=s32, b=b: e.activation(s32[:], s32[:], AF.Copy, scale=ebl[:, hd, b:b + 1]),
                                  reads=[s32.d, ebl.d], writes=[s32.d])
                            cx.op("dve", lambda e, hd=hd, pbS=pbS, so=so, b=b, s32=s32: e.scalar_tensor_tensor(so[:], pbS[:, :dv], ebl[:, hd, b:b + 1], s32[:],
                                                                                                   ALU.mult, ALU.add),
                                  reads=[pbS.d, ebl.d, s32.d], writes=[so.d])
                            cx.dma("sp", st_out[b, hd], so[:], reads=[so.d], writes=[outdep])
            return pbo

        def gla_post(g, n, dvc, pbo, gate32, gn, on16, on_c0):
            dv = 128 * dvc
            for hd in range(4):
                o32, osq, rstd = g["o32"][hd % 2], g["osq"][hd % 2], g["rstd"][hd % 2]
                pv_ = pbo[hd][:, :].rearrange("p (d c) -> p d c", d=2)[:, :dvc, :n]
                cx.op("act", lambda e, o32=o32, pv_=pv_: e.copy(o32[:, :dvc, :n], pv_), reads=[pbo[hd].d], writes=[o32.d])
                cx.op("act", lambda e, osq=osq, pv_=pv_: e.activation(osq[:, :dvc, :n], pv_, AF.Square), reads=[pbo[hd].d], writes=[osq.d])
                pb = bank()
                for dc in range(dvc):
                    cx.op("pe", lambda e, dc=dc, pb=pb, osq=osq: e.matmul(pb[:, :n], ones16[:], osq[:, dc, :n], start=(dc == 0), stop=(dc == dvc - 1)),
                          reads=[osq.d, ones16.d], writes=[pb.d], inc=(dc == dvc - 1))
                cx.op("act", lambda e, pb=pb, rstd=rstd: e.activation(rstd[:, :n], pb[:, :n], AF.Ln, bias=epsb[:, 0:1], scale=1.0 / dv),
                      reads=[pb.d, epsb.d], writes=[rstd.d])
                cx.op("act", lambda e, rstd=rstd: e.activation(rstd[:, :n], rstd[:, :n], AF.Exp, scale=-0.5), reads=[rstd.d], writes=[rstd.d])
                cx.op("pool", lambda e, o32=o32, rstd=rstd: e.tensor_tensor(o32[:, :dvc, :n], o32[:, :dvc, :n],
                                                                              rstd[:, :n].unsqueeze(1).broadcast_to([P, dvc, n]), ALU.mult),
                      reads=[o32.d, rstd.d], writes=[o32.d])
                for dc in range(dvc):
                    cx.op("dve", lambda e, hd=hd, dc=dc, o32=o32: e.scalar_tensor_tensor(
                        on16[:, on_c0 + hd * dvc + dc, :n], o32[:, dc, :n], gn[:, dc:dc + 1], gate32[:, hd * dvc + dc, :n], ALU.mult, ALU.mult),
                        reads=[o32.d, gn.d, gate32.d], writes=[on16.d])
            for b_ in pbo:
                release(b_)

        def a1_phase():
            NT = 256
            with contextlib.ExitStack() as ph:
                w = sb("w_in_c", [P, KD, 3088], BF16, ph)
                wv = I["w_in_c"].rearrange("k p n -> p k n")
                for c0 in range(0, 3088, 772):
                    cx.dma("pool", w[:, :, c0:c0 + 772], wv[:, :, c0:c0 + 772], writes=[w.d])
                wout = sb("w_out_c", [P, KD, D], BF16, ph)
                cx.dma("pool", wout[:], I["w_out_c"].rearrange("k p n -> p k n"), writes=[wout.d])
                wgu = sb("wgu", [16, 512], BF16, ph)
                cx.dma("pool", wgu[:], I["gla_wgu"], writes=[wgu.d])
                gsm = sb("gsm", [P, 6], F32, ph)
                cx.dma("sp", gsm[:], I["gla_small"], writes=[gsm.d])
                gnb = sb("gnb", [P, 2], F32, ph)
                cx.op("dve", lambda e: e.tensor_copy(gnb[:], gsm[:, 4:6]), reads=[gsm.d], writes=[gnb.d])
                nbg = sb("nbg", [P, 4], F32, ph)
                cx.op("dve", lambda e: e.tensor_scalar(nbg[:], gsm[:, 0:4], -1.0, None, ALU.mult), reads=[gsm.d], writes=[nbg.d])
                scr = {"sq16": sb("a_sq16", [P, KD, NT], BF16, ph), "rstd": sb("a_rstd", [P, NT], F32, ph), "eps": epsb}
                h = sb("a_h", [P, KD, NT], BF16, ph)
                q32 = sb("a_q32", [P, 4, NT], F32, ph)
                k32 = sb("a_k32", [P, 4, NT], F32, ph)
                lg32 = sb("a_lg32", [P, 4, NT], F32, ph)
                sg32 = lg32
                gate32 = sb("a_gate16", [P, 8, NT], BF16, ph)
                gd16 = sb("a_gd16", [16, NT], BF16, ph)
                CK1 = 128
                v_tm = sb("a_vtm", [CK1, NT // CK1, 1024], BF16, ph)
                cm2 = sb("a_cm2", [P, 384], F32, ph)
                cx.dma("sp", cm2[:], I["cmask2"], writes=[cm2.d])
                mask128 = sb("a_mask128", [P, 128], BF16, ph)
                cx.op("dve", lambda e: e.tensor_copy(mask128[:], cm2[:, 0:128]), reads=[cm2.d], writes=[mask128.d])
                mres128 = View(cm2[:, 128:384])
                mres128.d = cm2.d
                on16 = sb("a_on16", [P, 8, NT], BF16, ph)
                g = gla_alloc(ph, 2, NT, CK=CK1)
                ev = [0]

                def evac(fn_act, fn_dve, reads, writes):
                    ev[0] += 1
                    if ev[0] % 2:
                        cx.op("act", fn_act, reads=reads, writes=writes)
                    else:
                        cx.op("dve", fn_dve, reads=reads, writes=writes)

                hb2 = [h, sb("a_h2", [P, KD, NT], BF16, ph)]
                tl = tiles_of(LP, NT)

                def proj(h_, c0, n, M=128):
                    pb = bank()
                    for k in range(KD):
                        cx.op("pe", lambda e, k=k, pb=pb: e.matmul(pb[:M, :n], w[:, k, c0:c0 + M], h_[:, k, :n], start=(k == 0), stop=(k == KD - 1)),
                              reads=[w.d, h_.d], writes=[pb.d], inc=(k == KD - 1))
                    return pb

                def stA(ti):
                    s0, n, smp = tl[ti]
                    h_ = hb2[ti % 2]
                    rmsnorm(lambda k: x[:, k, s0:s0 + n], xdeps(s0, n), n, 1, lambda k: h_[:, k, :n], [h_.d], scr, src_all=x[:, :, s0:s0 + n])

                def stB(ti):
                    s0, n, smp = tl[ti]
                    h_ = hb2[ti % 2]
                    for hd in range(4):
                        pb = proj(h_, 128 * hd, n)
                        cx.op("act", lambda e, hd=hd, pb=pb: e.mul(q32[:, hd, :n], pb[:, :n], 128.0 ** -0.5), reads=[pb.d], writes=[q32.d])
                        pb = proj(h_, 512 + 128 * hd, n)
                        cx.op("dve", lambda e, hd=hd, pb=pb: e.tensor_copy(k32[:, hd, :n], pb[:, :n]), reads=[pb.d], writes=[k32.d])
                    pb = proj(h_, 3072, n, M=16)
                    cx.op("dve", lambda e, pb=pb: e.tensor_copy(gd16[:, :n], pb[:16, :n]), reads=[pb.d], writes=[gd16.d])
                    for hd in range(4):
                        pb = bank()
                        cx.op("pe", lambda e, hd=hd, pb=pb: e.matmul(pb[:, :n], wgu[:, 128 * hd:128 * hd + 128], gd16[:, :n], start=True, stop=True),
                              reads=[wgu.d, gd16.d], writes=[pb.d])
                        cx.op("act", lambda e, hd=hd, pb=pb: e.activation(sg32[:, hd, :n], pb[:, :n], AF.Exp, bias=nbg[:, hd:hd + 1], scale=-1.0),
                              reads=[pb.d, nbg.d], writes=[sg32.d])
                        cx.op("act", lambda e, hd=hd: e.activation(lg32[:, hd, :n], sg32[:, hd, :n], AF.Ln, bias=1.0), reads=[sg32.d], writes=[lg32.d])

                def stC(ti):
                    s0, n, smp = tl[ti]
                    gla_prep(g, n, smp, -1.0 / 16.0, q32, k32, lg32, CK=CK1, mres_p=mres128)

                def stDr(ti):
                    s0, n, smp = tl[ti]
                    h_ = hb2[ti % 2]
                    for c in range(8):
                        pb = proj(h_, 2048 + 128 * c, n)
                        cx.op("act", lambda e, c=c, pb=pb: e.activation(gate32[:, c, :n], pb[:, :n], AF.Silu), reads=[pb.d], writes=[gate32.d])

                def stDv(ti):
                    s0, n, smp = tl[ti]
                    h_ = hb2[ti % 2]
                    CT = n if smp else CK1
                    for ci in range(n // CT):
                        for cb_ in range(2):
                            pb = bank()
                            for k in range(KD):
                                cx.op("pe", lambda e, k=k, pb=pb, ci=ci, cb_=cb_: e.matmul(
                                    pb[:CT, :512], h_[:, k, ci * CT:ci * CT + CT], w[:, k, 1024 + 512 * cb_:1024 + 512 * cb_ + 512],
                                    start=(k == 0), stop=(k == KD - 1)),
                                    reads=[w.d, h_.d], writes=[pb.d], inc=(k == KD - 1))
                            cx.op("dve", lambda e, pb=pb, ci=ci, cb_=cb_: e.tensor_copy(v_tm[:CT, ci, 512 * cb_:512 * cb_ + 512], pb[:CT, :512]),
                                  reads=[pb.d], writes=[v_tm.d])

                pbos = {}

                def stE(ti):
                    s0, n, smp = tl[ti]
                    pbos[ti] = gla_chunks(g, n, smp, 2, v_tm, I["st_gla"], O["gla_s"], CK=CK1, maskCK=mask128)
                    if ti == len(tl) - 2:
                        for hd in range(4):
                            cx.dma("sp", O["gla_p"][hd], g["S32a"][:, hd, :], reads=[g["S32a"].d], writes=[outdep])

                def stF(ti):
                    s0, n, smp = tl[ti]
                    gla_post(g, n, 2, pbos.pop(ti), gate32, gnb, on16, 0)

                def stG(ti):
                    s0, n, smp = tl[ti]
                    for oc in range(KD):
                        pb = bank()
                        for k in range(KD):
                            cx.op("pe", lambda e, k=k, oc=oc, pb=pb: e.matmul(pb[:, :n], wout[:, k, 128 * oc:128 * oc + 128], on16[:, k, :n],
                                                                            start=(k == 0), stop=(k == KD - 1)),
                                  reads=[wout.d, on16.d], writes=[pb.d], inc=(k == KD - 1))
                        cx.op("dve", lambda e, pb=pb, oc=oc: e.tensor_tensor(x[:, oc, s0:s0 + n], x[:, oc, s0:s0 + n], pb[:, :n], ALU.add),
                              reads=[pb.d] + xdeps(s0, n), writes=xdeps(s0, n))

                NTL = len(tl)
                stA(0); stB(0); stC(0); stDv(0); stDr(0)
                if NTL > 1:
                    stA(1)
                stE(0)
                for ti in range(1, NTL):
                    stB(ti); stDv(ti); stF(ti - 1); stC(ti); stG(ti - 1); stDr(ti)
                    if ti + 1 < NTL:
                        stA(ti + 1)
                    if tl[ti][2]:
                        cx.barrier()
                        g["s32r"] = [View(q32[:, j, :]) for j in range(4)]
                        g["so"] = [View(k32[:, j, :]) for j in range(4)]
                        g["s16r"] = [View(hb2[0][:, j, :]) for j in range(4)]
                    stE(ti)
                stF(NTL - 1); stG(NTL - 1)
                cx.barrier()


        TWO_PI = 2.0 * math.pi

        def a0_phase(NTA, NTB):
            with contextlib.ExitStack() as ph0:
                hall = sb("hall0", [P, KD, T], BF16, ph0)
                hdp = {s0: Dep() for s0 in range(0, T, 128)}

                def hdeps(s0, n):
                    return [hdp[k] for k in range(s0, s0 + n, 128)]
                with contextlib.ExitStack() as ph:
                    scr = {"sq16": sb("a0_sq16", [P, KD, 256], BF16, ph), "rstd": sb("a0_rstd", [P, 256], F32, ph), "eps": epsb}
                    for (s0, n, smp) in tiles_of(LP, 256):
                        rmsnorm(lambda k: x[:, k, s0:s0 + n], xdeps(s0, n), n, 0, lambda k: hall[:, k, s0:s0 + n], hdeps(s0, n), scr, src_all=x[:, :, s0:s0 + n])
                    cx.barrier()
                wv = I["w_in_ab"].rearrange("k p n -> p k n")
                wov = I["w_out_ab"].rearrange("k p n -> p k n")
                if "A0a" in cfg.phases or "A0" in cfg.phases:
                  with contextlib.ExitStack() as ph:
                    NT = NTA
                    w = sb("w_in_a", [P, KD, 2048], BF16, ph)
                    for c0 in range(0, 2048, 512):
                        cx.dma("pool", w[:, :, c0:c0 + 512], wv[:, :, c0:c0 + 512], writes=[w.d])
                    wout = sb("w_out_a", [P, 4, D], BF16, ph)
                    cx.dma("pool", wout[:], wov[:, 0:4, :], writes=[wout.d])
                    hsm = sb("hsm", [P, 13], F32, ph)
                    cx.dma("sp", hsm[:], I["hg_small"], writes=[hsm.d])
                    lbe = sb("lbe", [P, 4, 3], F32, ph)
                    lbs = sb("lbs", [P, 4], F32, ph)
                    lb = sb("lb", [P, 4], F32, ph)
                    oml = sb("oml", [P, 4], F32, ph)
                    gnb = sb("gnb0", [P, 1], F32, ph)
                    cx.op("act", lambda e: e.activation(lbe[:], hsm[:, 0:12].rearrange("p (c j) -> p c j", j=3), AF.Exp), reads=[hsm.d], writes=[lbe.d])
                    cx.op("dve", lambda e: e.tensor_tensor(lbs[:], lbe[:, :, 0], lbe[:, :, 1], ALU.add), reads=[lbe.d], writes=[lbs.d])
                    cx.op("dve", lambda e: e.tensor_tensor(lbs[:], lbs[:], lbe[:, :, 2], ALU.add), reads=[lbe.d, lbs.d], writes=[lbs.d])
                    cx.op("dve", lambda e: e.reciprocal(lbs[:], lbs[:]), reads=[lbs.d], writes=[lbs.d])
                    cx.op("dve", lambda e: e.tensor_tensor(lb[:], lbe[:, :, 0], lbs[:], ALU.mult), reads=[lbe.d, lbs.d], writes=[lb.d])
                    cx.op("dve", lambda e: e.tensor_scalar(oml[:], lb[:], -1.0, 1.0, ALU.mult, ALU.add), reads=[lb.d], writes=[oml.d])
                    cx.op("dve", lambda e: e.tensor_copy(gnb[:], hsm[:, 12:13]), reads=[hsm.d], writes=[gnb.d])
                    q32 = sb("h_q32", [P, 4, NT], F32, ph)
                    k32 = sb("h_k32", [P, 4, NT], F32, ph)
                    fg32 = sb("h_fg32", [P, 4, NT], F32, ph)
                    lg32 = sb("h_lg32", [P, 4, NT], F32, ph)
                    gate32 = sb("h_gate32", [P, 4, NT], F32, ph)
                    v_tm = sb("h_vtm", [64, max(NT // GC, 1), 512], BF16, ph)
                    on16 = sb("h_on16", [P, 4, NT], BF16, ph)
                    g = gla_alloc(ph, 1, NT)

                    def proj(c0, s0, n):
                        pb = bank()
                        for k in range(KD):
                            cx.op("pe", lambda e, k=k, pb=pb: e.matmul(pb[:, :n], w[:, k, c0:c0 + 128], hall[:, k, s0:s0 + n], start=(k == 0), stop=(k == KD - 1)),
                                  reads=[w.d] + hdeps(s0, n), writes=[pb.d], inc=(k == KD - 1))
                        return pb

                    tl = tiles_of(LP, NT)

                    def stB(ti):
                        s0, n, smp = tl[ti]
                        for hd in range(4):
                            pb = proj(128 * hd, s0, n)
                            cx.op("act", lambda e, hd=hd, pb=pb: e.activation(q32[:, hd, :n], pb[:, :n], AF.Silu), reads=[pb.d], writes=[q32.d])
                        for hd in range(4):
                            pb = proj(512 + 128 * hd, s0, n)
                            cx.op("act", lambda e, hd=hd, pb=pb: e.activation(fg32[:, hd, :n], pb[:, :n], AF.Sigmoid), reads=[pb.d], writes=[fg32.d])
                            cx.op("dve", lambda e, hd=hd: e.tensor_scalar(fg32[:, hd, :n], fg32[:, hd, :n], oml[:, hd:hd + 1], lb[:, hd:hd + 1], ALU.mult, ALU.add),
                                  reads=[fg32.d, oml.d, lb.d], writes=[fg32.d])
                        for hd in range(4):
                            cx.op("act", lambda e, hd=hd: e.activation(lg32[:, hd, :n], fg32[:, hd, :n], AF.Ln), reads=[fg32.d], writes=[lg32.d])
                            cx.op("pool", lambda e, hd=hd: e.tensor_scalar(k32[:, hd, :n], fg32[:, hd, :n], -1.0, 1.0, ALU.mult, ALU.add),
                                  reads=[fg32.d], writes=[k32.d])

                    def stC(ti):
                        s0, n, smp = tl[ti]
                        gla_prep(g, n, smp, 1.0, q32, k32, lg32)

                    def stDr(ti):
                        s0, n, smp = tl[ti]
                        for hd in range(4):
                            pb = proj(1536 + 128 * hd, s0, n)
                            cx.op("act", lambda e, hd=hd, pb=pb: e.activation(gate32[:, hd, :n], pb[:, :n], AF.Silu), reads=[pb.d], writes=[gate32.d])

                    def stDv(ti):
                        s0, n, smp = tl[ti]
                        CT = n if smp else GC
                        for ci in range(n // CT):
                            pb = bank()
                            for k in range(KD):
                                cx.op("pe", lambda e, k=k, pb=pb, ci=ci: e.matmul(
                                    pb[:CT, :512], hall[:, k, s0 + ci * CT:s0 + ci * CT + CT], w[:, k, 1024:1536], start=(k == 0), stop=(k == KD - 1)),
                                    reads=[w.d] + hdeps(s0, n), writes=[pb.d], inc=(k == KD - 1))
                            cx.op("dve", lambda e, pb=pb, ci=ci: e.tensor_copy(v_tm[:CT, ci, :], pb[:CT, :512]), reads=[pb.d], writes=[v_tm.d])

                    pbos = {}

                    def stE(ti):
                        s0, n, smp = tl[ti]
                        pbos[ti] = gla_chunks(g, n, smp, 1, v_tm, I["st_hgrn"], O["hgrn_s"])
                        if ti == len(tl) - 2:
                            for hd in range(4):
                                cx.dma("sp", O["hgrn_p"][hd], g["S32a"][:, hd, :], reads=[g["S32a"].d], writes=[outdep])

                    def stF(ti):
                        s0, n, smp = tl[ti]
                        gla_post(g, n, 1, pbos.pop(ti), gate32, gnb, on16, 0)

                    def stG(ti):
                        s0, n, smp = tl[ti]
                        for oc in range(KD):
                            pb = bank()
                            for k in range(4):
                                cx.op("pe", lambda e, k=k, oc=oc, pb=pb: e.matmul(pb[:, :n], wout[:, k, 128 * oc:128 * oc + 128], on16[:, k, :n],
                                                                                start=(k == 0), stop=(k == 3)),
                                      reads=[wout.d, on16.d], writes=[pb.d], inc=(k == 3))
                            cx.op("dve", lambda e, pb=pb, oc=oc: e.tensor_tensor(x[:, oc, s0:s0 + n], x[:, oc, s0:s0 + n], pb[:, :n], ALU.add),
                                  reads=[pb.d] + xdeps(s0, n), writes=xdeps(s0, n))

                    stB(0); stC(0); stDv(0); stDr(0); stE(0)
                    for ti in range(1, len(tl)):
                        stB(ti); stDv(ti); stF(ti - 1); stC(ti); stG(ti - 1); stDr(ti); stE(ti)
                    stF(len(tl) - 1); stG(len(tl) - 1)
                    cx.barrier()
                if "A0b" in cfg.phases or "A0" in cfg.phases:
                  with contextlib.ExitStack() as ph:
                    NT = NTB
                    w = sb("w_in_u", [P, KD, 512], BF16, ph)
                    cx.dma("pool", w[:], wv[:, :, 2048:2560], writes=[w.d])
                    wout = sb("w_out_b", [P, 4, D], BF16, ph)
                    cx.dma("pool", wout[:], wov[:, 4:8, :], writes=[wout.d])
                    wglu = sb("wglu", [P, 4, 512], BF16, ph)
                    cx.dma("pool", wglu[:], I["s5_wglu"].rearrange("k p n -> p k n"), writes=[wglu.d])
                    ssm = sb("ssm", [P, 8], F32, ph)
                    cx.dma("sp", ssm[:], I["s5_small"], writes=[ssm.d])
                    x0 = sb("s5x0", [P, 2, 16, NSEQ_S], F32, ph)
                    cx.dma("sp", x0[:], I["s5_x0"].rearrange("r p s b -> p r s b"), writes=[x0.d])
                    Ere = sb("Ere", [P, 16, NT], F32, ph)
                    Eim = sb("Eim", [P, 16, NT], F32, ph)
                    rho = sb("rho", [P, 16], F32, ph)
                    rhoms = sb("rhoms", [P, 16, NS], F32, ph)
                    BB = [sb(f"BB{r}", [P, 4, 512], BF16, ph) for r in range(2)]
                    CC = [sb(f"CC{r}", [P, 16, 128], BF16, ph) for r in range(2)]
                    xst = sb("xst", [P, 2, 16], F32, ph)
                    xso = sb("xso", [P, 2, 16, NSEQ_S], F32, ph)
                    cx.op("pool", lambda e: e.memset(xst[:], 0.0), writes=[xst.d])
                    ones32 = sb("ones32", [P, P], F32, ph)
                    cx.op("pool", lambda e: e.memset(ones32[:], 1.0), writes=[ones32.d])
                    with contextlib.ExitStack() as ps_:
                        R_S5_MAX_RE = -1e-4

                        def sincos(th, F, tag):
                            outs = []
                            for off, nm_ in ((0.0, "sin"), (0.5 * math.pi, "cos")):
                                a = sb(f"sc_a_{tag}{nm_}", [P, F], F32, ps_)
                                ki = sb(f"sc_k_{tag}{nm_}", [P, F], mybir.dt.int32, ps_)
                                kf = sb(f"sc_kf_{tag}{nm_}", [P, F], F32, ps_)
                                cx.op("dve", lambda e, a=a, off=off: e.tensor_scalar(a[:], th[:], 1.0, off, ALU.mult, ALU.add), reads=[th.d], writes=[a.d])
                                cx.op("dve", lambda e, a=a, kf=kf: e.tensor_scalar(kf[:], a[:], 1.0 / TWO_PI, None, ALU.mult), reads=[a.d], writes=[kf.d])
                                cx.op("dve", lambda e, ki=ki, kf=kf: e.tensor_copy(ki[:], kf[:]), reads=[kf.d], writes=[ki.d])
                                cx.op("dve", lambda e, ki=ki, kf=kf: e.tensor_copy(kf[:], ki[:]), reads=[ki.d], writes=[kf.d])
                                cx.op("dve", lambda e, a=a, kf=kf: e.scalar_tensor_tensor(a[:], kf[:], -TWO_PI, a[:], ALU.mult, ALU.add), reads=[a.d, kf.d], writes=[a.d])
                                cx.op("dve", lambda e, a=a, kf=kf: e.tensor_single_scalar(kf[:], a[:], math.pi, ALU.is_gt), reads=[a.d], writes=[kf.d])
                                cx.op("dve", lambda e, a=a, kf=kf: e.scalar_tensor_tensor(a[:], kf[:], -TWO_PI, a[:], ALU.mult, ALU.add), reads=[a.d, kf.d], writes=[a.d])
                                cx.op("dve", lambda e, a=a, kf=kf: e.tensor_single_scalar(kf[:], a[:], -math.pi, ALU.is_lt), reads=[a.d], writes=[kf.d])
                                cx.op("dve", lambda e, a=a, kf=kf: e.scalar_tensor_tensor(a[:], kf[:], TWO_PI, a[:], ALU.mult, ALU.add), reads=[a.d, kf.d], writes=[a.d])
                                cx.op("dve", lambda e, a=a: e.tensor_scalar(a[:], a[:], 3.14159, -3.14159, ALU.min, ALU.max), reads=[a.d], writes=[a.d])
                                cx.op("act", lambda e, a=a: e.activation(a[:], a[:], AF.Sin), reads=[a.d], writes=[a.d])
                                outs.append(a)
                            return outs

                        sm = sb("s5sm", [P, 3, 16], F32, ps_)
                        cx.dma("sp", sm[:], I["s5_sm"], writes=[sm.d])
                        F_ = 16
                        lr = sb("d_lr", [P, F_], F32, ps_)
                        dt = sb("d_dt", [P, F_], F32, ps_)
                        mag = sb("d_mag", [P, F_], F32, ps_)
                        th = sb("d_th", [P, F_], F32, ps_)
                        li_ = sm[:, 1, :]
                        cx.op("dve", lambda e: e.tensor_scalar_min(lr[:], sm[:, 0, :], R_S5_MAX_RE), reads=[sm.d], writes=[lr.d])
                        cx.op("act", lambda e: e.activation(dt[:], sm[:, 2, :], AF.Exp), reads=[sm.d], writes=[dt.d])
                        cx.op("dve", lambda e: e.tensor_tensor(mag[:], lr[:], dt[:], ALU.mult), reads=[lr.d, dt.d], writes=[mag.d])
                        cx.op("act", lambda e: e.activation(mag[:], mag[:], AF.Exp), reads=[mag.d], writes=[mag.d])
                        cx.op("dve", lambda e: e.tensor_tensor(th[:], li_, dt[:], ALU.mult), reads=[dt.d, sm.d], writes=[th.d])
                        sn, cs_ = sincos(th, F_, "sm")
                        cx.op("dve", lambda e: e.tensor_copy(rho[:], mag[:]), reads=[mag.d], writes=[rho.d])
                        cx.op("dve", lambda e: e.tensor_tensor(rhoms[:], rho[:, :].unsqueeze(2).broadcast_to([P, 16, NS]),
                                                              cm[:, 400:464].unsqueeze(1).broadcast_to([P, 16, NS]), ALU.mult),
                              reads=[rho.d, cm.d], writes=[rhoms.d])
                        Ed = Dep()
                        cx.op("dve", lambda e: e.tensor_copy(Ere[:, :, 0], cs_[:]), reads=[cs_.d], writes=[Ed])
                        cx.op("dve", lambda e: e.tensor_scalar(Eim[:, :, 0], sn[:], -1.0, None, ALU.mult), reads=[sn.d], writes=[Ed])
                        t1 = sb("e_t1", [P, 16, NT // 2], F32, ps_)
                        t2 = sb("e_t2", [P, 16, NT // 2], F32, ps_)
                        m_ = 1
                        while m_ < NT:
                            br = Ere[:, :, m_ - 1:m_].broadcast_to([P, 16, m_])
                            bi = Eim[:, :, m_ - 1:m_].broadcast_to([P, 16, m_])
                            ar, ai = Ere[:, :, 0:m_], Eim[:, :, 0:m_]
                            cx.op("dve", lambda e, ar=ar, br=br, m_=m_: e.tensor_tensor(t1[:, :, :m_], ar, br, ALU.mult), reads=[Ed], writes=[t1.d])
                            cx.op("pool", lambda e, ai=ai, bi=bi, m_=m_: e.tensor_tensor(t2[:, :, :m_], ai, bi, ALU.mult), reads=[Ed], writes=[t2.d])
                            cx.op("dve", lambda e, m_=m_: e.tensor_tensor(Ere[:, :, m_:2 * m_], t1[:, :, :m_], t2[:, :, :m_], ALU.subtract), reads=[t1.d, t2.d], writes=[Ed])
                            cx.op("dve", lambda e, ar=ar, bi=bi, m_=m_: e.tensor_tensor(t1[:, :, :m_], ar, bi, ALU.mult), reads=[Ed], writes=[t1.d])
                            cx.op("pool", lambda e, ai=ai, br=br, m_=m_: e.tensor_tensor(t2[:, :, :m_], ai, br, ALU.mult), reads=[Ed], writes=[t2.d])
                            cx.op("dve", lambda e, m_=m_: e.tensor_tensor(Eim[:, :, m_:2 * m_], t1[:, :, :m_], t2[:, :, :m_], ALU.add), reads=[t1.d, t2.d], writes=[Ed])
                            m_ *= 2
                        Ere.d = Ed
                        Eim.d = Ed
                        are = sb("r_are", [P, F_], F32, ps_)
                        aim = sb("r_aim", [P, F_], F32, ps_)
                        den = sb("r_den", [P, F_], F32, ps_)
                        tz = sb("r_tz", [P, F_], F32, ps_)
                        zz = [sb("r_zre", [P, F_], F32, ps_), sb("r_zim", [P, F_], F32, ps_)]
                        zre, zim = zz
                        cx.op("dve", lambda e: e.tensor_tensor(are[:], mag[:], cs_[:], ALU.mult), reads=[mag.d, cs_.d], writes=[are.d])
                        cx.op("dve", lambda e: e.tensor_scalar_add(are[:], are[:], -1.0), reads=[are.d], writes=[are.d])
                        cx.op("dve", lambda e: e.tensor_tensor(aim[:], mag[:], sn[:], ALU.mult), reads=[mag.d, sn.d], writes=[aim.d])
                        cx.op("dve", lambda e: e.tensor_tensor(den[:], lr[:], lr[:], ALU.mult), reads=[lr.d], writes=[den.d])
                        cx.op("dve", lambda e: e.tensor_tensor(tz[:], li_, li_, ALU.mult), reads=[sm.d], writes=[tz.d])
                        cx.op("dve", lambda e: e.tensor_tensor(den[:], den[:], tz[:], ALU.add), reads=[den.d, tz.d], writes=[den.d])
                        cx.op("dve", lambda e: e.reciprocal(den[:], den[:]), reads=[den.d], writes=[den.d])
                        cx.op("dve", lambda e: e.tensor_tensor(zre[:], are[:], lr[:], ALU.mult), reads=[are.d, lr.d], writes=[zre.d])
                        cx.op("dve", lambda e: e.tensor_tensor(tz[:], aim[:], li_, ALU.mult), reads=[aim.d, sm.d, tz.d], writes=[tz.d])
                        cx.op("dve", lambda e: e.tensor_tensor(zre[:], zre[:], tz[:], ALU.add), reads=[zre.d, tz.d], writes=[zre.d])
                        cx.op("dve", lambda e: e.tensor_tensor(zre[:], zre[:], den[:], ALU.mult), reads=[zre.d, den.d], writes=[zre.d])
                        cx.op("dve", lambda e: e.tensor_tensor(zim[:], aim[:], lr[:], ALU.mult), reads=[aim.d, lr.d], writes=[zim.d])
                        cx.op("dve", lambda e: e.tensor_tensor(tz[:], are[:], li_, ALU.mult), reads=[are.d, sm.d, tz.d], writes=[tz.d])
                        cx.op("dve", lambda e: e.tensor_tensor(zim[:], zim[:], tz[:], ALU.subtract), reads=[zim.d, tz.d], writes=[zim.d])
                        cx.op("dve", lambda e: e.tensor_tensor(zim[:], zim[:], den[:], ALU.mult), reads=[zim.d, den.d], writes=[zim.d])
                        bexp = [sb(f"bexp{r}", [P, 4, 512], F32, ps_) for r in range(2)]
                        for r in range(2):
                            cx.dma("sp", bexp[r][:], I["s5_bexp"][r], writes=[bexp[r].d])
                        dg = [sb(f"dg{i}", [P, P], F32, ps_) for i in range(4)]
                        ta = sb("r_ta", [P, 512], F32, ps_)
                        tb = sb("r_tb", [P, 512], F32, ps_)
                        di = 0
                        for c in range(4):
                            pz = [bank(), bank()]
                            for r in range(2):
                                for m in range(4):
                                    dgt = dg[di % 4]
                                    di += 1
                                    cx.op("dve", lambda e, dgt=dgt, r=r, c=c, m=m: e.tensor_scalar(dgt[:], ident[:], zz[r][:, 4 * c + m:4 * c + m + 1], None, ALU.mult),
                                          reads=[ident.d, zz[r].d], writes=[dgt.d])
                                    cx.op("pe", lambda e, dgt=dgt, r=r, m=m, pz=pz: e.matmul(pz[r][:, 128 * m:128 * m + 128], ones32[:], dgt[:], start=True, stop=True),
                                          reads=[ones32.d, dgt.d], writes=[pz[r].d])
                            cx.op("dve", lambda e, c=c, pz=pz: e.tensor_tensor(ta[:], pz[0][:, :], bexp[0][:, c, :], ALU.mult), reads=[pz[0].d, bexp[0].d], writes=[ta.d])
                            cx.op("dve", lambda e, c=c, pz=pz: e.tensor_tensor(tb[:], pz[1][:, :], bexp[1][:, c, :], ALU.mult), reads=[pz[1].d, bexp[1].d], writes=[tb.d])
                            cx.op("dve", lambda e, c=c: e.tensor_tensor(BB[0][:, c, :], ta[:], tb[:], ALU.subtract), reads=[ta.d, tb.d], writes=[BB[0].d])
                            cx.op("dve", lambda e, c=c, pz=pz: e.tensor_tensor(ta[:], pz[0][:, :], bexp[1][:, c, :], ALU.mult), reads=[pz[0].d, bexp[1].d, ta.d], writes=[ta.d])
                            cx.op("dve", lambda e, c=c, pz=pz: e.tensor_tensor(tb[:], pz[1][:, :], bexp[0][:, c, :], ALU.mult), reads=[pz[1].d, bexp[0].d, tb.d], writes=[tb.d])
                            cx.op("dve", lambda e, c=c: e.tensor_tensor(BB[1][:, c, :], ta[:], tb[:], ALU.add), reads=[ta.d, tb.d], writes=[BB[1].d])
                        cexp = sb("cexp", [P, 16, 128], F32, ps_)
                        cx.dma("sp", cexp[:], I["s5_cexp"][0], writes=[cexp.d])
                        cx.op("act", lambda e: e.copy(CC[0][:], cexp[:]), reads=[cexp.d], writes=[CC[0].d])
                        cexp2 = cexp
                        cx.dma("sp", cexp2[:], I["s5_cexp"][1], writes=[cexp2.d])
                        cx.op("act", lambda e: e.mul(CC[1][:], cexp2[:], -1.0), reads=[cexp2.d], writes=[CC[1].d])
                        cx.barrier()
                    RG = 3
                    u32 = [sb(f"s_u32{i}", [P, 4, NT], F32, ph) for i in range(2)]
                    u16 = [sb(f"s_u16{i}", [P, 4, NT], BF16, ph) for i in range(2)]
                    tt = [[sb(f"s_tt{i}_{j}", [P, 2, NT], F32, ph) for j in range(4)] for i in range(1)] * 2
                    wre = [sb(f"s_wre{i}", [P, 2, NT], F32, ph) for i in range(1)] * 2
                    wim = [sb(f"s_wim{i}", [P, 2, NT], F32, ph) for i in range(1)] * 2
                    zre_ = [sb(f"s_zre{i}", [P, 2, NT], F32, ph) for i in range(RG)]
                    zim_ = [sb(f"s_zim{i}", [P, 2, NT], F32, ph) for i in range(RG)]
                    pp = [[sb(f"s_pp{i}_{j}", [P, 2, NT], F32, ph) for j in range(4)] for i in range(2)]
                    xre32 = [sb(f"s_xre{i}", [P, 2, NT], F32, ph) for i in range(2)]
                    xim32 = [sb(f"s_xim{i}", [P, 2, NT], F32, ph) for i in range(2)]
                    xre16 = [sb(f"s_xre16{i}", [P, 4, NT], BF16, ph) for i in range(2)]
                    xim16 = [sb(f"s_xim16{i}", [P, 4, NT], BF16, ph) for i in range(2)]
                    y32 = sb("s_y32", [P, 4, NT], F32, ph)
                    gt = sb("s_gt", [P, 4, NT], F32, ph)
                    yg32 = sb("s_yg32", [P, 4, NT], F32, ph)
                    yg16 = sb("s_yg16", [P, 4, NT], BF16, ph)
                    sgl = sb("s_sgl", [P, NT], F32, ph)
                    on16 = sb("s_on16", [P, 4, NT], BF16, ph)
                    tmpx = sb("s_tmpx", [P, NSEQ_S], F32, ph)
                    tl = tiles_of(LP, NT)

                    def uproj(ti):
                        s0, n, smp = tl[ti]
                        for c in range(4):
                            pb = bank()
                            for k in range(KD):
                                cx.op("pe", lambda e, k=k, pb=pb, c=c: e.matmul(pb[:, :n], w[:, k, 128 * c:128 * c + 128], hall[:, k, s0:s0 + n],
                                                                               start=(k == 0), stop=(k == KD - 1)),
                                      reads=[w.d] + hdeps(s0, n), writes=[pb.d], inc=(k == KD - 1))
                            cx.op("act", lambda e, pb=pb, c=c: e.copy(u32[ti % 2][:, c, :n], pb[:, :n]), reads=[pb.d], writes=[u32[ti % 2].d])
                            cx.op("act", lambda e, pb=pb, c=c: e.copy(u16[ti % 2][:, c, :n], pb[:, :n]), reads=[pb.d], writes=[u16[ti % 2].d])

                    def geom(ti, s_):
                        s0, n, smp = tl[ti]
                        nb, L = (NSEQ_S, LS) if smp else (1, n)
                        if smp:
                            er = Ere[:, s_:s_ + 2, 0:L].unsqueeze(2).broadcast_to([P, 2, nb, L])
                            ei = Eim[:, s_:s_ + 2, 0:L].unsqueeze(2).broadcast_to([P, 2, nb, L])
                            v3 = lambda ap: ap.rearrange("p c (b j) -> p c b j", b=nb)
                        else:
                            er = Ere[:, s_:s_ + 2, 0:n]
                            ei = Eim[:, s_:s_ + 2, 0:n]
                            v3 = lambda ap: ap
                        return s0, n, smp, nb, L, er, ei, v3

                    def stageA(it, ti, s_):
                        s0, n, smp, nb, L, er, ei, v3 = geom(ti, s_)
                        pbk = bank()
                        ut = u16[ti % 2]
                        for ri in range(2):
                            for q in range(2):
                                s2 = s_ + q
                                c, m = s2 // 4, s2 % 4
                                col = (2 * ri + q) * NT
                                cx.op("pe", lambda e, ri=ri, c=c, m=m, col=col: e.matmul(pbk[:, col:col + n], BB[ri][:, c, 128 * m:128 * m + 128], ut[:, c, :n],
                                                                                        start=True, stop=True),
                                      reads=[BB[ri].d, ut.d], writes=[pbk.d], inc=(ri == 1 and q == 1))
                        pk = pbk[:, :].rearrange("p (r q t) -> p r q t", r=2, q=2)
                        pbr, pbi = pk[:, 0, :, :n], pk[:, 1, :, :n]
                        t_ = tt[it % 2]
                        wr_, wi_ = wre[it % 2], wim[it % 2]
                        cx.op("dve", lambda e: e.tensor_tensor(v3(t_[0][:, :, :n]), er, v3(pbr), ALU.mult), reads=[Ere.d, pbk.d], writes=[t_[0].d])
                        cx.op("dve", lambda e: e.tensor_tensor(v3(t_[1][:, :, :n]), ei, v3(pbi), ALU.mult), reads=[Eim.d, pbk.d], writes=[t_[1].d])
                        cx.op("dve", lambda e: e.tensor_tensor(v3(t_[2][:, :, :n]), er, v3(pbi), ALU.mult), reads=[Ere.d, pbk.d], writes=[t_[2].d])
                        cx.op("dve", lambda e: e.tensor_tensor(v3(t_[3][:, :, :n]), ei, v3(pbr), ALU.mult), reads=[Eim.d, pbk.d], writes=[t_[3].d])

                    def stageA2(it, ti, s_):
                        s0, n, smp, nb, L, er, ei, v3 = geom(ti, s_)
                        t_ = tt[it % 2]
                        wr_, wi_ = wre[it % 2], wim[it % 2]
                        cx.op("dve", lambda e: e.tensor_tensor(wr_[:, :, :n], t_[0][:, :, :n], t_[1][:, :, :n], ALU.subtract), reads=[t_[0].d, t_[1].d], writes=[wr_.d])
                        cx.op("dve", lambda e: e.tensor_tensor(wi_[:, :, :n], t_[2][:, :, :n], t_[3][:, :, :n], ALU.add), reads=[t_[2].d, t_[3].d], writes=[wi_.d])

                    def stageA3(it, ti, s_):
                        s0, n, smp, nb, L, er, ei, v3 = geom(ti, s_)
                        wr_, wi_ = wre[it % 2], wim[it % 2]
                        zr_, zi_ = zre_[it % RG], zim_[it % RG]
                        for q in range(2):
                            s2 = s_ + q
                            if smp:
                                for ri, wt in ((0, wr_), (1, wi_)):
                                    cx.op("dve", lambda e, ri=ri, s2=s2: e.tensor_scalar(tmpx[:], x0[:, ri, s2, :], rho[:, s2:s2 + 1], None, ALU.mult),
                                          reads=[x0.d, rho.d], writes=[tmpx.d])
                                    w3 = wt[:, q, :n].rearrange("p (b j) -> p b j", b=nb)
                                    cx.op("dve", lambda e, w3=w3: e.tensor_tensor(w3[:, :, 0], w3[:, :, 0], tmpx[:], ALU.add), reads=[wt.d, tmpx.d], writes=[wt.d])
                                d0 = rhoms[:, s2, :]
                                cx.op("dve", lambda e, q=q, d0=d0: e.tensor_tensor_scan(zr_[:, q, :n], d0, wr_[:, q, :n], 0.0, ALU.mult, ALU.add),
                                      reads=[rhoms.d, wr_.d], writes=[zr_.d])
                                cx.op("dve", lambda e, q=q, d0=d0: e.tensor_tensor_scan(zi_[:, q, :n], d0, wi_[:, q, :n], 0.0, ALU.mult, ALU.add),
                                      reads=[rhoms.d, wi_.d], writes=[zi_.d])
                            else:
                                d0 = rho[:, s2:s2 + 1].broadcast_to([P, n])
                                cx.op("dve", lambda e, q=q, d0=d0, s2=s2: e.tensor_tensor_scan(zr_[:, q, :n], d0, wr_[:, q, :n], xst[:, 0, s2:s2 + 1], ALU.mult, ALU.add),
                                      reads=[rho.d, wr_.d, xst.d], writes=[zr_.d])
                                cx.op("dve", lambda e, q=q, d0=d0, s2=s2: e.tensor_tensor_scan(zi_[:, q, :n], d0, wi_[:, q, :n], xst[:, 1, s2:s2 + 1], ALU.mult, ALU.add),
                                      reads=[rho.d, wi_.d, xst.d], writes=[zi_.d])

                    def stageB(it, ti, s_):
                        s0, n, smp, nb, L, er, ei, v3 = geom(ti, s_)
                        zr_, zi_ = zre_[it % RG], zim_[it % RG]
                        p_ = pp[it % 2]
                        cx.op("pool", lambda e: e.tensor_tensor(v3(p_[0][:, :, :n]), er, v3(zr_[:, :, :n]), ALU.mult), reads=[Ere.d, zr_.d], writes=[p_[0].d])
                        cx.op("pool", lambda e: e.tensor_tensor(v3(p_[1][:, :, :n]), ei, v3(zi_[:, :, :n]), ALU.mult), reads=[Eim.d, zi_.d], writes=[p_[1].d])
                        cx.op("pool", lambda e: e.tensor_tensor(v3(p_[2][:, :, :n]), er, v3(zi_[:, :, :n]), ALU.mult), reads=[Ere.d, zi_.d], writes=[p_[2].d])
                        cx.op("pool", lambda e: e.tensor_tensor(v3(p_[3][:, :, :n]), ei, v3(zr_[:, :, :n]), ALU.mult), reads=[Eim.d, zr_.d], writes=[p_[3].d])

                    def stageC(it, ti, s_):
                        s0, n, smp, nb, L, er, ei, v3 = geom(ti, s_)
                        c, m = s_ // 4, s_ % 4
                        p_ = pp[it % 2]
                        xr_, xi_ = xre32[it % 2], xim32[it % 2]
                        x16r, x16i = xre16[(ti * 4 + c) % 2], xim16[(ti * 4 + c) % 2]
                        cx.op("pool", lambda e: e.tensor_tensor(xr_[:, :, :n], p_[0][:, :, :n], p_[1][:, :, :n], ALU.add), reads=[p_[0].d, p_[1].d], writes=[xr_.d])
                        cx.op("pool", lambda e: e.tensor_tensor(xi_[:, :, :n], p_[2][:, :, :n], p_[3][:, :, :n], ALU.subtract), reads=[p_[2].d, p_[3].d], writes=[xi_.d])
                        cx.op("act", lambda e: e.copy(x16r[:, m:m + 2, :n], xr_[:, :, :n]), reads=[xr_.d], writes=[x16r.d])
                        cx.op("act", lambda e: e.copy(x16i[:, m:m + 2, :n], xi_[:, :, :n]), reads=[xi_.d], writes=[x16i.d])
                        if smp:
                            for q in range(2):
                                xv = xr_[:, q, :n].rearrange("p (b j) -> p b j", b=nb)
                                cx.op("act", lambda e, q=q, xv=xv: e.copy(xso[:, 0, s_ + q, :], xv[:, :, L - 1]), reads=[xr_.d], writes=[xso.d])
                                xv2 = xi_[:, q, :n].rearrange("p (b j) -> p b j", b=nb)
                                cx.op("act", lambda e, q=q, xv2=xv2: e.copy(xso[:, 1, s_ + q, :], xv2[:, :, L - 1]), reads=[xi_.d], writes=[xso.d])
                        else:
                            cx.op("act", lambda e: e.copy(xst[:, 0, s_:s_ + 2], xr_[:, :, n - 1]), reads=[xr_.d], writes=[xst.d])
                            cx.op("act", lambda e: e.copy(xst[:, 1, s_:s_ + 2], xi_[:, :, n - 1]), reads=[xi_.d], writes=[xst.d])
                        if m == 2:
                            pby = bank()
                            for mm_ in range(4):
                                cx.op("pe", lambda e, mm_=mm_: e.matmul(pby[:, :n], CC[0][:, 4 * c + mm_, :], x16r[:, mm_, :n], start=(mm_ == 0), stop=False),
                                      reads=[CC[0].d, x16r.d], writes=[pby.d], inc=False)
                                cx.op("pe", lambda e, mm_=mm_: e.matmul(pby[:, :n], CC[1][:, 4 * c + mm_, :], x16i[:, mm_, :n], start=False, stop=(mm_ == 3)),
                                      reads=[CC[1].d, x16i.d], writes=[pby.d], inc=(mm_ == 3))
                            ut = u32[ti % 2]
                            cx.op("dve", lambda e: e.scalar_tensor_tensor(y32[:, c, :n], ut[:, c, :n], ssm[:, c:c + 1], pby[:, :n], ALU.mult, ALU.add),
                                  reads=[pby.d, ut.d, ssm.d], writes=[y32.d])
                        if s_ == 14:
                            tile_post(ti)

                    def tile_post(ti):
                        s0, n, smp = tl[ti]
                        if ti == len(tl) - 2:
                            cx.dma("sp", O["s5_p"].rearrange("r p s -> p r s"), xst[:], reads=[xst.d], writes=[outdep])
                        cx.op("pool", lambda e: e.tensor_tensor(gt[:, :, :n], y32[:, :, :n], y32[:, :, :n], ALU.mult), reads=[y32.d], writes=[gt.d])
                        cx.op("pool", lambda e: e.tensor_scalar(gt[:, :, :n], gt[:, :, :n], 0.044715, 1.0, ALU.mult, ALU.add), reads=[gt.d], writes=[gt.d])
                        cx.op("pool", lambda e: e.tensor_tensor(gt[:, :, :n], gt[:, :, :n], y32[:, :, :n], ALU.mult), reads=[gt.d, y32.d], writes=[gt.d])
                        cx.op("act", lambda e: e.activation(gt[:, :, :n], gt[:, :, :n], AF.Sigmoid, scale=2.0 * math.sqrt(2.0 / math.pi)), reads=[gt.d], writes=[gt.d])
                        cx.op("dve", lambda e: e.tensor_tensor(yg32[:, :, :n], y32[:, :, :n], gt[:, :, :n], ALU.mult), reads=[gt.d, y32.d], writes=[yg32.d])
                        cx.op("act", lambda e: e.copy(yg16[:, :, :n], yg32[:, :, :n]), reads=[yg32.d], writes=[yg16.d])
                        for c in range(4):
                            pb = bank()
                            for k in range(4):
                                cx.op("pe", lambda e, k=k, pb=pb, c=c: e.matmul(pb[:, :n], wglu[:, k, 128 * c:128 * c + 128], yg16[:, k, :n], start=(k == 0), stop=(k == 3)),
                                      reads=[wglu.d, yg16.d], writes=[pb.d], inc=(k == 3))
                            cx.op("act", lambda e, pb=pb, c=c: e.activation(sgl[:, :n], pb[:, :n], AF.Sigmoid, bias=ssm[:, 4 + c:5 + c]), reads=[pb.d, ssm.d], writes=[sgl.d])
                            cx.op("dve", lambda e, c=c: e.tensor_tensor(on16[:, c, :n], yg32[:, c, :n], sgl[:, :n], ALU.mult), reads=[yg32.d, sgl.d], writes=[on16.d])
                        for oc in range(KD):
                            pb = bank()
                            for k in range(4):
                                cx.op("pe", lambda e, k=k, oc=oc, pb=pb: e.matmul(pb[:, :n], wout[:, k, 128 * oc:128 * oc + 128], on16[:, k, :n], start=(k == 0), stop=(k == 3)),
                                      reads=[wout.d, on16.d], writes=[pb.d], inc=(k == 3))
                            cx.op("dve", lambda e, pb=pb, oc=oc: e.tensor_tensor(x[:, oc, s0:s0 + n], x[:, oc, s0:s0 + n], pb[:, :n], ALU.add),
                                  reads=[pb.d] + xdeps(s0, n), writes=xdeps(s0, n))

                    items = [(ti, s_) for ti in range(len(tl)) for s_ in range(0, 16, 2)]
                    NI = len(items)
                    uproj(0)
                    for i in range(NI + 3):
                        if i < NI:
                            ti, s_ = items[i]
                            if s_ == 8 and ti + 1 < len(tl):
                                uproj(ti + 1)
                            stageA(i, ti, s_)
                        if 0 <= i - 1 < NI:
                            stageA3(i - 1, *items[i - 1])
                        if i < NI:
                            stageA2(i, *items[i])
                        if 0 <= i - 2 < NI:
                            stageB(i - 2, *items[i - 2])
                        if 0 <= i - 3 < NI:
                            stageC(i - 3, *items[i - 3])
                    cx.dma("sp", O["s5_s"].rearrange("r p s b -> p r s b"), xso[:], reads=[xso.d], writes=[outdep])
                    cx.barrier()

        for l in range(2):
            if l == 0 and any(p in cfg.phases for p in ("A0", "A0a", "A0b")):
                a0_phase(cfg.NTA, cfg.NTB)
            if l == 1 and "A1" in cfg.phases:
                a1_phase()
            if f"X{l}" in cfg.phases:
                xattn_phase(l)
            if f"F{l}" in cfg.phases:
                ffn_phase(l)

        with contextlib.ExitStack() as ph:
            scr = {"sq16": sb("n_sq16", [P, KD, 512], BF16, ph), "rstd": sb("n_rstd", [P, 512], F32, ph), "eps": epsb}
            yo = [sb(f"yo{i}", [P, KD, 512], F32, ph) for i in range(2)]
            for i, (s0, n, smp) in enumerate(tiles_of(LP, 512)):
                yt = yo[i % 2]
                rmsnorm(lambda k: x[:, k, s0:s0 + n], xdeps(s0, n), n, 6, lambda k: yt[:, k, :n], [yt.d], scr, src_all=x[:, :, s0:s0 + n])
                cx.dma("sp", O["yT"][:, :, s0:s0 + n].rearrange("k p t -> p k t"), yt[:, :, :n], reads=[yt.d], writes=[outdep])
            cx.barrier()
        cx.barrier(engines=("sp",))
        stats = {k: (e.n_ins, e.cnt) for k, e in cx.E.items()}
    return nc, stats


def fm(v, nchunk):
    v = np.asarray(v)
    lead = v.shape[:-1]
    r = v.reshape(lead + (nchunk, 128))
    return np.ascontiguousarray(np.moveaxis(r, -1, 0))


def prep_core(inp, c, cfg):
    LP, T = cfg.LP, cfg.T
    m = {}
    xp = inp["x_prompt"][c, :LP]
    xs = inp["x_sample"][NSEQ_S * c:NSEQ_S * (c + 1)].reshape(NS, D)
    xT = np.concatenate([xp, xs], axis=0).T
    m["xT"] = np.ascontiguousarray(xT.reshape(KD, 128, T))
    m["ident"] = np.eye(128, dtype=np.float32)
    nl = [inp["norm_mix"][0], inp["norm_mix"][1], inp["norm_cross"][0], inp["norm_cross"][1],
          inp["norm_ffn"][0], inp["norm_ffn"][1], inp["norm_final"]]
    m["norms"] = np.ascontiguousarray(np.stack([v.reshape(KD, 128).T for v in nl], axis=1))
    m["ffn_up"] = np.ascontiguousarray(inp["ffn_w_up"].reshape(2, KD, 128, 2 * FFN))
    m["ffn_dn"] = np.ascontiguousarray(inp["ffn_w_down"].reshape(2, NFC, 128, D))
    cw = inp["ffn_conv_w"].reshape(2, 3, 2 * NFC, 128)
    m["convw"] = np.ascontiguousarray(cw.transpose(0, 3, 2, 1))
    m["convb"] = np.ascontiguousarray(inp["ffn_conv_b"].reshape(2, 2 * NFC, 128).transpose(0, 2, 1))
    cs = inp["state_ffn_conv"][:, NSEQ_S * c:NSEQ_S * (c + 1)]
    cs = cs.reshape(2, NSEQ_S, 2, 2 * NFC, 128)
    m["convst"] = np.ascontiguousarray(cs.transpose(0, 4, 3, 1, 2))
    cmk = np.zeros((128, 464), np.float32)
    jj, ii = np.meshgrid(np.arange(64), np.arange(64), indexing="ij")
    cmk[:64, 0:64] = (jj <= ii)
    cmk[:64, 64:128] = (jj <= ii) & (jj // LS == ii // LS)
    cmk[:64, 128:144] = (np.arange(64)[:, None] // LS == np.arange(NSEQ_S)[None, :])
    cmk[:, 144:400] = (np.arange(256) % GC != 0)[None, :]
    cmk[:, 400:464] = (np.arange(64) % LS != 0)[None, :]
    m["cmask"] = cmk
    cm2 = np.zeros((128, 384), np.float32)
    j2, i2 = np.meshgrid(np.arange(128), np.arange(128), indexing="ij")
    cm2[:, 0:128] = (j2 <= i2)
    cm2[:, 128:384] = (np.arange(256) % 128 != 0)[None, :]
    m["cmask2"] = cm2
    m["w_in_c"] = inp["w_in_c"][0].reshape(KD, 128, 3088)
    m["w_out_c"] = inp["w_out_c"][0].reshape(KD, 128, D)
    m["gla_wgu"] = inp["gla_w_gate_up"][0]
    m["gla_small"] = np.concatenate([inp["gla_b_gate"][0].reshape(4, 128).T, inp["gla_gnorm"][0].reshape(2, 128).T], axis=1)
    m["st_gla"] = inp["state_gla"][0, NSEQ_S * c:NSEQ_S * (c + 1)]
    m["w_in_ab"] = inp["w_in_ab"][0].reshape(KD, 128, 2560)
    m["w_out_ab"] = inp["w_out_ab"][0].reshape(KD, 128, D)
    lbr = inp["hgrn_lb"].reshape(3, 4, 128).transpose(2, 1, 0).reshape(128, 12)
    m["hg_small"] = np.concatenate([lbr, inp["hgrn_gnorm"][0].reshape(128, 1)], axis=1)
    m["st_hgrn"] = inp["state_hgrn"][0, NSEQ_S * c:NSEQ_S * (c + 1)]

    def sm16(a):
        return a.reshape(16, 128).T
    ls_full = np.repeat(inp["s5_log_step"][0][:, None], 64, axis=1)
    m["s5_sm"] = np.stack([sm16(inp["s5_lam_re"][0]), sm16(inp["s5_lam_im"][0]), sm16(ls_full)], axis=1)
    m["s5_row"] = np.stack([inp["s5_lam_re"][0].reshape(2048), inp["s5_lam_im"][0].reshape(2048), ls_full.reshape(2048)], axis=0)
    bexp = np.zeros((2, 128, 4, 512), np.float32)
    cexp = np.zeros((2, 128, 16, 128), np.float32)
    for r, (bsrc, csrc) in enumerate(((inp["s5_b_re"][0], inp["s5_c_re"][0]), (inp["s5_b_im"][0], inp["s5_c_im"][0]))):
        for g_ in range(32):
            cc_, gl = g_ // 8, g_ % 8
            bexp[r, gl * 16:(gl + 1) * 16, cc_, gl * 64:(gl + 1) * 64] = bsrc[g_].T
            s_, two = g_ // 2, g_ % 2
            cexp[r, two * 64:(two + 1) * 64, s_, gl * 16:(gl + 1) * 16] = csrc[g_].T
    m["s5_bexp"] = bexp
    m["s5_cexp"] = cexp
    m["s5_small"] = np.concatenate([inp["s5_d"][0].reshape(4, 128).T, inp["s5_b_glu"][0].reshape(4, 128).T], axis=1)
    m["s5_wglu"] = inp["s5_w_glu"][0].reshape(4, 128, 512)
    x0 = np.stack([inp["state_s5_re"][0, NSEQ_S * c:NSEQ_S * (c + 1)], inp["state_s5_im"][0, NSEQ_S * c:NSEQ_S * (c + 1)]], 0)
    m["s5_x0"] = x0.reshape(2, NSEQ_S, 16, 128).transpose(0, 3, 2, 1)
    m["memT"] = inp["mem_prompt"][c].T.reshape(KD, 128, NMEM)
    m["normmem"] = np.stack([inp["norm_mem"][l].reshape(KD, 128).T for l in range(2)], axis=1)
    m["wkv"] = inp["xa_w_kv"].reshape(2, KD, 128, 2 * D)
    m["wq"] = inp["xa_w_q"].reshape(2, KD, 128, D)
    m["wo"] = inp["xa_w_o"].reshape(2, KD, 128, D)
    ck = inp["cache_mem_k"][:, NSEQ_S * c:NSEQ_S * (c + 1)].reshape(2, NSEQ_S, NMEM, D)
    m["kTc"] = ck.transpose(0, 1, 3, 2).reshape(2, NSEQ_S, KD, 128, NMEM)
    m["vc"] = inp["cache_mem_v"][:, NSEQ_S * c:NSEQ_S * (c + 1)].reshape(2, NSEQ_S, 2, 128, D)
    return {k: np.ascontiguousarray(v, dtype=np.float32) for k, v in m.items()}


_CACHE = {}


def run(inputs, cfg, n_cores=8):
    key = (cfg.LP, tuple(cfg.phases), cfg.xmode)
    if key not in _CACHE:
        _CACHE[key] = build(cfg)
    nc, stats = _CACHE[key]
    in_maps = [prep_core(inputs, c, cfg) for c in range(n_cores)]
    res = run_bass_kernel_spmd(nc, in_maps, core_ids=list(range(n_cores)))
    return res.results, stats


def assemble(results, cfg, n_cores=8):
    LP, T = cfg.LP, cfg.T
    out = {}
    yp, ys, cp, cs = [], [], [], []
    mk, mv = [], []
    glp, gls = [], []
    hgp, hgs, s5p, s5s = [], [], [], []
    for c in range(n_cores):
        r = results[c]
        hgp.append(r["hgrn_p"]); hgs.append(r["hgrn_s"])
        s5p.append(r["s5_p"].transpose(0, 2, 1).reshape(2, 32, 64))
        s5s.append(r["s5_s"].transpose(0, 3, 2, 1).reshape(2, NSEQ_S, 32, 64))
        glp.append(r["gla_p"]); gls.append(r["gla_s"])
        mk.append(r["mem_kT"].reshape(2, D, NMEM).transpose(0, 2, 1).reshape(2, NMEM, 4, 256))
        mv.append(r["mem_v"].reshape(2, NMEM, 4, 256))
        yT = r["yT"].reshape(D, T)
        yp.append(yT[:, :LP].T)
        ys.append(yT[:, LP:].T.reshape(NSEQ_S, LS, D))
        cp.append(r["conv_p"].transpose(0, 3, 2, 1).reshape(2, 2, 2 * FFN))
        cs.append(r["conv_s"].transpose(0, 3, 4, 2, 1).reshape(2, NSEQ_S, 2, 2 * FFN))
    out["hgrn_p"] = np.stack(hgp, 0)[None]
    out["hgrn_s"] = np.concatenate(hgs, 0)[None]
    s5p = np.stack(s5p, 1); s5s = np.concatenate(s5s, 1)
    out["s5_re_p"] = s5p[0][None]; out["s5_im_p"] = s5p[1][None]
    out["s5_re_s"] = s5s[0][None]; out["s5_im_s"] = s5s[1][None]
    out["gla_p"] = np.stack(glp, 0)[None]
    out["gla_s"] = np.concatenate(gls, 0)[None]
    out["mem_k_p"] = np.stack(mk, 1)
    out["mem_v_p"] = np.stack(mv, 1)
    out["y_prompt"] = np.stack(yp, 0)
    out["y_sample"] = np.concatenate(ys, 0)
    out["conv_p"] = np.stack(cp, 1)
    out["conv_s"] = np.concatenate(cs, 1)
    return out


def kernel(**inputs):
    cfg = Cfg()
    inputs = {k: np.asarray(v) for k, v in inputs.items()}
    results, _ = run(inputs, cfg)
    o = assemble(results, cfg)
    names = ["y_prompt", "y_sample", "hgrn_p", "s5_re_p", "s5_im_p", "gla_p", "mem_k_p", "mem_v_p", "conv_p",
             "hgrn_s", "s5_re_s", "s5_im_s", "gla_s", "conv_s"]
    return tuple(np.ascontiguousarray(o[k], dtype=np.float32) for k in names)
```
